# Optimizing a Trainium2 kernel written in Bass

```python
import jax, jax.numpy as jnp
from jax import lax
import numpy as np

D_MODEL = 1024
BATCH = 2
SEQ = 8192
DEPTH = 4

GRID_W = 64
CTX_LEN = 256
HEAD_DIM = 64
RWKV_HEADS = 6
GQA_Q_HEADS = 6
GQA_KV_HEADS = 2
GQA_REP = GQA_Q_HEADS // GQA_KV_HEADS
NA_HEADS = 4
RWKV_W = RWKV_HEADS * HEAD_DIM
GQA_W = GQA_Q_HEADS * HEAD_DIM
GQA_KV_W = GQA_KV_HEADS * HEAD_DIM
NA_W = NA_HEADS * HEAD_DIM
D_MIX = RWKV_W + GQA_W + NA_W
DECAY_RANK = 32
ICLR_RANK = 32
RWKV_CONV_W = 3 * RWKV_W + 2 * DECAY_RANK + 2 * ICLR_RANK
RWKV_SPLIT = (RWKV_W, 2 * RWKV_W, 3 * RWKV_W, 3 * RWKV_W + 2 * DECAY_RANK)
_SPLIT_SIZES = (RWKV_CONV_W, RWKV_W, GQA_W, GQA_KV_W, GQA_KV_W, GQA_W, NA_W, NA_W, NA_W, NA_W)
D_IN = sum(_SPLIT_SIZES)
SPLIT_POINTS = tuple(int(s) for s in np.cumsum(_SPLIT_SIZES)[:-1])
Q_BLOCK = 128
NA_WIN_ROWS = 8
NA_WIN_COLS = 16
ROPE_THETA = 10000.0
DEEPNORM_ALPHA = (2 * DEPTH) ** 0.25
DEEPNORM_BETA = (8 * DEPTH) ** -0.25
LN_EPS = 1e-5
RMS_EPS = 1e-6
GN_EPS = 64e-5

kernel_name = 'hymba_style_rwkv7_gqa_natten_dit'


def _layer_norm(x, g, b):
    xf = x.astype(jnp.float32)
    mu = jnp.mean(xf, axis=-1, keepdims=True)
    var = jnp.mean(jnp.square(xf - mu), axis=-1, keepdims=True)
    return ((xf - mu) * lax.rsqrt(var + LN_EPS) * g + b).astype(x.dtype)


def _rms_norm(x, g):
    xf = x.astype(jnp.float32)
    return (xf * lax.rsqrt(jnp.mean(xf * xf, axis=-1, keepdims=True) + RMS_EPS) * g).astype(x.dtype)


def _centred_conv3(x, w):
    xp = jnp.pad(x, ((0, 0), (1, 1), (0, 0)))
    return w[0] * xp[:, :-2] + w[1] * xp[:, 1:-1] + w[2] * xp[:, 2:]


def _axial_rope_tables(n_tokens):
    t = jnp.arange(n_tokens, dtype=jnp.int32)
    row = (t // GRID_W).astype(jnp.float32)
    col = (t % GRID_W).astype(jnp.float32)
    n_freq = HEAD_DIM // 4
    inv_freq = ROPE_THETA ** (-jnp.arange(n_freq, dtype=jnp.float32) / n_freq)
    ang_r = row[:, None] * inv_freq
    ang_c = col[:, None] * inv_freq
    ang = jnp.concatenate([ang_r, ang_r, ang_c, ang_c], axis=-1)
    return jnp.cos(ang), jnp.sin(ang)


def _apply_axial_rope(x, cos, sin):
    x1, x2, x3, x4 = jnp.split(x, 4, axis=-1)
    rotated = jnp.concatenate([-x2, x1, -x4, x3], axis=-1)
    return (x * cos[:, None] + rotated * sin[:, None]).astype(x.dtype)


def _rwkv_inputs(u, conv_w, decay_w0, decay_w2, iclr_a0, iclr_a2, k_k, k_a):
    u = _centred_conv3(u, conv_w).astype(jnp.float32)
    B, T, _ = u.shape
    r, k, v, dd, ad = jnp.split(u, RWKV_SPLIT, axis=-1)
    dd = jnp.tanh(dd.reshape(B, T, 2, DECAY_RANK))
    ad = ad.reshape(B, T, 2, ICLR_RANK)
    w_log = -jax.nn.softplus(-(decay_w0 + jnp.einsum('btdr,drc->btdc', dd, decay_w2))) - 0.5
    decay = jnp.exp(-jnp.exp(w_log))
    iclr = jax.nn.sigmoid(iclr_a0 + jnp.einsum('btdr,drc->btdc', ad, iclr_a2))
    kk = (k * k_k).reshape(B, T, RWKV_HEADS, HEAD_DIM)
    kk = kk * lax.rsqrt(jnp.maximum(jnp.sum(kk * kk, axis=-1, keepdims=True), 1e-12))
    kk = kk.reshape(B, T, RWKV_W)
    kd = k[:, :, None] * (1.0 + (iclr - 1.0) * k_a)
    bd = kk[:, :, None] * iclr

    def heads_t(z):
        return jnp.moveaxis(z.reshape(z.shape[:-1] + (RWKV_HEADS, HEAD_DIM)), 1, 0)

    return (heads_t(r), heads_t(decay), heads_t(kd), heads_t(v), heads_t(kk), heads_t(bd))


def _rwkv7_step(S, inp):
    r, w, k, v, kk, b = inp
    sa = jnp.einsum('bhvk,bhk->bhv', S, kk)
    S = S * w[:, :, None, :] - sa[..., None] * b[:, :, None, :] + v[..., None] * k[:, :, None, :]
    return S, jnp.einsum('bhvk,bhk->bhv', S, r)


def _rwkv_bidir(rin, S0_f, S0_b):
    r, decay, kd, v, kk, bd = rin
    S_f, y_f = lax.scan(_rwkv7_step, S0_f, (r, decay[:, :, 0], kd[:, :, 0], v, kk, bd[:, :, 0]))
    S_b, y_b = lax.scan(_rwkv7_step, S0_b, (r, decay[:, :, 1], kd[:, :, 1], v, kk, bd[:, :, 1]),
                        reverse=True)
    return S_f, S_b, y_f + y_b


def _rwkv_output(ys, rin, r_k, gn_g, gn_b, dtype):
    r, _, kd, v, _, _ = rin
    T, B = ys.shape[:2]
    mu = jnp.mean(ys, axis=-1, keepdims=True)
    var = jnp.mean(jnp.square(ys - mu), axis=-1, keepdims=True)
    yn = ((ys - mu) * lax.rsqrt(var + GN_EPS)).reshape(T, B, RWKV_W) * gn_g + gn_b
    bonus = jnp.sum(jnp.sum(r[:, :, None] * kd * r_k, axis=-1, keepdims=True), axis=2) * v
    return jnp.moveaxis(yn + bonus.reshape(T, B, RWKV_W), 0, 1).astype(dtype)


def _block_attention(q, keys, vals):
    B, T, G, R, D = q.shape
    qb = jnp.moveaxis(q.reshape(B, T // Q_BLOCK, Q_BLOCK, G, R, D), 1, 0) * (D ** -0.5)

    def one_block(qi):
        s = jnp.einsum('bqgrd,bsgd->bgrqs', qi, keys).astype(jnp.float32)
        p = jax.nn.softmax(s, axis=-1).astype(vals.dtype)
        return jnp.einsum('bgrqs,bsgd->bqgrd', p, vals)

    o = lax.map(one_block, qb)
    return jnp.moveaxis(o, 0, 1).reshape(B, T, G * R * D)


def _neighbourhood_attention(q, k, v, k_ctx, v_ctx, rpb):
    B, L, H, D = q.shape
    rows = L // GRID_W
    win_r = min(NA_WIN_ROWS, rows)
    n_nb = win_r * NA_WIN_COLS
    qg = q.reshape(B, rows, GRID_W, H, D) * (D ** -0.5)
    kg = k.reshape(B, rows, GRID_W, H, D)
    vg = v.reshape(B, rows, GRID_W, H, D)
    cols = jnp.arange(GRID_W, dtype=jnp.int32)
    c_start = jnp.clip(cols - NA_WIN_COLS // 2, 0, GRID_W - NA_WIN_COLS)
    c_idx = c_start[:, None] + jnp.arange(NA_WIN_COLS, dtype=jnp.int32)[None, :]
    dc = c_idx - cols[:, None] + (NA_WIN_COLS - 1)

    def row_block(r):
        r_start = jnp.clip(r - win_r // 2, 0, rows - win_r)
        q_r = lax.dynamic_index_in_dim(qg, r, axis=1, keepdims=False)
        k_nb = lax.dynamic_slice_in_dim(kg, r_start, win_r, axis=1)[:, :, c_idx]
        v_nb = lax.dynamic_slice_in_dim(vg, r_start, win_r, axis=1)[:, :, c_idx]
        dr = r_start + jnp.arange(win_r, dtype=jnp.int32) - r + (NA_WIN_ROWS - 1)
        bias = rpb[:, dr[None, :, None], dc[:, None, :]]
        s_nb = jnp.einsum('bqhd,bpqjhd->bhqpj', q_r, k_nb).astype(jnp.float32) + bias
        s_ctx = jnp.einsum('bqhd,bchd->bhqc', q_r, k_ctx).astype(jnp.float32)
        s = jnp.concatenate([s_nb.reshape(B, H, GRID_W, n_nb), s_ctx], axis=-1)
        p = jax.nn.softmax(s, axis=-1).astype(v.dtype)
        p_nb = p[..., :n_nb].reshape(B, H, GRID_W, win_r, NA_WIN_COLS)
        return (jnp.einsum('bhqpj,bpqjhd->bqhd', p_nb, v_nb)
                + jnp.einsum('bhqc,bchd->bqhd', p[..., n_nb:], v_ctx))

    o = lax.map(row_block, jnp.arange(rows, dtype=jnp.int32))
    return jnp.moveaxis(o, 0, 1).reshape(B, L, H * D)


def _layer(x, ctx, c, c_ctx, rope, w_mod, b_mod, w_in, w_out, rwkv_conv, decay_w0, decay_w2,
           iclr_a0, iclr_a2, k_k, k_a, r_k, gn_g, gn_b, q_norm, k_norm, rpb, ln_g, ln_b, update_ctx):
    B, L, _ = x.shape
    cos, sin = rope
    shift, scale, gate = jnp.split(jax.nn.silu(c) @ w_mod + b_mod, 3, axis=-1)
    shift_c, scale_c, gate_c = jnp.split(jax.nn.silu(c_ctx) @ w_mod + b_mod, 3, axis=-1)
    h_lat = x * (1.0 + scale[:, None]) + shift[:, None]
    h_ctx = ctx * (1.0 + scale_c) + shift_c
    ru_l, rg_l, aq_l, ak_l, av_l, ag_l, nq_l, nk_l, nv_l, ng_l = jnp.split(h_lat @ w_in, SPLIT_POINTS, axis=-1)
    ru_c, rg_c, aq_c, ak_c, av_c, ag_c, nq_c, nk_c, nv_c, ng_c = jnp.split(h_ctx @ w_in, SPLIT_POINTS, axis=-1)

    rwkv_params = (rwkv_conv, decay_w0, decay_w2, iclr_a0, iclr_a2, k_k, k_a)
    rin_c = _rwkv_inputs(ru_c, *rwkv_params)
    rin_l = _rwkv_inputs(ru_l, *rwkv_params)
    S0 = jnp.zeros((B, RWKV_HEADS, HEAD_DIM, HEAD_DIM), jnp.float32)
    S_f, S_b, ys_c = _rwkv_bidir(rin_c, S0, S0)
    _, _, ys_l = _rwkv_bidir(rin_l, S_f, S_b)
    yr_l = _rwkv_output(ys_l, rin_l, r_k, gn_g, gn_b, x.dtype)

    def gqa_kv(kx, vx):
        T = kx.shape[1]
        return (_rms_norm(kx.reshape(B, T, GQA_KV_HEADS, HEAD_DIM), k_norm),
                vx.reshape(B, T, GQA_KV_HEADS, HEAD_DIM))

    def gqa_q(qx):
        return _rms_norm(qx.reshape(B, qx.shape[1], GQA_Q_HEADS, HEAD_DIM), q_norm)

    gk_l, gv_l = gqa_kv(ak_l, av_l)
    gk_l = _apply_axial_rope(gk_l, cos, sin)
    gq_l = _apply_axial_rope(gqa_q(aq_l), cos, sin)
    gk_c, gv_c = gqa_kv(ak_c, av_c)
    ya_l = _block_attention(gq_l.reshape(B, L, GQA_KV_HEADS, GQA_REP, HEAD_DIM),
                            jnp.concatenate([gk_c, gk_l], axis=1), jnp.concatenate([gv_c, gv_l], axis=1))

    def na_heads(u):
        return u.reshape(B, u.shape[1], NA_HEADS, HEAD_DIM)

    nk_c, nv_c = na_heads(nk_c), na_heads(nv_c)
    yn_l = _neighbourhood_attention(na_heads(nq_l), na_heads(nk_l), na_heads(nv_l), nk_c, nv_c, rpb)

    y_l = jnp.concatenate([yr_l * jax.nn.silu(rg_l), ya_l * jax.nn.silu(ag_l),
                           yn_l * jax.nn.silu(ng_l)], axis=-1) @ w_out
    x = _layer_norm(DEEPNORM_ALPHA * x + gate[:, None] * y_l, ln_g, ln_b)

    if update_ctx:
        n_ctx = ctx.shape[1]
        yr_c = _rwkv_output(ys_c, rin_c, r_k, gn_g, gn_b, ctx.dtype)
        ya_c = _block_attention(gqa_q(aq_c).reshape(B, n_ctx, GQA_KV_HEADS, GQA_REP, HEAD_DIM), gk_c, gv_c)
        yn_c = _block_attention(na_heads(nq_c)[:, :, :, None], nk_c, nv_c)
        y_c = jnp.concatenate([yr_c * jax.nn.silu(rg_c), ya_c * jax.nn.silu(ag_c),
                               yn_c * jax.nn.silu(ng_c)], axis=-1) @ w_out
        ctx = _layer_norm(DEEPNORM_ALPHA * ctx + gate_c * y_c, ln_g, ln_b)
    else:
        ctx = None
    return x, ctx


def setup_inputs(seed: int = 0) -> dict:
    key = jax.random.key(seed)
    ks = jax.random.split(key, 24)

    def n(k, shape):
        return jax.random.normal(k, shape, jnp.float32)

    inv_d = D_MODEL ** -0.5
    return {
        'x': n(ks[0], (BATCH, SEQ, D_MODEL)),
        'c': n(ks[1], (BATCH, D_MODEL)),
        'ctx': n(ks[2], (BATCH, CTX_LEN, D_MODEL)),
        'c_ctx': n(ks[3], (D_MODEL,)),
        'w_mod': n(ks[4], (DEPTH, D_MODEL, 3 * D_MODEL)) * (0.5 * inv_d),
        'b_mod': 0.01 * n(ks[5], (DEPTH, 3 * D_MODEL)),
        'w_in': n(ks[6], (DEPTH, D_MODEL, D_IN)) * inv_d,
        'w_out': n(ks[7], (DEPTH, D_MIX, D_MODEL)) * (DEEPNORM_BETA * D_MIX ** -0.5),
        'rwkv_conv': jnp.array([0.25, 0.5, 0.25], jnp.float32)[None, :, None]
                     + 0.05 * n(ks[8], (DEPTH, 3, RWKV_CONV_W)),
        'decay_w0': n(ks[9], (DEPTH, 2, RWKV_W)),
        'decay_w2': 0.1 * n(ks[10], (DEPTH, 2, DECAY_RANK, RWKV_W)),
        'iclr_a0': 0.5 * n(ks[11], (DEPTH, 2, RWKV_W)),
        'iclr_a2': 0.1 * n(ks[12], (DEPTH, 2, ICLR_RANK, RWKV_W)),
        'rwkv_k_k': 0.85 + 0.05 * n(ks[13], (DEPTH, RWKV_W)),
        'rwkv_k_a': 1.0 + 0.05 * n(ks[14], (DEPTH, RWKV_W)),
        'rwkv_r_k': 0.1 * n(ks[15], (DEPTH, RWKV_HEADS, HEAD_DIM)),
        'rwkv_gn_g': 1.0 + 0.05 * n(ks[16], (DEPTH, RWKV_W)),
        'rwkv_gn_b': 0.01 * n(ks[17], (DEPTH, RWKV_W)),
        'gqa_q_norm': 1.0 + 0.05 * n(ks[18], (DEPTH, HEAD_DIM)),
        'gqa_k_norm': 1.0 + 0.05 * n(ks[19], (DEPTH, HEAD_DIM)),
        'na_rpb': 0.1 * n(ks[20], (DEPTH, NA_HEADS, 2 * NA_WIN_ROWS - 1, 2 * NA_WIN_COLS - 1)),
        'ln_g': 1.0 + 0.05 * n(ks[21], (DEPTH, D_MODEL)),
        'ln_b': 0.01 * n(ks[22], (DEPTH, D_MODEL)),
    }


def reference(x, c, ctx, c_ctx, w_mod, b_mod, w_in, w_out, rwkv_conv, decay_w0, decay_w2,
              iclr_a0, iclr_a2, rwkv_k_k, rwkv_k_a, rwkv_r_k, rwkv_gn_g, rwkv_gn_b,
              gqa_q_norm, gqa_k_norm, na_rpb, ln_g, ln_b):
    rope = _axial_rope_tables(x.shape[1])
    for l in range(DEPTH):
        x, ctx = _layer(x, ctx, c, c_ctx, rope, w_mod[l], b_mod[l], w_in[l], w_out[l], rwkv_conv[l],
                        decay_w0[l], decay_w2[l], iclr_a0[l], iclr_a2[l], rwkv_k_k[l], rwkv_k_a[l],
                        rwkv_r_k[l], rwkv_gn_g[l], rwkv_gn_b[l], gqa_q_norm[l], gqa_k_norm[l],
                        na_rpb[l], ln_g[l], ln_b[l], l < DEPTH - 1)
    return x
```

```python
import contextlib
import numpy as np
import concourse.bass as bass
import concourse.mybir as mybir

F32 = mybir.dt.float32
BF16 = mybir.dt.bfloat16
AF = mybir.ActivationFunctionType
ALU = mybir.AluOpType
AX = mybir.AxisListType


class Tk:
    __slots__ = ("w", "r", "name", "excl", "acc")

    def __init__(self, name=""):
        self.w = None
        self.r = {}
        self.name = name
        self.excl = False
        self.acc = {}


class T:
    def __init__(self, S, t, name):
        self.t = t
        self.tk = Tk(name)
        self.name = name

    def __getitem__(self, idx):
        return V(self.t[idx], self.tk)


class V:
    __slots__ = ("ap", "tk")

    def __init__(self, ap, tk):
        self.ap = ap
        self.tk = tk


class S:
    ENG = ("pe", "act", "dve", "pool", "sp")

    def __init__(self, nc):
        self.nc = nc
        self.es = contextlib.ExitStack()
        self.eng = {"pe": nc.tensor, "act": nc.scalar, "dve": nc.vector, "pool": nc.gpsimd, "sp": nc.sync}
        self.sem = {e: self.es.enter_context(nc.semaphore("s_" + e)) for e in self.ENG}
        self.cnt = {e: 0 for e in self.ENG}
        self.dq = {}
        for q in ("sp", "act", "pool"):
            sems = [self.es.enter_context(nc.semaphore("d_%s%d" % (q, i))) for i in range(8)]
            self.dq[q] = dict(sems=sems, cnt=[0] * 8, nxt=0)
            for i, s_ in enumerate(sems):
                self.sem[(q, i)] = s_
        self.waited = {}
        self.n_tiles = 0
        self.n_instr = 0
        self.n_wait = 0

    def sb(self, shape, dt=F32, name=None):
        self.n_tiles += 1
        name = name or "t%d" % self.n_tiles
        t = self.es.enter_context(self.nc.sbuf_tensor("sb_" + name, list(shape), dt))
        return T(self, t, name)

    def ps(self, shape, dt=F32, name=None):
        self.n_tiles += 1
        name = name or "p%d" % self.n_tiles
        t = self.es.enter_context(self.nc.psum_tensor("ps_" + name, list(shape), dt))
        tt_ = T(self, t, name)
        tt_.tk.excl = True
        return tt_

    def close(self):
        self.es.close()

    def _wait(self, e, key, val):
        if val is None:
            return
        k = (e, key)
        if self.waited.get(k, 0) >= val:
            return
        self.waited[k] = val
        self.eng[e].wait_ge(self.sem[key], val)
        self.n_wait += 1

    def _deps(self, e, reads, writes, pe_acc=False):
        for v in list(reads) + list(writes):
            if v.tk.excl:
                for e2, n2 in v.tk.acc.items():
                    if e2 == e and e == "pe":
                        continue
                    self._wait(e, e2, n2)
        reads = [v for v in reads if not v.tk.excl]
        writes = [v for v in writes if not v.tk.excl]
        for v in reads:
            w = v.tk.w
            if w is not None:
                self._wait(e, w[0], w[1])
        for v in writes:
            tk = v.tk
            if tk.w is not None:
                if not (pe_acc and tk.w[0] == "pe" and e == "pe"):
                    self._wait(e, tk.w[0], tk.w[1])
            for re_, rn in tk.r.items():
                if re_ == e and e == "pe":
                    continue
                self._wait(e, re_, rn)

    def _mark(self, ticket, reads, writes):
        for v in list(reads) + list(writes):
            if v.tk.excl:
                v.tk.acc[ticket[0]] = ticket[1]
        reads = [v for v in reads if not v.tk.excl]
        writes = [v for v in writes if not v.tk.excl]
        for v in reads:
            v.tk.r[ticket[0]] = ticket[1]
        for v in writes:
            v.tk.w = ticket
            v.tk.r = {}

    def op(self, e, fn, reads, writes, pe_acc=False):
        reads = [v for v in reads if isinstance(v, V)]
        self._deps(e, reads, writes, pe_acc)
        ins = fn()
        self.cnt[e] += 1
        ins.then_inc(self.sem[e], 1)
        self._mark((e, self.cnt[e]), reads, writes)
        self.n_instr += 1
        return ins

    def dma(self, q, out, in_, **kw):
        d = self.dq[q]
        i = d["nxt"]
        d["nxt"] = (i + 1) % len(d["sems"])
        key = (q, i)
        self._wait(q, key, d["cnt"][i])
        reads = [in_] if isinstance(in_, V) else []
        writes = [out] if isinstance(out, V) else []
        self._deps(q, reads, writes)
        oa = out.ap if isinstance(out, V) else out
        ia = in_.ap if isinstance(in_, V) else in_
        ins = self.eng[q].dma_start(out=oa, in_=ia, **kw)
        d["cnt"][i] += 16
        ins.then_inc(self.sem[key], 16)
        self._mark((key, d["cnt"][i]), reads, writes)
        self.n_instr += 1
        return (key, d["cnt"][i])

    def wait_ticket(self, e, ticket):
        self._wait(e, ticket[0], ticket[1])

    def mm(self, out, lhsT, rhs, start=True, stop=True, **kw):
        return self.op("pe", lambda: self.nc.tensor.matmul(out.ap, lhsT.ap, rhs.ap, start=start, stop=stop, **kw),
                       [lhsT, rhs], [out], pe_acc=not start)

    def tr(self, out, in_, ident):
        return self.op("pe", lambda: self.nc.tensor.transpose(out.ap, in_.ap, ident.ap), [in_, ident], [out])

    def act(self, out, in_, func, bias=None, scale=None, accum_out=None, e="act"):
        kw = {}
        rd = [in_]
        if bias is not None:
            kw["bias"] = bias.ap if isinstance(bias, V) else bias
            rd.append(bias)
        if scale is not None:
            kw["scale"] = scale.ap if isinstance(scale, V) else scale
            rd.append(scale)
        wr = [out]
        if accum_out is not None:
            kw["accum_out"] = accum_out.ap
            wr.append(accum_out)
        return self.op("act", lambda: self.nc.scalar.activation(out.ap, in_.ap, func, **kw), rd, wr)

    def _ve(self, e):
        return {"dve": self.nc.vector, "pool": self.nc.gpsimd, "act": self.nc.scalar}[e]

    def tt(self, out, a, b, op, e="dve"):
        return self.op(e, lambda: self._ve(e).tensor_tensor(out.ap, a.ap, b.ap, op), [a, b], [out])

    def ts(self, out, a, s1, op0, s2=None, op1=None, e="dve", accum_out=None):
        rd = [a, s1, s2]
        a1 = s1.ap if isinstance(s1, V) else s1
        a2 = s2.ap if isinstance(s2, V) else s2
        kw = {}
        wr = [out]
        if op1 is not None:
            kw["op1"] = op1
        if accum_out is not None:
            kw["accum_out"] = accum_out.ap
            wr.append(accum_out)
        return self.op(e, lambda: self._ve(e).tensor_scalar(out.ap, a.ap, a1, a2, op0, **kw), rd, wr)

    def stt(self, out, a, s, b, op0, op1, e="dve"):
        sa = s.ap if isinstance(s, V) else s
        return self.op(e, lambda: self._ve(e).scalar_tensor_tensor(out.ap, a.ap, sa, b.ap, op0, op1), [a, s, b], [out])

    def cp(self, out, in_, e="dve"):
        if e == "act":
            return self.op("act", lambda: self.nc.scalar.copy(out.ap, in_.ap), [in_], [out])
        return self.op(e, lambda: self._ve(e).tensor_copy(out.ap, in_.ap), [in_], [out])

    def memset(self, out, val, e="pool"):
        return self.op(e, lambda: self._ve(e).memset(out.ap, val), [], [out])

    def red(self, out, in_, op, axis=AX.X, e="dve"):
        return self.op(e, lambda: self._ve(e).tensor_reduce(out.ap, in_.ap, axis, op), [in_], [out])

    def recip(self, out, in_):
        return self.op("dve", lambda: self.nc.vector.reciprocal(out.ap, in_.ap), [in_], [out])

    def finish(self, tickets):
        for t in tickets:
            self._wait("sp", t[0], t[1])


A_DEC = 0.6065306597126334
NG_L = 32
C = 64


class _Stop(Exception):
    pass


def build_A(n_lat_groups=NG_L, debug=False, stage=99):
    try:
        return _build_A(n_lat_groups, debug, stage)
    except _Stop as e:
        return e.args[0]


def _build_A(n_lat_groups, debug, stage):
    nc = bass.Bass("TRN2", target_bir_lowering=False)
    dt = nc.dram_tensor
    NT = 256 + 256 * n_lat_groups
    NTP = 258 + 256 * n_lat_groups + 2
    xTp = dt("xTp", [1024, NTP], F32, kind="ExternalInput").ap()
    cv = dt("cv", [128, 16], F32, kind="ExternalInput").ap()
    wms = dt("wms", [1024, 2048], F32, kind="ExternalInput").ap()
    bms = dt("bms", [128, 32], F32, kind="ExternalInput").ap()
    wa = dt("wa", [1024, 640], F32, kind="ExternalInput").ap()
    cw = dt("cw", [64, 33], F32, kind="ExternalInput").ap()
    pv = dt("pv", [64, 15], F32, kind="ExternalInput").ap()
    w2 = dt("w2", [32, 192], F32, kind="ExternalInput").ap()
    a2 = dt("a2", [32, 192], F32, kind="ExternalInput").ap()
    cst = dt("cst", [64, 64 + 960 + 256 + 384], F32, kind="ExternalInput").ap()
    y = dt("y", [NT, 192], F32, kind="ExternalOutput").ap()
    bon = dt("bon", [NT, 192], F32, kind="ExternalOutput").ap()

    s = S(nc)
    def ck(n):
        if stage <= n:
            s.finish([])
            s.close()
            raise _Stop(nc)
    cst_t = s.sb([64, 64 + 960 + 256 + 384], name="cst")
    s.dma("sp", cst_t[:], cst)
    ident = cst_t[:, 0:64]
    def mask3(h):
        return cst_t[:, 64 + h * 320: 64 + (h + 1) * 320]
    rmask = cst_t[:, 1024:1280]
    idt3 = cst_t[:, 1280:1664]
    ones = s.sb([64, 64], name="ones")
    s.memset(ones[:], 1.0)
    cw_t = s.sb([64, 33], name="cw"); s.dma("act", cw_t[:], cw)
    pv_t = s.sb([64, 16], name="pv"); s.dma("act", pv_t[:, 0:15], pv)
    omk = s.sb([64, 3], name="omk")
    for h in range(3):
        s.ts(omk[:, h:h + 1], pv_t[:, h * 5 + 3:h * 5 + 4], -1.0, ALU.mult, 1.0, ALU.add)
    w2_t = s.sb([32, 192], name="w2"); s.dma("act", w2_t[:], w2)
    a2_t = s.sb([32, 192], name="a2"); s.dma("act", a2_t[:], a2)

    pb = [s.ps([128, 512], name="pb%d" % i) for i in range(8)]

    cv_t = s.sb([128, 16], name="cv"); s.dma("sp", cv_t[:], cv)
    scv = s.sb([128, 16], name="scv")
    s.act(scv[:], cv_t[:], AF.Silu)
    bms_t = s.sb([128, 32], name="bms"); s.dma("sp", bms_t[:], bms)
    wm = [s.sb([128, 2048], name="wm%d" % i) for i in range(2)]
    modv = s.sb([128, 32], name="modv")
    for k in range(8):
        wmk = wm[k % 2]
        s.dma("sp" if k % 2 == 0 else "act", wmk[:], wms[k * 128:(k + 1) * 128, :])
        for f in range(16):
            s.mm(pb[0][:, 2 * f:2 * f + 2], wmk[:, f * 128:(f + 1) * 128], scv[:, 2 * k:2 * k + 2])
        s.tt(modv[:], pb[0][:, 0:32], (bms_t if k == 0 else modv)[:], ALU.add)
    sc1 = s.sb([128, 16], name="sc1")
    s.ts(sc1[:], modv[:, 16:32], 1.0, ALU.add)
    ck(1)

    wab = s.sb([128, 8, 640], BF16, name="wab")
    for k in range(8):
        wst = wm[k % 2]
        s.dma("sp" if k % 2 == 0 else "act", wst[:, 0:640], wa[k * 128:(k + 1) * 128, :])
        s.cp(wab[:, k, :], wst[:, 0:640], e="dve" if k % 2 == 0 else "pool")

    ck(2)
    cts = [(i * 64, 64) for i in range(9)] + [(576, 32), (608, 32)]

    xg = [s.sb([128, 8, 258], name="xg%d" % i) for i in range(2)]
    hg = [s.sb([128, 8, 258], BF16, name="hg%d" % i) for i in range(2)]
    raw = [s.sb([64, 258], name="raw%d" % i) for i in range(3)]
    ctmp = [s.sb([64, 256], name="ctmp%d" % i) for i in range(3)]
    NSET = 1
    def mk(nm, shape=(64, 256)):
        return [[s.sb(list(shape), name="%s_%d_%d" % (nm, st, h)) for h in range(3)] for st in range(NSET)]
    uR, uK, uV = mk("uR"), mk("uK"), mk("uV")
    uD = [s.sb([32, 256], name="uD%d" % i) for i in range(NSET)]
    uA = [s.sb([32, 256], name="uA%d" % i) for i in range(NSET)]
    ddt = [s.sb([32, 256], name="ddt%d" % i) for i in range(NSET)]
    sg, ic, kk, tmp, kd, bd = mk("sg"), mk("ic"), mk("kk"), mk("tmp"), mk("kd"), mk("bd")
    cs, csx, csr = mk("cs"), mk("csx"), mk("csr")
    E1, E3, E4 = mk("E1"), mk("E3"), mk("E4")
    RH, KKH, kt, bt, kc, bc, rk = mk("RH"), mk("KKH"), mk("kt"), mk("bt"), mk("kc"), mk("bc"), mk("rk")
    wc = mk("wc", (64, 4))
    rn = mk("rn")

    trT = [s.sb([64, 3, 256], name="trT%d" % i) for i in range(2)]
    scS = [s.sb([64, 3, 320], name="scS%d" % i) for i in range(2)]
    XY = [s.sb([64, 3, 128], name="XY%d" % i) for i in range(2)]
    PQ = [s.sb([64, 3, 128], name="PQ%d" % i) for i in range(2)]
    KKpT = s.sb([64, 192], name="KKpT")
    AV = s.sb([64, 192], name="AV")
    Uloc = s.sb([64, 192], name="Uloc")
    U = s.sb([64, 192], name="U")
    ST = [s.sb([64, 192], name="ST%d" % i) for i in range(2)]
    Ybuf = [s.sb([64, 4, 192], name="Ybuf%d" % i) for i in range(2)]
    Bbuf = [s.sb([64, 4, 192], name="Bbuf%d" % i) for i in range(2)]
    bs = s.sb([64, 4], name="bs")
    s.memset(ST[0][:], 0.0)
    sti = 0
    out_tickets = []

    groups = [(1, 0, 0, True, True)] + [(0, 258 + 256 * g, 256 + 256 * g, g == 0, g == n_lat_groups - 1)
                                         for g in range(n_lat_groups)]
    chunk_no = 0
    for gi, (j, cbase, tbase, first, last) in enumerate(groups):
        st = gi % NSET
        xgt, hgt = xg[gi % 2], hg[gi % 2]
        s.dma("sp" if gi % 2 == 0 else "act", xgt[:], xTp.rearrange("(k p) t -> p k t", p=128)[:, :, cbase:cbase + 258])
        for k in range(8):
            s.act(hgt[:, k, :], xgt[:, k, :], AF.Identity, bias=modv[:, 2 * k + j:2 * k + j + 1],
                  scale=sc1[:, 2 * k + j:2 * k + j + 1])
        for ci, (c0, M) in enumerate(cts):
            pr = pb[ci % 2]
            for k in range(8):
                s.mm(pr[0:M, 0:258], wab[:, k, c0:c0 + M], hgt[:, k, :], start=(k == 0), stop=(k == 7))
            rw = raw[ci % 3]
            if ci % 2 == 0:
                s.cp(rw[0:M, :], pr[0:M, 0:258], e="act")
            else:
                s.cp(rw[0:M, :], pr[0:M, 0:258], e="dve")
            if first:
                s.memset(rw[0:M, 0:1], 0.0, e="pool")
            if last:
                s.memset(rw[0:M, 257:258], 0.0, e="pool")
            if ci < 9:
                dst = (uR, uK, uV)[ci // 3][st][ci % 3]
            else:
                dst = (uD, uA)[ci - 9][st]
            tm = ctmp[ci % 3]
            s.act(tm[0:M, :], rw[0:M, 0:256], AF.Identity, scale=cw_t[0:M, ci * 3:ci * 3 + 1])
            s.stt(tm[0:M, :], rw[0:M, 1:257], cw_t[0:M, ci * 3 + 1:ci * 3 + 2], tm[0:M, :], ALU.mult, ALU.add, e="dve")
            s.stt(dst[0:M, :], rw[0:M, 2:258], cw_t[0:M, ci * 3 + 2:ci * 3 + 3], tm[0:M, :], ALU.mult, ALU.add, e="dve")
        ck(3)
        s.act(ddt[st][:], uD[st][:], AF.Tanh)
        for h in range(3):
            P = lambda i: pv_t[:, h * 5 + i:h * 5 + i + 1]
            pz = pb[2]
            s.mm(pz[0:64, 0:256], w2_t[:, h * 64:(h + 1) * 64], ddt[st][:])
            s.act(sg[st][h][:], pz[0:64, 0:256], AF.Sigmoid, bias=P(0))
            s.mm(pz[0:64, 256:512], a2_t[:, h * 64:(h + 1) * 64], uA[st][:])
            s.act(ic[st][h][:], pz[0:64, 256:512], AF.Sigmoid, bias=P(1))
            s.ts(kk[st][h][:], uK[st][h][:], P(2), ALU.mult, e="pool")
            s.tt(tmp[st][h][:], kk[st][h][:], kk[st][h][:], ALU.mult, e="pool")
            pss = pb[3]
            s.mm(pss[0:64, 0:256], ones[:, :], tmp[st][h][:])
            s.ts(rn[st][h][:], pss[0:64, 0:256], 1e-12, ALU.max)
            s.act(rn[st][h][:], rn[st][h][:], AF.Sqrt)
            s.recip(rn[st][h][:], rn[st][h][:])
            s.tt(kk[st][h][:], kk[st][h][:], rn[st][h][:], ALU.mult)
            s.ts(tmp[st][h][:], ic[st][h][:], P(3), ALU.mult, omk[:, h:h + 1], ALU.add, e="pool")
            s.tt(kd[st][h][:], uK[st][h][:], tmp[st][h][:], ALU.mult, e="pool")
            s.tt(bd[st][h][:], kk[st][h][:], ic[st][h][:], ALU.mult, e="pool")
            s.op("dve", lambda: nc.vector.tensor_tensor_scan(cs[st][h][:].ap, rmask.ap, sg[st][h][:].ap, 0.0, ALU.mult, ALU.add),
                 [rmask, sg[st][h][:]], [cs[st][h][:]])
            s.tt(csx[st][h][:], cs[st][h][:], sg[st][h][:], ALU.subtract)
            for c in range(4):
                s.ts(csr[st][h][:, c * 64:(c + 1) * 64], cs[st][h][:, c * 64:(c + 1) * 64],
                     cs[st][h][:, c * 64 + 63:c * 64 + 64], ALU.subtract)
            s.act(wc[st][h][:], cs[st][h][:, 63::64], AF.Exp, scale=-A_DEC)
            s.act(E1[st][h][:], cs[st][h][:], AF.Exp, scale=-A_DEC)
            s.tt(RH[st][h][:], uR[st][h][:], E1[st][h][:], ALU.mult, e="pool")
            s.act(E1[st][h][:], csx[st][h][:], AF.Exp, scale=-A_DEC)
            s.tt(KKH[st][h][:], kk[st][h][:], E1[st][h][:], ALU.mult, e="pool")
            s.act(E3[st][h][:], cs[st][h][:], AF.Exp, scale=A_DEC)
            s.tt(kt[st][h][:], kd[st][h][:], E3[st][h][:], ALU.mult)
            s.tt(bt[st][h][:], bd[st][h][:], E3[st][h][:], ALU.mult, e="pool")
            s.act(E4[st][h][:], csr[st][h][:], AF.Exp, scale=A_DEC)
            s.tt(kc[st][h][:], kd[st][h][:], E4[st][h][:], ALU.mult)
            s.tt(bc[st][h][:], bd[st][h][:], E4[st][h][:], ALU.mult, e="pool")
            s.stt(rk[st][h][:], uR[st][h][:], P(4), kd[st][h][:], ALU.mult, ALU.mult)
        ck(4)
        yb, bb = Ybuf[gi % 2], Bbuf[gi % 2]
        for c in range(4):
            cc = slice(c * 64, (c + 1) * 64)
            tT = trT[chunk_no % 2]
            sS = scS[chunk_no % 2]
            chunk_no += 1
            for h in range(3):
                ptr = pb[4]
                for i, src in enumerate((KKH, bc, kc, uV)):
                    s.tr(ptr[0:64, (h % 2) * 256 + i * 64:(h % 2) * 256 + (i + 1) * 64], src[st][h][:, cc], ident)
                s.cp(tT[:, h, :], ptr[0:64, (h % 2) * 256:(h % 2) * 256 + 256], e="act")
                psc = pb[5]
                o = 0
                s.mm(psc[0:64, 0:64], kt[st][h][:, cc], RH[st][h][:, cc])
                s.mm(psc[0:64, 64:128], kt[st][h][:, cc], KKH[st][h][:, cc])
                s.mm(psc[0:64, 128:192], bt[st][h][:, cc], RH[st][h][:, cc])
                s.mm(psc[0:64, 192:256], bt[st][h][:, cc], KKH[st][h][:, cc])
                s.mm(psc[0:64, 256:320], KKH[st][h][:, cc], bt[st][h][:, cc])
                s.tt(sS[:, h, :], psc[0:64, 0:320], mask3(h), ALU.mult)
            ck(5)
            s.tt(PQ[0][:, :, :], sS[:, :, 192:320], idt3.ap.rearrange("p (h c) -> p h c", c=128) and V(idt3.ap.rearrange("p (h c) -> p h c", c=128), idt3.tk), ALU.add, e="pool")
            Xc = lambda lvl, h: (sS[:, h, 192:256] if lvl == 0 else XY[lvl % 2][:, h, 0:64])
            Yc = lambda lvl, h: (sS[:, h, 256:320] if lvl == 0 else XY[lvl % 2][:, h, 64:128])
            for lvl in range(5):
                pn, pq = pb[6], pb[7]
                nxt = XY[(lvl + 1) % 2]
                for h in range(3):
                    s.mm(pn[0:64, h * 128:h * 128 + 64], Yc(lvl, h), Xc(lvl, h))
                    if lvl < 4:
                        s.mm(pn[0:64, h * 128 + 64:h * 128 + 128], Xc(lvl, h), Yc(lvl, h))
                if lvl < 4:
                    s.cp(nxt[:, :, :], V(pn.t[0:64, 0:384].rearrange("p (h c) -> p h c", c=128), pn.tk), e="act")
                else:
                    s.cp(nxt[:, :, 0:64], V(pn.t[0:64, 0:384].rearrange("p (h c) -> p h c", c=128)[:, :, 0:64], pn.tk), e="act")
                Pc, Pn = PQ[lvl % 2], PQ[(lvl + 1) % 2]
                for h in range(3):
                    s.mm(pq[0:64, h * 128:h * 128 + 64], Pc[:, h, 64:128], nxt[:, h, 0:64])
                    if lvl < 4:
                        s.mm(pq[0:64, h * 128 + 64:h * 128 + 128], Pc[:, h, 0:64], nxt[:, h, 64:128])
                if lvl < 4:
                    s.tt(Pn[:, :, :], V(pq.t[0:64, 0:384].rearrange("p (h c) -> p h c", c=128), pq.tk), Pc[:, :, :], ALU.add)
                else:
                    s.tt(Pn[:, :, 0:64], V(pq.t[0:64, 0:384].rearrange("p (h c) -> p h c", c=128)[:, :, 0:64], pq.tk), Pc[:, :, 0:64], ALU.add)
            ck(6)
            TT = PQ[1]
            pk = pb[2]
            for h in range(3):
                s.mm(pk[0:64, h * 64:(h + 1) * 64], tT[:, h, 0:64], TT[:, h, 0:64])
                s.mm(pk[0:64, 192 + h * 64:192 + (h + 1) * 64], sS[:, h, 64:128], tT[:, h, 192:256])
            s.cp(KKpT[:], pk[0:64, 0:192], e="act")
            s.cp(AV[:], pk[0:64, 192:384], e="dve")
            pk3 = pb[3]
            for h in range(3):
                s.mm(pk3[0:64, h * 64:(h + 1) * 64], TT[:, h, 0:64], AV[:, h * 64:(h + 1) * 64])
            s.cp(Uloc[:], pk3[0:64, 0:192], e="act")
            ck(7)
            Sc, Sn = ST[sti], ST[1 - sti]
            sti = 1 - sti
            pu = pb[0]
            for h in range(3):
                s.mm(pu[0:64, h * 64:(h + 1) * 64], KKpT[:, h * 64:(h + 1) * 64], Sc[:, h * 64:(h + 1) * 64])
            s.stt(U[:], pu[0:64, 0:192], -1.0, Uloc[:], ALU.mult, ALU.subtract)
            pS = pb[1]
            for h in range(3):
                hs = slice(h * 64, (h + 1) * 64)
                s.mm(pS[0:64, hs], tT[:, h, 128:192], tT[:, h, 192:256], start=True, stop=False)
                s.mm(pS[0:64, hs], tT[:, h, 64:128], U[:, hs], start=False, stop=True)
            for h in range(3):
                hs = slice(h * 64, (h + 1) * 64)
                s.stt(Sn[:, hs], Sc[:, hs], wc[st][h][:, c:c + 1], pS[0:64, hs], ALU.mult, ALU.add)
            ck(8)
            py = pb[0]
            for h in range(3):
                hs = slice(256 + h * 64, 256 + (h + 1) * 64)
                hh = slice(h * 64, (h + 1) * 64)
                s.mm(py[0:64, hs], RH[st][h][:, cc], Sc[:, hh], start=True, stop=False)
                s.mm(py[0:64, hs], sS[:, h, 128:192], U[:, hh], start=False, stop=False)
                s.mm(py[0:64, hs], sS[:, h, 0:64], tT[:, h, 192:256], start=False, stop=True)
            s.cp(yb[:, c, :], py[0:64, 256:448], e="act")
            pbn = pb[1]
            for h in range(3):
                s.mm(pbn[0:64, 256 + h:256 + h + 1], rk[st][h][:, cc], ones[:, 0:1])
            s.cp(bs[:, 0:3], pbn[0:64, 256:259], e="dve")
            for h in range(3):
                s.ts(bb[:, c, h * 64:(h + 1) * 64], tT[:, h, 192:256], bs[:, h:h + 1], ALU.mult, e="pool")
        q = "sp" if gi % 2 == 0 else "act"
        out_tickets.append(s.dma(q, y[tbase:tbase + 256, :].rearrange("(c t) f -> t c f", t=64), yb[:, :, :]))
        out_tickets.append(s.dma(q, bon[tbase:tbase + 256, :].rearrange("(c t) f -> t c f", t=64), bb[:, :, :]))
    s.finish(out_tickets)
    s.close()
    return nc


def consts_A():
    idx = np.arange(64)
    inclT = (idx[:, None] <= idx[None, :]).astype(np.float32)
    strictT = (idx[:, None] < idx[None, :]).astype(np.float32)
    strict = strictT.T.copy()
    mask = np.concatenate([inclT, strictT, inclT, -strictT, -strict], axis=1)
    rmask = np.ones((64, 256), np.float32); rmask[:, 0::64] = 0.0
    ident = np.eye(64, dtype=np.float32)
    return np.concatenate([ident, mask, mask, mask, rmask] + [ident] * 6, axis=1).astype(np.float32)


def host_A(inp, l, b, d, hh, n_lat_groups=NG_L, x_cur=None, ctx_cur=None):
    L = 256 * n_lat_groups
    x_cur = inp['x'] if x_cur is None else x_cur
    ctx_cur = inp['ctx'] if ctx_cur is None else ctx_cur
    ctx = ctx_cur[b]; x = x_cur[b][:L]
    if d == 1:
        ctx = ctx[::-1]; x = x[::-1]
    NTP = 258 + L + 2
    xTp = np.zeros((1024, NTP), np.float32)
    xTp[:, 1:257] = ctx.T
    xTp[:, 259:259 + L] = x.T
    cvec = np.stack([inp['c'][b], inp['c_ctx']], axis=1)
    cv = cvec.reshape(8, 128, 2).transpose(1, 0, 2).reshape(128, 16)
    wms = np.ascontiguousarray(inp['w_mod'][l][:, 0:2048])
    bm = inp['b_mod'][l][0:2048].reshape(16, 128).T
    bms = np.repeat(bm[:, :, None], 2, axis=2).reshape(128, 32)
    w_in = inp['w_in'][l]
    heads = [3 * hh + i for i in range(3)]
    cols = []
    for comp in (0, 384, 768):
        for h in heads:
            cols += list(range(comp + h * 64, comp + h * 64 + 64))
    cols += list(range(1152 + 32 * d, 1152 + 32 * d + 32))
    cols += list(range(1216 + 32 * d, 1216 + 32 * d + 32))
    cols = np.array(cols)
    wa = np.ascontiguousarray(w_in[:, cols])
    conv = inp['rwkv_conv'][l][:, cols]
    if d == 1:
        conv = conv[::-1]
    cw = np.zeros((64, 33), np.float32)
    for ci in range(9):
        cw[:, ci * 3:ci * 3 + 3] = conv[:, ci * 64:(ci + 1) * 64].T
    cw[0:32, 27:30] = conv[:, 576:608].T
    cw[0:32, 30:33] = conv[:, 608:640].T
    pv = np.zeros((64, 15), np.float32)
    for i, h in enumerate(heads):
        hs = slice(h * 64, (h + 1) * 64)
        pv[:, i * 5 + 0] = inp['decay_w0'][l][d, hs]
        pv[:, i * 5 + 1] = inp['iclr_a0'][l][d, hs]
        pv[:, i * 5 + 2] = inp['rwkv_k_k'][l][hs]
        pv[:, i * 5 + 3] = inp['rwkv_k_a'][l][hs]
        pv[:, i * 5 + 4] = inp['rwkv_r_k'][l][h]
    hcols = np.concatenate([np.arange(h * 64, (h + 1) * 64) for h in heads])
    w2 = np.ascontiguousarray(inp['decay_w2'][l][d][:, hcols])
    a2 = np.ascontiguousarray(inp['iclr_a2'][l][d][:, hcols])
    return dict(xTp=xTp, cv=np.ascontiguousarray(cv), wms=wms, bms=np.ascontiguousarray(bms), wa=wa, cw=cw, pv=pv,
                w2=w2, a2=a2, cst=consts_A())


ALPHA = (2 * 4) ** 0.25
NOWN = 18
NKT = 66
NNT = 22


def na_chunks(i):
    if i == 0:
        return [(c, 128) for c in range(0, 6)]
    if i == 1:
        return [(c, 128) for c in range(1, 6)]
    if i == 15:
        return [(c, 128) for c in range(14, 19)] + [(19, 64)]
    return [(c, 128) for c in range(i, i + 4)] + [(i + 4, 64)]


def na_class(i):
    return {0: 0, 1: 1, 14: 3, 15: 4}.get(i, 2)


class _StopB(Exception):
    pass


def build_B(stage=99, nkt=NKT, gqa_tiles=None):
    nc = bass.Bass("TRN2", target_bir_lowering=False)
    dt = nc.dram_tensor
    I = lambda n, sh: dt(n, sh, F32, kind="ExternalInput").ap()
    xo = I("xo", [2304, 1024]); xk = I("xk", [8448, 1024]); xn = I("xn", [2816, 1024])
    cv = I("cv", [128, 16]); wmod = I("wmod", [1024, 3072]); bmod = I("bmod", [128, 3072])
    wb = I("wb", [1024, 2432]); wout = I("wout", [1024, 1024])
    cosK = I("cosK", [8192, 128]); sinK = I("sinK", [8192, 128])
    cosQ = I("cosQ", [2048, 384]); sinQ = I("sinQ", [2048, 384])
    gk = I("gk", [128, 128]); gq = I("gq", [128, 384])
    nab = I("nab", [5, 4, 128, 768])
    yf = I("yf", [2304, 384]); yb = I("yb", [2304, 384]); bf = I("bf", [2304, 384]); bbn = I("bbn", [2304, 384])
    gng = I("gng", [128, 384]); gnb = I("gnb", [128, 384]); lng = I("lng", [128, 1024]); lnb = I("lnb", [128, 1024])
    ident_in = I("ident", [128, 128])
    out = dt("out", [2304, 1024], F32, kind="ExternalOutput").ap()

    s = S(nc)
    tickets = []

    def ck(n):
        if stage <= n:
            raise _StopB()

    pb = [s.ps([128, 512], name="pb%d" % i) for i in range(8)]
    ident = s.sb([128, 128], name="ident"); s.dma("sp", ident[:], ident_in)
    ones = s.sb([128, 128], name="ones"); s.memset(ones[:], 1.0)
    GK = s.sb([128, 128], name="GK"); s.dma("act", GK[:], gk)
    GQ = s.sb([128, 384], name="GQ"); s.dma("act", GQ[:], gq)
    GNG = s.sb([128, 384], name="GNG"); s.dma("act", GNG[:], gng)
    GNB = s.sb([128, 384], name="GNB"); s.dma("act", GNB[:], gnb)

    big = s.sb([128, 11520], BF16, name="big")
    stage_w = V(big.t[:, 0:8192].bitcast(F32).rearrange("p (k c) -> p k c", k=8), big.tk)
    YAN = V(big.t[:, 0:11520].rearrange("p (t c) -> p t c", c=640), big.tk)

    cv_t = s.sb([128, 16], name="cv"); s.dma("sp", cv_t[:], cv)
    scv = s.sb([128, 16], name="scv"); s.act(scv[:], cv_t[:], AF.Silu)
    mod = [s.sb([128, 3072], name="mod%d" % j) for j in range(2)]
    xt = [s.sb([128, 1024], name="xt%d" % i) for i in range(2)]
    ht = s.sb([128, 1024], name="ht")
    pe = [s.sb([128, 1024], name="pe%d" % i) for i in range(2)]
    Rl = V(ht.t[:, :].rearrange("p (k m) -> p k m", m=128), ht.tk)
    bmod_t = pe[1]
    for j in range(2):
        for k in range(8):
            s.ts(V(Rl.ap[:, k, :], Rl.tk), ones[:], scv[:, 2 * k + j:2 * k + j + 1], ALU.mult, e="dve" if k % 2 == 0 else "pool")
        for cb in range(6):
            s.dma("sp", stage_w, wmod.rearrange("(k p) c -> p k c", p=128)[:, :, cb * 512:(cb + 1) * 512])
            s.dma("act", bmod_t[:, 0:512], bmod[:, cb * 512:(cb + 1) * 512])
            ps = pb[cb % 2]
            for k in range(8):
                s.mm(ps[:, :], V(Rl.ap[:, k, :], Rl.tk), V(stage_w.ap[:, k, :], stage_w.tk), start=(k == 0), stop=(k == 7))
            s.tt(mod[j][:, cb * 512:(cb + 1) * 512], ps[:, :], bmod_t[:, 0:512], ALU.add)
    for j in range(2):
        s.ts(mod[j][:, 1024:2048], mod[j][:, 1024:2048], 1.0, ALU.add, e="pool")

    wbuf = s.sb([128, 8, 1024], BF16, name="wbuf")
    woutb = s.sb([128, 8, 1024], BF16, name="woutb")
    wst = xt

    def load_w(dst, src, c0, ncols, dcol=0):
        for k in range(8):
            st_ = wst[k % 2]
            s.dma("sp" if k % 2 == 0 else "act", st_[:, 0:ncols], src[k * 128:(k + 1) * 128, c0:c0 + ncols])
            s.cp(dst[:, k, dcol:dcol + ncols], st_[:, 0:ncols], e="dve" if k % 2 == 0 else "pool")

    load_w(woutb, wout, 0, 1024)

    kT = s.sb([128, 8448], BF16, name="kT")
    Vg = s.sb([128, NKT, 2, 65], BF16, name="Vg")
    qT = [s.sb([128, 2304], BF16, name="qT%d" % i) for i in range(3)]
    nqT = [s.sb([128, 2304], BF16, name="nqT%d" % i) for i in range(2)]
    nkT = [s.sb([128, 2816], BF16, name="nkT%d" % i) for i in range(2)]
    Vn = s.sb([128, NNT, 4, 65], BF16, name="Vn")
    s.memset(Vg[:, :, :, 64:65], 1.0)
    s.memset(Vn[:, :, :, 64:65], 1.0)

    hT = [s.sb([128, 8, 128], BF16, name="hT%d" % i) for i in range(2)]
    tc_ = s.sb([128, 384], name="tcos"); tsn = s.sb([128, 384], name="tsin")
    sm = s.sb([128, 64], name="sm")
    cnt = [0]

    def front(src, row0, j):
        i = cnt[0]; cnt[0] += 1
        x_ = xt[i % 2]
        s.dma("sp" if i % 2 == 0 else "act", x_[:], src[row0:row0 + 128, :])
        s.tt(ht[:], x_[:], mod[j][:, 1024:2048], ALU.mult, e="pool")
        s.tt(ht[:], ht[:], mod[j][:, 0:1024], ALU.add, e="dve")
        h_ = hT[i % 2]
        for half in range(2):
            p = pb[half]
            for k in range(4):
                kk = half * 4 + k
                s.tr(p[:, k * 128:(k + 1) * 128], ht[:, kk * 128:(kk + 1) * 128], ident[:])
            s.cp(h_[:, half * 4:half * 4 + 4, :], V(p.t[:, :].rearrange("p (k t) -> p k t", k=4), p.tk),
                 e="act" if half == 0 else "dve")
        return x_, h_

    def proj(h_, c0, ncols, dst, wsrc=None):
        wsrc = wsrc or wbuf
        o = 0
        bi = 2
        while o < ncols:
            n = min(512, ncols - o)
            p = pb[bi]
            for k in range(8):
                s.mm(p[:, 0:n], h_[:, k, :], wsrc[:, k, c0 + o:c0 + o + n], start=(k == 0), stop=(k == 7))
            s.cp(dst[:, o:o + n], p[:, 0:n], e="act" if bi == 2 else "dve")
            o += n
            bi = 5 - bi

    def rms_rope(src, H, gtab, scale_mode, rope, dst):
        sq = pe[1]
        s.tt(sq[:, 0:H * 64], src, src, ALU.mult, e="pool")
        s.red(sm[:, 0:H], V(sq.t[:, 0:H * 64].rearrange("p (h d) -> p h d", d=64), sq.tk), ALU.add)
        if scale_mode == "k":
            s.ts(sm[:, 0:H], sm[:, 0:H], 1.0 / 64, ALU.mult, 1e-6, ALU.add)
        else:
            s.ts(sm[:, 0:H], sm[:, 0:H], 64e-6, ALU.add)
        s.act(sm[:, 0:H], sm[:, 0:H], AF.Sqrt)
        s.recip(sm[:, 0:H], sm[:, 0:H])
        for h in range(H):
            s.stt(V(dst.ap[:, h * 64:(h + 1) * 64], dst.tk), V(src.ap[:, h * 64:(h + 1) * 64], src.tk), sm[:, h:h + 1],
                  gtab[:, h * 64:(h + 1) * 64], ALU.mult, ALU.mult)
        if rope is not None:
            cos_d, sin_d, r0 = rope
            s.dma("sp", tc_[:, 0:H * 64], cos_d[r0:r0 + 128, :])
            s.dma("act", tsn[:, 0:H * 64], sin_d[r0:r0 + 128, :])
            t1 = sq
            v4 = lambda ap: ap.rearrange("p (g a d) -> p g a d", a=2, d=16)
            d4 = v4(dst.ap); s4 = v4(tsn.t[:, 0:H * 64]); t4 = v4(t1.t[:, 0:H * 64])
            s.tt(V(t4[:, :, 0, :], t1.tk), V(d4[:, :, 1, :], dst.tk), V(s4[:, :, 0, :], tsn.tk), ALU.mult, e="pool")
            s.tt(V(t4[:, :, 1, :], t1.tk), V(d4[:, :, 0, :], dst.tk), V(s4[:, :, 1, :], tsn.tk), ALU.mult, e="pool")
            s.tt(dst, dst, tc_[:, 0:H * 64], ALU.mult)
            s.tt(dst, dst, t1[:, 0:H * 64], ALU.add)

    pt = [s.sb([128, 512], BF16, name="pt%d" % i) for i in range(3)]
    rcp = s.sb([128, 8], name="rcp")
    bias_t = [s.sb([128, 768], name="bias%d" % i) for i in range(2)]
    ptc = [0]

    try:
        load_w(wbuf, wb, 768, 256)
        for t in range(nkt):
            j = 1 if t < 2 else 0
            x_, h_ = front(xk, t * 128, j)
            proj(h_, 0, 256, pe[0])
            rope = None if t < 2 else (cosK, sinK, (t - 2) * 128)
            rms_rope(pe[0][:, 0:128], 2, GK, "k", rope, ht[:, 0:128])
            p = pb[6]
            s.tr(p[:, 0:128], ht[:, 0:128], ident[:])
            s.cp(kT[:, t * 128:(t + 1) * 128], p[:, 0:128], e="act")
            s.cp(Vg[:, t, :, 0:64], V(pe[0].t[:, 128:256].rearrange("p (g d) -> p g d", d=64), pe[0].tk), e="pool")
        ck(1)
        load_w(wbuf, wb, 1664, 512)
        for t in range(NNT):
            j = 1 if t >= 20 else 0
            x_, h_ = front(xn, t * 128, j)
            proj(h_, 0, 512, pe[0])
            for pr_ in range(2):
                p = pb[6 + pr_]
                s.tr(p[:, 0:128], pe[0][:, pr_ * 128:(pr_ + 1) * 128], ident[:])
                s.cp(nkT[pr_][:, t * 128:(t + 1) * 128], p[:, 0:128], e="act" if pr_ == 0 else "dve")
            s.cp(Vn[:, t, :, 0:64], V(pe[0].t[:, 256:512].rearrange("p (g d) -> p g d", d=64), pe[0].tk), e="pool")
        ck(2)
        for sl in range(6):
            hh_ = (sl // 2) + 3 * (sl % 2)
            load_w(wbuf, wb, 384 + hh_ * 64, 64, dcol=sl * 64)
        load_w(wbuf, wb, 1408, 256, dcol=384)
        for t in range(NOWN):
            j = 1 if t >= 16 else 0
            x_, h_ = front(xo, t * 128, j)
            proj(h_, 0, 640, pe[0])
            rope = None if t >= 16 else (cosQ, sinQ, t * 128)
            qr = ht
            rms_rope(pe[0][:, 0:384], 6, GQ, "q", rope, qr[:, 0:384])
            for pr_ in range(3):
                p = pb[6 + pr_ % 2]
                s.tr(p[:, 0:128], qr[:, pr_ * 128:(pr_ + 1) * 128], ident[:])
                s.cp(qT[pr_][:, t * 128:(t + 1) * 128], p[:, 0:128], e="act" if pr_ % 2 == 0 else "dve")
            s.ts(qr[:, 384:640], pe[0][:, 384:640], 0.125, ALU.mult, e="pool")
            for pr_ in range(2):
                p = pb[6 + pr_]
                s.tr(p[:, 0:128], qr[:, 384 + pr_ * 128:384 + (pr_ + 1) * 128], ident[:])
                s.cp(nqT[pr_][:, t * 128:(t + 1) * 128], p[:, 0:128], e="act" if pr_ == 0 else "dve")
        ck(3)

        def attend(qsrc, g, qc0, nq, chunks, ksrc, vsrc_fn, bias_fn, dst_fn):
            nqs = nq // 128
            lo, hi = 64 * g, 64 * g + 64
            for ci, (kc0, nk, cid) in enumerate(chunks):
                ps = pb[ci % 2]
                bsrc = bias_fn(cid, nk) if bias_fn else None
                s.mm(ps[0:nk, 0:nq], ksrc[lo:hi, kc0:kc0 + nk], qsrc[lo:hi, qc0:qc0 + nq], start=True, stop=(bsrc is None))
                if bsrc is not None:
                    s.mm(ps[0:nk, 0:nq], bsrc, ident[:, 0:nq], start=False, stop=True)
                p_ = pt[ptc[0] % 3]; ptc[0] += 1
                s.act(p_[0:nk, 0:nq], ps[0:nk, 0:nq], AF.Exp)
                for qs in range(nqs):
                    s.mm(pb[4 + qs][:, 0:65], p_[0:nk, qs * 128:(qs + 1) * 128], vsrc_fn(cid, nk),
                         start=(ci == 0), stop=(ci == len(chunks) - 1))
            for qs in range(nqs):
                s.recip(rcp[:, qs:qs + 1], pb[4 + qs][:, 64:65])
                s.ts(dst_fn(qs), pb[4 + qs][:, 0:64], rcp[:, qs:qs + 1], ALU.mult)

        gq_groups = list(range(4)) if gqa_tiles is None else gqa_tiles
        for qg in gq_groups:
            for h in range(6):
                pr_, g = h % 3, h // 3
                attend(qT[pr_], g, qg * 512, 512, [(c * 128, 128, c) for c in range(nkt)], kT,
                       lambda cid, nk, g=g: Vg[0:nk, cid, g, :], None,
                       lambda qs, qg=qg, h=h: V(YAN.ap[:, qg * 4 + qs, h * 64:(h + 1) * 64], YAN.tk))
        for h in range(6):
            pr_, g = h % 3, h // 3
            attend(qT[pr_], g, 2048, 256, [(c * 128, 128, c) for c in range(2)], kT,
                   lambda cid, nk, g=g: Vg[0:nk, cid, g, :], None,
                   lambda qs, h=h: V(YAN.ap[:, 16 + qs, h * 64:(h + 1) * 64], YAN.tk))
        ck(4)
        bc = [0]
        for i in range(16):
            chs = na_chunks(i)
            base = chs[0][0] * 128
            cls = na_class(i)
            for hn in range(4):
                pr_, g = hn // 2, hn % 2
                bt_ = bias_t[bc[0] % 2]; bc[0] += 1
                nkeys = sum(nk for _, nk in chs)
                s.dma("sp" if hn % 2 == 0 else "act", bt_[:, 0:nkeys], nab[cls, hn, :, 0:nkeys])
                chunks = [(c * 128, nk, c) for c, nk in chs] + [(20 * 128, 128, 20), (21 * 128, 128, 21)]
                attend(nqT[pr_], g, i * 128, 128, chunks, nkT[pr_],
                       lambda cid, nk, hn=hn: Vn[0:nk, cid, hn, :],
                       lambda cid, nk, bt_=bt_, base=base: (None if cid >= 20 else bt_[:, cid * 128 - base:cid * 128 - base + nk]),
                       lambda qs, i=i, hn=hn: V(YAN.ap[:, i, 384 + hn * 64:384 + (hn + 1) * 64], YAN.tk))
        for hn in range(4):
            pr_, g = hn // 2, hn % 2
            attend(nqT[pr_], g, 2048, 256, [(20 * 128, 128, 20), (21 * 128, 128, 21)], nkT[pr_],
                   lambda cid, nk, hn=hn: Vn[0:nk, cid, hn, :], None,
                   lambda qs, hn=hn: V(YAN.ap[:, 16 + qs, 384 + hn * 64:384 + (hn + 1) * 64], YAN.tk))
        ck(5)
        load_w(wbuf, wb, 0, 384)
        load_w(wbuf, wb, 1024, 384, dcol=384)
        load_w(wbuf, wb, 2176, 256, dcol=768)
        class _A:
            def __init__(self, ap, tk):
                self.t = ap; self.tk = tk
            def __getitem__(self, idx):
                return V(self.t[idx], self.tk)
        LNG = _A(kT.t[:, 0:2048].bitcast(F32), kT.tk); s.dma("sp", LNG[:], lng)
        LNB = _A(kT.t[:, 2048:4096].bitcast(F32), kT.tk); s.dma("act", LNB[:], lnb)
        ry = [_A(kT.t[:, 4096 + i * 768:4096 + (i + 1) * 768].bitcast(F32), kT.tk) for i in range(4)]
        cen = _A(kT.t[:, 7168:7936].bitcast(F32), kT.tk)
        YgT = _A(qT[0].t[:, 0:1024].rearrange("p (k t) -> p k t", k=8), qT[0].tk)
        for t in range(NOWN):
            j = 1 if t >= 16 else 0
            x_, h_ = front(xo, t * 128, j)
            G = pe[0]
            proj(h_, 0, 1024, G)
            s.act(G[:], G[:], AF.Silu)
            r0 = t * 128
            for i_, src in enumerate((yf, yb, bf, bbn)):
                s.dma("sp" if i_ % 2 == 0 else "act", ry[i_][:], src[r0:r0 + 128, :])
            ys = ry[0]
            s.tt(ys[:], ry[0][:], ry[1][:], ALU.add, e="pool")
            v3 = lambda T_: V(T_.t[:, 0:384].rearrange("p (h d) -> p h d", d=64), T_.tk)
            s.red(sm[:, 0:6], v3(ys), ALU.add)
            s.ts(sm[:, 0:6], sm[:, 0:6], 1.0 / 64, ALU.mult)
            s.tt(v3(cen), v3(ys), V(sm.t[:, 0:6].unsqueeze(2).to_broadcast([128, 6, 64]), sm.tk), ALU.subtract)
            s.tt(ry[1][:], cen[:], cen[:], ALU.mult, e="pool")
            s.red(sm[:, 8:14], v3(ry[1]), ALU.add)
            s.ts(sm[:, 8:14], sm[:, 8:14], 1.0 / 64, ALU.mult, 64e-5, ALU.add)
            s.act(sm[:, 8:14], sm[:, 8:14], AF.Sqrt)
            s.recip(sm[:, 8:14], sm[:, 8:14])
            s.tt(v3(cen), v3(cen), V(sm.t[:, 8:14].unsqueeze(2).to_broadcast([128, 6, 64]), sm.tk), ALU.mult)
            s.tt(cen[:], cen[:], GNG[:], ALU.mult)
            s.tt(cen[:], cen[:], GNB[:], ALU.add)
            s.tt(ry[2][:], ry[2][:], ry[3][:], ALU.add, e="pool")
            s.tt(cen[:], cen[:], ry[2][:], ALU.add)
            Yg = pe[1]
            s.tt(Yg[:, 0:384], cen[:], G[:, 0:384], ALU.mult)
            s.tt(Yg[:, 384:1024], V(YAN.ap[:, t, :], YAN.tk), G[:, 384:1024], ALU.mult)
            for half in range(2):
                p = pb[half]
                for k in range(4):
                    kk = half * 4 + k
                    s.tr(p[:, k * 128:(k + 1) * 128], Yg[:, kk * 128:(kk + 1) * 128], ident[:])
                s.cp(YgT[:, half * 4:half * 4 + 4, :], V(p.t[:, :].rearrange("p (k t) -> p k t", k=4), p.tk),
                     e="act" if half == 0 else "dve")
            yo = pe[0]
            proj(YgT, 0, 1024, yo, wsrc=woutb)
            s.tt(yo[:], yo[:], mod[j][:, 2048:3072], ALU.mult)
            z = pe[1]
            s.stt(z[:], x_[:], ALPHA, yo[:], ALU.mult, ALU.add)
            s.red(sm[:, 16:17], z[:], ALU.add)
            s.ts(sm[:, 16:17], sm[:, 16:17], 1.0 / 1024, ALU.mult)
            s.ts(z[:], z[:], sm[:, 16:17], ALU.subtract)
            s.tt(yo[:], z[:], z[:], ALU.mult, e="pool")
            s.red(sm[:, 17:18], yo[:], ALU.add)
            s.ts(sm[:, 17:18], sm[:, 17:18], 1.0 / 1024, ALU.mult, 1e-5, ALU.add)
            s.act(sm[:, 17:18], sm[:, 17:18], AF.Sqrt)
            s.recip(sm[:, 17:18], sm[:, 17:18])
            s.stt(z[:], z[:], sm[:, 17:18], LNG[:], ALU.mult, ALU.mult)
            s.tt(z[:], z[:], LNB[:], ALU.add)
            tickets.append(s.dma("sp" if t % 2 == 0 else "act", out[r0:r0 + 128, :], z[:]))
    except _StopB:
        pass
    s.finish(tickets)
    s.close()
    return nc


def rope_tables():
    t = np.arange(8192)
    row = (t // 64).astype(np.float32); col = (t % 64).astype(np.float32)
    inv = (10000.0 ** (-np.arange(16, dtype=np.float32) / 16)).astype(np.float32)
    ar = row[:, None] * inv; ac = col[:, None] * inv
    ang = np.concatenate([ar, ar, ac, ac], axis=-1).astype(np.float32)
    cos = np.cos(ang).astype(np.float32); sin = np.sin(ang).astype(np.float32)
    sgn = np.concatenate([-np.ones(16), np.ones(16), -np.ones(16), np.ones(16)]).astype(np.float32)
    return cos, sin * sgn


def na_bias(rpb, j):
    NEG = -30000.0
    out = np.full((5, 4, 128, 768), NEG, np.float32)
    tiles = {0: 0, 1: 1, 2: 7, 3: 14, 4: 15}
    for cls, i in tiles.items():
        chs = na_chunks(i)
        srow0 = chs[0][0] * 2
        nkeys = sum(nk for _, nk in chs)
        qrow_l = np.repeat(np.array([2 * i, 2 * i + 1]), 64)
        qcol = np.tile(np.arange(64), 2)
        r = 32 * j + qrow_l
        r_start = np.clip(r - 4, 0, 120)
        c_start = np.clip(qcol - 8, 0, 48)
        key = np.arange(nkeys)
        krow = (32 * j - 4) + srow0 + key // 64
        kcol = key % 64
        dr = krow[None, :] - r[:, None] + 7
        dc = kcol[None, :] - qcol[:, None] + 15
        inwin = ((krow[None, :] >= r_start[:, None]) & (krow[None, :] < r_start[:, None] + 8) &
                 (kcol[None, :] >= c_start[:, None]) & (kcol[None, :] < c_start[:, None] + 16))
        drc = np.clip(dr, 0, 14); dcc = np.clip(dc, 0, 30)
        for h in range(4):
            vals = rpb[h][drc, dcc]
            out[cls, h, :, 0:nkeys] = np.where(inwin, vals, NEG)
    return out


def host_B(inp, l, b, j, x_cur, ctx_cur, yf, yb, bf, bb):
    x = x_cur[b]; ctx = ctx_cur[b]
    own = slice(2048 * j, 2048 * j + 2048)
    xo = np.concatenate([x[own], ctx], axis=0)
    xk = np.concatenate([ctx, x], axis=0)
    xn = np.zeros((2816, 1024), np.float32)
    r0 = 32 * j - 4
    for sr in range(39):
        gr = r0 + sr
        if 0 <= gr < 128:
            xn[sr * 64:(sr + 1) * 64] = x[gr * 64:(gr + 1) * 64]
    xn[2560:2816] = ctx
    cvec = np.stack([inp['c'][b], inp['c_ctx']], axis=1)
    cv = cvec.reshape(8, 128, 2).transpose(1, 0, 2).reshape(128, 16)
    cos, sinS = rope_tables()
    bc = lambda v: np.ascontiguousarray(np.broadcast_to(v[None, :], (128, v.shape[0]))).astype(np.float32)
    sel = lambda a: np.concatenate([a[256 + 2048 * j:256 + 2048 * j + 2048], a[0:256]], axis=0)
    return dict(
        xo=np.ascontiguousarray(xo), xk=np.ascontiguousarray(xk), xn=xn, cv=np.ascontiguousarray(cv),
        wmod=inp['w_mod'][l], bmod=bc(inp['b_mod'][l]),
        wb=np.ascontiguousarray(inp['w_in'][l][:, 1280:]), wout=inp['w_out'][l],
        cosK=np.tile(cos, (1, 2)), sinK=np.tile(sinS, (1, 2)),
        cosQ=np.tile(cos[own], (1, 6)), sinQ=np.tile(sinS[own], (1, 6)),
        gk=bc(np.tile(inp['gqa_k_norm'][l], 2)), gq=bc(np.tile(inp['gqa_q_norm'][l], 6)),
        nab=na_bias(inp['na_rpb'][l], j),
        yf=sel(yf), yb=sel(yb), bf=sel(bf), bbn=sel(bb),
        gng=bc(inp['rwkv_gn_g'][l]), gnb=bc(inp['rwkv_gn_b'][l]), lng=bc(inp['ln_g'][l]), lnb=bc(inp['ln_b'][l]),
        ident=np.eye(128, dtype=np.float32),
    )


from concourse.bass_utils import run_bass_kernel_spmd

_NC = {}


def _unrev(a):
    return np.concatenate([a[:256][::-1], a[256:][::-1]], axis=0)


def kernel(**inputs):
    inp = {k: np.asarray(v) for k, v in inputs.items()}
    x_cur = np.array(inp['x'], dtype=np.float32, copy=True)
    ctx_cur = np.array(inp['ctx'], dtype=np.float32, copy=True)
    if 'A' not in _NC:
        _NC['A'] = build_A()
        _NC['B'] = build_B()
    ncA, ncB = _NC['A'], _NC['B']
    units = [(b, d, hh) for b in range(2) for d in range(2) for hh in range(2)]
    for l in range(4):
        insA = [host_A(inp, l, b, d, hh, NG_L, x_cur, ctx_cur) for (b, d, hh) in units]
        resA = run_bass_kernel_spmd(ncA, insA, core_ids=list(range(8))).results
        ya = {u: resA[i]["y"] for i, u in enumerate(units)}
        ba = {u: resA[i]["bon"] for i, u in enumerate(units)}
        insB = []
        for b in range(2):
            yf = np.concatenate([ya[(b, 0, 0)], ya[(b, 0, 1)]], axis=1)
            yb = np.concatenate([_unrev(ya[(b, 1, 0)]), _unrev(ya[(b, 1, 1)])], axis=1)
            bf = np.concatenate([ba[(b, 0, 0)], ba[(b, 0, 1)]], axis=1)
            bb = np.concatenate([_unrev(ba[(b, 1, 0)]), _unrev(ba[(b, 1, 1)])], axis=1)
            for j in range(4):
                insB.append(host_B(inp, l, b, j, x_cur, ctx_cur, yf, yb, bf, bb))
        resB = run_bass_kernel_spmd(ncB, insB, core_ids=list(range(8))).results
        x_new = np.empty_like(x_cur); ctx_new = np.empty_like(ctx_cur)
        for b in range(2):
            for j in range(4):
                o = resB[b * 4 + j]["out"]
                x_new[b, 2048 * j:2048 * j + 2048] = o[:2048]
            ctx_new[b] = resB[b * 4]["out"][2048:]
        x_cur, ctx_cur = x_new, ctx_new
    return x_cur.astype(np.float32)
```

```python
import contextlib
import numpy as np
import concourse.bass as bass
import concourse.mybir as mybir

F32 = mybir.dt.float32
BF16 = mybir.dt.bfloat16
AF = mybir.ActivationFunctionType
ALU = mybir.AluOpType
AX = mybir.AxisListType


class Tk:
    __slots__ = ("w", "r", "name", "excl", "acc")

    def __init__(self, name=""):
        self.w = None
        self.r = {}
        self.name = name
        self.excl = False
        self.acc = {}


class T:
    def __init__(self, S, t, name):
        self.t = t
        self.tk = Tk(name)
        self.name = name

    def __getitem__(self, idx):
        return V(self.t[idx], self.tk)


class V:
    __slots__ = ("ap", "tk")

    def __init__(self, ap, tk):
        self.ap = ap
        self.tk = tk


class S:
    ENG = ("pe", "act", "dve", "pool", "sp")

    def __init__(self, nc):
        self.nc = nc
        self.es = contextlib.ExitStack()
        self.eng = {"pe": nc.tensor, "act": nc.scalar, "dve": nc.vector, "pool": nc.gpsimd, "sp": nc.sync}
        self.sem = {e: self.es.enter_context(nc.semaphore("s_" + e)) for e in self.ENG}
        self.cnt = {e: 0 for e in self.ENG}
        self.dq = {}
        for q in ("sp", "act", "pool"):
            sems = [self.es.enter_context(nc.semaphore("d_%s%d" % (q, i))) for i in range(8)]
            self.dq[q] = dict(sems=sems, cnt=[0] * 8, nxt=0)
            for i, s_ in enumerate(sems):
                self.sem[(q, i)] = s_
        self.waited = {}
        self.cur = self.es
        self.cc_keys = []
        self.n_tiles = 0
        self.n_instr = 0
        self.n_wait = 0

    def sb(self, shape, dt=F32, name=None):
        self.n_tiles += 1
        name = "%s_%d" % (name or "t", self.n_tiles)
        t = self.cur.enter_context(self.nc.sbuf_tensor("sb_" + name, list(shape), dt))
        return T(self, t, name)

    def dram(self, shape, name, dt=F32):
        self.n_tiles += 1
        t = self.nc.dram_tensor("%s_%d" % (name, self.n_tiles), list(shape), dt)
        return T(self, t.ap(), name)

    @contextlib.contextmanager
    def phase(self):
        prev = self.cur
        self.cur = contextlib.ExitStack()
        try:
            yield
        finally:
            self.barrier()
            self.cur.close()
            self.cur = prev

    def barrier(self):
        for e in self.ENG:
            for e2 in self.ENG:
                if e2 != e and self.cnt[e2] > 0:
                    self._wait(e, e2, self.cnt[e2])
            for q, d in self.dq.items():
                for i, c in enumerate(d["cnt"]):
                    if c > 0:
                        self._wait(e, (q, i), c)

    def allgather(self, src, dst, groups):
        if "cc" not in self.dq:
            sems = [self.es.enter_context(self.nc.semaphore("cc%d" % i)) for i in range(8)]
            self.dq["cc"] = dict(sems=sems, cnt=[0] * 8, nxt=0)
            for i, s_ in enumerate(sems):
                self.sem[("cc", i)] = s_
        d = self.dq["cc"]
        i = d["nxt"]; d["nxt"] = (i + 1) % 8
        key = ("cc", i)
        self._wait("pool", key, d["cnt"][i])
        self._deps("pool", [src], [dst])
        ins = self.nc.gpsimd.collective_compute("AllGather", mybir.AluOpType.bypass, replica_groups=groups,
                                                ins=[src.ap.opt()], outs=[dst.ap.opt()])
        d["cnt"][i] += 1
        ins.then_inc(self.sem[key])
        self._mark((key, d["cnt"][i]), [src], [dst])
        self.n_instr += 1
        return (key, d["cnt"][i])

    def ps(self, shape, dt=F32, name=None):
        self.n_tiles += 1
        name = name or "p%d" % self.n_tiles
        t = self.es.enter_context(self.nc.psum_tensor("ps_" + name, list(shape), dt))
        tt_ = T(self, t, name)
        tt_.tk.excl = True
        return tt_

    def close(self):
        self.es.close()

    def _wait(self, e, key, val):
        if val is None:
            return
        k = (e, key)
        if self.waited.get(k, 0) >= val:
            return
        self.waited[k] = val
        self.eng[e].wait_ge(self.sem[key], val)
        self.n_wait += 1

    def _deps(self, e, reads, writes, pe_acc=False):
        for v in list(reads) + list(writes):
            if v.tk.excl:
                for e2, n2 in v.tk.acc.items():
                    if e2 == e and e == "pe":
                        continue
                    self._wait(e, e2, n2)
        reads = [v for v in reads if not v.tk.excl]
        writes = [v for v in writes if not v.tk.excl]
        for v in reads:
            w = v.tk.w
            if w is not None:
                self._wait(e, w[0], w[1])
        for v in writes:
            tk = v.tk
            if tk.w is not None:
                if not (pe_acc and tk.w[0] == "pe" and e == "pe"):
                    self._wait(e, tk.w[0], tk.w[1])
            for re_, rn in tk.r.items():
                if re_ == e and e == "pe":
                    continue
                self._wait(e, re_, rn)

    def _mark(self, ticket, reads, writes):
        for v in list(reads) + list(writes):
            if v.tk.excl:
                v.tk.acc[ticket[0]] = ticket[1]
        reads = [v for v in reads if not v.tk.excl]
        writes = [v for v in writes if not v.tk.excl]
        for v in reads:
            v.tk.r[ticket[0]] = ticket[1]
        for v in writes:
            v.tk.w = ticket
            v.tk.r = {}

    def op(self, e, fn, reads, writes, pe_acc=False):
        reads = [v for v in reads if isinstance(v, V)]
        self._deps(e, reads, writes, pe_acc)
        ins = fn()
        self.cnt[e] += 1
        ins.then_inc(self.sem[e], 1)
        self._mark((e, self.cnt[e]), reads, writes)
        self.n_instr += 1
        return ins

    def dma(self, q, out, in_, **kw):
        d = self.dq[q]
        i = d["nxt"]
        d["nxt"] = (i + 1) % len(d["sems"])
        key = (q, i)
        self._wait(q, key, d["cnt"][i])
        reads = [in_] if isinstance(in_, V) else []
        writes = [out] if isinstance(out, V) else []
        self._deps(q, reads, writes)
        oa = out.ap if isinstance(out, V) else out
        ia = in_.ap if isinstance(in_, V) else in_
        ins = self.eng[q].dma_start(out=oa, in_=ia, **kw)
        d["cnt"][i] += 16
        ins.then_inc(self.sem[key], 16)
        self._mark((key, d["cnt"][i]), reads, writes)
        self.n_instr += 1
        return (key, d["cnt"][i])

    def wait_ticket(self, e, ticket):
        self._wait(e, ticket[0], ticket[1])

    def mm(self, out, lhsT, rhs, start=True, stop=True, **kw):
        return self.op("pe", lambda: self.nc.tensor.matmul(out.ap, lhsT.ap, rhs.ap, start=start, stop=stop, **kw),
                       [lhsT, rhs], [out], pe_acc=not start)

    def tr(self, out, in_, ident):
        return self.op("pe", lambda: self.nc.tensor.transpose(out.ap, in_.ap, ident.ap), [in_, ident], [out])

    def act(self, out, in_, func, bias=None, scale=None, accum_out=None, e="act"):
        kw = {}
        rd = [in_]
        if bias is not None:
            kw["bias"] = bias.ap if isinstance(bias, V) else bias
            rd.append(bias)
        if scale is not None:
            kw["scale"] = scale.ap if isinstance(scale, V) else scale
            rd.append(scale)
        wr = [out]
        if accum_out is not None:
            kw["accum_out"] = accum_out.ap
            wr.append(accum_out)
        return self.op("act", lambda: self.nc.scalar.activation(out.ap, in_.ap, func, **kw), rd, wr)

    def _ve(self, e):
        return {"dve": self.nc.vector, "pool": self.nc.gpsimd, "act": self.nc.scalar}[e]

    def tt(self, out, a, b, op, e="dve"):
        return self.op(e, lambda: self._ve(e).tensor_tensor(out.ap, a.ap, b.ap, op), [a, b], [out])

    def ts(self, out, a, s1, op0, s2=None, op1=None, e="dve", accum_out=None):
        rd = [a, s1, s2]
        a1 = s1.ap if isinstance(s1, V) else s1
        a2 = s2.ap if isinstance(s2, V) else s2
        kw = {}
        wr = [out]
        if op1 is not None:
            kw["op1"] = op1
        if accum_out is not None:
            kw["accum_out"] = accum_out.ap
            wr.append(accum_out)
        return self.op(e, lambda: self._ve(e).tensor_scalar(out.ap, a.ap, a1, a2, op0, **kw), rd, wr)

    def stt(self, out, a, s, b, op0, op1, e="dve"):
        sa = s.ap if isinstance(s, V) else s
        return self.op(e, lambda: self._ve(e).scalar_tensor_tensor(out.ap, a.ap, sa, b.ap, op0, op1), [a, s, b], [out])

    def cp(self, out, in_, e="dve"):
        if e == "act":
            return self.op("act", lambda: self.nc.scalar.copy(out.ap, in_.ap), [in_], [out])
        return self.op(e, lambda: self._ve(e).tensor_copy(out.ap, in_.ap), [in_], [out])

    def memset(self, out, val, e="pool"):
        return self.op(e, lambda: self._ve(e).memset(out.ap, val), [], [out])

    def red(self, out, in_, op, axis=AX.X, e="dve"):
        return self.op(e, lambda: self._ve(e).tensor_reduce(out.ap, in_.ap, axis, op), [in_], [out])

    def recip(self, out, in_):
        return self.op("dve", lambda: self.nc.vector.reciprocal(out.ap, in_.ap), [in_], [out])

    def finish(self, tickets):
        for t in tickets:
            self._wait("sp", t[0], t[1])


A_DEC = 0.6065306597126334
ALPHA = (2 * 4) ** 0.25
NOWN = 18
NKT = 66
NNT = 22
GROUPS = [[0, 1, 2, 3], [4, 5, 6, 7]]
XROWS = 8960


def na_chunks(i):
    if i == 0:
        return [(c, 128) for c in range(0, 6)]
    if i == 1:
        return [(c, 128) for c in range(1, 6)]
    if i == 15:
        return [(c, 128) for c in range(14, 19)] + [(19, 64)]
    return [(c, 128) for c in range(i, i + 4)] + [(i + 4, 64)]


def na_class(i):
    return {0: 0, 1: 1, 14: 3, 15: 4}.get(i, 2)


def lat_row(tau):
    rho, rem = divmod(tau, 2048)
    k, i = divmod(rem, 256)
    return 512 + 1024 * k + 256 * rho + i


class _A:
    def __init__(self, ap, tk):
        self.t = ap; self.tk = tk

    def __getitem__(self, idx):
        return V(self.t[idx], self.tk)


def build_F(depth=4):
    nc = bass.Bass("TRN2", target_bir_lowering=False)
    dt = nc.dram_tensor
    I = lambda n, sh: dt(n, sh, F32, kind="ExternalInput").ap()
    xa0 = I("xa0", [XROWS, 1024]); xs0 = I("xs0", [2560, 1024])
    cv = I("cv", [128, 16]); wmod = I("wmod", [depth, 1024, 3072]); bmod = I("bmod", [depth, 128, 3072])
    wb = I("wb", [depth, 1024, 2432]); wout = I("wout", [depth, 1024, 1024])
    wa = I("wa", [depth, 1024, 640]); cw = I("cw", [depth, 64, 33]); pvi = I("pv", [depth, 64, 15])
    w2 = I("w2", [depth, 32, 192]); a2 = I("a2", [depth, 32, 192])
    cstA = I("cstA", [64, 1664])
    cosK = I("cosK", [8192, 128]); sinK = I("sinK", [8192, 128])
    cosQ = I("cosQ", [2048, 384]); sinQ = I("sinQ", [2048, 384])
    dsel_in = I("dsel", [128, 2])
    gk = I("gk", [depth, 128, 128]); gq = I("gq", [depth, 128, 384])
    nab = I("nab", [depth, 5, 4, 128, 768])
    gng = I("gng", [depth, 128, 384]); gnb = I("gnb", [depth, 128, 384])
    lng = I("lng", [depth, 128, 1024]); lnb = I("lnb", [depth, 128, 1024])
    ident_in = I("ident", [128, 128]); jmat_in = I("jmat", [128, 128]); jd_in = I("jd", [128, 128])
    out = dt("out", [2048, 1024], F32, kind="ExternalOutput").ap()

    s = S(nc)
    tickets = []
    pb = [s.ps([128, 512], name="pb%d" % i) for i in range(8)]
    XA = [s.dram([XROWS, 1024], "XA%d" % i) for i in range(2)]
    YB = s.dram([8448, 384], "YB")
    YGc = s.dram([4 * 256, 384], "YGc")
    YGl = s.dram([16 * 4 * 512, 384], "YGl")
    XN = s.dram([2048, 1024], "XN")
    XS = s.dram([2560, 1024], "XS")
    YF = [s.dram([2048, 384], "YF%d" % i) for i in range(2)]
    YBk = [s.dram([2048, 384], "YBk%d" % i) for i in range(2)]
    xa0_T = _A(xa0, Tk("xa0")); xs0_T = _A(xs0, Tk("xs0"))

    _rd = {}

    def RR(q):
        if q not in _rd:
            pid = s.eng[q].partition_id()
            _rd[q] = pid % 4
        return _rd[q]

    dynq = ["sp", "act", "pool"]
    dync = [0]

    def dyndma(dst_v, src_fn):
        q = dynq[dync[0] % 3]; dync[0] += 1
        return s.dma(q, dst_v, src_fn(RR(q)))

    ident = s.sb([128, 128], name="ident"); s.dma("sp", ident[:], ident_in)
    jmat = s.sb([128, 128], name="jmat"); s.dma("act", jmat[:], jmat_in)
    jd = s.sb([128, 128], name="jd"); s.dma("sp", jd[:], jd_in)
    dsel = s.sb([128, 2], name="dsel"); s.dma("act", dsel[:], dsel_in)
    ones = s.sb([128, 128], name="ones"); s.memset(ones[:], 1.0)
    cv_t = s.sb([128, 16], name="cv"); s.dma("sp", cv_t[:], cv)
    scv = s.sb([128, 16], name="scv"); s.act(scv[:], cv_t[:], AF.Silu)
    mod = [s.sb([128, 3072], name="mod%d" % j) for j in range(2)]
    for l in range(depth):
        Xc = xa0_T if l == 0 else XA[(l - 1) % 2]
        Xn = XA[l % 2]
        last = (l == depth - 1)

        if l == 0:
            XSc = xs0_T
        else:
            XSc = XS
            lat = Xc.t[512:8704, :]
            dyndma(V(XS.t[256:2304, :].rearrange("(o k i) c -> o k (i c)", o=1, k=8), XS.tk),
                   lambda r: V(lat.rearrange("(k rr i) c -> rr k (i c)", rr=4, i=256)[bass.ds(r, 1)], Xc.tk))
            units = lat.rearrange("(u i) c -> u (i c)", i=256)
            dyndma(V(XS.t[0:256, :].rearrange("(o i) c -> o (i c)", o=1), XS.tk),
                   lambda r: V(units[bass.ds(r + 27, 1), :], Xc.tk))
            dyndma(V(XS.t[2304:2560, :].rearrange("(o i) c -> o (i c)", o=1), XS.tk),
                   lambda r: V(units[bass.ds(r + 1, 1), :], Xc.tk))
        with s.phase():
            stage_w = s.sb([128, 8, 512], name="stage_w")
            Rl = s.sb([128, 8, 128], name="Rl")
            bmod_t = s.sb([128, 512], name="bmodt")
            for j in range(2):
                for k in range(8):
                    s.ts(Rl[:, k, :], ones[:], scv[:, 2 * k + j:2 * k + j + 1], ALU.mult, e="dve" if k % 2 == 0 else "pool")
                for cb in range(6):
                    s.dma("sp", stage_w[:], wmod[l].rearrange("(k p) c -> p k c", p=128)[:, :, cb * 512:(cb + 1) * 512])
                    s.dma("act", bmod_t[:], bmod[l][:, cb * 512:(cb + 1) * 512])
                    ps = pb[cb % 2]
                    for k in range(8):
                        s.mm(ps[:, :], Rl[:, k, :], stage_w[:, k, :], start=(k == 0), stop=(k == 7))
                    s.tt(mod[j][:, cb * 512:(cb + 1) * 512], ps[:, :], bmod_t[:], ALU.add)
                s.ts(mod[j][:, 1024:2048], mod[j][:, 1024:2048], 1.0, ALU.add, e="pool")

        with s.phase():
            cst_t = s.sb([64, 1664], name="cst"); s.dma("sp", cst_t[:], cstA)
            identA = cst_t[:, 0:64]
            mask3 = lambda h: cst_t[:, 64 + h * 320: 64 + (h + 1) * 320]
            rmask = cst_t[:, 1024:1280]
            idt3 = cst_t[:, 1280:1664]
            cw_t = s.sb([64, 33], name="cw"); s.dma("act", cw_t[:], cw[l])
            pv_t = s.sb([64, 16], name="pv"); s.dma("act", pv_t[:, 0:15], pvi[l])
            omk = s.sb([64, 3], name="omk")
            for h in range(3):
                s.ts(omk[:, h:h + 1], pv_t[:, h * 5 + 3:h * 5 + 4], -1.0, ALU.mult, 1.0, ALU.add)
            w2_t = s.sb([32, 192], name="w2"); s.dma("act", w2_t[:], w2[l])
            a2_t = s.sb([32, 192], name="a2"); s.dma("act", a2_t[:], a2[l])
            xt = [s.sb([128, 1024], name="xt%d" % i) for i in range(2)]
            ht = s.sb([128, 1024], name="ht")
            xr = s.sb([128, 1024], name="xr")
            wab = s.sb([128, 8, 640], BF16, name="wab")
            for k in range(8):
                st_ = xt[k % 2]
                s.dma("sp" if k % 2 == 0 else "act", st_[:, 0:640], wa[l][k * 128:(k + 1) * 128, :])
                s.cp(wab[:, k, :], st_[:, 0:640], e="dve" if k % 2 == 0 else "pool")
            cts = [(i * 64, 64) for i in range(9)] + [(576, 32), (608, 32)]
            hg = [s.sb([128, 8, 258], BF16, name="hg%d" % i) for i in range(2)]
            for h_ in hg:
                s.memset(h_[:], 0.0)
            raw = [s.sb([64, 258], name="raw%d" % i) for i in range(3)]
            ctmp = [s.sb([64, 256], name="ctmp%d" % i) for i in range(3)]
            mk = lambda nm, shape=(64, 256): [s.sb(list(shape), name="%s%d" % (nm, h)) for h in range(3)]
            uR, uK, uV = mk("uR"), mk("uK"), mk("uV")
            uD = s.sb([32, 256], name="uD"); uA = s.sb([32, 256], name="uA"); ddt = s.sb([32, 256], name="ddt")
            sg, ic, kk, tmp, kd, bd = mk("sg"), mk("ic"), mk("kk"), mk("tmp"), mk("kd"), mk("bd")
            cs, csx, csr = mk("cs"), mk("csx"), mk("csr")
            E1, E3, E4 = mk("E1"), mk("E3"), mk("E4")
            RH, KKH, kt, bt, kc, bc, rk = mk("RH"), mk("KKH"), mk("kt"), mk("bt"), mk("kc"), mk("bc"), mk("rk")
            wc = mk("wc", (64, 4)); rn = mk("rn")
            trT = [s.sb([64, 3, 256], name="trT%d" % i) for i in range(2)]
            scS = [s.sb([64, 3, 320], name="scS%d" % i) for i in range(2)]
            XY = [s.sb([64, 3, 128], name="XY%d" % i) for i in range(2)]
            PQ = [s.sb([64, 3, 128], name="PQ%d" % i) for i in range(2)]
            KKpT = s.sb([64, 192], name="KKpT"); AV = s.sb([64, 192], name="AV")
            Uloc = s.sb([64, 192], name="Uloc"); U = s.sb([64, 192], name="U")
            ST = [s.sb([64, 192], name="ST%d" % i) for i in range(2)]
            YBuf = [s.sb([64, 4, 384], name="YBuf%d" % i) for i in range(2)]
            bs_ = s.sb([64, 4], name="bs")
            s.memset(ST[0][:], 0.0)
            sti = 0
            acnt = [0]

            def frontA(g):
                hgt = hg[g % 2]
                for a in range(2):
                    u = 2 * g + a
                    i = acnt[0]; acnt[0] += 1
                    if u < 2:
                        bf_, br_ = 128 * u, 128 * (1 - u)
                        j = 1
                    else:
                        v = u - 2
                        bf_, br_ = lat_row(128 * v), lat_row(128 * (63 - v))
                        j = 0
                    x_ = xt[i % 2]
                    s.dma("sp", x_[:], V(Xc.t[bf_:bf_ + 128, :], Xc.tk))
                    s.dma("act", xr[:], V(Xc.t[br_:br_ + 128, :], Xc.tk))
                    s.ts(x_[:], x_[:], dsel[:, 0:1], ALU.mult, e="pool")
                    s.stt(x_[:], xr[:], dsel[:, 1:2], x_[:], ALU.mult, ALU.add)
                    s.tt(ht[:], x_[:], mod[j][:, 1024:2048], ALU.mult, e="pool")
                    s.tt(ht[:], ht[:], mod[j][:, 0:1024], ALU.add, e="dve")
                    for half in range(2):
                        p = pb[half]
                        for k in range(4):
                            kk_ = half * 4 + k
                            s.tr(p[:, k * 128:(k + 1) * 128], ht[:, kk_ * 128:(kk_ + 1) * 128], jd[:])
                        s.cp(hgt[:, half * 4:half * 4 + 4, 1 + 128 * a:1 + 128 * (a + 1)],
                             V(p.t[:, :].rearrange("p (k t) -> p k t", k=4), p.tk), e="act" if half == 0 else "dve")

            NGRP = 33
            frontA(0)
            chunk_no = 0
            for g in range(NGRP):
                first = g in (0, 1)
                lastg = g in (0, NGRP - 1)
                j = 1 if g == 0 else 0
                tbase = 256 * g
                hgt = hg[g % 2]
                if g + 1 < NGRP:
                    frontA(g + 1)
                    hn_ = hg[(g + 1) % 2]
                    s.cp(hgt[:, :, 257:258], hn_[:, :, 1:2], e="pool")
                    s.cp(hn_[:, :, 0:1], hgt[:, :, 256:257], e="pool")
                for ci, (c0, M) in enumerate(cts):
                    pr = pb[2 + ci % 2]
                    for k in range(8):
                        s.mm(pr[0:M, 0:258], wab[:, k, c0:c0 + M], hgt[:, k, :], start=(k == 0), stop=(k == 7))
                    rw = raw[ci % 3]
                    s.cp(rw[0:M, :], pr[0:M, 0:258], e="act" if ci % 2 == 0 else "dve")
                    if first:
                        s.memset(rw[0:M, 0:1], 0.0, e="pool")
                    if lastg:
                        s.memset(rw[0:M, 257:258], 0.0, e="pool")
                    dst = (uR, uK, uV)[ci // 3][ci % 3] if ci < 9 else (uD, uA)[ci - 9]
                    tm = ctmp[ci % 3]
                    s.act(tm[0:M, :], rw[0:M, 0:256], AF.Identity, scale=cw_t[0:M, ci * 3:ci * 3 + 1])
                    s.stt(tm[0:M, :], rw[0:M, 1:257], cw_t[0:M, ci * 3 + 1:ci * 3 + 2], tm[0:M, :], ALU.mult, ALU.add)
                    s.stt(dst[0:M, :], rw[0:M, 2:258], cw_t[0:M, ci * 3 + 2:ci * 3 + 3], tm[0:M, :], ALU.mult, ALU.add)
                s.act(ddt[:], uD[:], AF.Tanh)
                for h in range(3):
                    P = lambda i: pv_t[:, h * 5 + i:h * 5 + i + 1]
                    pz = pb[4]
                    s.mm(pz[0:64, 0:256], w2_t[:, h * 64:(h + 1) * 64], ddt[:])
                    s.act(sg[h][:], pz[0:64, 0:256], AF.Sigmoid, bias=P(0))
                    s.mm(pz[0:64, 256:512], a2_t[:, h * 64:(h + 1) * 64], uA[:])
                    s.act(ic[h][:], pz[0:64, 256:512], AF.Sigmoid, bias=P(1))
                    s.ts(kk[h][:], uK[h][:], P(2), ALU.mult, e="pool")
                    s.tt(tmp[h][:], kk[h][:], kk[h][:], ALU.mult, e="pool")
                    pss = pb[5]
                    s.mm(pss[0:64, 0:256], ones[0:64, 0:64], tmp[h][:])
                    s.ts(rn[h][:], pss[0:64, 0:256], 1e-12, ALU.max)
                    s.act(rn[h][:], rn[h][:], AF.Sqrt)
                    s.recip(rn[h][:], rn[h][:])
                    s.tt(kk[h][:], kk[h][:], rn[h][:], ALU.mult)
                    s.ts(tmp[h][:], ic[h][:], P(3), ALU.mult, omk[:, h:h + 1], ALU.add, e="pool")
                    s.tt(kd[h][:], uK[h][:], tmp[h][:], ALU.mult, e="pool")
                    s.tt(bd[h][:], kk[h][:], ic[h][:], ALU.mult, e="pool")
                    s.op("dve", lambda h=h: nc.vector.tensor_tensor_scan(cs[h][:].ap, rmask.ap, sg[h][:].ap, 0.0, ALU.mult, ALU.add),
                         [rmask, sg[h][:]], [cs[h][:]])
                    s.tt(csx[h][:], cs[h][:], sg[h][:], ALU.subtract)
                    for c in range(4):
                        s.ts(csr[h][:, c * 64:(c + 1) * 64], cs[h][:, c * 64:(c + 1) * 64],
                             cs[h][:, c * 64 + 63:c * 64 + 64], ALU.subtract)
                    s.act(wc[h][:], cs[h][:, 63::64], AF.Exp, scale=-A_DEC)
                    s.act(E1[h][:], cs[h][:], AF.Exp, scale=-A_DEC)
                    s.tt(RH[h][:], uR[h][:], E1[h][:], ALU.mult, e="pool")
                    s.act(E1[h][:], csx[h][:], AF.Exp, scale=-A_DEC)
                    s.tt(KKH[h][:], kk[h][:], E1[h][:], ALU.mult, e="pool")
                    s.act(E3[h][:], cs[h][:], AF.Exp, scale=A_DEC)
                    s.tt(kt[h][:], kd[h][:], E3[h][:], ALU.mult)
                    s.tt(bt[h][:], bd[h][:], E3[h][:], ALU.mult, e="pool")
                    s.act(E4[h][:], csr[h][:], AF.Exp, scale=A_DEC)
                    s.tt(kc[h][:], kd[h][:], E4[h][:], ALU.mult)
                    s.tt(bc[h][:], bd[h][:], E4[h][:], ALU.mult, e="pool")
                    s.stt(rk[h][:], uR[h][:], P(4), kd[h][:], ALU.mult, ALU.mult)
                yb = YBuf[g % 2]
                for c in range(4):
                    cc = slice(c * 64, (c + 1) * 64)
                    tT = trT[chunk_no % 2]; sS = scS[chunk_no % 2]
                    chunk_no += 1
                    for h in range(3):
                        ptr = pb[6]
                        for i, src in enumerate((KKH, bc, kc, uV)):
                            s.tr(ptr[0:64, (h % 2) * 256 + i * 64:(h % 2) * 256 + (i + 1) * 64], src[h][:, cc], identA)
                        s.cp(tT[:, h, :], ptr[0:64, (h % 2) * 256:(h % 2) * 256 + 256], e="act")
                        psc = pb[7]
                        s.mm(psc[0:64, 0:64], kt[h][:, cc], RH[h][:, cc])
                        s.mm(psc[0:64, 64:128], kt[h][:, cc], KKH[h][:, cc])
                        s.mm(psc[0:64, 128:192], bt[h][:, cc], RH[h][:, cc])
                        s.mm(psc[0:64, 192:256], bt[h][:, cc], KKH[h][:, cc])
                        s.mm(psc[0:64, 256:320], KKH[h][:, cc], bt[h][:, cc])
                        s.tt(sS[:, h, :], psc[0:64, 0:320], mask3(h), ALU.mult)
                    s.tt(PQ[0][:, :, :], sS[:, :, 192:320], V(idt3.ap.rearrange("p (h c) -> p h c", c=128), idt3.tk), ALU.add, e="pool")
                    Xc_ = lambda lvl, h: (sS[:, h, 192:256] if lvl == 0 else XY[lvl % 2][:, h, 0:64])
                    Yc_ = lambda lvl, h: (sS[:, h, 256:320] if lvl == 0 else XY[lvl % 2][:, h, 64:128])
                    for lvl in range(5):
                        pn, pq = pb[4], pb[5]
                        nxt = XY[(lvl + 1) % 2]
                        for h in range(3):
                            s.mm(pn[0:64, h * 128:h * 128 + 64], Yc_(lvl, h), Xc_(lvl, h))
                            if lvl < 4:
                                s.mm(pn[0:64, h * 128 + 64:h * 128 + 128], Xc_(lvl, h), Yc_(lvl, h))
                        pn3 = pn.t[0:64, 0:384].rearrange("p (h c) -> p h c", c=128)
                        if lvl < 4:
                            s.cp(nxt[:, :, :], V(pn3, pn.tk), e="act")
                        else:
                            s.cp(nxt[:, :, 0:64], V(pn3[:, :, 0:64], pn.tk), e="act")
                        Pc, Pn = PQ[lvl % 2], PQ[(lvl + 1) % 2]
                        for h in range(3):
                            s.mm(pq[0:64, h * 128:h * 128 + 64], Pc[:, h, 64:128], nxt[:, h, 0:64])
                            if lvl < 4:
                                s.mm(pq[0:64, h * 128 + 64:h * 128 + 128], Pc[:, h, 0:64], nxt[:, h, 64:128])
                        pq3 = pq.t[0:64, 0:384].rearrange("p (h c) -> p h c", c=128)
                        if lvl < 4:
                            s.tt(Pn[:, :, :], V(pq3, pq.tk), Pc[:, :, :], ALU.add)
                        else:
                            s.tt(Pn[:, :, 0:64], V(pq3[:, :, 0:64], pq.tk), Pc[:, :, 0:64], ALU.add)
                    TT = PQ[1]
                    pk = pb[6]
                    for h in range(3):
                        s.mm(pk[0:64, h * 64:(h + 1) * 64], tT[:, h, 0:64], TT[:, h, 0:64])
                        s.mm(pk[0:64, 192 + h * 64:192 + (h + 1) * 64], sS[:, h, 64:128], tT[:, h, 192:256])
                    s.cp(KKpT[:], pk[0:64, 0:192], e="act")
                    s.cp(AV[:], pk[0:64, 192:384], e="dve")
                    pk3 = pb[7]
                    for h in range(3):
                        s.mm(pk3[0:64, h * 64:(h + 1) * 64], TT[:, h, 0:64], AV[:, h * 64:(h + 1) * 64])
                    s.cp(Uloc[:], pk3[0:64, 0:192], e="act")
                    Sc, Sn = ST[sti], ST[1 - sti]
                    sti = 1 - sti
                    pu = pb[0]
                    for h in range(3):
                        s.mm(pu[0:64, h * 64:(h + 1) * 64], KKpT[:, h * 64:(h + 1) * 64], Sc[:, h * 64:(h + 1) * 64])
                    s.stt(U[:], pu[0:64, 0:192], -1.0, Uloc[:], ALU.mult, ALU.subtract)
                    pS = pb[1]
                    for h in range(3):
                        hs = slice(h * 64, (h + 1) * 64)
                        s.mm(pS[0:64, hs], tT[:, h, 128:192], tT[:, h, 192:256], start=True, stop=False)
                        s.mm(pS[0:64, hs], tT[:, h, 64:128], U[:, hs], start=False, stop=True)
                    for h in range(3):
                        hs = slice(h * 64, (h + 1) * 64)
                        s.stt(Sn[:, hs], Sc[:, hs], wc[h][:, c:c + 1], pS[0:64, hs], ALU.mult, ALU.add)
                    py = pb[0]
                    for h in range(3):
                        hs = slice(256 + h * 64, 256 + (h + 1) * 64)
                        hh = slice(h * 64, (h + 1) * 64)
                        s.mm(py[0:64, hs], RH[h][:, cc], Sc[:, hh], start=True, stop=False)
                        s.mm(py[0:64, hs], sS[:, h, 128:192], U[:, hh], start=False, stop=False)
                        s.mm(py[0:64, hs], sS[:, h, 0:64], tT[:, h, 192:256], start=False, stop=True)
                    s.cp(yb[:, c, 0:192], py[0:64, 256:448], e="act")
                    pbn = pb[1]
                    for h in range(3):
                        s.mm(pbn[0:64, 256 + h:256 + h + 1], rk[h][:, cc], ones[0:64, 0:1])
                    s.cp(bs_[:, 0:3], pbn[0:64, 256:259], e="dve")
                    for h in range(3):
                        s.ts(yb[:, c, 192 + h * 64:192 + (h + 1) * 64], tT[:, h, 192:256], bs_[:, h:h + 1], ALU.mult, e="pool")
                s.dma("sp" if g % 2 == 0 else "act",
                      V(YB.t[tbase:tbase + 256, :].rearrange("(c t) f -> t c f", t=64), YB.tk), yb[:, :, :])
        s.allgather(YB[0:256, :], YGc[:, :], GROUPS)
        for m in range(16):
            s.allgather(YB[256 + 512 * m:256 + 512 * (m + 1), :], YGl[2048 * m:2048 * (m + 1), :], GROUPS)
        ygv = YGl.t.rearrange("(m sr i) c -> sr m (i c)", sr=4, i=512)
        for sr in range(2):
            dyndma(V(YF[sr].t.rearrange("(m i) c -> m (i c)", i=512), YF[sr].tk),
                   lambda r, sr=sr: V(ygv[sr][bass.ds(r * 4, 4), :], YGl.tk))
            dyndma(V(YBk[sr].t.rearrange("(m i) c -> m (i c)", i=512), YBk[sr].tk),
                   lambda r, sr=sr: V(ygv[2 + sr][bass.ds((3 - r) * 4, 4), :], YGl.tk))
        with s.phase():
            GK = s.sb([128, 128], name="GK"); s.dma("act", GK[:], gk[l])
            GQ = s.sb([128, 384], name="GQ"); s.dma("act", GQ[:], gq[l])
            GNG = s.sb([128, 384], name="GNG"); s.dma("act", GNG[:], gng[l])
            GNB = s.sb([128, 384], name="GNB"); s.dma("act", GNB[:], gnb[l])
            YAN = s.sb([128, 18, 640], BF16, name="YAN")
            wbuf = s.sb([128, 8, 1024], BF16, name="wbuf")
            woutb = s.sb([128, 8, 1024], BF16, name="woutb")
            xt = [s.sb([128, 1024], name="xt%d" % i) for i in range(2)]
            ht = s.sb([128, 1024], name="ht")
            pe = [s.sb([128, 1024], name="pe%d" % i) for i in range(2)]
            wst = xt

            def load_w(dst, src, c0, ncols, dcol=0):
                for k in range(8):
                    st_ = wst[k % 2]
                    s.dma("sp" if k % 2 == 0 else "act", st_[:, 0:ncols], src[k * 128:(k + 1) * 128, c0:c0 + ncols])
                    s.cp(dst[:, k, dcol:dcol + ncols], st_[:, 0:ncols], e="dve" if k % 2 == 0 else "pool")

            load_w(woutb, wout[l], 0, 1024)
            kT = s.sb([128, 8448], BF16, name="kT")
            Vg = s.sb([128, NKT, 2, 65], BF16, name="Vg")
            qT = [s.sb([128, 2304], BF16, name="qT%d" % i) for i in range(3)]
            nqT = [s.sb([128, 2304], BF16, name="nqT%d" % i) for i in range(2)]
            nkT = [s.sb([128, 2816], BF16, name="nkT%d" % i) for i in range(2)]
            Vn = s.sb([128, NNT, 4, 65], BF16, name="Vn")
            s.memset(Vg[:, :, :, 64:65], 1.0)
            s.memset(Vn[:, :, :, 64:65], 1.0)
            hT = [s.sb([128, 8, 128], BF16, name="hT%d" % i) for i in range(2)]
            tc_ = s.sb([128, 384], name="tcos"); tsn = s.sb([128, 384], name="tsin")
            sm = s.sb([128, 64], name="sm")
            cnt = [0]

            def front(srcT, row, j):
                i = cnt[0]; cnt[0] += 1
                q = "sp" if i % 2 == 0 else "act"
                x_ = xt[i % 2]
                s.dma(q, x_[:], V(srcT.t[row:row + 128, :], srcT.tk))
                s.tt(ht[:], x_[:], mod[j][:, 1024:2048], ALU.mult, e="pool")
                s.tt(ht[:], ht[:], mod[j][:, 0:1024], ALU.add, e="dve")
                h_ = hT[i % 2]
                for half in range(2):
                    p = pb[half]
                    for k in range(4):
                        kk_ = half * 4 + k
                        s.tr(p[:, k * 128:(k + 1) * 128], ht[:, kk_ * 128:(kk_ + 1) * 128], ident[:])
                    s.cp(h_[:, half * 4:half * 4 + 4, :], V(p.t[:, :].rearrange("p (k t) -> p k t", k=4), p.tk),
                         e="act" if half == 0 else "dve")
                return x_, h_

            def proj(h_, c0, ncols, dst, wsrc=None):
                wsrc = wsrc or wbuf
                o = 0
                bi = 2
                while o < ncols:
                    n = min(512, ncols - o)
                    p = pb[bi]
                    for k in range(8):
                        s.mm(p[:, 0:n], h_[:, k, :], wsrc[:, k, c0 + o:c0 + o + n], start=(k == 0), stop=(k == 7))
                    s.cp(dst[:, o:o + n], p[:, 0:n], e="act" if bi == 2 else "dve")
                    o += n
                    bi = 5 - bi

            def rms_rope(src, H, gtab, scale_mode, rope, dst):
                sq = pe[1]
                s.tt(sq[:, 0:H * 64], src, src, ALU.mult, e="pool")
                s.red(sm[:, 0:H], V(sq.t[:, 0:H * 64].rearrange("p (h d) -> p h d", d=64), sq.tk), ALU.add)
                if scale_mode == "k":
                    s.ts(sm[:, 0:H], sm[:, 0:H], 1.0 / 64, ALU.mult, 1e-6, ALU.add)
                else:
                    s.ts(sm[:, 0:H], sm[:, 0:H], 64e-6, ALU.add)
                s.act(sm[:, 0:H], sm[:, 0:H], AF.Sqrt)
                s.recip(sm[:, 0:H], sm[:, 0:H])
                for h in range(H):
                    s.stt(V(dst.ap[:, h * 64:(h + 1) * 64], dst.tk), V(src.ap[:, h * 64:(h + 1) * 64], src.tk), sm[:, h:h + 1],
                          gtab[:, h * 64:(h + 1) * 64], ALU.mult, ALU.mult)
                if rope is not None:
                    cos_d, sin_d, rowfn = rope
                    s.dma("sp", tc_[:, 0:H * 64], cos_d[rowfn:rowfn + 128, :])
                    s.dma("act", tsn[:, 0:H * 64], sin_d[rowfn:rowfn + 128, :])
                    t1 = sq
                    v4 = lambda ap: ap.rearrange("p (g a d) -> p g a d", a=2, d=16)
                    d4 = v4(dst.ap); s4 = v4(tsn.t[:, 0:H * 64]); t4 = v4(t1.t[:, 0:H * 64])
                    s.tt(V(t4[:, :, 0, :], t1.tk), V(d4[:, :, 1, :], dst.tk), V(s4[:, :, 0, :], tsn.tk), ALU.mult, e="pool")
                    s.tt(V(t4[:, :, 1, :], t1.tk), V(d4[:, :, 0, :], dst.tk), V(s4[:, :, 1, :], tsn.tk), ALU.mult, e="pool")
                    s.tt(dst, dst, tc_[:, 0:H * 64], ALU.mult)
                    s.tt(dst, dst, t1[:, 0:H * 64], ALU.add)

            pt = [s.sb([128, 512], BF16, name="pt%d" % i) for i in range(3)]
            rcp = s.sb([128, 8], name="rcp")
            bias_t = [s.sb([128, 768], name="bias%d" % i) for i in range(2)]
            ptc = [0]

            load_w(wbuf, wb[l], 768, 256)
            for t in range(NKT):
                j = 1 if t < 2 else 0
                x_, h_ = front(Xc, 128 * t if t < 2 else lat_row(128 * (t - 2)), j)
                proj(h_, 0, 256, pe[0])
                rope = None if t < 2 else (cosK, sinK, (t - 2) * 128)
                rms_rope(pe[0][:, 0:128], 2, GK, "k", rope, ht[:, 0:128])
                p = pb[6]
                s.tr(p[:, 0:128], ht[:, 0:128], ident[:])
                s.cp(kT[:, t * 128:(t + 1) * 128], p[:, 0:128], e="act")
                s.cp(Vg[:, t, :, 0:64], V(pe[0].t[:, 128:256].rearrange("p (g d) -> p g d", d=64), pe[0].tk), e="pool")
            load_w(wbuf, wb[l], 1664, 512)
            for t in range(NNT):
                j = 1 if t >= 20 else 0
                if t >= 20:
                    x_, h_ = front(Xc, 128 * (t - 20), j)
                else:
                    x_, h_ = front(XSc, 128 * t, j)
                proj(h_, 0, 512, pe[0])
                for pr_ in range(2):
                    p = pb[6 + pr_]
                    s.tr(p[:, 0:128], pe[0][:, pr_ * 128:(pr_ + 1) * 128], ident[:])
                    s.cp(nkT[pr_][:, t * 128:(t + 1) * 128], p[:, 0:128], e="act" if pr_ == 0 else "dve")
                s.cp(Vn[:, t, :, 0:64], V(pe[0].t[:, 256:512].rearrange("p (g d) -> p g d", d=64), pe[0].tk), e="pool")
            for sl in range(6):
                hh_ = (sl // 2) + 3 * (sl % 2)
                load_w(wbuf, wb[l], 384 + hh_ * 64, 64, dcol=sl * 64)
            load_w(wbuf, wb[l], 1408, 256, dcol=384)
            own_src = lambda t: ((Xc, 128 * (t - 16)) if t >= 16 else (XSc, 256 + 128 * t))
            for t in range(NOWN):
                j = 1 if t >= 16 else 0
                x_, h_ = front(*own_src(t), j)
                proj(h_, 0, 640, pe[0])
                rope = None if t >= 16 else (cosQ, sinQ, 128 * t)
                qr = ht
                rms_rope(pe[0][:, 0:384], 6, GQ, "q", rope, qr[:, 0:384])
                for pr_ in range(3):
                    p = pb[6 + pr_ % 2]
                    s.tr(p[:, 0:128], qr[:, pr_ * 128:(pr_ + 1) * 128], ident[:])
                    s.cp(qT[pr_][:, t * 128:(t + 1) * 128], p[:, 0:128], e="act" if pr_ % 2 == 0 else "dve")
                s.ts(qr[:, 384:640], pe[0][:, 384:640], 0.125, ALU.mult, e="pool")
                for pr_ in range(2):
                    p = pb[6 + pr_]
                    s.tr(p[:, 0:128], qr[:, 384 + pr_ * 128:384 + (pr_ + 1) * 128], ident[:])
                    s.cp(nqT[pr_][:, t * 128:(t + 1) * 128], p[:, 0:128], e="act" if pr_ == 0 else "dve")

            def attend(qsrc, g, qc0, nq, chunks, ksrc, vsrc_fn, bias_fn, dst_fn):
                nqs = nq // 128
                lo, hi = 64 * g, 64 * g + 64
                for ci, (kc0, nk, cid) in enumerate(chunks):
                    ps = pb[ci % 2]
                    bsrc = bias_fn(cid, nk) if bias_fn else None
                    s.mm(ps[0:nk, 0:nq], ksrc[lo:hi, kc0:kc0 + nk], qsrc[lo:hi, qc0:qc0 + nq], start=True, stop=(bsrc is None))
                    if bsrc is not None:
                        s.mm(ps[0:nk, 0:nq], bsrc, ident[:, 0:nq], start=False, stop=True)
                    p_ = pt[ptc[0] % 3]; ptc[0] += 1
                    s.act(p_[0:nk, 0:nq], ps[0:nk, 0:nq], AF.Exp)
                    for qs in range(nqs):
                        s.mm(pb[4 + qs][:, 0:65], p_[0:nk, qs * 128:(qs + 1) * 128], vsrc_fn(cid, nk),
                             start=(ci == 0), stop=(ci == len(chunks) - 1))
                for qs in range(nqs):
                    s.recip(rcp[:, qs:qs + 1], pb[4 + qs][:, 64:65])
                    s.ts(dst_fn(qs), pb[4 + qs][:, 0:64], rcp[:, qs:qs + 1], ALU.mult)

            for qg in range(4):
                for h in range(6):
                    pr_, g = h % 3, h // 3
                    attend(qT[pr_], g, qg * 512, 512, [(c * 128, 128, c) for c in range(NKT)], kT,
                           lambda cid, nk, g=g: Vg[0:nk, cid, g, :], None,
                           lambda qs, qg=qg, h=h: YAN[:, qg * 4 + qs, h * 64:(h + 1) * 64])
            for h in range(6):
                pr_, g = h % 3, h // 3
                attend(qT[pr_], g, 2048, 256, [(c * 128, 128, c) for c in range(2)], kT,
                       lambda cid, nk, g=g: Vg[0:nk, cid, g, :], None,
                       lambda qs, h=h: YAN[:, 16 + qs, h * 64:(h + 1) * 64])
            bc_ = [0]
            for i in range(16):
                chs = na_chunks(i)
                base = chs[0][0] * 128
                cls = na_class(i)
                for hn in range(4):
                    pr_, g = hn // 2, hn % 2
                    bt_ = bias_t[bc_[0] % 2]; bc_[0] += 1
                    nkeys = sum(nk for _, nk in chs)
                    s.dma("sp" if hn % 2 == 0 else "act", bt_[:, 0:nkeys], nab[l, cls, hn, :, 0:nkeys])
                    chunks = [(c * 128, nk, c) for c, nk in chs] + [(20 * 128, 128, 20), (21 * 128, 128, 21)]
                    attend(nqT[pr_], g, i * 128, 128, chunks, nkT[pr_],
                           lambda cid, nk, hn=hn: Vn[0:nk, cid, hn, :],
                           lambda cid, nk, bt_=bt_, base=base: (None if cid >= 20 else bt_[:, cid * 128 - base:cid * 128 - base + nk]),
                           lambda qs, i=i, hn=hn: YAN[:, i, 384 + hn * 64:384 + (hn + 1) * 64])
            for hn in range(4):
                pr_, g = hn // 2, hn % 2
                attend(nqT[pr_], g, 2048, 256, [(20 * 128, 128, 20), (21 * 128, 128, 21)], nkT[pr_],
                       lambda cid, nk, hn=hn: Vn[0:nk, cid, hn, :], None,
                       lambda qs, hn=hn: YAN[:, 16 + qs, 384 + hn * 64:384 + (hn + 1) * 64])
            load_w(wbuf, wb[l], 0, 384)
            load_w(wbuf, wb[l], 1024, 384, dcol=384)
            load_w(wbuf, wb[l], 2176, 256, dcol=768)
            LNG = _A(kT.t[:, 0:2048].bitcast(F32), kT.tk); s.dma("sp", LNG[:], lng[l])
            LNB = _A(kT.t[:, 2048:4096].bitcast(F32), kT.tk); s.dma("act", LNB[:], lnb[l])
            Ff = _A(kT.t[:, 4096:5632].bitcast(F32).rearrange("p (s c) -> p s c", s=2), kT.tk)
            Bk = _A(kT.t[:, 5632:7168].bitcast(F32), kT.tk)
            cen = _A(kT.t[:, 7168:7936].bitcast(F32), kT.tk)
            ysb = _A(qT[1].t[:, 0:768].bitcast(F32), qT[1].tk)
            bsb = _A(qT[1].t[:, 768:1536].bitcast(F32), qT[1].tk)
            sqb = _A(qT[2].t[:, 0:768].bitcast(F32), qT[2].tk)
            YgT = _A(qT[0].t[:, 0:1024].rearrange("p (k t) -> p k t", k=8), qT[0].tk)
            for t in range(NOWN):
                j = 1 if t >= 16 else 0
                x_, h_ = front(*own_src(t), j)
                G = pe[0]
                proj(h_, 0, 1024, G)
                s.act(G[:], G[:], AF.Silu)
                for sr in range(2):
                    q = "sp" if sr == 0 else "act"
                    if t >= 16:
                        s.dma(q, Ff[:, sr, :], YGc[sr * 256 + 128 * (t - 16):sr * 256 + 128 * (t - 16) + 128, :])
                        rb = (2 + sr) * 256 + 128 - 128 * (t - 16)
                        s.dma(q, Bk[:, sr * 384:(sr + 1) * 384], YGc[rb:rb + 128, :])
                    else:
                        s.dma(q, Ff[:, sr, :], YF[sr][128 * t:128 * t + 128, :])
                        s.dma(q, Bk[:, sr * 384:(sr + 1) * 384], YBk[sr][1920 - 128 * t:1920 - 128 * t + 128, :])
                for sr in range(2):
                    s.mm(pb[6 + sr][:, 0:384], jmat[:], Bk[:, sr * 384:(sr + 1) * 384])
                v2 = lambda A_: V(A_.t[:, 0:384].rearrange("p (s c) -> p s c", s=2), A_.tk)
                for sr in range(2):
                    s.tt(ysb[:, sr * 192:(sr + 1) * 192], Ff[:, sr, 0:192], pb[6 + sr][:, 0:192], ALU.add)
                    s.tt(bsb[:, sr * 192:(sr + 1) * 192], Ff[:, sr, 192:384], pb[6 + sr][:, 192:384], ALU.add)
                v3 = lambda A_: V(A_.t[:, 0:384].rearrange("p (h d) -> p h d", d=64), A_.tk)
                s.red(sm[:, 0:6], v3(ysb), ALU.add)
                s.ts(sm[:, 0:6], sm[:, 0:6], 1.0 / 64, ALU.mult)
                s.tt(v3(cen), v3(ysb), V(sm.t[:, 0:6].unsqueeze(2).to_broadcast([128, 6, 64]), sm.tk), ALU.subtract)
                s.tt(sqb[:], cen[:], cen[:], ALU.mult, e="pool")
                s.red(sm[:, 8:14], v3(sqb), ALU.add)
                s.ts(sm[:, 8:14], sm[:, 8:14], 1.0 / 64, ALU.mult, 64e-5, ALU.add)
                s.act(sm[:, 8:14], sm[:, 8:14], AF.Sqrt)
                s.recip(sm[:, 8:14], sm[:, 8:14])
                s.tt(v3(cen), v3(cen), V(sm.t[:, 8:14].unsqueeze(2).to_broadcast([128, 6, 64]), sm.tk), ALU.mult)
                s.tt(cen[:], cen[:], GNG[:], ALU.mult)
                s.tt(cen[:], cen[:], GNB[:], ALU.add)
                s.tt(cen[:], cen[:], bsb[:], ALU.add)
                Yg = pe[1]
                s.tt(Yg[:, 0:384], cen[:], G[:, 0:384], ALU.mult)
                s.tt(Yg[:, 384:1024], YAN[:, t, :], G[:, 384:1024], ALU.mult)
                for half in range(2):
                    p = pb[half]
                    for k in range(4):
                        kk_ = half * 4 + k
                        s.tr(p[:, k * 128:(k + 1) * 128], Yg[:, kk_ * 128:(kk_ + 1) * 128], ident[:])
                    s.cp(YgT[:, half * 4:half * 4 + 4, :], V(p.t[:, :].rearrange("p (k t) -> p k t", k=4), p.tk),
                         e="act" if half == 0 else "dve")
                yo = pe[0]
                proj(YgT, 0, 1024, yo, wsrc=woutb)
                s.tt(yo[:], yo[:], mod[j][:, 2048:3072], ALU.mult)
                z = pe[1]
                s.stt(z[:], x_[:], ALPHA, yo[:], ALU.mult, ALU.add)
                s.red(sm[:, 16:17], z[:], ALU.add)
                s.ts(sm[:, 16:17], sm[:, 16:17], 1.0 / 1024, ALU.mult)
                s.ts(z[:], z[:], sm[:, 16:17], ALU.subtract)
                s.tt(yo[:], z[:], z[:], ALU.mult, e="pool")
                s.red(sm[:, 17:18], yo[:], ALU.add)
                s.ts(sm[:, 17:18], sm[:, 17:18], 1.0 / 1024, ALU.mult, 1e-5, ALU.add)
                s.act(sm[:, 17:18], sm[:, 17:18], AF.Sqrt)
                s.recip(sm[:, 17:18], sm[:, 17:18])
                s.stt(z[:], z[:], sm[:, 17:18], LNG[:], ALU.mult, ALU.mult)
                s.tt(z[:], z[:], LNB[:], ALU.add)
                q = "sp" if t % 2 == 0 else "act"
                if t < 16:
                    if last:
                        tickets.append(s.dma(q, out[128 * t:128 * t + 128, :], z[:]))
                    else:
                        s.dma(q, XN[128 * t:128 * t + 128, :], z[:])
                elif not last:
                    s.dma(q, Xn[128 * (t - 16):128 * (t - 16) + 128, :], z[:])
        if not last:
            for k in range(8):
                s.allgather(XN[256 * k:256 * (k + 1), :], Xn[512 + 1024 * k:512 + 1024 * (k + 1), :], GROUPS)
    s.finish(tickets)
    s.close()
    return nc


def rope_tables():
    t = np.arange(8192)
    row = (t // 64).astype(np.float32); col = (t % 64).astype(np.float32)
    inv = (10000.0 ** (-np.arange(16, dtype=np.float32) / 16)).astype(np.float32)
    ar = row[:, None] * inv; ac = col[:, None] * inv
    ang = np.concatenate([ar, ar, ac, ac], axis=-1).astype(np.float32)
    cos = np.cos(ang).astype(np.float32); sin = np.sin(ang).astype(np.float32)
    sgn = np.concatenate([-np.ones(16), np.ones(16), -np.ones(16), np.ones(16)]).astype(np.float32)
    return cos, sin * sgn


def na_bias(rpb, j):
    NEG = -30000.0
    out = np.full((5, 4, 128, 768), NEG, np.float32)
    tiles = {0: 0, 1: 1, 2: 7, 3: 14, 4: 15}
    for cls, i in tiles.items():
        chs = na_chunks(i)
        srow0 = chs[0][0] * 2
        nkeys = sum(nk for _, nk in chs)
        qrow_l = np.repeat(np.array([2 * i, 2 * i + 1]), 64)
        qcol = np.tile(np.arange(64), 2)
        r = 32 * j + qrow_l
        r_start = np.clip(r - 4, 0, 120)
        c_start = np.clip(qcol - 8, 0, 48)
        key = np.arange(nkeys)
        krow = (32 * j - 4) + srow0 + key // 64
        kcol = key % 64
        dr = krow[None, :] - r[:, None] + 7
        dc = kcol[None, :] - qcol[:, None] + 15
        inwin = ((krow[None, :] >= r_start[:, None]) & (krow[None, :] < r_start[:, None] + 8) &
                 (kcol[None, :] >= c_start[:, None]) & (kcol[None, :] < c_start[:, None] + 16))
        drc = np.clip(dr, 0, 14); dcc = np.clip(dc, 0, 30)
        for h in range(4):
            vals = rpb[h][drc, dcc]
            out[cls, h, :, 0:nkeys] = np.where(inwin, vals, NEG)
    return out


def consts_A():
    idx = np.arange(64)
    inclT = (idx[:, None] <= idx[None, :]).astype(np.float32)
    strictT = (idx[:, None] < idx[None, :]).astype(np.float32)
    strict = strictT.T.copy()
    mask = np.concatenate([inclT, strictT, inclT, -strictT, -strict], axis=1)
    rmask = np.ones((64, 256), np.float32); rmask[:, 0::64] = 0.0
    ident = np.eye(64, dtype=np.float32)
    return np.concatenate([ident, mask, mask, mask, rmask] + [ident] * 6, axis=1).astype(np.float32)


def host_F(inp, depth=4):
    cos, sinS = rope_tables()
    bc = lambda v: np.ascontiguousarray(np.broadcast_to(v[None, :], (128, v.shape[0]))).astype(np.float32)
    L = range(depth)
    shared = dict(
        wmod=np.ascontiguousarray(inp['w_mod'][:depth]),
        bmod=np.stack([bc(inp['b_mod'][l]) for l in L]),
        wb=np.ascontiguousarray(inp['w_in'][:depth, :, 1280:]),
        wout=np.ascontiguousarray(inp['w_out'][:depth]),
        cstA=consts_A(),
        cosK=np.tile(cos, (1, 2)), sinK=np.tile(sinS, (1, 2)),
        gk=np.stack([bc(np.tile(inp['gqa_k_norm'][l], 2)) for l in L]),
        gq=np.stack([bc(np.tile(inp['gqa_q_norm'][l], 6)) for l in L]),
        gng=np.stack([bc(inp['rwkv_gn_g'][l]) for l in L]), gnb=np.stack([bc(inp['rwkv_gn_b'][l]) for l in L]),
        lng=np.stack([bc(inp['ln_g'][l]) for l in L]), lnb=np.stack([bc(inp['ln_b'][l]) for l in L]),
        ident=np.eye(128, dtype=np.float32), jmat=np.ascontiguousarray(np.eye(128, dtype=np.float32)[::-1]),
    )
    per_batch = []
    for b in range(2):
        xa0 = np.zeros((XROWS, 1024), np.float32)
        xa0[0:256] = inp['ctx'][b]
        xa0[512:8704] = inp['x'][b].reshape(4, 8, 256, 1024).transpose(1, 0, 2, 3).reshape(8192, 1024)
        cvec = np.stack([inp['c'][b], inp['c_ctx']], axis=1)
        cv = np.ascontiguousarray(cvec.reshape(8, 128, 2).transpose(1, 0, 2).reshape(128, 16))
        per_batch.append(dict(xa0=xa0, cv=cv))
    nabs = [np.stack([na_bias(inp['na_rpb'][l], j) for l in L]) for j in range(4)]
    maps = []
    for c in range(8):
        b, r = c // 4, c % 4
        d, hh = r // 2, r % 2
        heads = [3 * hh + i for i in range(3)]
        cols = []
        for comp in (0, 384, 768):
            for h in heads:
                cols += list(range(comp + h * 64, comp + h * 64 + 64))
        cols += list(range(1152 + 32 * d, 1152 + 32 * d + 32))
        cols += list(range(1216 + 32 * d, 1216 + 32 * d + 32))
        cols = np.array(cols)
        hcols = np.concatenate([np.arange(h * 64, (h + 1) * 64) for h in heads])
        wa = np.ascontiguousarray(inp['w_in'][:depth][:, :, cols])
        cw = np.zeros((depth, 64, 33), np.float32); pv = np.zeros((depth, 64, 15), np.float32)
        for l in L:
            conv = inp['rwkv_conv'][l][:, cols]
            if d == 1:
                conv = conv[::-1]
            for ci in range(9):
                cw[l, :, ci * 3:ci * 3 + 3] = conv[:, ci * 64:(ci + 1) * 64].T
            cw[l, 0:32, 27:30] = conv[:, 576:608].T
            cw[l, 0:32, 30:33] = conv[:, 608:640].T
            for i, h in enumerate(heads):
                hs = slice(h * 64, (h + 1) * 64)
                pv[l, :, i * 5 + 0] = inp['decay_w0'][l][d, hs]
                pv[l, :, i * 5 + 1] = inp['iclr_a0'][l][d, hs]
                pv[l, :, i * 5 + 2] = inp['rwkv_k_k'][l][hs]
                pv[l, :, i * 5 + 3] = inp['rwkv_k_a'][l][hs]
                pv[l, :, i * 5 + 4] = inp['rwkv_r_k'][l][h]
        w2 = np.ascontiguousarray(inp['decay_w2'][:depth, d][:, :, hcols])
        a2 = np.ascontiguousarray(inp['iclr_a2'][:depth, d][:, :, hcols])
        own = slice(2048 * r, 2048 * r + 2048)
        jd = np.eye(128, dtype=np.float32)
        if d == 1:
            jd = np.ascontiguousarray(jd[::-1])
        dsel = np.zeros((128, 2), np.float32); dsel[:, d] = 1.0
        xs0 = np.zeros((2560, 1024), np.float32)
        lo = 2048 * r - 256
        for sr_ in range(2560):
            pass
        a0, a1 = max(lo, 0), min(lo + 2560, 8192)
        xs0[a0 - lo:a1 - lo] = inp['x'][b][a0:a1]
        m = dict(shared)
        m['xs0'] = xs0
        m.update(per_batch[b])
        m.update(wa=wa, cw=cw, pv=pv, w2=w2, a2=a2, nab=nabs[r],
                 cosQ=np.tile(cos[own], (1, 6)), sinQ=np.tile(sinS[own], (1, 6)), jd=jd, dsel=dsel)
        maps.append(m)
    return maps


from concourse.bass_utils import run_bass_kernel_spmd

_NC = {}


def kernel(**inputs):
    inp = {k: np.asarray(v, dtype=np.float32) for k, v in inputs.items()}
    if 'F' not in _NC:
        _NC['F'] = build_F(4)
    maps = host_F(inp, 4)
    res = run_bass_kernel_spmd(_NC['F'], maps, core_ids=list(range(8))).results
    x = np.stack([np.concatenate([res[b * 4 + r]["out"] for r in range(4)], axis=0) for b in range(2)])
    return np.ascontiguousarray(x.astype(np.float32))
```

```python
import contextlib
import numpy as np
import concourse.bass as bass
import concourse.mybir as mybir

F32 = mybir.dt.float32
BF16 = mybir.dt.bfloat16
AF = mybir.ActivationFunctionType
ALU = mybir.AluOpType
AX = mybir.AxisListType


class Tk:
    __slots__ = ("w", "r", "name", "excl", "acc")

    def __init__(self, name=""):
        self.w = None
        self.r = {}
        self.name = name
        self.excl = False
        self.acc = {}


class T:
    def __init__(self, S, t, name):
        self.t = t
        self.tk = Tk(name)
        self.name = name

    def __getitem__(self, idx):
        return V(self.t[idx], self.tk)


class V:
    __slots__ = ("ap", "tk")

    def __init__(self, ap, tk):
        self.ap = ap
        self.tk = tk


import threading


class _Worker(threading.Thread):
    def __init__(self, il, fn):
        super().__init__(daemon=True)
        self.il = il
        self.fn = fn
        self.go = threading.Event()
        self.done = False
        self.exc = None

    def run(self):
        self.go.wait(); self.go.clear()
        try:
            self.fn()
        except BaseException as e:
            self.exc = e
        self.done = True
        self.il.main_ev.set()

    def pause(self):
        self.il.main_ev.set()
        self.go.wait(); self.go.clear()


class Interleaver:
    def __init__(self, s):
        self.s = s
        self.main_ev = threading.Event()
        self.cur = None

    def run(self, fns, width):
        pending = list(fns)
        active = []
        self.s.yield_hook = self._hook
        try:
            while pending or active:
                while pending and len(active) < width:
                    w = _Worker(self, pending.pop(0)); w.start(); active.append(w)
                for w in list(active):
                    self.cur = w
                    self.main_ev.clear()
                    w.go.set()
                    self.main_ev.wait()
                    if w.exc is not None:
                        raise w.exc
                    if w.done:
                        active.remove(w)
        finally:
            self.s.yield_hook = None
            self.cur = None

    def _hook(self):
        w = self.cur
        if w is not None and threading.current_thread() is w and self.s.atomic_depth == 0:
            w.pause()


class S:
    ENG = ("pe", "act", "dve", "pool", "sp")
    yield_hook = None
    atomic_depth = 0

    @contextlib.contextmanager
    def atomic(self):
        self.atomic_depth += 1
        try:
            yield
        finally:
            self.atomic_depth -= 1
            if self.atomic_depth == 0 and self.yield_hook is not None:
                self.yield_hook()

    def __init__(self, nc):
        self.nc = nc
        self.es = contextlib.ExitStack()
        self.eng = {"pe": nc.tensor, "act": nc.scalar, "dve": nc.vector, "pool": nc.gpsimd, "sp": nc.sync}
        self.sem = {e: self.es.enter_context(nc.semaphore("s_" + e)) for e in self.ENG}
        self.cnt = {e: 0 for e in self.ENG}
        self.dq = {}
        for q in ("sp", "act", "pool"):
            sems = [self.es.enter_context(nc.semaphore("d_%s%d" % (q, i))) for i in range(8)]
            self.dq[q] = dict(sems=sems, cnt=[0] * 8, nxt=0)
            for i, s_ in enumerate(sems):
                self.sem[(q, i)] = s_
        self.waited = {}
        self.cur = self.es
        self.cc_keys = []
        self.n_tiles = 0
        self.n_instr = 0
        self.n_wait = 0

    def sb(self, shape, dt=F32, name=None):
        self.n_tiles += 1
        name = "%s_%d" % (name or "t", self.n_tiles)
        t = self.cur.enter_context(self.nc.sbuf_tensor("sb_" + name, list(shape), dt))
        return T(self, t, name)

    def dram(self, shape, name, dt=F32):
        self.n_tiles += 1
        t = self.nc.dram_tensor("%s_%d" % (name, self.n_tiles), list(shape), dt)
        return T(self, t.ap(), name)

    @contextlib.contextmanager
    def phase(self):
        prev = self.cur
        self.cur = contextlib.ExitStack()
        try:
            yield
        finally:
            self.barrier()
            self.cur.close()
            self.cur = prev

    def barrier(self):
        for e in self.ENG:
            for e2 in self.ENG:
                if e2 != e and self.cnt[e2] > 0:
                    self._wait(e, e2, self.cnt[e2])
            for q, d in self.dq.items():
                for i, c in enumerate(d["cnt"]):
                    if c > 0:
                        self._wait(e, (q, i), c)

    def allgather(self, src, dst, groups):
        if "cc" not in self.dq:
            sems = [self.es.enter_context(self.nc.semaphore("cc%d" % i)) for i in range(8)]
            self.dq["cc"] = dict(sems=sems, cnt=[0] * 8, nxt=0)
            for i, s_ in enumerate(sems):
                self.sem[("cc", i)] = s_
        d = self.dq["cc"]
        i = d["nxt"]; d["nxt"] = (i + 1) % 8
        key = ("cc", i)
        self._wait("pool", key, d["cnt"][i])
        self._deps("pool", [src], [dst])
        ins = self.nc.gpsimd.collective_compute("AllGather", mybir.AluOpType.bypass, replica_groups=groups,
                                                ins=[src.ap.opt()], outs=[dst.ap.opt()])
        d["cnt"][i] += 1
        ins.then_inc(self.sem[key])
        self._mark((key, d["cnt"][i]), [src], [dst])
        self.n_instr += 1
        return (key, d["cnt"][i])

    def ps(self, shape, dt=F32, name=None):
        self.n_tiles += 1
        name = name or "p%d" % self.n_tiles
        t = self.es.enter_context(self.nc.psum_tensor("ps_" + name, list(shape), dt))
        tt_ = T(self, t, name)
        tt_.tk.excl = True
        return tt_

    def close(self):
        self.es.close()

    def _wait(self, e, key, val):
        if val is None:
            return
        k = (e, key)
        if self.waited.get(k, 0) >= val:
            return
        self.waited[k] = val
        self.eng[e].wait_ge(self.sem[key], val)
        self.n_wait += 1

    def _deps(self, e, reads, writes, pe_acc=False):
        for v in list(reads) + list(writes):
            if v.tk.excl:
                for e2, n2 in v.tk.acc.items():
                    if e2 == e and e == "pe":
                        continue
                    self._wait(e, e2, n2)
        reads = [v for v in reads if not v.tk.excl]
        writes = [v for v in writes if not v.tk.excl]
        for v in reads:
            w = v.tk.w
            if w is not None:
                self._wait(e, w[0], w[1])
        for v in writes:
            tk = v.tk
            if tk.w is not None:
                if not (pe_acc and tk.w[0] == "pe" and e == "pe"):
                    self._wait(e, tk.w[0], tk.w[1])
            for re_, rn in tk.r.items():
                if re_ == e and e == "pe":
                    continue
                self._wait(e, re_, rn)

    def _mark(self, ticket, reads, writes):
        for v in list(reads) + list(writes):
            if v.tk.excl:
                v.tk.acc[ticket[0]] = ticket[1]
        reads = [v for v in reads if not v.tk.excl]
        writes = [v for v in writes if not v.tk.excl]
        for v in reads:
            v.tk.r[ticket[0]] = ticket[1]
        for v in writes:
            v.tk.w = ticket
            v.tk.r = {}

    def op(self, e, fn, reads, writes, pe_acc=False):
        reads = [v for v in reads if isinstance(v, V)]
        self._deps(e, reads, writes, pe_acc)
        ins = fn()
        self.cnt[e] += 1
        ins.then_inc(self.sem[e], 1)
        self._mark((e, self.cnt[e]), reads, writes)
        self.n_instr += 1
        if self.yield_hook is not None:
            self.yield_hook()
        return ins

    def dma(self, q, out, in_, **kw):
        d = self.dq[q]
        i = d["nxt"]
        d["nxt"] = (i + 1) % len(d["sems"])
        key = (q, i)
        self._wait(q, key, d["cnt"][i])
        reads = [in_] if isinstance(in_, V) else []
        writes = [out] if isinstance(out, V) else []
        self._deps(q, reads, writes)
        oa = out.ap if isinstance(out, V) else out
        ia = in_.ap if isinstance(in_, V) else in_
        ins = self.eng[q].dma_start(out=oa, in_=ia, **kw)
        d["cnt"][i] += 16
        ins.then_inc(self.sem[key], 16)
        self._mark((key, d["cnt"][i]), reads, writes)
        self.n_instr += 1
        if self.yield_hook is not None:
            self.yield_hook()
        return (key, d["cnt"][i])

    def wait_ticket(self, e, ticket):
        self._wait(e, ticket[0], ticket[1])

    def mm(self, out, lhsT, rhs, start=True, stop=True, **kw):
        return self.op("pe", lambda: self.nc.tensor.matmul(out.ap, lhsT.ap, rhs.ap, start=start, stop=stop, **kw),
                       [lhsT, rhs], [out], pe_acc=not start)

    def tr(self, out, in_, ident):
        return self.op("pe", lambda: self.nc.tensor.transpose(out.ap, in_.ap, ident.ap), [in_, ident], [out])

    def act(self, out, in_, func, bias=None, scale=None, accum_out=None, e="act"):
        kw = {}
        rd = [in_]
        if bias is not None:
            kw["bias"] = bias.ap if isinstance(bias, V) else bias
            rd.append(bias)
        if scale is not None:
            kw["scale"] = scale.ap if isinstance(scale, V) else scale
            rd.append(scale)
        wr = [out]
        if accum_out is not None:
            kw["accum_out"] = accum_out.ap
            wr.append(accum_out)
        return self.op("act", lambda: self.nc.scalar.activation(out.ap, in_.ap, func, **kw), rd, wr)

    def _ve(self, e):
        return {"dve": self.nc.vector, "pool": self.nc.gpsimd, "act": self.nc.scalar}[e]

    def tt(self, out, a, b, op, e="dve"):
        return self.op(e, lambda: self._ve(e).tensor_tensor(out.ap, a.ap, b.ap, op), [a, b], [out])

    def ts(self, out, a, s1, op0, s2=None, op1=None, e="dve", accum_out=None):
        rd = [a, s1, s2]
        a1 = s1.ap if isinstance(s1, V) else s1
        a2 = s2.ap if isinstance(s2, V) else s2
        kw = {}
        wr = [out]
        if op1 is not None:
            kw["op1"] = op1
        if accum_out is not None:
            kw["accum_out"] = accum_out.ap
            wr.append(accum_out)
        return self.op(e, lambda: self._ve(e).tensor_scalar(out.ap, a.ap, a1, a2, op0, **kw), rd, wr)

    def stt(self, out, a, s, b, op0, op1, e="dve"):
        sa = s.ap if isinstance(s, V) else s
        return self.op(e, lambda: self._ve(e).scalar_tensor_tensor(out.ap, a.ap, sa, b.ap, op0, op1), [a, s, b], [out])

    def cp(self, out, in_, e="dve"):
        if e == "act":
            return self.op("act", lambda: self.nc.scalar.copy(out.ap, in_.ap), [in_], [out])
        return self.op(e, lambda: self._ve(e).tensor_copy(out.ap, in_.ap), [in_], [out])

    def memset(self, out, val, e="pool"):
        return self.op(e, lambda: self._ve(e).memset(out.ap, val), [], [out])

    def red(self, out, in_, op, axis=AX.X, e="dve"):
        return self.op(e, lambda: self._ve(e).tensor_reduce(out.ap, in_.ap, axis, op), [in_], [out])

    def recip(self, out, in_):
        return self.op("dve", lambda: self.nc.vector.reciprocal(out.ap, in_.ap), [in_], [out])

    def finish(self, tickets):
        for t in tickets:
            self._wait("sp", t[0], t[1])


A_DEC = 0.6065306597126334
ALPHA = (2 * 4) ** 0.25
NOWN = 18
NKT = 66
NNT = 22
GROUPS = [[0, 1, 2, 3], [4, 5, 6, 7]]
XROWS = 8960


def na_chunks(i):
    if i == 0:
        return [(c, 128) for c in range(0, 6)]
    if i == 1:
        return [(c, 128) for c in range(1, 6)]
    if i == 15:
        return [(c, 128) for c in range(14, 19)] + [(19, 64)]
    return [(c, 128) for c in range(i, i + 4)] + [(i + 4, 64)]


def na_class(i):
    return {0: 0, 1: 1, 14: 3, 15: 4}.get(i, 2)


def lat_row(tau):
    rho, rem = divmod(tau, 2048)
    k, i = divmod(rem, 256)
    return 512 + 1024 * k + 256 * rho + i


class _A:
    def __init__(self, ap, tk):
        self.t = ap; self.tk = tk

    def __getitem__(self, idx):
        return V(self.t[idx], self.tk)


def build_F(depth=4):
    nc = bass.Bass("TRN2", target_bir_lowering=False)
    dt = nc.dram_tensor
    I = lambda n, sh: dt(n, sh, F32, kind="ExternalInput").ap()
    xa0 = I("xa0", [XROWS, 1024]); xs0 = I("xs0", [2560, 1024])
    cv = I("cv", [128, 16]); wmod = I("wmod", [depth, 1024, 3072]); bmod = I("bmod", [depth, 128, 3072])
    wb = I("wb", [depth, 1024, 2432]); wout = I("wout", [depth, 1024, 1024])
    wa = I("wa", [depth, 1024, 640]); cw = I("cw", [depth, 64, 33]); pvi = I("pv", [depth, 64, 15])
    w2 = I("w2", [depth, 32, 192]); a2 = I("a2", [depth, 32, 192])
    cstA = I("cstA", [64, 1664])
    cosK = I("cosK", [8192, 128]); sinK = I("sinK", [8192, 128])
    cosQ = I("cosQ", [2048, 384]); sinQ = I("sinQ", [2048, 384])
    dsel_in = I("dsel", [128, 2])
    gk = I("gk", [depth, 128, 128]); gq = I("gq", [depth, 128, 384])
    nab = I("nab", [depth, 5, 4, 128, 768])
    gng = I("gng", [depth, 128, 384]); gnb = I("gnb", [depth, 128, 384])
    lng = I("lng", [depth, 128, 1024]); lnb = I("lnb", [depth, 128, 1024])
    ident_in = I("ident", [128, 128]); jmat_in = I("jmat", [128, 128]); jd_in = I("jd", [128, 128])
    out = dt("out", [2048, 1024], F32, kind="ExternalOutput").ap()

    s = S(nc)
    tickets = []
    pb = [s.ps([128, 512], name="pb%d" % i) for i in range(8)]
    XA = [s.dram([XROWS, 1024], "XA%d" % i) for i in range(2)]
    YB = s.dram([8448, 384], "YB")
    YGc = s.dram([4 * 256, 384], "YGc")
    YGl = s.dram([16 * 4 * 512, 384], "YGl")
    XN = s.dram([2048, 1024], "XN")
    XS = s.dram([2560, 1024], "XS")
    YF = [s.dram([2048, 384], "YF%d" % i) for i in range(2)]
    YBk = [s.dram([2048, 384], "YBk%d" % i) for i in range(2)]
    xa0_T = _A(xa0, Tk("xa0")); xs0_T = _A(xs0, Tk("xs0"))

    _rd = {}

    def RR(q):
        if q not in _rd:
            pid = s.eng[q].partition_id()
            _rd[q] = pid % 4
        return _rd[q]

    dynq = ["sp", "act", "pool"]
    dync = [0]

    def dyndma(dst_v, src_fn):
        q = dynq[dync[0] % 3]; dync[0] += 1
        return s.dma(q, dst_v, src_fn(RR(q)))

    ident = s.sb([128, 128], name="ident"); s.dma("sp", ident[:], ident_in)
    jmat = s.sb([128, 128], name="jmat"); s.dma("act", jmat[:], jmat_in)
    jd = s.sb([128, 128], name="jd"); s.dma("sp", jd[:], jd_in)
    dsel = s.sb([128, 2], name="dsel"); s.dma("act", dsel[:], dsel_in)
    ones = s.sb([128, 128], name="ones"); s.memset(ones[:], 1.0)
    cv_t = s.sb([128, 16], name="cv"); s.dma("sp", cv_t[:], cv)
    scv = s.sb([128, 16], name="scv"); s.act(scv[:], cv_t[:], AF.Silu)
    mod = [s.sb([128, 3072], name="mod%d" % j) for j in range(2)]
    for l in range(depth):
        Xc = xa0_T if l == 0 else XA[(l - 1) % 2]
        Xn = XA[l % 2]
        last = (l == depth - 1)

        if l == 0:
            XSc = xs0_T
        else:
            XSc = XS
            lat = Xc.t[512:8704, :]
            dyndma(V(XS.t[256:2304, :].rearrange("(o k i) c -> o k (i c)", o=1, k=8), XS.tk),
                   lambda r: V(lat.rearrange("(k rr i) c -> rr k (i c)", rr=4, i=256)[bass.ds(r, 1)], Xc.tk))
            units = lat.rearrange("(u i) c -> u (i c)", i=256)
            dyndma(V(XS.t[0:256, :].rearrange("(o i) c -> o (i c)", o=1), XS.tk),
                   lambda r: V(units[bass.ds(r + 27, 1), :], Xc.tk))
            dyndma(V(XS.t[2304:2560, :].rearrange("(o i) c -> o (i c)", o=1), XS.tk),
                   lambda r: V(units[bass.ds(r + 1, 1), :], Xc.tk))
        with s.phase():
            stage_w = s.sb([128, 8, 512], name="stage_w")
            Rl = s.sb([128, 8, 128], name="Rl")
            bmod_t = s.sb([128, 512], name="bmodt")
            for j in range(2):
                for k in range(8):
                    s.ts(Rl[:, k, :], ones[:], scv[:, 2 * k + j:2 * k + j + 1], ALU.mult, e="dve" if k % 2 == 0 else "pool")
                for cb in range(6):
                    s.dma("sp", stage_w[:], wmod[l].rearrange("(k p) c -> p k c", p=128)[:, :, cb * 512:(cb + 1) * 512])
                    s.dma("act", bmod_t[:], bmod[l][:, cb * 512:(cb + 1) * 512])
                    ps = pb[cb % 2]
                    for k in range(8):
                        s.mm(ps[:, :], Rl[:, k, :], stage_w[:, k, :], start=(k == 0), stop=(k == 7))
                    s.tt(mod[j][:, cb * 512:(cb + 1) * 512], ps[:, :], bmod_t[:], ALU.add)
                s.ts(mod[j][:, 1024:2048], mod[j][:, 1024:2048], 1.0, ALU.add, e="pool")

        with s.phase():
            cst_t = s.sb([64, 1664], name="cst"); s.dma("sp", cst_t[:], cstA)
            identA = cst_t[:, 0:64]
            mask3 = lambda h: cst_t[:, 64 + h * 320: 64 + (h + 1) * 320]
            rmask = cst_t[:, 1024:1280]
            idt3 = cst_t[:, 1280:1664]
            cw_t = s.sb([64, 33], name="cw"); s.dma("act", cw_t[:], cw[l])
            pv_t = s.sb([64, 16], name="pv"); s.dma("act", pv_t[:, 0:15], pvi[l])
            omk = s.sb([64, 3], name="omk")
            for h in range(3):
                s.ts(omk[:, h:h + 1], pv_t[:, h * 5 + 3:h * 5 + 4], -1.0, ALU.mult, 1.0, ALU.add)
            w2_t = s.sb([32, 192], name="w2"); s.dma("act", w2_t[:], w2[l])
            a2_t = s.sb([32, 192], name="a2"); s.dma("act", a2_t[:], a2[l])
            xt = [s.sb([128, 1024], name="xt%d" % i) for i in range(2)]
            ht = s.sb([128, 1024], name="ht")
            xr = s.sb([128, 1024], name="xr")
            wab = s.sb([128, 8, 640], BF16, name="wab")
            for k in range(8):
                st_ = xt[k % 2]
                s.dma("sp" if k % 2 == 0 else "act", st_[:, 0:640], wa[l][k * 128:(k + 1) * 128, :])
                s.cp(wab[:, k, :], st_[:, 0:640], e="dve" if k % 2 == 0 else "pool")
            cts = [(i * 64, 64) for i in range(9)] + [(576, 32), (608, 32)]
            hg = [s.sb([128, 8, 258], BF16, name="hg%d" % i) for i in range(2)]
            for h_ in hg:
                s.memset(h_[:], 0.0)
            raw = [s.sb([64, 258], name="raw%d" % i) for i in range(2)]
            ctmp = [s.sb([64, 256], name="ctmp%d" % i) for i in range(2)]
            mk = lambda nm, shape=(64, 256), dt_=F32: [s.sb(list(shape), dt_, name="%s%d" % (nm, h)) for h in range(3)]
            uR, uK, uV = mk("uR"), mk("uK"), mk("uV", dt_=BF16)
            uD = s.sb([32, 256], name="uD"); uA = s.sb([32, 256], name="uA"); ddt = s.sb([32, 256], name="ddt")
            sg, ic, kk, tmp, kd, bd = mk("sg"), mk("ic"), mk("kk"), mk("tmp"), mk("kd"), mk("bd")
            cs, csx, csr = mk("cs"), mk("csx"), mk("csr")
            E1, E3 = mk("E1"), mk("E3")
            E4 = E1
            RH, KKH, kt, bt, kc, bc, rk = (mk("RH", dt_=BF16), mk("KKH", dt_=BF16), mk("kt", dt_=BF16), mk("bt", dt_=BF16),
                                           mk("kc", dt_=BF16), mk("bc", dt_=BF16), mk("rk", dt_=BF16))
            identAb = s.sb([64, 64], BF16, name="identAb"); s.cp(identAb[:], identA)
            onesb = s.sb([64, 1], BF16, name="onesb"); s.memset(onesb[:], 1.0)
            wc = mk("wc", (64, 4)); rn = tmp
            trT = [s.sb([64, 3, 256], BF16, name="trT%d" % i) for i in range(4)]
            scS = [s.sb([64, 3, 320], BF16, name="scS%d" % i) for i in range(4)]
            XYs = [[s.sb([64, 3, 128], BF16, name="XY%d_%d" % (c, i)) for i in range(2)] for c in range(4)]
            PQs = [[s.sb([64, 3, 128], BF16, name="PQ%d_%d" % (c, i)) for i in range(2)] for c in range(4)]
            KKpTs = [s.sb([64, 192], BF16, name="KKpT%d" % c) for c in range(4)]
            AVs = [s.sb([64, 192], BF16, name="AV%d" % c) for c in range(4)]
            Ulocs = [s.sb([64, 192], name="Uloc%d" % c) for c in range(4)]
            Us = [s.sb([64, 192], BF16, name="U%d" % c) for c in range(4)]
            STb = [s.sb([64, 192], BF16, name="STb%d" % i) for i in range(2)]
            bss = [s.sb([64, 4], name="bs%d" % c) for c in range(4)]
            il = Interleaver(s)
            ST = [s.sb([64, 192], name="ST%d" % i) for i in range(2)]
            YBuf = [s.sb([64, 4, 384], name="YBuf0")] * 2
            bs_ = s.sb([64, 4], name="bs")
            s.memset(ST[0][:], 0.0)
            s.memset(STb[0][:], 0.0)
            sti = 0
            acnt = [0]

            def frontA(g):
                hgt = hg[g % 2]
                for a in range(2):
                    u = 2 * g + a
                    i = acnt[0]; acnt[0] += 1
                    if u < 2:
                        bf_, br_ = 128 * u, 128 * (1 - u)
                        j = 1
                    else:
                        v = u - 2
                        bf_, br_ = lat_row(128 * v), lat_row(128 * (63 - v))
                        j = 0
                    x_ = xt[i % 2]
                    s.dma("sp", x_[:], V(Xc.t[bf_:bf_ + 128, :], Xc.tk))
                    s.dma("act", xr[:], V(Xc.t[br_:br_ + 128, :], Xc.tk))
                    s.ts(x_[:], x_[:], dsel[:, 0:1], ALU.mult, e="pool")
                    s.stt(x_[:], xr[:], dsel[:, 1:2], x_[:], ALU.mult, ALU.add)
                    s.tt(ht[:], x_[:], mod[j][:, 1024:2048], ALU.mult, e="pool")
                    s.tt(ht[:], ht[:], mod[j][:, 0:1024], ALU.add, e="dve")
                    for half in range(2):
                        p = pb[half]
                        for k in range(4):
                            kk_ = half * 4 + k
                            s.tr(p[:, k * 128:(k + 1) * 128], ht[:, kk_ * 128:(kk_ + 1) * 128], jd[:])
                        s.cp(hgt[:, half * 4:half * 4 + 4, 1 + 128 * a:1 + 128 * (a + 1)],
                             V(p.t[:, :].rearrange("p (k t) -> p k t", k=4), p.tk), e="act" if half == 0 else "dve")

            NGRP = 33
            frontA(0)
            chunk_no = 0
            for g in range(NGRP):
                first = g in (0, 1)
                lastg = g in (0, NGRP - 1)
                j = 1 if g == 0 else 0
                tbase = 256 * g
                hgt = hg[g % 2]
                if g + 1 < NGRP:
                    frontA(g + 1)
                    hn_ = hg[(g + 1) % 2]
                    s.cp(hgt[:, :, 257:258], hn_[:, :, 1:2], e="pool")
                    s.cp(hn_[:, :, 0:1], hgt[:, :, 256:257], e="pool")
                for ci, (c0, M) in enumerate(cts):
                    pr = pb[2 + ci % 2]
                    for k in range(8):
                        s.mm(pr[0:M, 0:258], wab[:, k, c0:c0 + M], hgt[:, k, :], start=(k == 0), stop=(k == 7))
                    rw = raw[ci % 2]
                    s.cp(rw[0:M, :], pr[0:M, 0:258], e="act" if ci % 2 == 0 else "dve")
                    if first:
                        s.memset(rw[0:M, 0:1], 0.0, e="pool")
                    if lastg:
                        s.memset(rw[0:M, 257:258], 0.0, e="pool")
                    dst = (uR, uK, uV)[ci // 3][ci % 3] if ci < 9 else (uD, uA)[ci - 9]
                    tm = ctmp[ci % 2]
                    s.act(tm[0:M, :], rw[0:M, 0:256], AF.Identity, scale=cw_t[0:M, ci * 3:ci * 3 + 1])
                    s.stt(tm[0:M, :], rw[0:M, 1:257], cw_t[0:M, ci * 3 + 1:ci * 3 + 2], tm[0:M, :], ALU.mult, ALU.add)
                    s.stt(dst[0:M, :], rw[0:M, 2:258], cw_t[0:M, ci * 3 + 2:ci * 3 + 3], tm[0:M, :], ALU.mult, ALU.add)
                s.act(ddt[:], uD[:], AF.Tanh)
                for h in range(3):
                    P = lambda i: pv_t[:, h * 5 + i:h * 5 + i + 1]
                    pz = pb[4]
                    s.mm(pz[0:64, 0:256], w2_t[:, h * 64:(h + 1) * 64], ddt[:])
                    s.act(sg[h][:], pz[0:64, 0:256], AF.Sigmoid, bias=P(0))
                    s.mm(pz[0:64, 256:512], a2_t[:, h * 64:(h + 1) * 64], uA[:])
                    s.act(ic[h][:], pz[0:64, 256:512], AF.Sigmoid, bias=P(1))
                    s.ts(kk[h][:], uK[h][:], P(2), ALU.mult, e="pool")
                    s.tt(tmp[h][:], kk[h][:], kk[h][:], ALU.mult, e="pool")
                    pss = pb[5]
                    s.mm(pss[0:64, 0:256], ones[0:64, 0:64], tmp[h][:])
                    s.ts(rn[h][:], pss[0:64, 0:256], 1e-12, ALU.max)
                    s.act(rn[h][:], rn[h][:], AF.Sqrt)
                    s.recip(rn[h][:], rn[h][:])
                    s.tt(kk[h][:], kk[h][:], rn[h][:], ALU.mult)
                    s.ts(tmp[h][:], ic[h][:], P(3), ALU.mult, omk[:, h:h + 1], ALU.add, e="pool")
                    s.tt(kd[h][:], uK[h][:], tmp[h][:], ALU.mult, e="pool")
                    s.tt(bd[h][:], kk[h][:], ic[h][:], ALU.mult, e="pool")
                    s.op("dve", lambda h=h: nc.vector.tensor_tensor_scan(cs[h][:].ap, rmask.ap, sg[h][:].ap, 0.0, ALU.mult, ALU.add),
                         [rmask, sg[h][:]], [cs[h][:]])
                    s.tt(csx[h][:], cs[h][:], sg[h][:], ALU.subtract)
                    for c in range(4):
                        s.ts(csr[h][:, c * 64:(c + 1) * 64], cs[h][:, c * 64:(c + 1) * 64],
                             cs[h][:, c * 64 + 63:c * 64 + 64], ALU.subtract)
                    s.act(wc[h][:], cs[h][:, 63::64], AF.Exp, scale=-A_DEC)
                    s.act(E1[h][:], cs[h][:], AF.Exp, scale=-A_DEC)
                    s.tt(RH[h][:], uR[h][:], E1[h][:], ALU.mult, e="pool")
                    s.act(E1[h][:], csx[h][:], AF.Exp, scale=-A_DEC)
                    s.tt(KKH[h][:], kk[h][:], E1[h][:], ALU.mult, e="pool")
                    s.act(E3[h][:], cs[h][:], AF.Exp, scale=A_DEC)
                    s.tt(kt[h][:], kd[h][:], E3[h][:], ALU.mult)
                    s.tt(bt[h][:], bd[h][:], E3[h][:], ALU.mult, e="pool")
                    s.act(E4[h][:], csr[h][:], AF.Exp, scale=A_DEC)
                    s.tt(kc[h][:], kd[h][:], E4[h][:], ALU.mult)
                    s.tt(bc[h][:], bd[h][:], E4[h][:], ALU.mult, e="pool")
                    s.stt(rk[h][:], uR[h][:], P(4), kd[h][:], ALU.mult, ALU.mult)
                yb = YBuf[g % 2]

                def chunk_body(c, n, yb=yb):
                    cc = slice(c * 64, (c + 1) * 64)
                    tT = trT[c]; sS = scS[c]
                    XY = XYs[c]; PQ = PQs[c]; KKpT = KKpTs[c]; AV = AVs[c]; Uloc = Ulocs[c]; U = Us[c]; bs_ = bss[c]
                    p_ = c % 2
                    for h in range(3):
                        ptr = pb[2 + p_]
                        with s.atomic():
                            ptrb = ptr.t[0:64, 0:128].bitcast(BF16)
                            for i, src_ in enumerate((KKH, bc, kc, uV)):
                                s.tr(V(ptrb[:, i * 64:(i + 1) * 64], ptr.tk), src_[h][:, cc], identAb[:])
                            s.cp(tT[:, h, :], V(ptrb[:, 0:256], ptr.tk), e="act")
                        psc = pb[4 + p_]
                        with s.atomic():
                            s.mm(psc[0:64, 0:64], kt[h][:, cc], RH[h][:, cc])
                            s.mm(psc[0:64, 64:128], kt[h][:, cc], KKH[h][:, cc])
                            s.mm(psc[0:64, 128:192], bt[h][:, cc], RH[h][:, cc])
                            s.mm(psc[0:64, 192:256], bt[h][:, cc], KKH[h][:, cc])
                            s.mm(psc[0:64, 256:320], KKH[h][:, cc], bt[h][:, cc])
                            s.tt(sS[:, h, :], psc[0:64, 0:320], mask3(h), ALU.mult)
                    s.tt(PQ[0][:, :, :], sS[:, :, 192:320], V(idt3.ap.rearrange("p (h c) -> p h c", c=128), idt3.tk), ALU.add, e="pool")
                    Xc_ = lambda lvl, h: (sS[:, h, 192:256] if lvl == 0 else XY[lvl % 2][:, h, 0:64])
                    Yc_ = lambda lvl, h: (sS[:, h, 256:320] if lvl == 0 else XY[lvl % 2][:, h, 64:128])
                    for lvl in range(5):
                        pn, pq = pb[4 + p_], pb[6 + p_]
                        nxt = XY[(lvl + 1) % 2]
                        with s.atomic():
                            for h in range(3):
                                s.mm(pn[0:64, h * 128:h * 128 + 64], Yc_(lvl, h), Xc_(lvl, h))
                                if lvl < 4:
                                    s.mm(pn[0:64, h * 128 + 64:h * 128 + 128], Xc_(lvl, h), Yc_(lvl, h))
                            pn3 = pn.t[0:64, 0:384].rearrange("p (h c) -> p h c", c=128)
                            if lvl < 4:
                                s.cp(nxt[:, :, :], V(pn3, pn.tk), e="act")
                            else:
                                s.cp(nxt[:, :, 0:64], V(pn3[:, :, 0:64], pn.tk), e="act")
                        Pc, Pn = PQ[lvl % 2], PQ[(lvl + 1) % 2]
                        with s.atomic():
                            for h in range(3):
                                s.mm(pq[0:64, h * 128:h * 128 + 64], Pc[:, h, 64:128], nxt[:, h, 0:64])
                                if lvl < 4:
                                    s.mm(pq[0:64, h * 128 + 64:h * 128 + 128], Pc[:, h, 0:64], nxt[:, h, 64:128])
                            pq3 = pq.t[0:64, 0:384].rearrange("p (h c) -> p h c", c=128)
                            if lvl < 4:
                                s.tt(Pn[:, :, :], V(pq3, pq.tk), Pc[:, :, :], ALU.add)
                            else:
                                s.tt(Pn[:, :, 0:64], V(pq3[:, :, 0:64], pq.tk), Pc[:, :, 0:64], ALU.add)
                    TT = PQ[1]
                    pk = pb[2 + p_]
                    with s.atomic():
                        for h in range(3):
                            s.mm(pk[0:64, h * 64:(h + 1) * 64], tT[:, h, 0:64], TT[:, h, 0:64])
                            s.mm(pk[0:64, 192 + h * 64:192 + (h + 1) * 64], sS[:, h, 64:128], tT[:, h, 192:256])
                        s.cp(KKpT[:], pk[0:64, 0:192], e="act")
                        s.cp(AV[:], pk[0:64, 192:384], e="dve")
                    pk3 = pb[6 + p_]
                    with s.atomic():
                        for h in range(3):
                            s.mm(pk3[0:64, h * 64:(h + 1) * 64], TT[:, h, 0:64], AV[:, h * 64:(h + 1) * 64])
                        s.cp(Uloc[:], pk3[0:64, 0:192], e="act")

                def chunk_seq(c, n, yb=yb):
                    cc = slice(c * 64, (c + 1) * 64)
                    tT = trT[c]; sS = scS[c]
                    KKpT = KKpTs[c]; Uloc = Ulocs[c]; U = Us[c]; bs_ = bss[c]
                    Sc, Sn = ST[n % 2], ST[(n + 1) % 2]
                    Scb, Snb = STb[n % 2], STb[(n + 1) % 2]
                    pu = pb[0]
                    with s.atomic():
                        for h in range(3):
                            s.mm(pu[0:64, h * 64:(h + 1) * 64], KKpT[:, h * 64:(h + 1) * 64], Scb[:, h * 64:(h + 1) * 64])
                        s.stt(U[:], pu[0:64, 0:192], -1.0, Uloc[:], ALU.mult, ALU.subtract)
                    pS = pb[1]
                    with s.atomic():
                        for h in range(3):
                            hs = slice(h * 64, (h + 1) * 64)
                            s.mm(pS[0:64, hs], tT[:, h, 128:192], tT[:, h, 192:256], start=True, stop=False)
                            s.mm(pS[0:64, hs], tT[:, h, 64:128], U[:, hs], start=False, stop=True)
                        for h in range(3):
                            hs = slice(h * 64, (h + 1) * 64)
                            s.stt(Sn[:, hs], Sc[:, hs], wc[h][:, c:c + 1], pS[0:64, hs], ALU.mult, ALU.add)
                        s.cp(Snb[:], Sn[:], e="act")
                    py = pb[0]
                    with s.atomic():
                        for h in range(3):
                            hs = slice(256 + h * 64, 256 + (h + 1) * 64)
                            hh = slice(h * 64, (h + 1) * 64)
                            s.mm(py[0:64, hs], RH[h][:, cc], Scb[:, hh], start=True, stop=False)
                            s.mm(py[0:64, hs], sS[:, h, 128:192], U[:, hh], start=False, stop=False)
                            s.mm(py[0:64, hs], sS[:, h, 0:64], tT[:, h, 192:256], start=False, stop=True)
                        s.cp(yb[:, c, 0:192], py[0:64, 256:448], e="act")
                    pbn = pb[1]
                    with s.atomic():
                        for h in range(3):
                            s.mm(pbn[0:64, 256 + h:256 + h + 1], rk[h][:, cc], onesb[:, 0:1])
                        s.cp(bs_[:, 0:3], pbn[0:64, 256:259], e="dve")
                    for h in range(3):
                        s.ts(yb[:, c, 192 + h * 64:192 + (h + 1) * 64], tT[:, h, 192:256], bs_[:, h:h + 1], ALU.mult, e="pool")
                il.run([(lambda c=c: chunk_body(c, 4 * g + c)) for c in range(4)], 4)
                for c in range(4):
                    chunk_seq(c, 4 * g + c)
                s.dma("sp" if g % 2 == 0 else "act",
                      V(YB.t[tbase:tbase + 256, :].rearrange("(c t) f -> t c f", t=64), YB.tk), yb[:, :, :])
        s.allgather(YB[0:256, :], YGc[:, :], GROUPS)
        for m in range(16):
            s.allgather(YB[256 + 512 * m:256 + 512 * (m + 1), :], YGl[2048 * m:2048 * (m + 1), :], GROUPS)
        ygv = YGl.t.rearrange("(m sr i) c -> sr m (i c)", sr=4, i=512)
        for sr in range(2):
            dyndma(V(YF[sr].t.rearrange("(m i) c -> m (i c)", i=512), YF[sr].tk),
                   lambda r, sr=sr: V(ygv[sr][bass.ds(r * 4, 4), :], YGl.tk))
            dyndma(V(YBk[sr].t.rearrange("(m i) c -> m (i c)", i=512), YBk[sr].tk),
                   lambda r, sr=sr: V(ygv[2 + sr][bass.ds((3 - r) * 4, 4), :], YGl.tk))
        with s.phase():
            GK = s.sb([128, 128], name="GK"); s.dma("act", GK[:], gk[l])
            GQ = s.sb([128, 384], name="GQ"); s.dma("act", GQ[:], gq[l])
            GNG = s.sb([128, 384], name="GNG"); s.dma("act", GNG[:], gng[l])
            GNB = s.sb([128, 384], name="GNB"); s.dma("act", GNB[:], gnb[l])
            YAN = s.sb([128, 18, 640], BF16, name="YAN")
            wbuf = s.sb([128, 8, 1024], BF16, name="wbuf")
            woutb = s.sb([128, 8, 1024], BF16, name="woutb")
            xt = [s.sb([128, 1024], name="xt%d" % i) for i in range(2)]
            ht = s.sb([128, 1024], name="ht")
            pe = [s.sb([128, 1024], name="pe%d" % i) for i in range(2)]
            wst = xt

            def load_w(dst, src, c0, ncols, dcol=0):
                for k in range(8):
                    st_ = wst[k % 2]
                    s.dma("sp" if k % 2 == 0 else "act", st_[:, 0:ncols], src[k * 128:(k + 1) * 128, c0:c0 + ncols])
                    s.cp(dst[:, k, dcol:dcol + ncols], st_[:, 0:ncols], e="dve" if k % 2 == 0 else "pool")

            load_w(woutb, wout[l], 0, 1024)
            kT = s.sb([128, 8448], BF16, name="kT")
            Vg = s.sb([128, NKT, 2, 65], BF16, name="Vg")
            qT = [s.sb([128, 2304], BF16, name="qT%d" % i) for i in range(3)]
            nqT = [s.sb([128, 2304], BF16, name="nqT%d" % i) for i in range(2)]
            nkT = [s.sb([128, 2816], BF16, name="nkT%d" % i) for i in range(2)]
            Vn = s.sb([128, NNT, 4, 65], BF16, name="Vn")
            s.memset(Vg[:, :, :, 64:65], 1.0)
            s.memset(Vn[:, :, :, 64:65], 1.0)
            hT = [s.sb([128, 8, 128], BF16, name="hT%d" % i) for i in range(2)]
            tc_ = s.sb([128, 384], name="tcos"); tsn = s.sb([128, 384], name="tsin")
            sm = s.sb([128, 64], name="sm")
            cnt = [0]

            def front(srcT, row, j):
                i = cnt[0]; cnt[0] += 1
                q = "sp" if i % 2 == 0 else "act"
                x_ = xt[i % 2]
                s.dma(q, x_[:], V(srcT.t[row:row + 128, :], srcT.tk))
                s.tt(ht[:], x_[:], mod[j][:, 1024:2048], ALU.mult, e="pool")
                s.tt(ht[:], ht[:], mod[j][:, 0:1024], ALU.add, e="dve")
                h_ = hT[i % 2]
                for half in range(2):
                    p = pb[half]
                    for k in range(4):
                        kk_ = half * 4 + k
                        s.tr(p[:, k * 128:(k + 1) * 128], ht[:, kk_ * 128:(kk_ + 1) * 128], ident[:])
                    s.cp(h_[:, half * 4:half * 4 + 4, :], V(p.t[:, :].rearrange("p (k t) -> p k t", k=4), p.tk),
                         e="act" if half == 0 else "dve")
                return x_, h_

            def proj(h_, c0, ncols, dst, wsrc=None):
                wsrc = wsrc or wbuf
                o = 0
                bi = 2
                while o < ncols:
                    n = min(512, ncols - o)
                    p = pb[bi]
                    for k in range(8):
                        s.mm(p[:, 0:n], h_[:, k, :], wsrc[:, k, c0 + o:c0 + o + n], start=(k == 0), stop=(k == 7))
                    s.cp(dst[:, o:o + n], p[:, 0:n], e="act" if bi == 2 else "dve")
                    o += n
                    bi = 5 - bi

            def rms_rope(src, H, gtab, scale_mode, rope, dst):
                sq = pe[1]
                s.tt(sq[:, 0:H * 64], src, src, ALU.mult, e="pool")
                s.red(sm[:, 0:H], V(sq.t[:, 0:H * 64].rearrange("p (h d) -> p h d", d=64), sq.tk), ALU.add)
                if scale_mode == "k":
                    s.ts(sm[:, 0:H], sm[:, 0:H], 1.0 / 64, ALU.mult, 1e-6, ALU.add)
                else:
                    s.ts(sm[:, 0:H], sm[:, 0:H], 64e-6, ALU.add)
                s.act(sm[:, 0:H], sm[:, 0:H], AF.Sqrt)
                s.recip(sm[:, 0:H], sm[:, 0:H])
                for h in range(H):
                    s.stt(V(dst.ap[:, h * 64:(h + 1) * 64], dst.tk), V(src.ap[:, h * 64:(h + 1) * 64], src.tk), sm[:, h:h + 1],
                          gtab[:, h * 64:(h + 1) * 64], ALU.mult, ALU.mult)
                if rope is not None:
                    cos_d, sin_d, rowfn = rope
                    s.dma("sp", tc_[:, 0:H * 64], cos_d[rowfn:rowfn + 128, :])
                    s.dma("act", tsn[:, 0:H * 64], sin_d[rowfn:rowfn + 128, :])
                    t1 = sq
                    v4 = lambda ap: ap.rearrange("p (g a d) -> p g a d", a=2, d=16)
                    d4 = v4(dst.ap); s4 = v4(tsn.t[:, 0:H * 64]); t4 = v4(t1.t[:, 0:H * 64])
                    s.tt(V(t4[:, :, 0, :], t1.tk), V(d4[:, :, 1, :], dst.tk), V(s4[:, :, 0, :], tsn.tk), ALU.mult, e="pool")
                    s.tt(V(t4[:, :, 1, :], t1.tk), V(d4[:, :, 0, :], dst.tk), V(s4[:, :, 1, :], tsn.tk), ALU.mult, e="pool")
                    s.tt(dst, dst, tc_[:, 0:H * 64], ALU.mult)
                    s.tt(dst, dst, t1[:, 0:H * 64], ALU.add)

            pt = [s.sb([128, 512], BF16, name="pt%d" % i) for i in range(3)]
            rcp = s.sb([128, 8], name="rcp")
            bias_t = [s.sb([128, 768], name="bias%d" % i) for i in range(2)]
            ptc = [0]

            load_w(wbuf, wb[l], 768, 256)
            for t in range(NKT):
                j = 1 if t < 2 else 0
                x_, h_ = front(Xc, 128 * t if t < 2 else lat_row(128 * (t - 2)), j)
                proj(h_, 0, 256, pe[0])
                rope = None if t < 2 else (cosK, sinK, (t - 2) * 128)
                rms_rope(pe[0][:, 0:128], 2, GK, "k", rope, ht[:, 0:128])
                p = pb[6]
                s.tr(p[:, 0:128], ht[:, 0:128], ident[:])
                s.cp(kT[:, t * 128:(t + 1) * 128], p[:, 0:128], e="act")
                s.cp(Vg[:, t, :, 0:64], V(pe[0].t[:, 128:256].rearrange("p (g d) -> p g d", d=64), pe[0].tk), e="pool")
            load_w(wbuf, wb[l], 1664, 512)
            for t in range(NNT):
                j = 1 if t >= 20 else 0
                if t >= 20:
                    x_, h_ = front(Xc, 128 * (t - 20), j)
                else:
                    x_, h_ = front(XSc, 128 * t, j)
                proj(h_, 0, 512, pe[0])
                for pr_ in range(2):
                    p = pb[6 + pr_]
                    s.tr(p[:, 0:128], pe[0][:, pr_ * 128:(pr_ + 1) * 128], ident[:])
                    s.cp(nkT[pr_][:, t * 128:(t + 1) * 128], p[:, 0:128], e="act" if pr_ == 0 else "dve")
                s.cp(Vn[:, t, :, 0:64], V(pe[0].t[:, 256:512].rearrange("p (g d) -> p g d", d=64), pe[0].tk), e="pool")
            for sl in range(6):
                hh_ = (sl // 2) + 3 * (sl % 2)
                load_w(wbuf, wb[l], 384 + hh_ * 64, 64, dcol=sl * 64)
            load_w(wbuf, wb[l], 1408, 256, dcol=384)
            own_src = lambda t: ((Xc, 128 * (t - 16)) if t >= 16 else (XSc, 256 + 128 * t))
            for t in range(NOWN):
                j = 1 if t >= 16 else 0
                x_, h_ = front(*own_src(t), j)
                proj(h_, 0, 640, pe[0])
                rope = None if t >= 16 else (cosQ, sinQ, 128 * t)
                qr = ht
                rms_rope(pe[0][:, 0:384], 6, GQ, "q", rope, qr[:, 0:384])
                for pr_ in range(3):
                    p = pb[6 + pr_ % 2]
                    s.tr(p[:, 0:128], qr[:, pr_ * 128:(pr_ + 1) * 128], ident[:])
                    s.cp(qT[pr_][:, t * 128:(t + 1) * 128], p[:, 0:128], e="act" if pr_ % 2 == 0 else "dve")
                s.ts(qr[:, 384:640], pe[0][:, 384:640], 0.125, ALU.mult, e="pool")
                for pr_ in range(2):
                    p = pb[6 + pr_]
                    s.tr(p[:, 0:128], qr[:, 384 + pr_ * 128:384 + (pr_ + 1) * 128], ident[:])
                    s.cp(nqT[pr_][:, t * 128:(t + 1) * 128], p[:, 0:128], e="act" if pr_ == 0 else "dve")

            def attend(qsrc, g, qc0, nq, chunks, ksrc, vsrc_fn, bias_fn, dst_fn):
                nqs = nq // 128
                lo, hi = 64 * g, 64 * g + 64
                for ci, (kc0, nk, cid) in enumerate(chunks):
                    ps = pb[ci % 2]
                    bsrc = bias_fn(cid, nk) if bias_fn else None
                    s.mm(ps[0:nk, 0:nq], ksrc[lo:hi, kc0:kc0 + nk], qsrc[lo:hi, qc0:qc0 + nq], start=True, stop=(bsrc is None))
                    if bsrc is not None:
                        s.mm(ps[0:nk, 0:nq], bsrc, ident[:, 0:nq], start=False, stop=True)
                    p_ = pt[ptc[0] % 3]; ptc[0] += 1
                    s.act(p_[0:nk, 0:nq], ps[0:nk, 0:nq], AF.Exp)
                    for qs in range(nqs):
                        s.mm(pb[4 + qs][:, 0:65], p_[0:nk, qs * 128:(qs + 1) * 128], vsrc_fn(cid, nk),
                             start=(ci == 0), stop=(ci == len(chunks) - 1))
                for qs in range(nqs):
                    s.recip(rcp[:, qs:qs + 1], pb[4 + qs][:, 64:65])
                    s.ts(dst_fn(qs), pb[4 + qs][:, 0:64], rcp[:, qs:qs + 1], ALU.mult)

            oT = _A(pe[0].t[0:65, 0:512], pe[0].tk)
            fin = [0]

            def attend_T(qsrc, g, qc0, nq, chunks, ksrc, vsrc_fn, dst_fn):
                nqs = nq // 128
                lo, hi = 64 * g, 64 * g + 64
                po = pb[4]
                n_ = len(chunks)
                for ci in range(n_ + 1):
                    if ci < n_:
                        kc0, nk, cid = chunks[ci]
                        s.mm(pb[ci % 3][0:nk, 0:nq], ksrc[lo:hi, kc0:kc0 + nk], qsrc[lo:hi, qc0:qc0 + nq])
                    if ci >= 1:
                        kc0p, nkp, cidp = chunks[ci - 1]
                        p_ = pt[ptc[0] % 3]; ptc[0] += 1
                        s.act(p_[0:nkp, 0:nq], pb[(ci - 1) % 3][0:nkp, 0:nq], AF.Exp)
                        s.mm(po[0:65, 0:nq], vsrc_fn(cidp, nkp), p_[0:nkp, 0:nq], start=(ci == 1), stop=(ci == n_))
                s.cp(oT[:, 0:nq], po[0:65, 0:nq], e="dve")
                for qs in range(nqs):
                    pf = pb[5 + fin[0] % 3]; fin[0] += 1
                    s.tr(pf[:, 0:65], oT[:, qs * 128:(qs + 1) * 128], ident[0:65, 0:65])
                    s.recip(rcp[:, qs:qs + 1], pf[:, 64:65])
                    s.ts(dst_fn(qs), pf[:, 0:64], rcp[:, qs:qs + 1], ALU.mult)

            for qg in range(4):
                for h in range(6):
                    pr_, g = h % 3, h // 3
                    attend_T(qT[pr_], g, qg * 512, 512, [(c * 128, 128, c) for c in range(NKT)], kT,
                             lambda cid, nk, g=g: Vg[0:nk, cid, g, :],
                             lambda qs, qg=qg, h=h: YAN[:, qg * 4 + qs, h * 64:(h + 1) * 64])
            for h in range(6):
                pr_, g = h % 3, h // 3
                attend_T(qT[pr_], g, 2048, 256, [(c * 128, 128, c) for c in range(2)], kT,
                         lambda cid, nk, g=g: Vg[0:nk, cid, g, :],
                         lambda qs, h=h: YAN[:, 16 + qs, h * 64:(h + 1) * 64])
            bc_ = [0]
            for i in range(16):
                chs = na_chunks(i)
                base = chs[0][0] * 128
                cls = na_class(i)
                for hn in range(4):
                    pr_, g = hn // 2, hn % 2
                    bt_ = bias_t[bc_[0] % 2]; bc_[0] += 1
                    nkeys = sum(nk for _, nk in chs)
                    s.dma("sp" if hn % 2 == 0 else "act", bt_[:, 0:nkeys], nab[l, cls, hn, :, 0:nkeys])
                    chunks = [(c * 128, nk, c) for c, nk in chs] + [(20 * 128, 128, 20), (21 * 128, 128, 21)]
                    attend(nqT[pr_], g, i * 128, 128, chunks, nkT[pr_],
                           lambda cid, nk, hn=hn: Vn[0:nk, cid, hn, :],
                           lambda cid, nk, bt_=bt_, base=base: (None if cid >= 20 else bt_[:, cid * 128 - base:cid * 128 - base + nk]),
                           lambda qs, i=i, hn=hn: YAN[:, i, 384 + hn * 64:384 + (hn + 1) * 64])
            for hn in range(4):
                pr_, g = hn // 2, hn % 2
                attend(nqT[pr_], g, 2048, 256, [(20 * 128, 128, 20), (21 * 128, 128, 21)], nkT[pr_],
                       lambda cid, nk, hn=hn: Vn[0:nk, cid, hn, :], None,
                       lambda qs, hn=hn: YAN[:, 16 + qs, 384 + hn * 64:384 + (hn + 1) * 64])
            load_w(wbuf, wb[l], 0, 384)
            load_w(wbuf, wb[l], 1024, 384, dcol=384)
            load_w(wbuf, wb[l], 2176, 256, dcol=768)
            LNG = _A(kT.t[:, 0:2048].bitcast(F32), kT.tk); s.dma("sp", LNG[:], lng[l])
            LNB = _A(kT.t[:, 2048:4096].bitcast(F32), kT.tk); s.dma("act", LNB[:], lnb[l])
            Ff = _A(kT.t[:, 4096:5632].bitcast(F32).rearrange("p (s c) -> p s c", s=2), kT.tk)
            Bk = _A(kT.t[:, 5632:7168].bitcast(F32), kT.tk)
            cen = _A(kT.t[:, 7168:7936].bitcast(F32), kT.tk)
            ysb = _A(qT[1].t[:, 0:768].bitcast(F32), qT[1].tk)
            bsb = _A(qT[1].t[:, 768:1536].bitcast(F32), qT[1].tk)
            sqb = _A(qT[2].t[:, 0:768].bitcast(F32), qT[2].tk)
            YgT = _A(qT[0].t[:, 0:1024].rearrange("p (k t) -> p k t", k=8), qT[0].tk)
            for t in range(NOWN):
                j = 1 if t >= 16 else 0
                x_, h_ = front(*own_src(t), j)
                G = pe[0]
                proj(h_, 0, 1024, G)
                s.act(G[:], G[:], AF.Silu)
                for sr in range(2):
                    q = "sp" if sr == 0 else "act"
                    if t >= 16:
                        s.dma(q, Ff[:, sr, :], YGc[sr * 256 + 128 * (t - 16):sr * 256 + 128 * (t - 16) + 128, :])
                        rb = (2 + sr) * 256 + 128 - 128 * (t - 16)
                        s.dma(q, Bk[:, sr * 384:(sr + 1) * 384], YGc[rb:rb + 128, :])
                    else:
                        s.dma(q, Ff[:, sr, :], YF[sr][128 * t:128 * t + 128, :])
                        s.dma(q, Bk[:, sr * 384:(sr + 1) * 384], YBk[sr][1920 - 128 * t:1920 - 128 * t + 128, :])
                for sr in range(2):
                    s.mm(pb[6 + sr][:, 0:384], jmat[:], Bk[:, sr * 384:(sr + 1) * 384])
                v2 = lambda A_: V(A_.t[:, 0:384].rearrange("p (s c) -> p s c", s=2), A_.tk)
                for sr in range(2):
                    s.tt(ysb[:, sr * 192:(sr + 1) * 192], Ff[:, sr, 0:192], pb[6 + sr][:, 0:192], ALU.add)
                    s.tt(bsb[:, sr * 192:(sr + 1) * 192], Ff[:, sr, 192:384], pb[6 + sr][:, 192:384], ALU.add)
                v3 = lambda A_: V(A_.t[:, 0:384].rearrange("p (h d) -> p h d", d=64), A_.tk)
                s.red(sm[:, 0:6], v3(ysb), ALU.add)
                s.ts(sm[:, 0:6], sm[:, 0:6], 1.0 / 64, ALU.mult)
                s.tt(v3(cen), v3(ysb), V(sm.t[:, 0:6].unsqueeze(2).to_broadcast([128, 6, 64]), sm.tk), ALU.subtract)
                s.tt(sqb[:], cen[:], cen[:], ALU.mult, e="pool")
                s.red(sm[:, 8:14], v3(sqb), ALU.add)
                s.ts(sm[:, 8:14], sm[:, 8:14], 1.0 / 64, ALU.mult, 64e-5, ALU.add)
                s.act(sm[:, 8:14], sm[:, 8:14], AF.Sqrt)
                s.recip(sm[:, 8:14], sm[:, 8:14])
                s.tt(v3(cen), v3(cen), V(sm.t[:, 8:14].unsqueeze(2).to_broadcast([128, 6, 64]), sm.tk), ALU.mult)
                s.tt(cen[:], cen[:], GNG[:], ALU.mult)
                s.tt(cen[:], cen[:], GNB[:], ALU.add)
                s.tt(cen[:], cen[:], bsb[:], ALU.add)
                Yg = pe[1]
                s.tt(Yg[:, 0:384], cen[:], G[:, 0:384], ALU.mult)
                s.tt(Yg[:, 384:1024], YAN[:, t, :], G[:, 384:1024], ALU.mult)
                for half in range(2):
                    p = pb[half]
                    for k in range(4):
                        kk_ = half * 4 + k
                        s.tr(p[:, k * 128:(k + 1) * 128], Yg[:, kk_ * 128:(kk_ + 1) * 128], ident[:])
                    s.cp(YgT[:, half * 4:half * 4 + 4, :], V(p.t[:, :].rearrange("p (k t) -> p k t", k=4), p.tk),
                         e="act" if half == 0 else "dve")
                yo = pe[0]
                proj(YgT, 0, 1024, yo, wsrc=woutb)
                s.tt(yo[:], yo[:], mod[j][:, 2048:3072], ALU.mult)
                z = pe[1]
                s.stt(z[:], x_[:], ALPHA, yo[:], ALU.mult, ALU.add)
                s.red(sm[:, 16:17], z[:], ALU.add)
                s.ts(sm[:, 16:17], sm[:, 16:17], 1.0 / 1024, ALU.mult)
                s.ts(z[:], z[:], sm[:, 16:17], ALU.subtract)
                s.tt(yo[:], z[:], z[:], ALU.mult, e="pool")
                s.red(sm[:, 17:18], yo[:], ALU.add)
                s.ts(sm[:, 17:18], sm[:, 17:18], 1.0 / 1024, ALU.mult, 1e-5, ALU.add)
                s.act(sm[:, 17:18], sm[:, 17:18], AF.Sqrt)
                s.recip(sm[:, 17:18], sm[:, 17:18])
                s.stt(z[:], z[:], sm[:, 17:18], LNG[:], ALU.mult, ALU.mult)
                s.tt(z[:], z[:], LNB[:], ALU.add)
                q = "sp" if t % 2 == 0 else "act"
                if t < 16:
                    if last:
                        tickets.append(s.dma(q, out[128 * t:128 * t + 128, :], z[:]))
                    else:
                        s.dma(q, XN[128 * t:128 * t + 128, :], z[:])
                elif not last:
                    s.dma(q, Xn[128 * (t - 16):128 * (t - 16) + 128, :], z[:])
        if not last:
            for k in range(8):
                s.allgather(XN[256 * k:256 * (k + 1), :], Xn[512 + 1024 * k:512 + 1024 * (k + 1), :], GROUPS)
    s.finish(tickets)
    s.close()
    return nc


def rope_tables():
    t = np.arange(8192)
    row = (t // 64).astype(np.float32); col = (t % 64).astype(np.float32)
    inv = (10000.0 ** (-np.arange(16, dtype=np.float32) / 16)).astype(np.float32)
    ar = row[:, None] * inv; ac = col[:, None] * inv
    ang = np.concatenate([ar, ar, ac, ac], axis=-1).astype(np.float32)
    cos = np.cos(ang).astype(np.float32); sin = np.sin(ang).astype(np.float32)
    sgn = np.concatenate([-np.ones(16), np.ones(16), -np.ones(16), np.ones(16)]).astype(np.float32)
    return cos, sin * sgn


def na_bias(rpb, j):
    NEG = -30000.0
    out = np.full((5, 4, 128, 768), NEG, np.float32)
    tiles = {0: 0, 1: 1, 2: 7, 3: 14, 4: 15}
    for cls, i in tiles.items():
        chs = na_chunks(i)
        srow0 = chs[0][0] * 2
        nkeys = sum(nk for _, nk in chs)
        qrow_l = np.repeat(np.array([2 * i, 2 * i + 1]), 64)
        qcol = np.tile(np.arange(64), 2)
        r = 32 * j + qrow_l
        r_start = np.clip(r - 4, 0, 120)
        c_start = np.clip(qcol - 8, 0, 48)
        key = np.arange(nkeys)
        krow = (32 * j - 4) + srow0 + key // 64
        kcol = key % 64
        dr = krow[None, :] - r[:, None] + 7
        dc = kcol[None, :] - qcol[:, None] + 15
        inwin = ((krow[None, :] >= r_start[:, None]) & (krow[None, :] < r_start[:, None] + 8) &
                 (kcol[None, :] >= c_start[:, None]) & (kcol[None, :] < c_start[:, None] + 16))
        drc = np.clip(dr, 0, 14); dcc = np.clip(dc, 0, 30)
        for h in range(4):
            vals = rpb[h][drc, dcc]
            out[cls, h, :, 0:nkeys] = np.where(inwin, vals, NEG)
    return out


def consts_A():
    idx = np.arange(64)
    inclT = (idx[:, None] <= idx[None, :]).astype(np.float32)
    strictT = (idx[:, None] < idx[None, :]).astype(np.float32)
    strict = strictT.T.copy()
    mask = np.concatenate([inclT, strictT, inclT, -strictT, -strict], axis=1)
    rmask = np.ones((64, 256), np.float32); rmask[:, 0::64] = 0.0
    ident = np.eye(64, dtype=np.float32)
    return np.concatenate([ident, mask, mask, mask, rmask] + [ident] * 6, axis=1).astype(np.float32)


def host_F(inp, depth=4):
    cos, sinS = rope_tables()
    bc = lambda v: np.ascontiguousarray(np.broadcast_to(v[None, :], (128, v.shape[0]))).astype(np.float32)
    L = range(depth)
    shared = dict(
        wmod=np.ascontiguousarray(inp['w_mod'][:depth]),
        bmod=np.stack([bc(inp['b_mod'][l]) for l in L]),
        wb=np.ascontiguousarray(inp['w_in'][:depth, :, 1280:]),
        wout=np.ascontiguousarray(inp['w_out'][:depth]),
        cstA=consts_A(),
        cosK=np.tile(cos, (1, 2)), sinK=np.tile(sinS, (1, 2)),
        gk=np.stack([bc(np.tile(inp['gqa_k_norm'][l], 2)) for l in L]),
        gq=np.stack([bc(np.tile(inp['gqa_q_norm'][l], 6)) for l in L]),
        gng=np.stack([bc(inp['rwkv_gn_g'][l]) for l in L]), gnb=np.stack([bc(inp['rwkv_gn_b'][l]) for l in L]),
        lng=np.stack([bc(inp['ln_g'][l]) for l in L]), lnb=np.stack([bc(inp['ln_b'][l]) for l in L]),
        ident=np.eye(128, dtype=np.float32), jmat=np.ascontiguousarray(np.eye(128, dtype=np.float32)[::-1]),
    )
    per_batch = []
    for b in range(2):
        xa0 = np.zeros((XROWS, 1024), np.float32)
        xa0[0:256] = inp['ctx'][b]
        xa0[512:8704] = inp['x'][b].reshape(4, 8, 256, 1024).transpose(1, 0, 2, 3).reshape(8192, 1024)
        cvec = np.stack([inp['c'][b], inp['c_ctx']], axis=1)
        cv = np.ascontiguousarray(cvec.reshape(8, 128, 2).transpose(1, 0, 2).reshape(128, 16))
        per_batch.append(dict(xa0=xa0, cv=cv))
    nabs = [np.stack([na_bias(inp['na_rpb'][l], j) for l in L]) for j in range(4)]
    maps = []
    for c in range(8):
        b, r = c // 4, c % 4
        d, hh = r // 2, r % 2
        heads = [3 * hh + i for i in range(3)]
        cols = []
        for comp in (0, 384, 768):
            for h in heads:
                cols += list(range(comp + h * 64, comp + h * 64 + 64))
        cols += list(range(1152 + 32 * d, 1152 + 32 * d + 32))
        cols += list(range(1216 + 32 * d, 1216 + 32 * d + 32))
        cols = np.array(cols)
        hcols = np.concatenate([np.arange(h * 64, (h + 1) * 64) for h in heads])
        wa = np.ascontiguousarray(inp['w_in'][:depth][:, :, cols])
        cw = np.zeros((depth, 64, 33), np.float32); pv = np.zeros((depth, 64, 15), np.float32)
        for l in L:
            conv = inp['rwkv_conv'][l][:, cols]
            if d == 1:
                conv = conv[::-1]
            for ci in range(9):
                cw[l, :, ci * 3:ci * 3 + 3] = conv[:, ci * 64:(ci + 1) * 64].T
            cw[l, 0:32, 27:30] = conv[:, 576:608].T
            cw[l, 0:32, 30:33] = conv[:, 608:640].T
            for i, h in enumerate(heads):
                hs = slice(h * 64, (h + 1) * 64)
                pv[l, :, i * 5 + 0] = inp['decay_w0'][l][d, hs]
                pv[l, :, i * 5 + 1] = inp['iclr_a0'][l][d, hs]
                pv[l, :, i * 5 + 2] = inp['rwkv_k_k'][l][hs]
                pv[l, :, i * 5 + 3] = inp['rwkv_k_a'][l][hs]
                pv[l, :, i * 5 + 4] = inp['rwkv_r_k'][l][h]
        w2 = np.ascontiguousarray(inp['decay_w2'][:depth, d][:, :, hcols])
        a2 = np.ascontiguousarray(inp['iclr_a2'][:depth, d][:, :, hcols])
        own = slice(2048 * r, 2048 * r + 2048)
        jd = np.eye(128, dtype=np.float32)
        if d == 1:
            jd = np.ascontiguousarray(jd[::-1])
        dsel = np.zeros((128, 2), np.float32); dsel[:, d] = 1.0
        xs0 = np.zeros((2560, 1024), np.float32)
        lo = 2048 * r - 256
        for sr_ in range(2560):
            pass
        a0, a1 = max(lo, 0), min(lo + 2560, 8192)
        xs0[a0 - lo:a1 - lo] = inp['x'][b][a0:a1]
        m = dict(shared)
        m['xs0'] = xs0
        m.update(per_batch[b])
        m.update(wa=wa, cw=cw, pv=pv, w2=w2, a2=a2, nab=nabs[r],
                 cosQ=np.tile(cos[own], (1, 6)), sinQ=np.tile(sinS[own], (1, 6)), jd=jd, dsel=dsel)
        maps.append(m)
    return maps


from concourse.bass_utils import run_bass_kernel_spmd

_NC = {}


def kernel(**inputs):
    inp = {k: np.asarray(v, dtype=np.float32) for k, v in inputs.items()}
    if 'F' not in _NC:
        _NC['F'] = build_F(4)
    maps = host_F(inp, 4)
    res = run_bass_kernel_spmd(_NC['F'], maps, core_ids=list(range(8))).results
    x = np.stack([np.concatenate([res[b * 4 + r]["out"] for r in range(4)], axis=0) for b in range(2)])
    return np.ascontiguousarray(x.astype(np.float32))
```

```python
import contextlib
import numpy as np
import concourse.bass as bass
import concourse.mybir as mybir

F32 = mybir.dt.float32
BF16 = mybir.dt.bfloat16
AF = mybir.ActivationFunctionType
ALU = mybir.AluOpType
AX = mybir.AxisListType


class Tk:
    __slots__ = ("w", "r", "name", "excl", "acc")

    def __init__(self, name=""):
        self.w = None
        self.r = {}
        self.name = name
        self.excl = False
        self.acc = {}


class T:
    def __init__(self, S, t, name):
        self.t = t
        self.tk = Tk(name)
        self.name = name

    def __getitem__(self, idx):
        return V(self.t[idx], self.tk)


class V:
    __slots__ = ("ap", "tk")

    def __init__(self, ap, tk):
        self.ap = ap
        self.tk = tk


import threading


class _Worker(threading.Thread):
    def __init__(self, il, fn):
        super().__init__(daemon=True)
        self.il = il
        self.fn = fn
        self.go = threading.Event()
        self.done = False
        self.exc = None

    def run(self):
        self.go.wait(); self.go.clear()
        try:
            self.fn()
        except BaseException as e:
            self.exc = e
        self.done = True
        self.il.main_ev.set()

    def pause(self):
        self.il.main_ev.set()
        self.go.wait(); self.go.clear()


class Interleaver:
    def __init__(self, s):
        self.s = s
        self.main_ev = threading.Event()
        self.cur = None

    def run(self, fns, width):
        pending = list(fns)
        active = []
        self.s.yield_hook = self._hook
        try:
            while pending or active:
                while pending and len(active) < width:
                    w = _Worker(self, pending.pop(0)); w.start(); active.append(w)
                for w in list(active):
                    self.cur = w
                    self.main_ev.clear()
                    w.go.set()
                    self.main_ev.wait()
                    if w.exc is not None:
                        raise w.exc
                    if w.done:
                        active.remove(w)
        finally:
            self.s.yield_hook = None
            self.cur = None

    def _hook(self):
        w = self.cur
        if w is not None and threading.current_thread() is w and self.s.atomic_depth == 0:
            w.pause()


class S:
    ENG = ("pe", "act", "dve", "pool", "sp")
    yield_hook = None
    atomic_depth = 0

    @contextlib.contextmanager
    def atomic(self):
        self.atomic_depth += 1
        try:
            yield
        finally:
            self.atomic_depth -= 1
            if self.atomic_depth == 0 and self.yield_hook is not None:
                self.yield_hook()

    def __init__(self, nc):
        self.nc = nc
        self.es = contextlib.ExitStack()
        self.eng = {"pe": nc.tensor, "act": nc.scalar, "dve": nc.vector, "pool": nc.gpsimd, "sp": nc.sync}
        self.sem = {e: self.es.enter_context(nc.semaphore("s_" + e)) for e in self.ENG}
        self.cnt = {e: 0 for e in self.ENG}
        self.dq = {}
        for q in ("sp", "act", "pool"):
            sems = [self.es.enter_context(nc.semaphore("d_%s%d" % (q, i))) for i in range(8)]
            self.dq[q] = dict(sems=sems, cnt=[0] * 8, nxt=0)
            for i, s_ in enumerate(sems):
                self.sem[(q, i)] = s_
        self.waited = {}
        self.cur = self.es
        self.cc_keys = []
        self.n_tiles = 0
        self.n_instr = 0
        self.n_wait = 0

    def sb(self, shape, dt=F32, name=None):
        self.n_tiles += 1
        name = "%s_%d" % (name or "t", self.n_tiles)
        t = self.cur.enter_context(self.nc.sbuf_tensor("sb_" + name, list(shape), dt))
        return T(self, t, name)

    def dram(self, shape, name, dt=F32):
        self.n_tiles += 1
        t = self.nc.dram_tensor("%s_%d" % (name, self.n_tiles), list(shape), dt)
        return T(self, t.ap(), name)

    @contextlib.contextmanager
    def phase(self):
        prev = self.cur
        self.cur = contextlib.ExitStack()
        try:
            yield
        finally:
            self.barrier()
            self.cur.close()
            self.cur = prev

    def barrier(self):
        for e in self.ENG:
            for e2 in self.ENG:
                if e2 != e and self.cnt[e2] > 0:
                    self._wait(e, e2, self.cnt[e2])
            for q, d in self.dq.items():
                for i, c in enumerate(d["cnt"]):
                    if c > 0:
                        self._wait(e, (q, i), c)

    def allgather(self, src, dst, groups):
        if "cc" not in self.dq:
            sems = [self.es.enter_context(self.nc.semaphore("cc%d" % i)) for i in range(8)]
            self.dq["cc"] = dict(sems=sems, cnt=[0] * 8, nxt=0)
            for i, s_ in enumerate(sems):
                self.sem[("cc", i)] = s_
        d = self.dq["cc"]
        i = d["nxt"]; d["nxt"] = (i + 1) % 8
        key = ("cc", i)
        self._wait("pool", key, d["cnt"][i])
        self._deps("pool", [src], [dst])
        ins = self.nc.gpsimd.collective_compute("AllGather", mybir.AluOpType.bypass, replica_groups=groups,
                                                ins=[src.ap.opt()], outs=[dst.ap.opt()])
        d["cnt"][i] += 1
        ins.then_inc(self.sem[key])
        self._mark((key, d["cnt"][i]), [src], [dst])
        self.n_instr += 1
        return (key, d["cnt"][i])

    def ps(self, shape, dt=F32, name=None):
        self.n_tiles += 1
        name = name or "p%d" % self.n_tiles
        t = self.es.enter_context(self.nc.psum_tensor("ps_" + name, list(shape), dt))
        tt_ = T(self, t, name)
        tt_.tk.excl = True
        return tt_

    def close(self):
        self.es.close()

    def _wait(self, e, key, val):
        if val is None:
            return
        k = (e, key)
        if self.waited.get(k, 0) >= val:
            return
        self.waited[k] = val
        self.eng[e].wait_ge(self.sem[key], val)
        self.n_wait += 1

    def _deps(self, e, reads, writes, pe_acc=False):
        for v in list(reads) + list(writes):
            if v.tk.excl:
                for e2, n2 in v.tk.acc.items():
                    if e2 == e and e == "pe":
                        continue
                    self._wait(e, e2, n2)
        reads = [v for v in reads if not v.tk.excl]
        writes = [v for v in writes if not v.tk.excl]
        for v in reads:
            w = v.tk.w
            if w is not None:
                self._wait(e, w[0], w[1])
        for v in writes:
            tk = v.tk
            if tk.w is not None:
                if not (pe_acc and tk.w[0] == "pe" and e == "pe"):
                    self._wait(e, tk.w[0], tk.w[1])
            for re_, rn in tk.r.items():
                if re_ == e and e == "pe":
                    continue
                self._wait(e, re_, rn)

    def _mark(self, ticket, reads, writes):
        for v in list(reads) + list(writes):
            if v.tk.excl:
                v.tk.acc[ticket[0]] = ticket[1]
        reads = [v for v in reads if not v.tk.excl]
        writes = [v for v in writes if not v.tk.excl]
        for v in reads:
            v.tk.r[ticket[0]] = ticket[1]
        for v in writes:
            v.tk.w = ticket
            v.tk.r = {}

    def op(self, e, fn, reads, writes, pe_acc=False):
        reads = [v for v in reads if isinstance(v, V)]
        self._deps(e, reads, writes, pe_acc)
        ins = fn()
        self.cnt[e] += 1
        ins.then_inc(self.sem[e], 1)
        self._mark((e, self.cnt[e]), reads, writes)
        self.n_instr += 1
        if self.yield_hook is not None:
            self.yield_hook()
        return ins

    def dma(self, q, out, in_, **kw):
        d = self.dq[q]
        i = d["nxt"]
        d["nxt"] = (i + 1) % len(d["sems"])
        key = (q, i)
        self._wait(q, key, d["cnt"][i])
        reads = [in_] if isinstance(in_, V) else []
        writes = [out] if isinstance(out, V) else []
        self._deps(q, reads, writes)
        oa = out.ap if isinstance(out, V) else out
        ia = in_.ap if isinstance(in_, V) else in_
        ins = self.eng[q].dma_start(out=oa, in_=ia, **kw)
        d["cnt"][i] += 16
        ins.then_inc(self.sem[key], 16)
        self._mark((key, d["cnt"][i]), reads, writes)
        self.n_instr += 1
        if self.yield_hook is not None:
            self.yield_hook()
        return (key, d["cnt"][i])

    def wait_ticket(self, e, ticket):
        self._wait(e, ticket[0], ticket[1])

    def mm(self, out, lhsT, rhs, start=True, stop=True, **kw):
        return self.op("pe", lambda: self.nc.tensor.matmul(out.ap, lhsT.ap, rhs.ap, start=start, stop=stop, **kw),
                       [lhsT, rhs], [out], pe_acc=not start)

    def tr(self, out, in_, ident):
        return self.op("pe", lambda: self.nc.tensor.transpose(out.ap, in_.ap, ident.ap), [in_, ident], [out])

    def act(self, out, in_, func, bias=None, scale=None, accum_out=None, e="act"):
        kw = {}
        rd = [in_]
        if bias is not None:
            kw["bias"] = bias.ap if isinstance(bias, V) else bias
            rd.append(bias)
        if scale is not None:
            kw["scale"] = scale.ap if isinstance(scale, V) else scale
            rd.append(scale)
        wr = [out]
        if accum_out is not None:
            kw["accum_out"] = accum_out.ap
            wr.append(accum_out)
        return self.op("act", lambda: self.nc.scalar.activation(out.ap, in_.ap, func, **kw), rd, wr)

    def _ve(self, e):
        return {"dve": self.nc.vector, "pool": self.nc.gpsimd, "act": self.nc.scalar}[e]

    def tt(self, out, a, b, op, e="dve"):
        return self.op(e, lambda: self._ve(e).tensor_tensor(out.ap, a.ap, b.ap, op), [a, b], [out])

    def ts(self, out, a, s1, op0, s2=None, op1=None, e="dve", accum_out=None):
        rd = [a, s1, s2]
        a1 = s1.ap if isinstance(s1, V) else s1
        a2 = s2.ap if isinstance(s2, V) else s2
        kw = {}
        wr = [out]
        if op1 is not None:
            kw["op1"] = op1
        if accum_out is not None:
            kw["accum_out"] = accum_out.ap
            wr.append(accum_out)
        return self.op(e, lambda: self._ve(e).tensor_scalar(out.ap, a.ap, a1, a2, op0, **kw), rd, wr)

    def stt(self, out, a, s, b, op0, op1, e="dve"):
        sa = s.ap if isinstance(s, V) else s
        return self.op(e, lambda: self._ve(e).scalar_tensor_tensor(out.ap, a.ap, sa, b.ap, op0, op1), [a, s, b], [out])

    def cp(self, out, in_, e="dve"):
        if e == "act":
            return self.op("act", lambda: self.nc.scalar.copy(out.ap, in_.ap), [in_], [out])
        return self.op(e, lambda: self._ve(e).tensor_copy(out.ap, in_.ap), [in_], [out])

    def memset(self, out, val, e="pool"):
        return self.op(e, lambda: self._ve(e).memset(out.ap, val), [], [out])

    def red(self, out, in_, op, axis=AX.X, e="dve"):
        return self.op(e, lambda: self._ve(e).tensor_reduce(out.ap, in_.ap, axis, op), [in_], [out])

    def recip(self, out, in_):
        return self.op("dve", lambda: self.nc.vector.reciprocal(out.ap, in_.ap), [in_], [out])

    def finish(self, tickets):
        for t in tickets:
            self._wait("sp", t[0], t[1])


A_DEC = 0.6065306597126334
ALPHA = (2 * 4) ** 0.25
NOWN = 18
NKT = 66
NNT = 22
GROUPS = [[0, 1, 2, 3], [4, 5, 6, 7]]
XROWS = 8960


def na_chunks(i):
    if i == 0:
        return [(c, 128) for c in range(0, 6)]
    if i == 1:
        return [(c, 128) for c in range(1, 6)]
    if i == 15:
        return [(c, 128) for c in range(14, 19)] + [(19, 64)]
    return [(c, 128) for c in range(i, i + 4)] + [(i + 4, 64)]


def na_class(i):
    return {0: 0, 1: 1, 14: 3, 15: 4}.get(i, 2)


def lat_row(tau):
    rho, rem = divmod(tau, 2048)
    k, i = divmod(rem, 256)
    return 512 + 1024 * k + 256 * rho + i


class _A:
    def __init__(self, ap, tk):
        self.t = ap; self.tk = tk

    def __getitem__(self, idx):
        return V(self.t[idx], self.tk)


def build_F(depth=4):
    nc = bass.Bass("TRN2", target_bir_lowering=False)
    dt = nc.dram_tensor
    I = lambda n, sh: dt(n, sh, F32, kind="ExternalInput").ap()
    xa0 = I("xa0", [XROWS, 1024]); xs0 = I("xs0", [2560, 1024])
    cv = I("cv", [128, 16]); wmod = I("wmod", [depth, 1024, 3072]); bmod = I("bmod", [depth, 128, 3072])
    wb = I("wb", [depth, 1024, 2432]); wout = I("wout", [depth, 1024, 1024])
    wa = I("wa", [depth, 1024, 640]); cw = I("cw", [depth, 64, 33]); pvi = I("pv", [depth, 64, 15])
    w2 = I("w2", [depth, 32, 192]); a2 = I("a2", [depth, 32, 192])
    cstA = I("cstA", [64, 1664])
    cosK = I("cosK", [8192, 128]); sinK = I("sinK", [8192, 128])
    cosQ = I("cosQ", [2048, 384]); sinQ = I("sinQ", [2048, 384])
    dsel_in = I("dsel", [128, 2])
    gk = I("gk", [depth, 128, 128]); gq = I("gq", [depth, 128, 384])
    nab = I("nab", [depth, 5, 4, 128, 768])
    gng = I("gng", [depth, 128, 384]); gnb = I("gnb", [depth, 128, 384])
    lng = I("lng", [depth, 128, 1024]); lnb = I("lnb", [depth, 128, 1024])
    ident_in = I("ident", [128, 128]); jmat_in = I("jmat", [128, 128]); jd_in = I("jd", [128, 128])
    out = dt("out", [2048, 1024], F32, kind="ExternalOutput").ap()

    s = S(nc)
    tickets = []
    pb = [s.ps([128, 512], name="pb%d" % i) for i in range(8)]
    XA = [s.dram([XROWS, 1024], "XA%d" % i) for i in range(2)]
    YB = s.dram([8448, 384], "YB")
    YGc = s.dram([4 * 256, 384], "YGc")
    YGl = s.dram([16 * 4 * 512, 384], "YGl")
    XN = s.dram([2048, 1024], "XN")
    XS = s.dram([2560, 1024], "XS")
    YF = [s.dram([2048, 384], "YF%d" % i) for i in range(2)]
    YBk = [s.dram([2048, 384], "YBk%d" % i) for i in range(2)]
    xa0_T = _A(xa0, Tk("xa0")); xs0_T = _A(xs0, Tk("xs0"))

    _rd = {}

    def RR(q):
        if q not in _rd:
            pid = s.eng[q].partition_id()
            _rd[q] = pid % 4
        return _rd[q]

    dynq = ["sp", "act", "pool"]
    dync = [0]

    def dyndma(dst_v, src_fn):
        q = dynq[dync[0] % 3]; dync[0] += 1
        return s.dma(q, dst_v, src_fn(RR(q)))

    ident = s.sb([128, 128], name="ident"); s.dma("sp", ident[:], ident_in)
    jmat = s.sb([128, 128], name="jmat"); s.dma("act", jmat[:], jmat_in)
    jd = s.sb([128, 128], name="jd"); s.dma("sp", jd[:], jd_in)
    dsel = s.sb([128, 2], name="dsel"); s.dma("act", dsel[:], dsel_in)
    ones = s.sb([128, 128], name="ones"); s.memset(ones[:], 1.0)
    cv_t = s.sb([128, 16], name="cv"); s.dma("sp", cv_t[:], cv)
    scv = s.sb([128, 16], name="scv"); s.act(scv[:], cv_t[:], AF.Silu)
    mod = [s.sb([128, 3072], name="mod%d" % j) for j in range(2)]
    for l in range(depth):
        Xc = xa0_T if l == 0 else XA[(l - 1) % 2]
        Xn = XA[l % 2]
        last = (l == depth - 1)

        if l == 0:
            XSc = xs0_T
        else:
            XSc = XS
            lat = Xc.t[512:8704, :]
            dyndma(V(XS.t[256:2304, :].rearrange("(o k i) c -> o k (i c)", o=1, k=8), XS.tk),
                   lambda r: V(lat.rearrange("(k rr i) c -> rr k (i c)", rr=4, i=256)[bass.ds(r, 1)], Xc.tk))
            units = lat.rearrange("(u i) c -> u (i c)", i=256)
            dyndma(V(XS.t[0:256, :].rearrange("(o i) c -> o (i c)", o=1), XS.tk),
                   lambda r: V(units[bass.ds(r + 27, 1), :], Xc.tk))
            dyndma(V(XS.t[2304:2560, :].rearrange("(o i) c -> o (i c)", o=1), XS.tk),
                   lambda r: V(units[bass.ds(r + 1, 1), :], Xc.tk))
        with s.phase():
            stage_w = s.sb([128, 8, 512], name="stage_w")
            Rl = s.sb([128, 8, 128], name="Rl")
            bmod_t = s.sb([128, 512], name="bmodt")
            for j in range(2):
                for k in range(8):
                    s.ts(Rl[:, k, :], ones[:], scv[:, 2 * k + j:2 * k + j + 1], ALU.mult, e="dve" if k % 2 == 0 else "pool")
                for cb in range(6):
                    s.dma("sp", stage_w[:], wmod[l].rearrange("(k p) c -> p k c", p=128)[:, :, cb * 512:(cb + 1) * 512])
                    s.dma("act", bmod_t[:], bmod[l][:, cb * 512:(cb + 1) * 512])
                    ps = pb[cb % 2]
                    for k in range(8):
                        s.mm(ps[:, :], Rl[:, k, :], stage_w[:, k, :], start=(k == 0), stop=(k == 7))
                    s.tt(mod[j][:, cb * 512:(cb + 1) * 512], ps[:, :], bmod_t[:], ALU.add)
                s.ts(mod[j][:, 1024:2048], mod[j][:, 1024:2048], 1.0, ALU.add, e="pool")

        with s.phase():
            cst_t = s.sb([64, 1664], name="cst"); s.dma("sp", cst_t[:], cstA)
            identA = cst_t[:, 0:64]
            mask3 = lambda h: cst_t[:, 64 + h * 320: 64 + (h + 1) * 320]
            rmask = cst_t[:, 1024:1280]
            idt3 = cst_t[:, 1280:1664]
            cw_t = s.sb([64, 33], name="cw"); s.dma("act", cw_t[:], cw[l])
            pv_t = s.sb([64, 16], name="pv"); s.dma("act", pv_t[:, 0:15], pvi[l])
            omk = s.sb([64, 3], name="omk")
            for h in range(3):
                s.ts(omk[:, h:h + 1], pv_t[:, h * 5 + 3:h * 5 + 4], -1.0, ALU.mult, 1.0, ALU.add)
            w2_t = s.sb([32, 192], name="w2"); s.dma("act", w2_t[:], w2[l])
            a2_t = s.sb([32, 192], name="a2"); s.dma("act", a2_t[:], a2[l])
            xt = [s.sb([128, 1024], name="xt%d" % i) for i in range(2)]
            ht = s.sb([128, 1024], name="ht")
            xr = s.sb([128, 1024], name="xr")
            wab = s.sb([128, 8, 640], BF16, name="wab")
            for k in range(8):
                st_ = xt[k % 2]
                s.dma("sp" if k % 2 == 0 else "act", st_[:, 0:640], wa[l][k * 128:(k + 1) * 128, :])
                s.cp(wab[:, k, :], st_[:, 0:640], e="dve" if k % 2 == 0 else "pool")
            cts = [(i * 64, 64) for i in range(9)] + [(576, 32), (608, 32)]
            hg = [s.sb([128, 8, 258], BF16, name="hg%d" % i) for i in range(2)]
            for h_ in hg:
                s.memset(h_[:], 0.0)
            raw = [s.sb([64, 258], name="raw%d" % i) for i in range(2)]
            ctmp = [s.sb([64, 256], name="ctmp%d" % i) for i in range(2)]
            mk = lambda nm, shape=(64, 256), dt_=F32: [s.sb(list(shape), dt_, name="%s%d" % (nm, h)) for h in range(3)]
            uR, uK, uV = mk("uR"), mk("uK"), mk("uV", dt_=BF16)
            uD = s.sb([32, 256], name="uD"); uA = s.sb([32, 256], name="uA"); ddt = s.sb([32, 256], name="ddt")
            sg, ic, kk, tmp, kd, bd = mk("sg"), mk("ic"), mk("kk"), mk("tmp"), mk("kd"), mk("bd")
            cs, csx, csr = mk("cs"), mk("csx"), mk("csr")
            E1, E3 = mk("E1"), mk("E3")
            E4 = E1
            RH, KKH, kt, bt, kc, bc, rk = (mk("RH", dt_=BF16), mk("KKH", dt_=BF16), mk("kt", dt_=BF16), mk("bt", dt_=BF16),
                                           mk("kc", dt_=BF16), mk("bc", dt_=BF16), mk("rk", dt_=BF16))
            identAb = s.sb([64, 64], BF16, name="identAb"); s.cp(identAb[:], identA)
            onesb = s.sb([64, 1], BF16, name="onesb"); s.memset(onesb[:], 1.0)
            wc = mk("wc", (64, 4)); rn = tmp
            trT = [s.sb([64, 3, 256], BF16, name="trT%d" % i) for i in range(4)]
            scS = [s.sb([64, 3, 320], BF16, name="scS%d" % i) for i in range(4)]
            XYs = [[s.sb([64, 3, 128], BF16, name="XY%d_%d" % (c, i)) for i in range(2)] for c in range(4)]
            PQs = [[s.sb([64, 3, 128], BF16, name="PQ%d_%d" % (c, i)) for i in range(2)] for c in range(4)]
            KKpTs = [s.sb([64, 192], BF16, name="KKpT%d" % c) for c in range(4)]
            AVs = [s.sb([64, 192], BF16, name="AV%d" % c) for c in range(4)]
            Ulocs = [s.sb([64, 192], name="Uloc%d" % c) for c in range(4)]
            Us = [s.sb([64, 192], BF16, name="U%d" % c) for c in range(4)]
            STb = [s.sb([64, 192], BF16, name="STb%d" % i) for i in range(2)]
            bss = [s.sb([64, 4], name="bs%d" % c) for c in range(4)]
            il = Interleaver(s)
            ST = [s.sb([64, 192], name="ST%d" % i) for i in range(2)]
            YBuf = [s.sb([64, 4, 384], name="YBuf0")] * 2
            bs_ = s.sb([64, 4], name="bs")
            s.memset(ST[0][:], 0.0)
            s.memset(STb[0][:], 0.0)
            sti = 0
            acnt = [0]

            def frontA(g):
                hgt = hg[g % 2]
                for a in range(2):
                    u = 2 * g + a
                    i = acnt[0]; acnt[0] += 1
                    if u < 2:
                        bf_, br_ = 128 * u, 128 * (1 - u)
                        j = 1
                    else:
                        v = u - 2
                        bf_, br_ = lat_row(128 * v), lat_row(128 * (63 - v))
                        j = 0
                    x_ = xt[i % 2]
                    s.dma("sp", x_[:], V(Xc.t[bf_:bf_ + 128, :], Xc.tk))
                    s.dma("act", xr[:], V(Xc.t[br_:br_ + 128, :], Xc.tk))
                    s.ts(x_[:], x_[:], dsel[:, 0:1], ALU.mult, e="pool")
                    s.stt(x_[:], xr[:], dsel[:, 1:2], x_[:], ALU.mult, ALU.add)
                    s.tt(ht[:], x_[:], mod[j][:, 1024:2048], ALU.mult, e="pool")
                    s.tt(ht[:], ht[:], mod[j][:, 0:1024], ALU.add, e="dve")
                    for half in range(2):
                        p = pb[half]
                        for k in range(4):
                            kk_ = half * 4 + k
                            s.tr(p[:, k * 128:(k + 1) * 128], ht[:, kk_ * 128:(kk_ + 1) * 128], jd[:])
                        s.cp(hgt[:, half * 4:half * 4 + 4, 1 + 128 * a:1 + 128 * (a + 1)],
                             V(p.t[:, :].rearrange("p (k t) -> p k t", k=4), p.tk), e="act" if half == 0 else "dve")

            NGRP = 33
            frontA(0)
            chunk_no = 0
            for g in range(NGRP):
                first = g in (0, 1)
                lastg = g in (0, NGRP - 1)
                j = 1 if g == 0 else 0
                tbase = 256 * g
                hgt = hg[g % 2]
                if g + 1 < NGRP:
                    frontA(g + 1)
                    hn_ = hg[(g + 1) % 2]
                    s.cp(hgt[:, :, 257:258], hn_[:, :, 1:2], e="pool")
                    s.cp(hn_[:, :, 0:1], hgt[:, :, 256:257], e="pool")
                for ci, (c0, M) in enumerate(cts):
                    pr = pb[2 + ci % 2]
                    for k in range(8):
                        s.mm(pr[0:M, 0:258], wab[:, k, c0:c0 + M], hgt[:, k, :], start=(k == 0), stop=(k == 7))
                    rw = raw[ci % 2]
                    s.cp(rw[0:M, :], pr[0:M, 0:258], e="act" if ci % 2 == 0 else "dve")
                    if first:
                        s.memset(rw[0:M, 0:1], 0.0, e="pool")
                    if lastg:
                        s.memset(rw[0:M, 257:258], 0.0, e="pool")
                    dst = (uR, uK, uV)[ci // 3][ci % 3] if ci < 9 else (uD, uA)[ci - 9]
                    tm = ctmp[ci % 2]
                    s.act(tm[0:M, :], rw[0:M, 0:256], AF.Identity, scale=cw_t[0:M, ci * 3:ci * 3 + 1])
                    s.stt(tm[0:M, :], rw[0:M, 1:257], cw_t[0:M, ci * 3 + 1:ci * 3 + 2], tm[0:M, :], ALU.mult, ALU.add)
                    s.stt(dst[0:M, :], rw[0:M, 2:258], cw_t[0:M, ci * 3 + 2:ci * 3 + 3], tm[0:M, :], ALU.mult, ALU.add)
                s.act(ddt[:], uD[:], AF.Tanh)
                for h in range(3):
                    P = lambda i: pv_t[:, h * 5 + i:h * 5 + i + 1]
                    pz = pb[4]
                    s.mm(pz[0:64, 0:256], w2_t[:, h * 64:(h + 1) * 64], ddt[:])
                    s.act(sg[h][:], pz[0:64, 0:256], AF.Sigmoid, bias=P(0))
                    s.mm(pz[0:64, 256:512], a2_t[:, h * 64:(h + 1) * 64], uA[:])
                    s.act(ic[h][:], pz[0:64, 256:512], AF.Sigmoid, bias=P(1))
                    s.ts(kk[h][:], uK[h][:], P(2), ALU.mult, e="pool")
                    s.tt(tmp[h][:], kk[h][:], kk[h][:], ALU.mult, e="pool")
                    pss = pb[5]
                    s.mm(pss[0:64, 0:256], ones[0:64, 0:64], tmp[h][:])
                    s.ts(rn[h][:], pss[0:64, 0:256], 1e-12, ALU.max)
                    s.act(rn[h][:], rn[h][:], AF.Sqrt)
                    s.recip(rn[h][:], rn[h][:])
                    s.tt(kk[h][:], kk[h][:], rn[h][:], ALU.mult)
                    s.ts(tmp[h][:], ic[h][:], P(3), ALU.mult, omk[:, h:h + 1], ALU.add, e="pool")
                    s.tt(kd[h][:], uK[h][:], tmp[h][:], ALU.mult, e="pool")
                    s.tt(bd[h][:], kk[h][:], ic[h][:], ALU.mult, e="pool")
                    s.op("dve", lambda h=h: nc.vector.tensor_tensor_scan(cs[h][:].ap, rmask.ap, sg[h][:].ap, 0.0, ALU.mult, ALU.add),
                         [rmask, sg[h][:]], [cs[h][:]])
                    s.tt(csx[h][:], cs[h][:], sg[h][:], ALU.subtract)
                    for c in range(4):
                        s.ts(csr[h][:, c * 64:(c + 1) * 64], cs[h][:, c * 64:(c + 1) * 64],
                             cs[h][:, c * 64 + 63:c * 64 + 64], ALU.subtract)
                    s.act(wc[h][:], cs[h][:, 63::64], AF.Exp, scale=-A_DEC)
                    s.act(E1[h][:], cs[h][:], AF.Exp, scale=-A_DEC)
                    s.tt(RH[h][:], uR[h][:], E1[h][:], ALU.mult, e="pool")
                    s.act(E1[h][:], csx[h][:], AF.Exp, scale=-A_DEC)
                    s.tt(KKH[h][:], kk[h][:], E1[h][:], ALU.mult, e="pool")
                    s.act(E3[h][:], cs[h][:], AF.Exp, scale=A_DEC)
                    s.tt(kt[h][:], kd[h][:], E3[h][:], ALU.mult)
                    s.tt(bt[h][:], bd[h][:], E3[h][:], ALU.mult, e="pool")
                    s.act(E4[h][:], csr[h][:], AF.Exp, scale=A_DEC)
                    s.tt(kc[h][:], kd[h][:], E4[h][:], ALU.mult)
                    s.tt(bc[h][:], bd[h][:], E4[h][:], ALU.mult, e="pool")
                    s.stt(rk[h][:], uR[h][:], P(4), kd[h][:], ALU.mult, ALU.mult)
                yb = YBuf[g % 2]

                def chunk_body(c, n, yb=yb):
                    cc = slice(c * 64, (c + 1) * 64)
                    tT = trT[c]; sS = scS[c]
                    XY = XYs[c]; PQ = PQs[c]; KKpT = KKpTs[c]; AV = AVs[c]; Uloc = Ulocs[c]; U = Us[c]; bs_ = bss[c]
                    p_ = c % 2
                    for h in range(3):
                        ptr = pb[2 + p_]
                        with s.atomic():
                            ptrb = ptr.t[0:64, 0:128].bitcast(BF16)
                            for i, src_ in enumerate((KKH, bc, kc, uV)):
                                s.tr(V(ptrb[:, i * 64:(i + 1) * 64], ptr.tk), src_[h][:, cc], identAb[:])
                            s.cp(tT[:, h, :], V(ptrb[:, 0:256], ptr.tk), e="act")
                        psc = pb[4 + p_]
                        with s.atomic():
                            s.mm(psc[0:64, 0:64], kt[h][:, cc], RH[h][:, cc])
                            s.mm(psc[0:64, 64:128], kt[h][:, cc], KKH[h][:, cc])
                            s.mm(psc[0:64, 128:192], bt[h][:, cc], RH[h][:, cc])
                            s.mm(psc[0:64, 192:256], bt[h][:, cc], KKH[h][:, cc])
                            s.mm(psc[0:64, 256:320], KKH[h][:, cc], bt[h][:, cc])
                            s.tt(sS[:, h, :], psc[0:64, 0:320], mask3(h), ALU.mult)
                    s.tt(PQ[0][:, :, :], sS[:, :, 192:320], V(idt3.ap.rearrange("p (h c) -> p h c", c=128), idt3.tk), ALU.add, e="pool")
                    Xc_ = lambda lvl, h: (sS[:, h, 192:256] if lvl == 0 else XY[lvl % 2][:, h, 0:64])
                    Yc_ = lambda lvl, h: (sS[:, h, 256:320] if lvl == 0 else XY[lvl % 2][:, h, 64:128])
                    for lvl in range(5):
                        pn, pq = pb[4 + p_], pb[6 + p_]
                        nxt = XY[(lvl + 1) % 2]
                        with s.atomic():
                            for h in range(3):
                                s.mm(pn[0:64, h * 128:h * 128 + 64], Yc_(lvl, h), Xc_(lvl, h))
                                if lvl < 4:
                                    s.mm(pn[0:64, h * 128 + 64:h * 128 + 128], Xc_(lvl, h), Yc_(lvl, h))
                            pn3 = pn.t[0:64, 0:384].rearrange("p (h c) -> p h c", c=128)
                            if lvl < 4:
                                s.cp(nxt[:, :, :], V(pn3, pn.tk), e="act")
                            else:
                                s.cp(nxt[:, :, 0:64], V(pn3[:, :, 0:64], pn.tk), e="act")
                        Pc, Pn = PQ[lvl % 2], PQ[(lvl + 1) % 2]
                        with s.atomic():
                            for h in range(3):
                                s.mm(pq[0:64, h * 128:h * 128 + 64], Pc[:, h, 64:128], nxt[:, h, 0:64])
                                if lvl < 4:
                                    s.mm(pq[0:64, h * 128 + 64:h * 128 + 128], Pc[:, h, 0:64], nxt[:, h, 64:128])
                            pq3 = pq.t[0:64, 0:384].rearrange("p (h c) -> p h c", c=128)
                            if lvl < 4:
                                s.tt(Pn[:, :, :], V(pq3, pq.tk), Pc[:, :, :], ALU.add)
                            else:
                                s.tt(Pn[:, :, 0:64], V(pq3[:, :, 0:64], pq.tk), Pc[:, :, 0:64], ALU.add)
                    TT = PQ[1]
                    pk = pb[2 + p_]
                    with s.atomic():
                        for h in range(3):
                            s.mm(pk[0:64, h * 64:(h + 1) * 64], tT[:, h, 0:64], TT[:, h, 0:64])
                            s.mm(pk[0:64, 192 + h * 64:192 + (h + 1) * 64], sS[:, h, 64:128], tT[:, h, 192:256])
                        s.cp(KKpT[:], pk[0:64, 0:192], e="act")
                        s.cp(AV[:], pk[0:64, 192:384], e="dve")
                    pk3 = pb[6 + p_]
                    with s.atomic():
                        for h in range(3):
                            s.mm(pk3[0:64, h * 64:(h + 1) * 64], TT[:, h, 0:64], AV[:, h * 64:(h + 1) * 64])
                        s.cp(Uloc[:], pk3[0:64, 0:192], e="act")

                def chunk_seq(c, n, yb=yb):
                    cc = slice(c * 64, (c + 1) * 64)
                    tT = trT[c]; sS = scS[c]
                    KKpT = KKpTs[c]; Uloc = Ulocs[c]; U = Us[c]; bs_ = bss[c]
                    Sc, Sn = ST[n % 2], ST[(n + 1) % 2]
                    Scb, Snb = STb[n % 2], STb[(n + 1) % 2]
                    pu = pb[0]
                    with s.atomic():
                        for h in range(3):
                            s.mm(pu[0:64, h * 64:(h + 1) * 64], KKpT[:, h * 64:(h + 1) * 64], Scb[:, h * 64:(h + 1) * 64])
                        s.stt(U[:], pu[0:64, 0:192], -1.0, Uloc[:], ALU.mult, ALU.subtract)
                    pS = pb[1]
                    with s.atomic():
                        for h in range(3):
                            hs = slice(h * 64, (h + 1) * 64)
                            s.mm(pS[0:64, hs], tT[:, h, 128:192], tT[:, h, 192:256], start=True, stop=False)
                            s.mm(pS[0:64, hs], tT[:, h, 64:128], U[:, hs], start=False, stop=True)
                        for h in range(3):
                            hs = slice(h * 64, (h + 1) * 64)
                            s.stt(Sn[:, hs], Sc[:, hs], wc[h][:, c:c + 1], pS[0:64, hs], ALU.mult, ALU.add)
                        s.cp(Snb[:], Sn[:], e="act")
                    py = pb[0]
                    with s.atomic():
                        for h in range(3):
                            hs = slice(256 + h * 64, 256 + (h + 1) * 64)
                            hh = slice(h * 64, (h + 1) * 64)
                            s.mm(py[0:64, hs], RH[h][:, cc], Scb[:, hh], start=True, stop=False)
                            s.mm(py[0:64, hs], sS[:, h, 128:192], U[:, hh], start=False, stop=False)
                            s.mm(py[0:64, hs], sS[:, h, 0:64], tT[:, h, 192:256], start=False, stop=True)
                        s.cp(yb[:, c, 0:192], py[0:64, 256:448], e="act")
                    pbn = pb[1]
                    with s.atomic():
                        for h in range(3):
                            s.mm(pbn[0:64, 256 + h:256 + h + 1], rk[h][:, cc], onesb[:, 0:1])
                        s.cp(bs_[:, 0:3], pbn[0:64, 256:259], e="dve")
                    for h in range(3):
                        s.ts(yb[:, c, 192 + h * 64:192 + (h + 1) * 64], tT[:, h, 192:256], bs_[:, h:h + 1], ALU.mult, e="pool")
                il.run([(lambda c=c: chunk_body(c, 4 * g + c)) for c in range(4)], 4)
                for c in range(4):
                    chunk_seq(c, 4 * g + c)
                s.dma("sp" if g % 2 == 0 else "act",
                      V(YB.t[tbase:tbase + 256, :].rearrange("(c t) f -> t c f", t=64), YB.tk), yb[:, :, :])
                if g == 0:
                    s.allgather(YB[0:256, :], YGc[:, :], GROUPS)
                elif g % 2 == 0:
                    m = g // 2 - 1
                    s.allgather(YB[256 + 512 * m:256 + 512 * (m + 1), :], YGl[2048 * m:2048 * (m + 1), :], GROUPS)
        ygv = YGl.t.rearrange("(m sr i) c -> sr m (i c)", sr=4, i=512)
        for sr in range(2):
            dyndma(V(YF[sr].t.rearrange("(m i) c -> m (i c)", i=512), YF[sr].tk),
                   lambda r, sr=sr: V(ygv[sr][bass.ds(r * 4, 4), :], YGl.tk))
            dyndma(V(YBk[sr].t.rearrange("(m i) c -> m (i c)", i=512), YBk[sr].tk),
                   lambda r, sr=sr: V(ygv[2 + sr][bass.ds((3 - r) * 4, 4), :], YGl.tk))
        with s.phase():
            GK = s.sb([128, 128], name="GK"); s.dma("act", GK[:], gk[l])
            GQ = s.sb([128, 384], name="GQ"); s.dma("act", GQ[:], gq[l])
            GNG = s.sb([128, 384], name="GNG"); s.dma("act", GNG[:], gng[l])
            GNB = s.sb([128, 384], name="GNB"); s.dma("act", GNB[:], gnb[l])
            YAN = s.sb([128, 18, 640], BF16, name="YAN")
            wbuf = s.sb([128, 8, 1024], BF16, name="wbuf")
            woutb = s.sb([128, 8, 1024], BF16, name="woutb")
            xt = [s.sb([128, 1024], name="xt%d" % i) for i in range(2)]
            hts = [s.sb([128, 1024], name="ht%d" % i) for i in range(2)]
            ht = hts[0]
            pe = [s.sb([128, 1024], name="pe%d" % i) for i in range(2)]
            sqs = [pe[1], None]
            wst = xt
            ilB = Interleaver(s)
            pjB = _A(YAN.t[:, 0:2, :].rearrange("p a c -> p (a c)").bitcast(F32), YAN.tk)
            sqs[1] = _A(YAN.t[:, 2:4, :].rearrange("p a c -> p (a c)").bitcast(F32), YAN.tk)

            def load_w(dst, src, c0, ncols, dcol=0):
                for k in range(8):
                    st_ = wst[k % 2]
                    s.dma("sp" if k % 2 == 0 else "act", st_[:, 0:ncols], src[k * 128:(k + 1) * 128, c0:c0 + ncols])
                    s.cp(dst[:, k, dcol:dcol + ncols], st_[:, 0:ncols], e="dve" if k % 2 == 0 else "pool")

            load_w(woutb, wout[l], 0, 1024)
            kT = s.sb([128, 8448], BF16, name="kT")
            Vg = s.sb([128, NKT, 2, 65], BF16, name="Vg")
            qT = [s.sb([128, 2304], BF16, name="qT%d" % i) for i in range(3)]
            nqT = [s.sb([128, 2304], BF16, name="nqT%d" % i) for i in range(2)]
            nkT = [s.sb([128, 2816], BF16, name="nkT%d" % i) for i in range(2)]
            Vn = s.sb([128, NNT, 4, 65], BF16, name="Vn")
            s.memset(Vg[:, :, :, 64:65], 1.0)
            s.memset(Vn[:, :, :, 64:65], 1.0)
            hT = [s.sb([128, 8, 128], BF16, name="hT%d" % i) for i in range(2)]
            tcs = [s.sb([128, 384], name="tcos%d" % i) for i in range(2)]
            tsns = [s.sb([128, 384], name="tsin%d" % i) for i in range(2)]
            sms = [s.sb([128, 64], name="sm%d" % i) for i in range(2)]
            tc_, tsn, sm = tcs[0], tsns[0], sms[0]
            cnt = [0]

            def front(srcT, row, j, w=None):
                if w is None:
                    i = cnt[0]; cnt[0] += 1
                    w = i % 2
                    banks = (pb[0], pb[1])
                else:
                    banks = (pb[w], pb[w])
                q = "sp" if w == 0 else "act"
                x_ = xt[w]
                ht_ = hts[w]
                s.dma(q, x_[:], V(srcT.t[row:row + 128, :], srcT.tk))
                s.tt(ht_[:], x_[:], mod[j][:, 1024:2048], ALU.mult, e="pool")
                s.tt(ht_[:], ht_[:], mod[j][:, 0:1024], ALU.add, e="dve")
                h_ = hT[w]
                for half in range(2):
                    p = banks[half]
                    with s.atomic():
                        for k in range(4):
                            kk_ = half * 4 + k
                            s.tr(p[:, k * 128:(k + 1) * 128], ht_[:, kk_ * 128:(kk_ + 1) * 128], ident[:])
                        s.cp(h_[:, half * 4:half * 4 + 4, :], V(p.t[:, :].rearrange("p (k t) -> p k t", k=4), p.tk),
                             e="act" if half == 0 else "dve")
                return x_, h_

            def proj(h_, c0, ncols, dst, wsrc=None, w=None):
                wsrc = wsrc or wbuf
                o = 0
                bi = 2
                while o < ncols:
                    n = min(512, ncols - o)
                    p = pb[bi] if w is None else pb[2 + w]
                    with s.atomic():
                        for k in range(8):
                            s.mm(p[:, 0:n], h_[:, k, :], wsrc[:, k, c0 + o:c0 + o + n], start=(k == 0), stop=(k == 7))
                        s.cp(dst[:, o:o + n], p[:, 0:n], e="act" if bi == 2 else "dve")
                    o += n
                    bi = 5 - bi

            def rms_rope(src, H, gtab, scale_mode, rope, dst, w=0):
                sq = sqs[w]; sm = sms[w]; tc_ = tcs[w]; tsn = tsns[w]
                s.tt(sq[:, 0:H * 64], src, src, ALU.mult, e="pool")
                s.red(sm[:, 0:H], V(sq.t[:, 0:H * 64].rearrange("p (h d) -> p h d", d=64), sq.tk), ALU.add)
                if scale_mode == "k":
                    s.ts(sm[:, 0:H], sm[:, 0:H], 1.0 / 64, ALU.mult, 1e-6, ALU.add)
                else:
                    s.ts(sm[:, 0:H], sm[:, 0:H], 64e-6, ALU.add)
                s.act(sm[:, 0:H], sm[:, 0:H], AF.Sqrt)
                s.recip(sm[:, 0:H], sm[:, 0:H])
                for h in range(H):
                    s.stt(V(dst.ap[:, h * 64:(h + 1) * 64], dst.tk), V(src.ap[:, h * 64:(h + 1) * 64], src.tk), sm[:, h:h + 1],
                          gtab[:, h * 64:(h + 1) * 64], ALU.mult, ALU.mult)
                if rope is not None:
                    cos_d, sin_d, rowfn = rope
                    s.dma("sp", tc_[:, 0:H * 64], cos_d[rowfn:rowfn + 128, :])
                    s.dma("act", tsn[:, 0:H * 64], sin_d[rowfn:rowfn + 128, :])
                    t1 = sq
                    v4 = lambda ap: ap.rearrange("p (g a d) -> p g a d", a=2, d=16)
                    d4 = v4(dst.ap); s4 = v4(tsn.t[:, 0:H * 64]); t4 = v4(t1.t[:, 0:H * 64])
                    s.tt(V(t4[:, :, 0, :], t1.tk), V(d4[:, :, 1, :], dst.tk), V(s4[:, :, 0, :], tsn.tk), ALU.mult, e="pool")
                    s.tt(V(t4[:, :, 1, :], t1.tk), V(d4[:, :, 0, :], dst.tk), V(s4[:, :, 1, :], tsn.tk), ALU.mult, e="pool")
                    s.tt(dst, dst, tc_[:, 0:H * 64], ALU.mult)
                    s.tt(dst, dst, t1[:, 0:H * 64], ALU.add)

            pt = [s.sb([128, 512], BF16, name="pt%d" % i) for i in range(3)]
            rcp = s.sb([128, 8], name="rcp")
            bias_t = [s.sb([128, 768], name="bias%d" % i) for i in range(2)]
            ptc = [0]

            load_w(wbuf, wb[l], 768, 256)

            def k_body(t):
                w = t % 2
                j = 1 if t < 2 else 0
                x_, h_ = front(Xc, 128 * t if t < 2 else lat_row(128 * (t - 2)), j, w)
                pj = pe[0] if w == 0 else pjB
                proj(h_, 0, 256, pj, w=w)
                rope = None if t < 2 else (cosK, sinK, (t - 2) * 128)
                kr = hts[w]
                rms_rope(pj[:, 0:128], 2, GK, "k", rope, kr[:, 0:128], w)
                p = pb[6 + w]
                with s.atomic():
                    s.tr(p[:, 0:128], kr[:, 0:128], ident[:])
                    s.cp(kT[:, t * 128:(t + 1) * 128], p[:, 0:128], e="act")
                s.cp(Vg[:, t, :, 0:64], V(pj.t[:, 128:256].rearrange("p (g d) -> p g d", d=64), pj.tk), e="pool")

            ilB.run([(lambda t=t: k_body(t)) for t in range(NKT)], 2)
            load_w(wbuf, wb[l], 1664, 512)

            def n_body(t):
                w = t % 2
                j = 1 if t >= 20 else 0
                if t >= 20:
                    x_, h_ = front(Xc, 128 * (t - 20), j, w)
                else:
                    x_, h_ = front(XSc, 128 * t, j, w)
                pj = pe[0] if w == 0 else pjB
                proj(h_, 0, 512, pj, w=w)
                for pr_ in range(2):
                    p = pb[6 + w]
                    with s.atomic():
                        s.tr(p[:, 0:128], pj[:, pr_ * 128:(pr_ + 1) * 128], ident[:])
                        s.cp(nkT[pr_][:, t * 128:(t + 1) * 128], p[:, 0:128], e="act" if pr_ == 0 else "dve")
                s.cp(Vn[:, t, :, 0:64], V(pj.t[:, 256:512].rearrange("p (g d) -> p g d", d=64), pj.tk), e="pool")

            ilB.run([(lambda t=t: n_body(t)) for t in range(NNT)], 2)
            for sl in range(6):
                hh_ = (sl // 2) + 3 * (sl % 2)
                load_w(wbuf, wb[l], 384 + hh_ * 64, 64, dcol=sl * 64)
            load_w(wbuf, wb[l], 1408, 256, dcol=384)
            own_src = lambda t: ((Xc, 128 * (t - 16)) if t >= 16 else (XSc, 256 + 128 * t))

            def q_body(t):
                w = t % 2
                j = 1 if t >= 16 else 0
                x_, h_ = front(*own_src(t), j, w)
                pj = pe[0] if w == 0 else pjB
                proj(h_, 0, 640, pj, w=w)
                rope = None if t >= 16 else (cosQ, sinQ, 128 * t)
                qr = hts[w]
                rms_rope(pj[:, 0:384], 6, GQ, "q", rope, qr[:, 0:384], w)
                for pr_ in range(3):
                    p = pb[6 + w]
                    with s.atomic():
                        s.tr(p[:, 0:128], qr[:, pr_ * 128:(pr_ + 1) * 128], ident[:])
                        s.cp(qT[pr_][:, t * 128:(t + 1) * 128], p[:, 0:128], e="act" if pr_ % 2 == 0 else "dve")
                s.ts(qr[:, 384:640], pj[:, 384:640], 0.125, ALU.mult, e="pool")
                for pr_ in range(2):
                    p = pb[6 + w]
                    with s.atomic():
                        s.tr(p[:, 0:128], qr[:, 384 + pr_ * 128:384 + (pr_ + 1) * 128], ident[:])
                        s.cp(nqT[pr_][:, t * 128:(t + 1) * 128], p[:, 0:128], e="act" if pr_ == 0 else "dve")

            ilB.run([(lambda t=t: q_body(t)) for t in range(NOWN)], 2)

            def attend(qsrc, g, qc0, nq, chunks, ksrc, vsrc_fn, bias_fn, dst_fn):
                nqs = nq // 128
                lo, hi = 64 * g, 64 * g + 64
                n_ = len(chunks)
                for ci in range(n_ + 1):
                    if ci < n_:
                        kc0, nk, cid = chunks[ci]
                        ps = pb[ci % 3]
                        bsrc = bias_fn(cid, nk) if bias_fn else None
                        s.mm(ps[0:nk, 0:nq], ksrc[lo:hi, kc0:kc0 + nk], qsrc[lo:hi, qc0:qc0 + nq], start=True, stop=(bsrc is None))
                        if bsrc is not None:
                            s.mm(ps[0:nk, 0:nq], bsrc, ident[:, 0:nq], start=False, stop=True)
                    if ci >= 1:
                        kc0p, nkp, cidp = chunks[ci - 1]
                        p_ = pt[ptc[0] % 3]; ptc[0] += 1
                        s.act(p_[0:nkp, 0:nq], pb[(ci - 1) % 3][0:nkp, 0:nq], AF.Exp)
                        for qs in range(nqs):
                            s.mm(pb[4 + qs][:, 0:65], p_[0:nkp, qs * 128:(qs + 1) * 128], vsrc_fn(cidp, nkp),
                                 start=(ci == 1), stop=(ci == n_))
                for qs in range(nqs):
                    s.recip(rcp[:, qs:qs + 1], pb[4 + qs][:, 64:65])
                    s.ts(dst_fn(qs), pb[4 + qs][:, 0:64], rcp[:, qs:qs + 1], ALU.mult)

            oT = _A(pe[0].t[0:65, 0:512], pe[0].tk)
            fin = [0]

            def attend_T(qsrc, g, qc0, nq, chunks, ksrc, vsrc_fn, dst_fn):
                nqs = nq // 128
                lo, hi = 64 * g, 64 * g + 64
                po = pb[4]
                n_ = len(chunks)
                for ci in range(n_ + 1):
                    if ci < n_:
                        kc0, nk, cid = chunks[ci]
                        s.mm(pb[ci % 3][0:nk, 0:nq], ksrc[lo:hi, kc0:kc0 + nk], qsrc[lo:hi, qc0:qc0 + nq])
                    if ci >= 1:
                        kc0p, nkp, cidp = chunks[ci - 1]
                        p_ = pt[ptc[0] % 3]; ptc[0] += 1
                        s.act(p_[0:nkp, 0:nq], pb[(ci - 1) % 3][0:nkp, 0:nq], AF.Exp)
                        s.mm(po[0:65, 0:nq], vsrc_fn(cidp, nkp), p_[0:nkp, 0:nq], start=(ci == 1), stop=(ci == n_))
                s.cp(oT[:, 0:nq], po[0:65, 0:nq], e="dve")
                for qs in range(nqs):
                    pf = pb[5 + fin[0] % 3]; fin[0] += 1
                    s.tr(pf[:, 0:65], oT[:, qs * 128:(qs + 1) * 128], ident[0:65, 0:65])
                    s.recip(rcp[:, qs:qs + 1], pf[:, 64:65])
                    s.ts(dst_fn(qs), pf[:, 0:64], rcp[:, qs:qs + 1], ALU.mult)

            for qg in range(4):
                for h in range(6):
                    pr_, g = h % 3, h // 3
                    attend_T(qT[pr_], g, qg * 512, 512, [(c * 128, 128, c) for c in range(NKT)], kT,
                             lambda cid, nk, g=g: Vg[0:nk, cid, g, :],
                             lambda qs, qg=qg, h=h: YAN[:, qg * 4 + qs, h * 64:(h + 1) * 64])
            for h in range(6):
                pr_, g = h % 3, h // 3
                attend_T(qT[pr_], g, 2048, 256, [(c * 128, 128, c) for c in range(2)], kT,
                         lambda cid, nk, g=g: Vg[0:nk, cid, g, :],
                         lambda qs, h=h: YAN[:, 16 + qs, h * 64:(h + 1) * 64])
            bc_ = [0]
            for i in range(16):
                chs = na_chunks(i)
                base = chs[0][0] * 128
                cls = na_class(i)
                for hn in range(4):
                    pr_, g = hn // 2, hn % 2
                    bt_ = bias_t[bc_[0] % 2]; bc_[0] += 1
                    nkeys = sum(nk for _, nk in chs)
                    s.dma("sp" if hn % 2 == 0 else "act", bt_[:, 0:nkeys], nab[l, cls, hn, :, 0:nkeys])
                    chunks = [(c * 128, nk, c) for c, nk in chs] + [(20 * 128, 128, 20), (21 * 128, 128, 21)]
                    attend(nqT[pr_], g, i * 128, 128, chunks, nkT[pr_],
                           lambda cid, nk, hn=hn: Vn[0:nk, cid, hn, :],
                           lambda cid, nk, bt_=bt_, base=base: (None if cid >= 20 else bt_[:, cid * 128 - base:cid * 128 - base + nk]),
                           lambda qs, i=i, hn=hn: YAN[:, i, 384 + hn * 64:384 + (hn + 1) * 64])
            for hn in range(4):
                pr_, g = hn // 2, hn % 2
                attend(nqT[pr_], g, 2048, 256, [(20 * 128, 128, 20), (21 * 128, 128, 21)], nkT[pr_],
                       lambda cid, nk, hn=hn: Vn[0:nk, cid, hn, :], None,
                       lambda qs, hn=hn: YAN[:, 16 + qs, 384 + hn * 64:384 + (hn + 1) * 64])
            load_w(wbuf, wb[l], 0, 384)
            load_w(wbuf, wb[l], 1024, 384, dcol=384)
            load_w(wbuf, wb[l], 2176, 256, dcol=768)
            LNG = _A(kT.t[:, 0:2048].bitcast(F32), kT.tk); s.dma("sp", LNG[:], lng[l])
            LNB = _A(kT.t[:, 2048:4096].bitcast(F32), kT.tk); s.dma("act", LNB[:], lnb[l])
            Ff = _A(kT.t[:, 4096:5632].bitcast(F32).rearrange("p (s c) -> p s c", s=2), kT.tk)
            Bk = _A(kT.t[:, 5632:7168].bitcast(F32), kT.tk)
            cen = _A(kT.t[:, 7168:7936].bitcast(F32), kT.tk)
            ysb = _A(qT[1].t[:, 0:768].bitcast(F32), qT[1].tk)
            bsb = _A(qT[1].t[:, 768:1536].bitcast(F32), qT[1].tk)
            sqb = _A(qT[2].t[:, 0:768].bitcast(F32), qT[2].tk)
            YgT = _A(qT[0].t[:, 0:1024].rearrange("p (k t) -> p k t", k=8), qT[0].tk)
            for t in range(NOWN):
                j = 1 if t >= 16 else 0
                x_, h_ = front(*own_src(t), j)
                G = pe[0]
                proj(h_, 0, 1024, G)
                s.act(G[:], G[:], AF.Silu)
                for sr in range(2):
                    q = "sp" if sr == 0 else "act"
                    if t >= 16:
                        s.dma(q, Ff[:, sr, :], YGc[sr * 256 + 128 * (t - 16):sr * 256 + 128 * (t - 16) + 128, :])
                        rb = (2 + sr) * 256 + 128 - 128 * (t - 16)
                        s.dma(q, Bk[:, sr * 384:(sr + 1) * 384], YGc[rb:rb + 128, :])
                    else:
                        s.dma(q, Ff[:, sr, :], YF[sr][128 * t:128 * t + 128, :])
                        s.dma(q, Bk[:, sr * 384:(sr + 1) * 384], YBk[sr][1920 - 128 * t:1920 - 128 * t + 128, :])
                for sr in range(2):
                    s.mm(pb[6 + sr][:, 0:384], jmat[:], Bk[:, sr * 384:(sr + 1) * 384])
                v2 = lambda A_: V(A_.t[:, 0:384].rearrange("p (s c) -> p s c", s=2), A_.tk)
                for sr in range(2):
                    s.tt(ysb[:, sr * 192:(sr + 1) * 192], Ff[:, sr, 0:192], pb[6 + sr][:, 0:192], ALU.add)
                    s.tt(bsb[:, sr * 192:(sr + 1) * 192], Ff[:, sr, 192:384], pb[6 + sr][:, 192:384], ALU.add)
                v3 = lambda A_: V(A_.t[:, 0:384].rearrange("p (h d) -> p h d", d=64), A_.tk)
                s.red(sm[:, 0:6], v3(ysb), ALU.add)
                s.ts(sm[:, 0:6], sm[:, 0:6], 1.0 / 64, ALU.mult)
                s.tt(v3(cen), v3(ysb), V(sm.t[:, 0:6].unsqueeze(2).to_broadcast([128, 6, 64]), sm.tk), ALU.subtract)
                s.tt(sqb[:], cen[:], cen[:], ALU.mult, e="pool")
                s.red(sm[:, 8:14], v3(sqb), ALU.add)
                s.ts(sm[:, 8:14], sm[:, 8:14], 1.0 / 64, ALU.mult, 64e-5, ALU.add)
                s.act(sm[:, 8:14], sm[:, 8:14], AF.Sqrt)
                s.recip(sm[:, 8:14], sm[:, 8:14])
                s.tt(v3(cen), v3(cen), V(sm.t[:, 8:14].unsqueeze(2).to_broadcast([128, 6, 64]), sm.tk), ALU.mult)
                s.tt(cen[:], cen[:], GNG[:], ALU.mult)
                s.tt(cen[:], cen[:], GNB[:], ALU.add)
                s.tt(cen[:], cen[:], bsb[:], ALU.add)
                Yg = pe[1]
                s.tt(Yg[:, 0:384], cen[:], G[:, 0:384], ALU.mult)
                s.tt(Yg[:, 384:1024], YAN[:, t, :], G[:, 384:1024], ALU.mult)
                for half in range(2):
                    p = pb[half]
                    for k in range(4):
                        kk_ = half * 4 + k
                        s.tr(p[:, k * 128:(k + 1) * 128], Yg[:, kk_ * 128:(kk_ + 1) * 128], ident[:])
                    s.cp(YgT[:, half * 4:half * 4 + 4, :], V(p.t[:, :].rearrange("p (k t) -> p k t", k=4), p.tk),
                         e="act" if half == 0 else "dve")
                yo = pe[0]
                proj(YgT, 0, 1024, yo, wsrc=woutb)
                s.tt(yo[:], yo[:], mod[j][:, 2048:3072], ALU.mult)
                z = pe[1]
                s.stt(z[:], x_[:], ALPHA, yo[:], ALU.mult, ALU.add)
                s.red(sm[:, 16:17], z[:], ALU.add)
                s.ts(sm[:, 16:17], sm[:, 16:17], 1.0 / 1024, ALU.mult)
                s.ts(z[:], z[:], sm[:, 16:17], ALU.subtract)
                s.tt(yo[:], z[:], z[:], ALU.mult, e="pool")
                s.red(sm[:, 17:18], yo[:], ALU.add)
                s.ts(sm[:, 17:18], sm[:, 17:18], 1.0 / 1024, ALU.mult, 1e-5, ALU.add)
                s.act(sm[:, 17:18], sm[:, 17:18], AF.Sqrt)
                s.recip(sm[:, 17:18], sm[:, 17:18])
                s.stt(z[:], z[:], sm[:, 17:18], LNG[:], ALU.mult, ALU.mult)
                s.tt(z[:], z[:], LNB[:], ALU.add)
                q = "sp" if t % 2 == 0 else "act"
                if t < 16:
                    if last:
                        tickets.append(s.dma(q, out[128 * t:128 * t + 128, :], z[:]))
                    else:
                        s.dma(q, XN[128 * t:128 * t + 128, :], z[:])
                        if t % 2 == 1:
                            k = t // 2
                            s.allgather(XN[256 * k:256 * (k + 1), :], Xn[512 + 1024 * k:512 + 1024 * (k + 1), :], GROUPS)
                elif not last:
                    s.dma(q, Xn[128 * (t - 16):128 * (t - 16) + 128, :], z[:])
    s.finish(tickets)
    s.close()
    return nc


def rope_tables():
    t = np.arange(8192)
    row = (t // 64).astype(np.float32); col = (t % 64).astype(np.float32)
    inv = (10000.0 ** (-np.arange(16, dtype=np.float32) / 16)).astype(np.float32)
    ar = row[:, None] * inv; ac = col[:, None] * inv
    ang = np.concatenate([ar, ar, ac, ac], axis=-1).astype(np.float32)
    cos = np.cos(ang).astype(np.float32); sin = np.sin(ang).astype(np.float32)
    sgn = np.concatenate([-np.ones(16), np.ones(16), -np.ones(16), np.ones(16)]).astype(np.float32)
    return cos, sin * sgn


def na_bias(rpb, j):
    NEG = -30000.0
    out = np.full((5, 4, 128, 768), NEG, np.float32)
    tiles = {0: 0, 1: 1, 2: 7, 3: 14, 4: 15}
    for cls, i in tiles.items():
        chs = na_chunks(i)
        srow0 = chs[0][0] * 2
        nkeys = sum(nk for _, nk in chs)
        qrow_l = np.repeat(np.array([2 * i, 2 * i + 1]), 64)
        qcol = np.tile(np.arange(64), 2)
        r = 32 * j + qrow_l
        r_start = np.clip(r - 4, 0, 120)
        c_start = np.clip(qcol - 8, 0, 48)
        key = np.arange(nkeys)
        krow = (32 * j - 4) + srow0 + key // 64
        kcol = key % 64
        dr = krow[None, :] - r[:, None] + 7
        dc = kcol[None, :] - qcol[:, None] + 15
        inwin = ((krow[None, :] >= r_start[:, None]) & (krow[None, :] < r_start[:, None] + 8) &
                 (kcol[None, :] >= c_start[:, None]) & (kcol[None, :] < c_start[:, None] + 16))
        drc = np.clip(dr, 0, 14); dcc = np.clip(dc, 0, 30)
        for h in range(4):
            vals = rpb[h][drc, dcc]
            out[cls, h, :, 0:nkeys] = np.where(inwin, vals, NEG)
    return out


def consts_A():
    idx = np.arange(64)
    inclT = (idx[:, None] <= idx[None, :]).astype(np.float32)
    strictT = (idx[:, None] < idx[None, :]).astype(np.float32)
    strict = strictT.T.copy()
    mask = np.concatenate([inclT, strictT, inclT, -strictT, -strict], axis=1)
    rmask = np.ones((64, 256), np.float32); rmask[:, 0::64] = 0.0
    ident = np.eye(64, dtype=np.float32)
    return np.concatenate([ident, mask, mask, mask, rmask] + [ident] * 6, axis=1).astype(np.float32)


def host_F(inp, depth=4):
    cos, sinS = rope_tables()
    bc = lambda v: np.ascontiguousarray(np.broadcast_to(v[None, :], (128, v.shape[0]))).astype(np.float32)
    L = range(depth)
    shared = dict(
        wmod=np.ascontiguousarray(inp['w_mod'][:depth]),
        bmod=np.stack([bc(inp['b_mod'][l]) for l in L]),
        wb=np.ascontiguousarray(inp['w_in'][:depth, :, 1280:]),
        wout=np.ascontiguousarray(inp['w_out'][:depth]),
        cstA=consts_A(),
        cosK=np.tile(cos, (1, 2)), sinK=np.tile(sinS, (1, 2)),
        gk=np.stack([bc(np.tile(inp['gqa_k_norm'][l], 2)) for l in L]),
        gq=np.stack([bc(np.tile(inp['gqa_q_norm'][l], 6)) for l in L]),
        gng=np.stack([bc(inp['rwkv_gn_g'][l]) for l in L]), gnb=np.stack([bc(inp['rwkv_gn_b'][l]) for l in L]),
        lng=np.stack([bc(inp['ln_g'][l]) for l in L]), lnb=np.stack([bc(inp['ln_b'][l]) for l in L]),
        ident=np.eye(128, dtype=np.float32), jmat=np.ascontiguousarray(np.eye(128, dtype=np.float32)[::-1]),
    )
    per_batch = []
    for b in range(2):
        xa0 = np.zeros((XROWS, 1024), np.float32)
        xa0[0:256] = inp['ctx'][b]
        xa0[512:8704] = inp['x'][b].reshape(4, 8, 256, 1024).transpose(1, 0, 2, 3).reshape(8192, 1024)
        cvec = np.stack([inp['c'][b], inp['c_ctx']], axis=1)
        cv = np.ascontiguousarray(cvec.reshape(8, 128, 2).transpose(1, 0, 2).reshape(128, 16))
        per_batch.append(dict(xa0=xa0, cv=cv))
    nabs = [np.stack([na_bias(inp['na_rpb'][l], j) for l in L]) for j in range(4)]
    maps = []
    for c in range(8):
        b, r = c // 4, c % 4
        d, hh = r // 2, r % 2
        heads = [3 * hh + i for i in range(3)]
        cols = []
        for comp in (0, 384, 768):
            for h in heads:
                cols += list(range(comp + h * 64, comp + h * 64 + 64))
        cols += list(range(1152 + 32 * d, 1152 + 32 * d + 32))
        cols += list(range(1216 + 32 * d, 1216 + 32 * d + 32))
        cols = np.array(cols)
        hcols = np.concatenate([np.arange(h * 64, (h + 1) * 64) for h in heads])
        wa = np.ascontiguousarray(inp['w_in'][:depth][:, :, cols])
        cw = np.zeros((depth, 64, 33), np.float32); pv = np.zeros((depth, 64, 15), np.float32)
        for l in L:
            conv = inp['rwkv_conv'][l][:, cols]
            if d == 1:
                conv = conv[::-1]
            for ci in range(9):
                cw[l, :, ci * 3:ci * 3 + 3] = conv[:, ci * 64:(ci + 1) * 64].T
            cw[l, 0:32, 27:30] = conv[:, 576:608].T
            cw[l, 0:32, 30:33] = conv[:, 608:640].T
            for i, h in enumerate(heads):
                hs = slice(h * 64, (h + 1) * 64)
                pv[l, :, i * 5 + 0] = inp['decay_w0'][l][d, hs]
                pv[l, :, i * 5 + 1] = inp['iclr_a0'][l][d, hs]
                pv[l, :, i * 5 + 2] = inp['rwkv_k_k'][l][hs]
                pv[l, :, i * 5 + 3] = inp['rwkv_k_a'][l][hs]
                pv[l, :, i * 5 + 4] = inp['rwkv_r_k'][l][h]
        w2 = np.ascontiguousarray(inp['decay_w2'][:depth, d][:, :, hcols])
        a2 = np.ascontiguousarray(inp['iclr_a2'][:depth, d][:, :, hcols])
        own = slice(2048 * r, 2048 * r + 2048)
        jd = np.eye(128, dtype=np.float32)
        if d == 1:
            jd = np.ascontiguousarray(jd[::-1])
        dsel = np.zeros((128, 2), np.float32); dsel[:, d] = 1.0
        xs0 = np.zeros((2560, 1024), np.float32)
        lo = 2048 * r - 256
        for sr_ in range(2560):
            pass
        a0, a1 = max(lo, 0), min(lo + 2560, 8192)
        xs0[a0 - lo:a1 - lo] = inp['x'][b][a0:a1]
        m = dict(shared)
        m['xs0'] = xs0
        m.update(per_batch[b])
        m.update(wa=wa, cw=cw, pv=pv, w2=w2, a2=a2, nab=nabs[r],
                 cosQ=np.tile(cos[own], (1, 6)), sinQ=np.tile(sinS[own], (1, 6)), jd=jd, dsel=dsel)
        maps.append(m)
    return maps


from concourse.bass_utils import run_bass_kernel_spmd

_NC = {}


def kernel(**inputs):
    inp = {k: np.asarray(v, dtype=np.float32) for k, v in inputs.items()}
    if 'F' not in _NC:
        _NC['F'] = build_F(4)
    maps = host_F(inp, 4)
    res = run_bass_kernel_spmd(_NC['F'], maps, core_ids=list(range(8))).results
    x = np.stack([np.concatenate([res[b * 4 + r]["out"] for r in range(4)], axis=0) for b in range(2)])
    return np.ascontiguousarray(x.astype(np.float32))
```

```python
import contextlib
import numpy as np
import concourse.bass as bass
import concourse.mybir as mybir

F32 = mybir.dt.float32
BF16 = mybir.dt.bfloat16
AF = mybir.ActivationFunctionType
ALU = mybir.AluOpType
AX = mybir.AxisListType


class Tk:
    __slots__ = ("w", "r", "name", "excl", "acc")

    def __init__(self, name=""):
        self.w = None
        self.r = {}
        self.name = name
        self.excl = False
        self.acc = {}


class T:
    def __init__(self, S, t, name):
        self.t = t
        self.tk = Tk(name)
        self.name = name

    def __getitem__(self, idx):
        return V(self.t[idx], self.tk)


class V:
    __slots__ = ("ap", "tk")

    def __init__(self, ap, tk):
        self.ap = ap
        self.tk = tk


import threading


class _Worker(threading.Thread):
    def __init__(self, il, fn):
        super().__init__(daemon=True)
        self.il = il
        self.fn = fn
        self.go = threading.Event()
        self.done = False
        self.exc = None

    def run(self):
        self.go.wait(); self.go.clear()
        try:
            self.fn()
        except BaseException as e:
            self.exc = e
        self.done = True
        self.il.main_ev.set()

    def pause(self):
        self.il.main_ev.set()
        self.go.wait(); self.go.clear()


class Interleaver:
    def __init__(self, s):
        self.s = s
        self.main_ev = threading.Event()
        self.cur = None

    def run(self, fns, width):
        pending = list(fns)
        active = []
        self.s.yield_hook = self._hook
        try:
            while pending or active:
                while pending and len(active) < width:
                    w = _Worker(self, pending.pop(0)); w.start(); active.append(w)
                for w in list(active):
                    self.cur = w
                    self.main_ev.clear()
                    w.go.set()
                    self.main_ev.wait()
                    if w.exc is not None:
                        raise w.exc
                    if w.done:
                        active.remove(w)
        finally:
            self.s.yield_hook = None
            self.cur = None

    def _hook(self):
        w = self.cur
        if w is not None and threading.current_thread() is w and self.s.atomic_depth == 0:
            w.pause()


class S:
    ENG = ("pe", "act", "dve", "pool", "sp")
    yield_hook = None
    atomic_depth = 0

    @contextlib.contextmanager
    def atomic(self):
        self.atomic_depth += 1
        try:
            yield
        finally:
            self.atomic_depth -= 1
            if self.atomic_depth == 0 and self.yield_hook is not None:
                self.yield_hook()

    def __init__(self, nc):
        self.nc = nc
        self.es = contextlib.ExitStack()
        self.eng = {"pe": nc.tensor, "act": nc.scalar, "dve": nc.vector, "pool": nc.gpsimd, "sp": nc.sync}
        self.sem = {e: self.es.enter_context(nc.semaphore("s_" + e)) for e in self.ENG}
        self.cnt = {e: 0 for e in self.ENG}
        self.dq = {}
        for q in ("sp", "act", "pool"):
            sems = [self.es.enter_context(nc.semaphore("d_%s%d" % (q, i))) for i in range(8)]
            self.dq[q] = dict(sems=sems, cnt=[0] * 8, nxt=0)
            for i, s_ in enumerate(sems):
                self.sem[(q, i)] = s_
        self.waited = {}
        self.cur = self.es
        self.cc_keys = []
        self.n_tiles = 0
        self.n_instr = 0
        self.n_wait = 0

    def sb(self, shape, dt=F32, name=None):
        self.n_tiles += 1
        name = "%s_%d" % (name or "t", self.n_tiles)
        t = self.cur.enter_context(self.nc.sbuf_tensor("sb_" + name, list(shape), dt))
        return T(self, t, name)

    def dram(self, shape, name, dt=F32):
        self.n_tiles += 1
        t = self.nc.dram_tensor("%s_%d" % (name, self.n_tiles), list(shape), dt)
        return T(self, t.ap(), name)

    @contextlib.contextmanager
    def phase(self):
        prev = self.cur
        self.cur = contextlib.ExitStack()
        try:
            yield
        finally:
            self.barrier()
            self.cur.close()
            self.cur = prev

    def barrier(self):
        for e in self.ENG:
            for e2 in self.ENG:
                if e2 != e and self.cnt[e2] > 0:
                    self._wait(e, e2, self.cnt[e2])
            for q, d in self.dq.items():
                for i, c in enumerate(d["cnt"]):
                    if c > 0:
                        self._wait(e, (q, i), c)

    def allgather(self, src, dst, groups):
        if "cc" not in self.dq:
            sems = [self.es.enter_context(self.nc.semaphore("cc%d" % i)) for i in range(8)]
            self.dq["cc"] = dict(sems=sems, cnt=[0] * 8, nxt=0)
            for i, s_ in enumerate(sems):
                self.sem[("cc", i)] = s_
        d = self.dq["cc"]
        i = d["nxt"]; d["nxt"] = (i + 1) % 8
        key = ("cc", i)
        self._wait("pool", key, d["cnt"][i])
        self._deps("pool", [src], [dst])
        ins = self.nc.gpsimd.collective_compute("AllGather", mybir.AluOpType.bypass, replica_groups=groups,
                                                ins=[src.ap.opt()], outs=[dst.ap.opt()])
        d["cnt"][i] += 1
        ins.then_inc(self.sem[key])
        self._mark((key, d["cnt"][i]), [src], [dst])
        self.n_instr += 1
        return (key, d["cnt"][i])

    def ps(self, shape, dt=F32, name=None):
        self.n_tiles += 1
        name = name or "p%d" % self.n_tiles
        t = self.es.enter_context(self.nc.psum_tensor("ps_" + name, list(shape), dt))
        tt_ = T(self, t, name)
        tt_.tk.excl = True
        return tt_

    def close(self):
        self.es.close()

    def _wait(self, e, key, val):
        if val is None:
            return
        k = (e, key)
        if self.waited.get(k, 0) >= val:
            return
        self.waited[k] = val
        self.eng[e].wait_ge(self.sem[key], val)
        self.n_wait += 1

    def _deps(self, e, reads, writes, pe_acc=False):
        for v in list(reads) + list(writes):
            if v.tk.excl:
                for e2, n2 in v.tk.acc.items():
                    if e2 == e and e == "pe":
                        continue
                    self._wait(e, e2, n2)
        reads = [v for v in reads if not v.tk.excl]
        writes = [v for v in writes if not v.tk.excl]
        for v in reads:
            w = v.tk.w
            if w is not None:
                self._wait(e, w[0], w[1])
        for v in writes:
            tk = v.tk
            if tk.w is not None:
                if not (pe_acc and tk.w[0] == "pe" and e == "pe"):
                    self._wait(e, tk.w[0], tk.w[1])
            for re_, rn in tk.r.items():
                if re_ == e and e == "pe":
                    continue
                self._wait(e, re_, rn)

    def _mark(self, ticket, reads, writes):
        for v in list(reads) + list(writes):
            if v.tk.excl:
                v.tk.acc[ticket[0]] = ticket[1]
        reads = [v for v in reads if not v.tk.excl]
        writes = [v for v in writes if not v.tk.excl]
        for v in reads:
            v.tk.r[ticket[0]] = ticket[1]
        for v in writes:
            v.tk.w = ticket
            v.tk.r = {}

    def op(self, e, fn, reads, writes, pe_acc=False):
        reads = [v for v in reads if isinstance(v, V)]
        self._deps(e, reads, writes, pe_acc)
        ins = fn()
        self.cnt[e] += 1
        ins.then_inc(self.sem[e], 1)
        self._mark((e, self.cnt[e]), reads, writes)
        self.n_instr += 1
        if self.yield_hook is not None:
            self.yield_hook()
        return ins

    def dma(self, q, out, in_, **kw):
        d = self.dq[q]
        i = d["nxt"]
        d["nxt"] = (i + 1) % len(d["sems"])
        key = (q, i)
        self._wait(q, key, d["cnt"][i])
        reads = [in_] if isinstance(in_, V) else []
        writes = [out] if isinstance(out, V) else []
        self._deps(q, reads, writes)
        oa = out.ap if isinstance(out, V) else out
        ia = in_.ap if isinstance(in_, V) else in_
        ins = self.eng[q].dma_start(out=oa, in_=ia, **kw)
        d["cnt"][i] += 16
        ins.then_inc(self.sem[key], 16)
        self._mark((key, d["cnt"][i]), reads, writes)
        self.n_instr += 1
        if self.yield_hook is not None:
            self.yield_hook()
        return (key, d["cnt"][i])

    def wait_ticket(self, e, ticket):
        self._wait(e, ticket[0], ticket[1])

    def mm(self, out, lhsT, rhs, start=True, stop=True, **kw):
        return self.op("pe", lambda: self.nc.tensor.matmul(out.ap, lhsT.ap, rhs.ap, start=start, stop=stop, **kw),
                       [lhsT, rhs], [out], pe_acc=not start)

    def tr(self, out, in_, ident):
        return self.op("pe", lambda: self.nc.tensor.transpose(out.ap, in_.ap, ident.ap), [in_, ident], [out])

    def act(self, out, in_, func, bias=None, scale=None, accum_out=None, e="act"):
        kw = {}
        rd = [in_]
        if bias is not None:
            kw["bias"] = bias.ap if isinstance(bias, V) else bias
            rd.append(bias)
        if scale is not None:
            kw["scale"] = scale.ap if isinstance(scale, V) else scale
            rd.append(scale)
        wr = [out]
        if accum_out is not None:
            kw["accum_out"] = accum_out.ap
            wr.append(accum_out)
        return self.op("act", lambda: self.nc.scalar.activation(out.ap, in_.ap, func, **kw), rd, wr)

    def _ve(self, e):
        return {"dve": self.nc.vector, "pool": self.nc.gpsimd, "act": self.nc.scalar}[e]

    def tt(self, out, a, b, op, e="dve"):
        return self.op(e, lambda: self._ve(e).tensor_tensor(out.ap, a.ap, b.ap, op), [a, b], [out])

    def ts(self, out, a, s1, op0, s2=None, op1=None, e="dve", accum_out=None):
        rd = [a, s1, s2]
        a1 = s1.ap if isinstance(s1, V) else s1
        a2 = s2.ap if isinstance(s2, V) else s2
        kw = {}
        wr = [out]
        if op1 is not None:
            kw["op1"] = op1
        if accum_out is not None:
            kw["accum_out"] = accum_out.ap
            wr.append(accum_out)
        return self.op(e, lambda: self._ve(e).tensor_scalar(out.ap, a.ap, a1, a2, op0, **kw), rd, wr)

    def stt(self, out, a, s, b, op0, op1, e="dve"):
        sa = s.ap if isinstance(s, V) else s
        return self.op(e, lambda: self._ve(e).scalar_tensor_tensor(out.ap, a.ap, sa, b.ap, op0, op1), [a, s, b], [out])

    def cp(self, out, in_, e="dve"):
        if e == "act":
            return self.op("act", lambda: self.nc.scalar.copy(out.ap, in_.ap), [in_], [out])
        return self.op(e, lambda: self._ve(e).tensor_copy(out.ap, in_.ap), [in_], [out])

    def memset(self, out, val, e="pool"):
        return self.op(e, lambda: self._ve(e).memset(out.ap, val), [], [out])

    def red(self, out, in_, op, axis=AX.X, e="dve"):
        return self.op(e, lambda: self._ve(e).tensor_reduce(out.ap, in_.ap, axis, op), [in_], [out])

    def recip(self, out, in_):
        return self.op("dve", lambda: self.nc.vector.reciprocal(out.ap, in_.ap), [in_], [out])

    def finish(self, tickets):
        for t in tickets:
            self._wait("sp", t[0], t[1])


A_DEC = 0.6065306597126334
ALPHA = (2 * 4) ** 0.25
NOWN = 18
NKT = 66
NNT = 22
GROUPS = [[0, 1, 2, 3], [4, 5, 6, 7]]
XROWS = 8960


def na_chunks(i):
    if i == 0:
        return [(c, 128) for c in range(0, 6)]
    if i == 1:
        return [(c, 128) for c in range(1, 6)]
    if i == 15:
        return [(c, 128) for c in range(14, 19)] + [(19, 64)]
    return [(c, 128) for c in range(i, i + 4)] + [(i + 4, 64)]


def na_class(i):
    return {0: 0, 1: 1, 14: 3, 15: 4}.get(i, 2)


def lat_row(tau):
    rho, rem = divmod(tau, 2048)
    k, i = divmod(rem, 256)
    return 512 + 1024 * k + 256 * rho + i


class _A:
    def __init__(self, ap, tk):
        self.t = ap; self.tk = tk

    def __getitem__(self, idx):
        return V(self.t[idx], self.tk)


def build_F(depth=4):
    nc = bass.Bass("TRN2", target_bir_lowering=False)
    dt = nc.dram_tensor
    I = lambda n, sh: dt(n, sh, F32, kind="ExternalInput").ap()
    xa0 = I("xa0", [XROWS, 1024]); xs0 = I("xs0", [2560, 1024])
    cv = I("cv", [128, 16]); wmod = I("wmod", [depth, 1024, 3072]); bmod = I("bmod", [depth, 128, 3072])
    wb = I("wb", [depth, 1024, 2432]); wout = I("wout", [depth, 1024, 1024])
    wa = I("wa", [depth, 1024, 640]); cw = I("cw", [depth, 64, 33]); pvi = I("pv", [depth, 64, 15])
    w2 = I("w2", [depth, 32, 192]); a2 = I("a2", [depth, 32, 192])
    cstA = I("cstA", [64, 1664])
    cosK = I("cosK", [8192, 128]); sinK = I("sinK", [8192, 128])
    cosQ = I("cosQ", [2048, 384]); sinQ = I("sinQ", [2048, 384])
    dsel_in = I("dsel", [128, 2])
    gk = I("gk", [depth, 128, 128]); gq = I("gq", [depth, 128, 384])
    nab = I("nab", [depth, 5, 4, 128, 768])
    gng = I("gng", [depth, 128, 384]); gnb = I("gnb", [depth, 128, 384])
    lng = I("lng", [depth, 128, 1024]); lnb = I("lnb", [depth, 128, 1024])
    ident_in = I("ident", [128, 128]); jmat_in = I("jmat", [128, 128]); jd_in = I("jd", [128, 128])
    out = dt("out", [2048, 1024], F32, kind="ExternalOutput").ap()

    s = S(nc)
    tickets = []
    pb = [s.ps([128, 512], name="pb%d" % i) for i in range(8)]
    XA = [s.dram([XROWS, 1024], "XA%d" % i) for i in range(2)]
    YB = s.dram([8448, 384], "YB")
    YGc = s.dram([4 * 256, 384], "YGc")
    YGl = s.dram([16 * 4 * 512, 384], "YGl")
    XN = s.dram([2048, 1024], "XN")
    XS = s.dram([2560, 1024], "XS")
    YF = [s.dram([2048, 384], "YF%d" % i) for i in range(2)]
    YBk = [s.dram([2048, 384], "YBk%d" % i) for i in range(2)]
    xa0_T = _A(xa0, Tk("xa0")); xs0_T = _A(xs0, Tk("xs0"))

    _rd = {}

    def RR(q):
        if q not in _rd:
            pid = s.eng[q].partition_id()
            _rd[q] = pid % 4
        return _rd[q]

    dynq = ["sp", "act", "pool"]
    dync = [0]

    def dyndma(dst_v, src_fn):
        q = dynq[dync[0] % 3]; dync[0] += 1
        return s.dma(q, dst_v, src_fn(RR(q)))

    ident = s.sb([128, 128], name="ident"); s.dma("sp", ident[:], ident_in)
    jmat = s.sb([128, 128], name="jmat"); s.dma("act", jmat[:], jmat_in)
    jd = s.sb([128, 128], name="jd"); s.dma("sp", jd[:], jd_in)
    dsel = s.sb([128, 2], name="dsel"); s.dma("act", dsel[:], dsel_in)
    ones = s.sb([128, 128], name="ones"); s.memset(ones[:], 1.0)
    cv_t = s.sb([128, 16], name="cv"); s.dma("sp", cv_t[:], cv)
    scv = s.sb([128, 16], name="scv"); s.act(scv[:], cv_t[:], AF.Silu)
    mod = [s.sb([128, 3072], name="mod%d" % j) for j in range(2)]
    for l in range(depth):
        Xc = xa0_T if l == 0 else XA[(l - 1) % 2]
        Xn = XA[l % 2]
        last = (l == depth - 1)

        if l == 0:
            XSc = xs0_T
        else:
            XSc = XS
            lat = Xc.t[512:8704, :]
            dyndma(V(XS.t[256:2304, :].rearrange("(o k i) c -> o k (i c)", o=1, k=8), XS.tk),
                   lambda r: V(lat.rearrange("(k rr i) c -> rr k (i c)", rr=4, i=256)[bass.ds(r, 1)], Xc.tk))
            units = lat.rearrange("(u i) c -> u (i c)", i=256)
            dyndma(V(XS.t[0:256, :].rearrange("(o i) c -> o (i c)", o=1), XS.tk),
                   lambda r: V(units[bass.ds(r + 27, 1), :], Xc.tk))
            dyndma(V(XS.t[2304:2560, :].rearrange("(o i) c -> o (i c)", o=1), XS.tk),
                   lambda r: V(units[bass.ds(r + 1, 1), :], Xc.tk))
        with s.phase():
            stage_w = s.sb([128, 8, 512], name="stage_w")
            Rl = s.sb([128, 8, 128], name="Rl")
            bmod_t = s.sb([128, 512], name="bmodt")
            for j in range(2):
                for k in range(8):
                    s.ts(Rl[:, k, :], ones[:], scv[:, 2 * k + j:2 * k + j + 1], ALU.mult, e="dve" if k % 2 == 0 else "pool")
                for cb in range(6):
                    s.dma("sp", stage_w[:], wmod[l].rearrange("(k p) c -> p k c", p=128)[:, :, cb * 512:(cb + 1) * 512])
                    s.dma("act", bmod_t[:], bmod[l][:, cb * 512:(cb + 1) * 512])
                    ps = pb[cb % 2]
                    for k in range(8):
                        s.mm(ps[:, :], Rl[:, k, :], stage_w[:, k, :], start=(k == 0), stop=(k == 7))
                    s.tt(mod[j][:, cb * 512:(cb + 1) * 512], ps[:, :], bmod_t[:], ALU.add)
                s.ts(mod[j][:, 1024:2048], mod[j][:, 1024:2048], 1.0, ALU.add, e="pool")

        with s.phase():
            cst_t = s.sb([64, 1664], name="cst"); s.dma("sp", cst_t[:], cstA)
            identA = cst_t[:, 0:64]
            mask3 = lambda h: cst_t[:, 64 + h * 320: 64 + (h + 1) * 320]
            rmask = cst_t[:, 1024:1280]
            idt3 = cst_t[:, 1280:1664]
            cw_t = s.sb([64, 33], name="cw"); s.dma("act", cw_t[:], cw[l])
            pv_t = s.sb([64, 16], name="pv"); s.dma("act", pv_t[:, 0:15], pvi[l])
            omk = s.sb([64, 3], name="omk")
            for h in range(3):
                s.ts(omk[:, h:h + 1], pv_t[:, h * 5 + 3:h * 5 + 4], -1.0, ALU.mult, 1.0, ALU.add)
            w2_t = s.sb([32, 192], name="w2"); s.dma("act", w2_t[:], w2[l])
            a2_t = s.sb([32, 192], name="a2"); s.dma("act", a2_t[:], a2[l])
            xt = [s.sb([128, 1024], name="xt%d" % i) for i in range(2)]
            ht = s.sb([128, 1024], name="ht")
            xr = s.sb([128, 1024], name="xr")
            wab = s.sb([128, 8, 640], BF16, name="wab")
            for k in range(8):
                st_ = xt[k % 2]
                s.dma("sp" if k % 2 == 0 else "act", st_[:, 0:640], wa[l][k * 128:(k + 1) * 128, :])
                s.cp(wab[:, k, :], st_[:, 0:640], e="dve" if k % 2 == 0 else "pool")
            cts = [(i * 64, 64) for i in range(9)] + [(576, 32), (608, 32)]
            hg = [s.sb([128, 8, 258], BF16, name="hg%d" % i) for i in range(2)]
            for h_ in hg:
                s.memset(h_[:], 0.0)
            raw = [s.sb([64, 258], name="raw%d" % i) for i in range(2)]
            ctmp = [s.sb([64, 256], name="ctmp%d" % i) for i in range(2)]
            mk = lambda nm, shape=(64, 256), dt_=F32: [s.sb(list(shape), dt_, name="%s%d" % (nm, h)) for h in range(3)]
            uR, uK, uV = mk("uR"), mk("uK"), mk("uV", dt_=BF16)
            uD = s.sb([32, 256], name="uD"); uA = s.sb([32, 256], name="uA"); ddt = s.sb([32, 256], name="ddt")
            sg, ic, kk, tmp, kd, bd = mk("sg"), mk("ic"), mk("kk"), mk("tmp"), mk("kd"), mk("bd")
            cs, csx, csr = mk("cs"), mk("csx"), mk("csr")
            E1, E3 = mk("E1"), mk("E3")
            E4 = E1
            RH, KKH, kt, bt, kc, bc, rk = (mk("RH", dt_=BF16), mk("KKH", dt_=BF16), mk("kt", dt_=BF16), mk("bt", dt_=BF16),
                                           mk("kc", dt_=BF16), mk("bc", dt_=BF16), mk("rk", dt_=BF16))
            identAb = s.sb([64, 64], BF16, name="identAb"); s.cp(identAb[:], identA)
            onesb = s.sb([64, 1], BF16, name="onesb"); s.memset(onesb[:], 1.0)
            wc = mk("wc", (64, 4)); rn = tmp
            trT = [s.sb([64, 3, 256], BF16, name="trT%d" % i) for i in range(4)]
            scS = [s.sb([64, 3, 320], BF16, name="scS%d" % i) for i in range(4)]
            XYs = [[s.sb([64, 3, 128], BF16, name="XY%d_%d" % (c, i)) for i in range(2)] for c in range(4)]
            PQs = [[s.sb([64, 3, 128], BF16, name="PQ%d_%d" % (c, i)) for i in range(2)] for c in range(4)]
            KKpTs = [s.sb([64, 192], BF16, name="KKpT%d" % c) for c in range(4)]
            AVs = [s.sb([64, 192], BF16, name="AV%d" % c) for c in range(4)]
            Ulocs = [s.sb([64, 192], name="Uloc%d" % c) for c in range(4)]
            Us = [s.sb([64, 192], BF16, name="U%d" % c) for c in range(4)]
            STb = [s.sb([64, 192], BF16, name="STb%d" % i) for i in range(2)]
            bss = [s.sb([64, 4], name="bs%d" % c) for c in range(4)]
            il = Interleaver(s)
            ST = [s.sb([64, 192], name="ST%d" % i) for i in range(2)]
            YBuf = [s.sb([64, 4, 384], name="YBuf0")] * 2
            bs_ = s.sb([64, 4], name="bs")
            s.memset(ST[0][:], 0.0)
            s.memset(STb[0][:], 0.0)
            sti = 0
            acnt = [0]

            def frontA(g):
                hgt = hg[g % 2]
                for a in range(2):
                    u = 2 * g + a
                    i = acnt[0]; acnt[0] += 1
                    if u < 2:
                        bf_, br_ = 128 * u, 128 * (1 - u)
                        j = 1
                    else:
                        v = u - 2
                        bf_, br_ = lat_row(128 * v), lat_row(128 * (63 - v))
                        j = 0
                    x_ = xt[i % 2]
                    s.dma("sp", x_[:], V(Xc.t[bf_:bf_ + 128, :], Xc.tk))
                    s.dma("act", xr[:], V(Xc.t[br_:br_ + 128, :], Xc.tk))
                    s.ts(x_[:], x_[:], dsel[:, 0:1], ALU.mult, e="pool")
                    s.stt(x_[:], xr[:], dsel[:, 1:2], x_[:], ALU.mult, ALU.add)
                    s.tt(ht[:], x_[:], mod[j][:, 1024:2048], ALU.mult, e="pool")
                    s.tt(ht[:], ht[:], mod[j][:, 0:1024], ALU.add, e="dve")
                    for half in range(2):
                        p = pb[half]
                        with s.atomic():
                            for k in range(4):
                                kk_ = half * 4 + k
                                s.tr(p[:, k * 128:(k + 1) * 128], ht[:, kk_ * 128:(kk_ + 1) * 128], jd[:])
                            s.cp(hgt[:, half * 4:half * 4 + 4, 1 + 128 * a:1 + 128 * (a + 1)],
                                 V(p.t[:, :].rearrange("p (k t) -> p k t", k=4), p.tk), e="act" if half == 0 else "dve")

            NGRP = 33
            OPN = ('RH', 'KKH', 'kt', 'bt', 'kc', 'bc', 'rk', 'uV')
            opsets = [dict(RH=RH, KKH=KKH, kt=kt, bt=bt, kc=kc, bc=bc, rk=rk, uV=uV, wc=wc),
                      dict(RH=mk('RHb', dt_=BF16), KKH=mk('KKHb', dt_=BF16), kt=mk('ktb', dt_=BF16), bt=mk('btb', dt_=BF16),
                           kc=mk('kcb', dt_=BF16), bc=mk('bcb', dt_=BF16), rk=mk('rkb', dt_=BF16), uV=mk('uVb', dt_=BF16),
                           wc=mk('wcb', (64, 4)))]

            def pro1(g):
                O = opsets[g % 2]; uV = O['uV']
                first = g in (0, 1)
                lastg = g in (0, NGRP - 1)
                hgt = hg[g % 2]
                if g + 1 < NGRP:
                    frontA(g + 1)
                    hn_ = hg[(g + 1) % 2]
                    s.cp(hgt[:, :, 257:258], hn_[:, :, 1:2], e="pool")
                    s.cp(hn_[:, :, 0:1], hgt[:, :, 256:257], e="pool")
                for ci, (c0, M) in enumerate(cts):
                    pr = pb[ci % 2]
                    rw = raw[ci % 2]
                    with s.atomic():
                        for k in range(8):
                            s.mm(pr[0:M, 0:258], wab[:, k, c0:c0 + M], hgt[:, k, :], start=(k == 0), stop=(k == 7))
                        s.cp(rw[0:M, :], pr[0:M, 0:258], e="act" if ci % 2 == 0 else "dve")
                    if first:
                        s.memset(rw[0:M, 0:1], 0.0, e="pool")
                    if lastg:
                        s.memset(rw[0:M, 257:258], 0.0, e="pool")
                    dst = (uR, uK, uV)[ci // 3][ci % 3] if ci < 9 else (uD, uA)[ci - 9]
                    tm = ctmp[ci % 2]
                    s.act(tm[0:M, :], rw[0:M, 0:256], AF.Identity, scale=cw_t[0:M, ci * 3:ci * 3 + 1])
                    s.stt(tm[0:M, :], rw[0:M, 1:257], cw_t[0:M, ci * 3 + 1:ci * 3 + 2], tm[0:M, :], ALU.mult, ALU.add)
                    s.stt(dst[0:M, :], rw[0:M, 2:258], cw_t[0:M, ci * 3 + 2:ci * 3 + 3], tm[0:M, :], ALU.mult, ALU.add)
                s.act(ddt[:], uD[:], AF.Tanh)

            def prep_head(g, h):
                O = opsets[g % 2]
                RH, KKH, kt, bt, kc, bc, rk, wc = O['RH'], O['KKH'], O['kt'], O['bt'], O['kc'], O['bc'], O['rk'], O['wc']
                P = lambda i: pv_t[:, h * 5 + i:h * 5 + i + 1]
                pz = pb[2 + h]
                with s.atomic():
                    s.mm(pz[0:64, 0:256], w2_t[:, h * 64:(h + 1) * 64], ddt[:])
                    s.act(sg[h][:], pz[0:64, 0:256], AF.Sigmoid, bias=P(0))
                with s.atomic():
                    s.mm(pz[0:64, 256:512], a2_t[:, h * 64:(h + 1) * 64], uA[:])
                    s.act(ic[h][:], pz[0:64, 256:512], AF.Sigmoid, bias=P(1))
                s.ts(kk[h][:], uK[h][:], P(2), ALU.mult, e="pool")
                s.tt(tmp[h][:], kk[h][:], kk[h][:], ALU.mult, e="pool")
                pss = pb[2 + h]
                with s.atomic():
                    s.mm(pss[0:64, 0:256], ones[0:64, 0:64], tmp[h][:])
                    s.ts(rn[h][:], pss[0:64, 0:256], 1e-12, ALU.max)
                s.act(rn[h][:], rn[h][:], AF.Sqrt)
                s.recip(rn[h][:], rn[h][:])
                s.tt(kk[h][:], kk[h][:], rn[h][:], ALU.mult)
                s.ts(tmp[h][:], ic[h][:], P(3), ALU.mult, omk[:, h:h + 1], ALU.add, e="pool")
                s.tt(kd[h][:], uK[h][:], tmp[h][:], ALU.mult, e="pool")
                s.tt(bd[h][:], kk[h][:], ic[h][:], ALU.mult, e="pool")
                s.op("dve", lambda h=h: nc.vector.tensor_tensor_scan(cs[h][:].ap, rmask.ap, sg[h][:].ap, 0.0, ALU.mult, ALU.add),
                     [rmask, sg[h][:]], [cs[h][:]])
                s.tt(csx[h][:], cs[h][:], sg[h][:], ALU.subtract)
                for c in range(4):
                    s.ts(csr[h][:, c * 64:(c + 1) * 64], cs[h][:, c * 64:(c + 1) * 64],
                         cs[h][:, c * 64 + 63:c * 64 + 64], ALU.subtract)
                s.act(wc[h][:], cs[h][:, 63::64], AF.Exp, scale=-A_DEC)
                s.act(E1[h][:], cs[h][:], AF.Exp, scale=-A_DEC)
                s.tt(RH[h][:], uR[h][:], E1[h][:], ALU.mult, e="pool")
                s.act(E1[h][:], csx[h][:], AF.Exp, scale=-A_DEC)
                s.tt(KKH[h][:], kk[h][:], E1[h][:], ALU.mult, e="pool")
                s.act(E3[h][:], cs[h][:], AF.Exp, scale=A_DEC)
                s.tt(kt[h][:], kd[h][:], E3[h][:], ALU.mult)
                s.tt(bt[h][:], bd[h][:], E3[h][:], ALU.mult, e="pool")
                s.act(E4[h][:], csr[h][:], AF.Exp, scale=A_DEC)
                s.tt(kc[h][:], kd[h][:], E4[h][:], ALU.mult)
                s.tt(bc[h][:], bd[h][:], E4[h][:], ALU.mult, e="pool")
                s.stt(rk[h][:], uR[h][:], P(4), kd[h][:], ALU.mult, ALU.mult)

            def chunk_body(g, c):
                n = 4 * g + c
                O = opsets[g % 2]
                RH, KKH, kt, bt, kc, bc, rk, uV = O['RH'], O['KKH'], O['kt'], O['bt'], O['kc'], O['bc'], O['rk'], O['uV']
                cc = slice(c * 64, (c + 1) * 64)
                tT = trT[c]; sS = scS[c]
                XY = XYs[c]; PQ = PQs[c]; KKpT = KKpTs[c]; AV = AVs[c]; Uloc = Ulocs[c]; U = Us[c]; bs_ = bss[c]
                p_ = c % 2
                for h in range(3):
                    ptr = pb[2 + p_]
                    with s.atomic():
                        ptrb = ptr.t[0:64, 0:128].bitcast(BF16)
                        for i, src_ in enumerate((KKH, bc, kc, uV)):
                            s.tr(V(ptrb[:, i * 64:(i + 1) * 64], ptr.tk), src_[h][:, cc], identAb[:])
                        s.cp(tT[:, h, :], V(ptrb[:, 0:256], ptr.tk), e="act")
                    psc = pb[4 + p_]
                    with s.atomic():
                        s.mm(psc[0:64, 0:64], kt[h][:, cc], RH[h][:, cc])
                        s.mm(psc[0:64, 64:128], kt[h][:, cc], KKH[h][:, cc])
                        s.mm(psc[0:64, 128:192], bt[h][:, cc], RH[h][:, cc])
                        s.mm(psc[0:64, 192:256], bt[h][:, cc], KKH[h][:, cc])
                        s.mm(psc[0:64, 256:320], KKH[h][:, cc], bt[h][:, cc])
                        s.tt(sS[:, h, :], psc[0:64, 0:320], mask3(h), ALU.mult)
                s.tt(PQ[0][:, :, :], sS[:, :, 192:320], V(idt3.ap.rearrange("p (h c) -> p h c", c=128), idt3.tk), ALU.add, e="pool")
                Xc_ = lambda lvl, h: (sS[:, h, 192:256] if lvl == 0 else XY[lvl % 2][:, h, 0:64])
                Yc_ = lambda lvl, h: (sS[:, h, 256:320] if lvl == 0 else XY[lvl % 2][:, h, 64:128])
                for lvl in range(5):
                    pn, pq = pb[4 + p_], pb[6 + p_]
                    nxt = XY[(lvl + 1) % 2]
                    with s.atomic():
                        for h in range(3):
                            s.mm(pn[0:64, h * 128:h * 128 + 64], Yc_(lvl, h), Xc_(lvl, h))
                            if lvl < 4:
                                s.mm(pn[0:64, h * 128 + 64:h * 128 + 128], Xc_(lvl, h), Yc_(lvl, h))
                        pn3 = pn.t[0:64, 0:384].rearrange("p (h c) -> p h c", c=128)
                        if lvl < 4:
                            s.cp(nxt[:, :, :], V(pn3, pn.tk), e="act")
                        else:
                            s.cp(nxt[:, :, 0:64], V(pn3[:, :, 0:64], pn.tk), e="act")
                    Pc, Pn = PQ[lvl % 2], PQ[(lvl + 1) % 2]
                    with s.atomic():
                        for h in range(3):
                            s.mm(pq[0:64, h * 128:h * 128 + 64], Pc[:, h, 64:128], nxt[:, h, 0:64])
                            if lvl < 4:
                                s.mm(pq[0:64, h * 128 + 64:h * 128 + 128], Pc[:, h, 0:64], nxt[:, h, 64:128])
                        pq3 = pq.t[0:64, 0:384].rearrange("p (h c) -> p h c", c=128)
                        if lvl < 4:
                            s.tt(Pn[:, :, :], V(pq3, pq.tk), Pc[:, :, :], ALU.add)
                        else:
                            s.tt(Pn[:, :, 0:64], V(pq3[:, :, 0:64], pq.tk), Pc[:, :, 0:64], ALU.add)
                TT = PQ[1]
                pk = pb[2 + p_]
                with s.atomic():
                    for h in range(3):
                        s.mm(pk[0:64, h * 64:(h + 1) * 64], tT[:, h, 0:64], TT[:, h, 0:64])
                        s.mm(pk[0:64, 192 + h * 64:192 + (h + 1) * 64], sS[:, h, 64:128], tT[:, h, 192:256])
                    s.cp(KKpT[:], pk[0:64, 0:192], e="act")
                    s.cp(AV[:], pk[0:64, 192:384], e="dve")
                pk3 = pb[6 + p_]
                with s.atomic():
                    for h in range(3):
                        s.mm(pk3[0:64, h * 64:(h + 1) * 64], TT[:, h, 0:64], AV[:, h * 64:(h + 1) * 64])
                    s.cp(Uloc[:], pk3[0:64, 0:192], e="act")

            def chunk_seq(g, c):
                n = 4 * g + c
                yb = YBuf[0]
                O = opsets[g % 2]
                RH, rk, wc = O['RH'], O['rk'], O['wc']
                cc = slice(c * 64, (c + 1) * 64)
                tT = trT[c]; sS = scS[c]
                KKpT = KKpTs[c]; Uloc = Ulocs[c]; U = Us[c]; bs_ = bss[c]
                Sc, Sn = ST[n % 2], ST[(n + 1) % 2]
                Scb, Snb = STb[n % 2], STb[(n + 1) % 2]
                pu = pb[0]
                with s.atomic():
                    for h in range(3):
                        s.mm(pu[0:64, h * 64:(h + 1) * 64], KKpT[:, h * 64:(h + 1) * 64], Scb[:, h * 64:(h + 1) * 64])
                    s.stt(U[:], pu[0:64, 0:192], -1.0, Uloc[:], ALU.mult, ALU.subtract)
                pS = pb[1]
                with s.atomic():
                    for h in range(3):
                        hs = slice(h * 64, (h + 1) * 64)
                        s.mm(pS[0:64, hs], tT[:, h, 128:192], tT[:, h, 192:256], start=True, stop=False)
                        s.mm(pS[0:64, hs], tT[:, h, 64:128], U[:, hs], start=False, stop=True)
                    for h in range(3):
                        hs = slice(h * 64, (h + 1) * 64)
                        s.stt(Sn[:, hs], Sc[:, hs], wc[h][:, c:c + 1], pS[0:64, hs], ALU.mult, ALU.add)
                    s.cp(Snb[:], Sn[:], e="act")
                py = pb[0]
                with s.atomic():
                    for h in range(3):
                        hs = slice(256 + h * 64, 256 + (h + 1) * 64)
                        hh = slice(h * 64, (h + 1) * 64)
                        s.mm(py[0:64, hs], RH[h][:, cc], Scb[:, hh], start=True, stop=False)
                        s.mm(py[0:64, hs], sS[:, h, 128:192], U[:, hh], start=False, stop=False)
                        s.mm(py[0:64, hs], sS[:, h, 0:64], tT[:, h, 192:256], start=False, stop=True)
                    s.cp(yb[:, c, 0:192], py[0:64, 256:448], e="act")
                pbn = pb[1]
                with s.atomic():
                    for h in range(3):
                        s.mm(pbn[0:64, 256 + h:256 + h + 1], rk[h][:, cc], onesb[:, 0:1])
                    s.cp(bs_[:, 0:3], pbn[0:64, 256:259], e="dve")
                for h in range(3):
                    s.ts(yb[:, c, 192 + h * 64:192 + (h + 1) * 64], tT[:, h, 192:256], bs_[:, h:h + 1], ALU.mult, e="pool")
            def seq_group(g):
                tbase = 256 * g
                for c in range(4):
                    chunk_seq(g, c)
                s.dma("sp" if g % 2 == 0 else "act",
                      V(YB.t[tbase:tbase + 256, :].rearrange("(c t) f -> t c f", t=64), YB.tk), YBuf[0][:, :, :])
                if g == 0:
                    s.allgather(YB[0:256, :], YGc[:, :], GROUPS)
                elif g % 2 == 0:
                    m = g // 2 - 1
                    s.allgather(YB[256 + 512 * m:256 + 512 * (m + 1), :], YGl[2048 * m:2048 * (m + 1), :], GROUPS)

            frontA(0)
            pro1(0)
            il.run([(lambda h=h: prep_head(0, h)) for h in range(3)], 3)
            for g in range(NGRP):
                wk1 = [(lambda c=c: chunk_body(g, c)) for c in range(4)]
                if g + 1 < NGRP:
                    wk1.append(lambda: pro1(g + 1))
                il.run(wk1, 5)
                wk2 = [lambda: seq_group(g)]
                if g + 1 < NGRP:
                    wk2 += [(lambda h=h: prep_head(g + 1, h)) for h in range(3)]
                il.run(wk2, 4)
        ygv = YGl.t.rearrange("(m sr i) c -> sr m (i c)", sr=4, i=512)
        for sr in range(2):
            dyndma(V(YF[sr].t.rearrange("(m i) c -> m (i c)", i=512), YF[sr].tk),
                   lambda r, sr=sr: V(ygv[sr][bass.ds(r * 4, 4), :], YGl.tk))
            dyndma(V(YBk[sr].t.rearrange("(m i) c -> m (i c)", i=512), YBk[sr].tk),
                   lambda r, sr=sr: V(ygv[2 + sr][bass.ds((3 - r) * 4, 4), :], YGl.tk))
        with s.phase():
            GK = s.sb([128, 128], name="GK"); s.dma("act", GK[:], gk[l])
            GQ = s.sb([128, 384], name="GQ"); s.dma("act", GQ[:], gq[l])
            GNG = s.sb([128, 384], name="GNG"); s.dma("act", GNG[:], gng[l])
            GNB = s.sb([128, 384], name="GNB"); s.dma("act", GNB[:], gnb[l])
            YAN = s.sb([128, 18, 640], BF16, name="YAN")
            wbuf = s.sb([128, 8, 1024], BF16, name="wbuf")
            woutb = s.sb([128, 8, 1024], BF16, name="woutb")
            xt = [s.sb([128, 1024], name="xt%d" % i) for i in range(2)]
            hts = [s.sb([128, 1024], name="ht%d" % i) for i in range(2)]
            ht = hts[0]
            pe = [s.sb([128, 1024], name="pe%d" % i) for i in range(2)]
            sqs = [pe[1], None]
            wst = xt
            ilB = Interleaver(s)
            pjB = _A(YAN.t[:, 0:2, :].rearrange("p a c -> p (a c)").bitcast(F32), YAN.tk)
            sqs[1] = _A(YAN.t[:, 2:4, :].rearrange("p a c -> p (a c)").bitcast(F32), YAN.tk)

            def load_w(dst, src, c0, ncols, dcol=0):
                for k in range(8):
                    st_ = wst[k % 2]
                    s.dma("sp" if k % 2 == 0 else "act", st_[:, 0:ncols], src[k * 128:(k + 1) * 128, c0:c0 + ncols])
                    s.cp(dst[:, k, dcol:dcol + ncols], st_[:, 0:ncols], e="dve" if k % 2 == 0 else "pool")

            load_w(woutb, wout[l], 0, 1024)
            kT = s.sb([128, 8448], BF16, name="kT")
            Vg = s.sb([128, NKT, 2, 65], BF16, name="Vg")
            qT = [s.sb([128, 2304], BF16, name="qT%d" % i) for i in range(3)]
            nqT = [s.sb([128, 2304], BF16, name="nqT%d" % i) for i in range(2)]
            nkT = [s.sb([128, 2816], BF16, name="nkT%d" % i) for i in range(2)]
            Vn = s.sb([128, NNT, 4, 65], BF16, name="Vn")
            s.memset(Vg[:, :, :, 64:65], 1.0)
            s.memset(Vn[:, :, :, 64:65], 1.0)
            hT = [s.sb([128, 8, 128], BF16, name="hT%d" % i) for i in range(2)]
            tcs = [s.sb([128, 384], name="tcos%d" % i) for i in range(2)]
            tsns = [s.sb([128, 384], name="tsin%d" % i) for i in range(2)]
            sms = [s.sb([128, 64], name="sm%d" % i) for i in range(2)]
            tc_, tsn, sm = tcs[0], tsns[0], sms[0]
            cnt = [0]

            def front(srcT, row, j, w=None):
                if w is None:
                    i = cnt[0]; cnt[0] += 1
                    w = i % 2
                    banks = (pb[0], pb[1])
                else:
                    banks = (pb[w], pb[w])
                q = "sp" if w == 0 else "act"
                x_ = xt[w]
                ht_ = hts[w]
                s.dma(q, x_[:], V(srcT.t[row:row + 128, :], srcT.tk))
                s.tt(ht_[:], x_[:], mod[j][:, 1024:2048], ALU.mult, e="pool")
                s.tt(ht_[:], ht_[:], mod[j][:, 0:1024], ALU.add, e="dve")
                h_ = hT[w]
                for half in range(2):
                    p = banks[half]
                    with s.atomic():
                        for k in range(4):
                            kk_ = half * 4 + k
                            s.tr(p[:, k * 128:(k + 1) * 128], ht_[:, kk_ * 128:(kk_ + 1) * 128], ident[:])
                        s.cp(h_[:, half * 4:half * 4 + 4, :], V(p.t[:, :].rearrange("p (k t) -> p k t", k=4), p.tk),
                             e="act" if half == 0 else "dve")
                return x_, h_

            def proj(h_, c0, ncols, dst, wsrc=None, w=None):
                wsrc = wsrc or wbuf
                o = 0
                bi = 2
                while o < ncols:
                    n = min(512, ncols - o)
                    p = pb[bi] if w is None else pb[2 + w]
                    with s.atomic():
                        for k in range(8):
                            s.mm(p[:, 0:n], h_[:, k, :], wsrc[:, k, c0 + o:c0 + o + n], start=(k == 0), stop=(k == 7))
                        s.cp(dst[:, o:o + n], p[:, 0:n], e="act" if bi == 2 else "dve")
                    o += n
                    bi = 5 - bi

            def rms_rope(src, H, gtab, scale_mode, rope, dst, w=0):
                sq = sqs[w]; sm = sms[w]; tc_ = tcs[w]; tsn = tsns[w]
                s.tt(sq[:, 0:H * 64], src, src, ALU.mult, e="pool")
                s.red(sm[:, 0:H], V(sq.t[:, 0:H * 64].rearrange("p (h d) -> p h d", d=64), sq.tk), ALU.add)
                if scale_mode == "k":
                    s.ts(sm[:, 0:H], sm[:, 0:H], 1.0 / 64, ALU.mult, 1e-6, ALU.add)
                else:
                    s.ts(sm[:, 0:H], sm[:, 0:H], 64e-6, ALU.add)
                s.act(sm[:, 0:H], sm[:, 0:H], AF.Sqrt)
                s.recip(sm[:, 0:H], sm[:, 0:H])
                for h in range(H):
                    s.stt(V(dst.ap[:, h * 64:(h + 1) * 64], dst.tk), V(src.ap[:, h * 64:(h + 1) * 64], src.tk), sm[:, h:h + 1],
                          gtab[:, h * 64:(h + 1) * 64], ALU.mult, ALU.mult)
                if rope is not None:
                    cos_d, sin_d, rowfn = rope
                    s.dma("sp", tc_[:, 0:H * 64], cos_d[rowfn:rowfn + 128, :])
                    s.dma("act", tsn[:, 0:H * 64], sin_d[rowfn:rowfn + 128, :])
                    t1 = sq
                    v4 = lambda ap: ap.rearrange("p (g a d) -> p g a d", a=2, d=16)
                    d4 = v4(dst.ap); s4 = v4(tsn.t[:, 0:H * 64]); t4 = v4(t1.t[:, 0:H * 64])
                    s.tt(V(t4[:, :, 0, :], t1.tk), V(d4[:, :, 1, :], dst.tk), V(s4[:, :, 0, :], tsn.tk), ALU.mult, e="pool")
                    s.tt(V(t4[:, :, 1, :], t1.tk), V(d4[:, :, 0, :], dst.tk), V(s4[:, :, 1, :], tsn.tk), ALU.mult, e="pool")
                    s.tt(dst, dst, tc_[:, 0:H * 64], ALU.mult)
                    s.tt(dst, dst, t1[:, 0:H * 64], ALU.add)

            pt = [s.sb([128, 512], BF16, name="pt%d" % i) for i in range(3)]
            rcp = s.sb([128, 8], name="rcp")
            bias_t = [s.sb([128, 768], name="bias%d" % i) for i in range(2)]
            ptc = [0]

            load_w(wbuf, wb[l], 768, 256)

            def k_body(t):
                w = t % 2
                j = 1 if t < 2 else 0
                x_, h_ = front(Xc, 128 * t if t < 2 else lat_row(128 * (t - 2)), j, w)
                pj = pe[0] if w == 0 else pjB
                proj(h_, 0, 256, pj, w=w)
                rope = None if t < 2 else (cosK, sinK, (t - 2) * 128)
                kr = hts[w]
                rms_rope(pj[:, 0:128], 2, GK, "k", rope, kr[:, 0:128], w)
                p = pb[6 + w]
                with s.atomic():
                    s.tr(p[:, 0:128], kr[:, 0:128], ident[:])
                    s.cp(kT[:, t * 128:(t + 1) * 128], p[:, 0:128], e="act")
                s.cp(Vg[:, t, :, 0:64], V(pj.t[:, 128:256].rearrange("p (g d) -> p g d", d=64), pj.tk), e="pool")

            ilB.run([(lambda t=t: k_body(t)) for t in range(NKT)], 2)
            load_w(wbuf, wb[l], 1664, 512)

            def n_body(t):
                w = t % 2
                j = 1 if t >= 20 else 0
                if t >= 20:
                    x_, h_ = front(Xc, 128 * (t - 20), j, w)
                else:
                    x_, h_ = front(XSc, 128 * t, j, w)
                pj = pe[0] if w == 0 else pjB
                proj(h_, 0, 512, pj, w=w)
                for pr_ in range(2):
                    p = pb[6 + w]
                    with s.atomic():
                        s.tr(p[:, 0:128], pj[:, pr_ * 128:(pr_ + 1) * 128], ident[:])
                        s.cp(nkT[pr_][:, t * 128:(t + 1) * 128], p[:, 0:128], e="act" if pr_ == 0 else "dve")
                s.cp(Vn[:, t, :, 0:64], V(pj.t[:, 256:512].rearrange("p (g d) -> p g d", d=64), pj.tk), e="pool")

            ilB.run([(lambda t=t: n_body(t)) for t in range(NNT)], 2)
            for sl in range(6):
                hh_ = (sl // 2) + 3 * (sl % 2)
                load_w(wbuf, wb[l], 384 + hh_ * 64, 64, dcol=sl * 64)
            load_w(wbuf, wb[l], 1408, 256, dcol=384)
            own_src = lambda t: ((Xc, 128 * (t - 16)) if t >= 16 else (XSc, 256 + 128 * t))

            def q_body(t):
                w = t % 2
                j = 1 if t >= 16 else 0
                x_, h_ = front(*own_src(t), j, w)
                pj = pe[0] if w == 0 else pjB
                proj(h_, 0, 640, pj, w=w)
                rope = None if t >= 16 else (cosQ, sinQ, 128 * t)
                qr = hts[w]
                rms_rope(pj[:, 0:384], 6, GQ, "q", rope, qr[:, 0:384], w)
                for pr_ in range(3):
                    p = pb[6 + w]
                    with s.atomic():
                        s.tr(p[:, 0:128], qr[:, pr_ * 128:(pr_ + 1) * 128], ident[:])
                        s.cp(qT[pr_][:, t * 128:(t + 1) * 128], p[:, 0:128], e="act" if pr_ % 2 == 0 else "dve")
                s.ts(qr[:, 384:640], pj[:, 384:640], 0.125, ALU.mult, e="pool")
                for pr_ in range(2):
                    p = pb[6 + w]
                    with s.atomic():
                        s.tr(p[:, 0:128], qr[:, 384 + pr_ * 128:384 + (pr_ + 1) * 128], ident[:])
                        s.cp(nqT[pr_][:, t * 128:(t + 1) * 128], p[:, 0:128], e="act" if pr_ == 0 else "dve")

            ilB.run([(lambda t=t: q_body(t)) for t in range(NOWN)], 2)

            def attend(qsrc, g, qc0, nq, chunks, ksrc, vsrc_fn, bias_fn, dst_fn):
                nqs = nq // 128
                lo, hi = 64 * g, 64 * g + 64
                n_ = len(chunks)
                for ci in range(n_ + 1):
                    if ci < n_:
                        kc0, nk, cid = chunks[ci]
                        ps = pb[ci % 3]
                        bsrc = bias_fn(cid, nk) if bias_fn else None
                        s.mm(ps[0:nk, 0:nq], ksrc[lo:hi, kc0:kc0 + nk], qsrc[lo:hi, qc0:qc0 + nq], start=True, stop=(bsrc is None))
                        if bsrc is not None:
                            s.mm(ps[0:nk, 0:nq], bsrc, ident[:, 0:nq], start=False, stop=True)
                    if ci >= 1:
                        kc0p, nkp, cidp = chunks[ci - 1]
                        p_ = pt[ptc[0] % 3]; ptc[0] += 1
                        s.act(p_[0:nkp, 0:nq], pb[(ci - 1) % 3][0:nkp, 0:nq], AF.Exp)
                        for qs in range(nqs):
                            s.mm(pb[4 + qs][:, 0:65], p_[0:nkp, qs * 128:(qs + 1) * 128], vsrc_fn(cidp, nkp),
                                 start=(ci == 1), stop=(ci == n_))
                for qs in range(nqs):
                    s.recip(rcp[:, qs:qs + 1], pb[4 + qs][:, 64:65])
                    s.ts(dst_fn(qs), pb[4 + qs][:, 0:64], rcp[:, qs:qs + 1], ALU.mult)

            oT = _A(pe[0].t[0:65, 0:512], pe[0].tk)
            fin = [0]

            def attend_T(qsrc, g, qc0, nq, chunks, ksrc, vsrc_fn, dst_fn):
                nqs = nq // 128
                lo, hi = 64 * g, 64 * g + 64
                po = pb[4]
                n_ = len(chunks)
                for ci in range(n_ + 1):
                    if ci < n_:
                        kc0, nk, cid = chunks[ci]
                        s.mm(pb[ci % 3][0:nk, 0:nq], ksrc[lo:hi, kc0:kc0 + nk], qsrc[lo:hi, qc0:qc0 + nq])
                    if ci >= 1:
                        kc0p, nkp, cidp = chunks[ci - 1]
                        p_ = pt[ptc[0] % 3]; ptc[0] += 1
                        s.act(p_[0:nkp, 0:nq], pb[(ci - 1) % 3][0:nkp, 0:nq], AF.Exp)
                        s.mm(po[0:65, 0:nq], vsrc_fn(cidp, nkp), p_[0:nkp, 0:nq], start=(ci == 1), stop=(ci == n_))
                s.cp(oT[:, 0:nq], po[0:65, 0:nq], e="dve")
                for qs in range(nqs):
                    pf = pb[5 + fin[0] % 3]; fin[0] += 1
                    s.tr(pf[:, 0:65], oT[:, qs * 128:(qs + 1) * 128], ident[0:65, 0:65])
                    s.recip(rcp[:, qs:qs + 1], pf[:, 64:65])
                    s.ts(dst_fn(qs), pf[:, 0:64], rcp[:, qs:qs + 1], ALU.mult)

            for qg in range(4):
                for h in range(6):
                    pr_, g = h % 3, h // 3
                    attend_T(qT[pr_], g, qg * 512, 512, [(c * 128, 128, c) for c in range(NKT)], kT,
                             lambda cid, nk, g=g: Vg[0:nk, cid, g, :],
                             lambda qs, qg=qg, h=h: YAN[:, qg * 4 + qs, h * 64:(h + 1) * 64])
            for h in range(6):
                pr_, g = h % 3, h // 3
                attend_T(qT[pr_], g, 2048, 256, [(c * 128, 128, c) for c in range(2)], kT,
                         lambda cid, nk, g=g: Vg[0:nk, cid, g, :],
                         lambda qs, h=h: YAN[:, 16 + qs, h * 64:(h + 1) * 64])
            bc_ = [0]
            for i in range(16):
                chs = na_chunks(i)
                base = chs[0][0] * 128
                cls = na_class(i)
                for hn in range(4):
                    pr_, g = hn // 2, hn % 2
                    bt_ = bias_t[bc_[0] % 2]; bc_[0] += 1
                    nkeys = sum(nk for _, nk in chs)
                    s.dma("sp" if hn % 2 == 0 else "act", bt_[:, 0:nkeys], nab[l, cls, hn, :, 0:nkeys])
                    chunks = [(c * 128, nk, c) for c, nk in chs] + [(20 * 128, 128, 20), (21 * 128, 128, 21)]
                    attend(nqT[pr_], g, i * 128, 128, chunks, nkT[pr_],
                           lambda cid, nk, hn=hn: Vn[0:nk, cid, hn, :],
                           lambda cid, nk, bt_=bt_, base=base: (None if cid >= 20 else bt_[:, cid * 128 - base:cid * 128 - base + nk]),
                           lambda qs, i=i, hn=hn: YAN[:, i, 384 + hn * 64:384 + (hn + 1) * 64])
            for hn in range(4):
                pr_, g = hn // 2, hn % 2
                attend(nqT[pr_], g, 2048, 256, [(20 * 128, 128, 20), (21 * 128, 128, 21)], nkT[pr_],
                       lambda cid, nk, hn=hn: Vn[0:nk, cid, hn, :], None,
                       lambda qs, hn=hn: YAN[:, 16 + qs, 384 + hn * 64:384 + (hn + 1) * 64])
            load_w(wbuf, wb[l], 0, 384)
            load_w(wbuf, wb[l], 1024, 384, dcol=384)
            load_w(wbuf, wb[l], 2176, 256, dcol=768)
            LNG = _A(kT.t[:, 0:2048].bitcast(F32), kT.tk); s.dma("sp", LNG[:], lng[l])
            LNB = _A(kT.t[:, 2048:4096].bitcast(F32), kT.tk); s.dma("act", LNB[:], lnb[l])
            Ff = _A(kT.t[:, 4096:5632].bitcast(F32).rearrange("p (s c) -> p s c", s=2), kT.tk)
            Bk = _A(kT.t[:, 5632:7168].bitcast(F32), kT.tk)
            cen = _A(kT.t[:, 7168:7936].bitcast(F32), kT.tk)
            ysb = _A(qT[1].t[:, 0:768].bitcast(F32), qT[1].tk)
            bsb = _A(qT[1].t[:, 768:1536].bitcast(F32), qT[1].tk)
            sqb = _A(qT[2].t[:, 0:768].bitcast(F32), qT[2].tk)
            YgT = _A(qT[0].t[:, 0:1024].rearrange("p (k t) -> p k t", k=8), qT[0].tk)
            for t in range(NOWN):
                j = 1 if t >= 16 else 0
                x_, h_ = front(*own_src(t), j)
                G = pe[0]
                proj(h_, 0, 1024, G)
                s.act(G[:], G[:], AF.Silu)
                for sr in range(2):
                    q = "sp" if sr == 0 else "act"
                    if t >= 16:
                        s.dma(q, Ff[:, sr, :], YGc[sr * 256 + 128 * (t - 16):sr * 256 + 128 * (t - 16) + 128, :])
                        rb = (2 + sr) * 256 + 128 - 128 * (t - 16)
                        s.dma(q, Bk[:, sr * 384:(sr + 1) * 384], YGc[rb:rb + 128, :])
                    else:
                        s.dma(q, Ff[:, sr, :], YF[sr][128 * t:128 * t + 128, :])
                        s.dma(q, Bk[:, sr * 384:(sr + 1) * 384], YBk[sr][1920 - 128 * t:1920 - 128 * t + 128, :])
                for sr in range(2):
                    s.mm(pb[6 + sr][:, 0:384], jmat[:], Bk[:, sr * 384:(sr + 1) * 384])
                v2 = lambda A_: V(A_.t[:, 0:384].rearrange("p (s c) -> p s c", s=2), A_.tk)
                for sr in range(2):
                    s.tt(ysb[:, sr * 192:(sr + 1) * 192], Ff[:, sr, 0:192], pb[6 + sr][:, 0:192], ALU.add)
                    s.tt(bsb[:, sr * 192:(sr + 1) * 192], Ff[:, sr, 192:384], pb[6 + sr][:, 192:384], ALU.add)
                v3 = lambda A_: V(A_.t[:, 0:384].rearrange("p (h d) -> p h d", d=64), A_.tk)
                s.red(sm[:, 0:6], v3(ysb), ALU.add)
                s.ts(sm[:, 0:6], sm[:, 0:6], 1.0 / 64, ALU.mult)
                s.tt(v3(cen), v3(ysb), V(sm.t[:, 0:6].unsqueeze(2).to_broadcast([128, 6, 64]), sm.tk), ALU.subtract)
                s.tt(sqb[:], cen[:], cen[:], ALU.mult, e="pool")
                s.red(sm[:, 8:14], v3(sqb), ALU.add)
                s.ts(sm[:, 8:14], sm[:, 8:14], 1.0 / 64, ALU.mult, 64e-5, ALU.add)
                s.act(sm[:, 8:14], sm[:, 8:14], AF.Sqrt)
                s.recip(sm[:, 8:14], sm[:, 8:14])
                s.tt(v3(cen), v3(cen), V(sm.t[:, 8:14].unsqueeze(2).to_broadcast([128, 6, 64]), sm.tk), ALU.mult)
                s.tt(cen[:], cen[:], GNG[:], ALU.mult)
                s.tt(cen[:], cen[:], GNB[:], ALU.add)
                s.tt(cen[:], cen[:], bsb[:], ALU.add)
                Yg = pe[1]
                s.tt(Yg[:, 0:384], cen[:], G[:, 0:384], ALU.mult)
                s.tt(Yg[:, 384:1024], YAN[:, t, :], G[:, 384:1024], ALU.mult)
                for half in range(2):
                    p = pb[half]
                    for k in range(4):
                        kk_ = half * 4 + k
                        s.tr(p[:, k * 128:(k + 1) * 128], Yg[:, kk_ * 128:(kk_ + 1) * 128], ident[:])
                    s.cp(YgT[:, half * 4:half * 4 + 4, :], V(p.t[:, :].rearrange("p (k t) -> p k t", k=4), p.tk),
                         e="act" if half == 0 else "dve")
                yo = pe[0]
                proj(YgT, 0, 1024, yo, wsrc=woutb)
                s.tt(yo[:], yo[:], mod[j][:, 2048:3072], ALU.mult)
                z = pe[1]
                s.stt(z[:], x_[:], ALPHA, yo[:], ALU.mult, ALU.add)
                s.red(sm[:, 16:17], z[:], ALU.add)
                s.ts(sm[:, 16:17], sm[:, 16:17], 1.0 / 1024, ALU.mult)
                s.ts(z[:], z[:], sm[:, 16:17], ALU.subtract)
                s.tt(yo[:], z[:], z[:], ALU.mult, e="pool")
                s.red(sm[:, 17:18], yo[:], ALU.add)
                s.ts(sm[:, 17:18], sm[:, 17:18], 1.0 / 1024, ALU.mult, 1e-5, ALU.add)
                s.act(sm[:, 17:18], sm[:, 17:18], AF.Sqrt)
                s.recip(sm[:, 17:18], sm[:, 17:18])
                s.stt(z[:], z[:], sm[:, 17:18], LNG[:], ALU.mult, ALU.mult)
                s.tt(z[:], z[:], LNB[:], ALU.add)
                q = "sp" if t % 2 == 0 else "act"
                if t < 16:
                    if last:
                        tickets.append(s.dma(q, out[128 * t:128 * t + 128, :], z[:]))
                    else:
                        s.dma(q, XN[128 * t:128 * t + 128, :], z[:])
                        if t % 2 == 1:
                            k = t // 2
                            s.allgather(XN[256 * k:256 * (k + 1), :], Xn[512 + 1024 * k:512 + 1024 * (k + 1), :], GROUPS)
                elif not last:
                    s.dma(q, Xn[128 * (t - 16):128 * (t - 16) + 128, :], z[:])
    s.finish(tickets)
    s.close()
    return nc


def rope_tables():
    t = np.arange(8192)
    row = (t // 64).astype(np.float32); col = (t % 64).astype(np.float32)
    inv = (10000.0 ** (-np.arange(16, dtype=np.float32) / 16)).astype(np.float32)
    ar = row[:, None] * inv; ac = col[:, None] * inv
    ang = np.concatenate([ar, ar, ac, ac], axis=-1).astype(np.float32)
    cos = np.cos(ang).astype(np.float32); sin = np.sin(ang).astype(np.float32)
    sgn = np.concatenate([-np.ones(16), np.ones(16), -np.ones(16), np.ones(16)]).astype(np.float32)
    return cos, sin * sgn


def na_bias(rpb, j):
    NEG = -30000.0
    out = np.full((5, 4, 128, 768), NEG, np.float32)
    tiles = {0: 0, 1: 1, 2: 7, 3: 14, 4: 15}
    for cls, i in tiles.items():
        chs = na_chunks(i)
        srow0 = chs[0][0] * 2
        nkeys = sum(nk for _, nk in chs)
        qrow_l = np.repeat(np.array([2 * i, 2 * i + 1]), 64)
        qcol = np.tile(np.arange(64), 2)
        r = 32 * j + qrow_l
        r_start = np.clip(r - 4, 0, 120)
        c_start = np.clip(qcol - 8, 0, 48)
        key = np.arange(nkeys)
        krow = (32 * j - 4) + srow0 + key // 64
        kcol = key % 64
        dr = krow[None, :] - r[:, None] + 7
        dc = kcol[None, :] - qcol[:, None] + 15
        inwin = ((krow[None, :] >= r_start[:, None]) & (krow[None, :] < r_start[:, None] + 8) &
                 (kcol[None, :] >= c_start[:, None]) & (kcol[None, :] < c_start[:, None] + 16))
        drc = np.clip(dr, 0, 14); dcc = np.clip(dc, 0, 30)
        for h in range(4):
            vals = rpb[h][drc, dcc]
            out[cls, h, :, 0:nkeys] = np.where(inwin, vals, NEG)
    return out


def consts_A():
    idx = np.arange(64)
    inclT = (idx[:, None] <= idx[None, :]).astype(np.float32)
    strictT = (idx[:, None] < idx[None, :]).astype(np.float32)
    strict = strictT.T.copy()
    mask = np.concatenate([inclT, strictT, inclT, -strictT, -strict], axis=1)
    rmask = np.ones((64, 256), np.float32); rmask[:, 0::64] = 0.0
    ident = np.eye(64, dtype=np.float32)
    return np.concatenate([ident, mask, mask, mask, rmask] + [ident] * 6, axis=1).astype(np.float32)


def host_F(inp, depth=4):
    cos, sinS = rope_tables()
    bc = lambda v: np.ascontiguousarray(np.broadcast_to(v[None, :], (128, v.shape[0]))).astype(np.float32)
    L = range(depth)
    shared = dict(
        wmod=np.ascontiguousarray(inp['w_mod'][:depth]),
        bmod=np.stack([bc(inp['b_mod'][l]) for l in L]),
        wb=np.ascontiguousarray(inp['w_in'][:depth, :, 1280:]),
        wout=np.ascontiguousarray(inp['w_out'][:depth]),
        cstA=consts_A(),
        cosK=np.tile(cos, (1, 2)), sinK=np.tile(sinS, (1, 2)),
        gk=np.stack([bc(np.tile(inp['gqa_k_norm'][l], 2)) for l in L]),
        gq=np.stack([bc(np.tile(inp['gqa_q_norm'][l], 6)) for l in L]),
        gng=np.stack([bc(inp['rwkv_gn_g'][l]) for l in L]), gnb=np.stack([bc(inp['rwkv_gn_b'][l]) for l in L]),
        lng=np.stack([bc(inp['ln_g'][l]) for l in L]), lnb=np.stack([bc(inp['ln_b'][l]) for l in L]),
        ident=np.eye(128, dtype=np.float32), jmat=np.ascontiguousarray(np.eye(128, dtype=np.float32)[::-1]),
    )
    per_batch = []
    for b in range(2):
        xa0 = np.zeros((XROWS, 1024), np.float32)
        xa0[0:256] = inp['ctx'][b]
        xa0[512:8704] = inp['x'][b].reshape(4, 8, 256, 1024).transpose(1, 0, 2, 3).reshape(8192, 1024)
        cvec = np.stack([inp['c'][b], inp['c_ctx']], axis=1)
        cv = np.ascontiguousarray(cvec.reshape(8, 128, 2).transpose(1, 0, 2).reshape(128, 16))
        per_batch.append(dict(xa0=xa0, cv=cv))
    nabs = [np.stack([na_bias(inp['na_rpb'][l], j) for l in L]) for j in range(4)]
    maps = []
    for c in range(8):
        b, r = c // 4, c % 4
        d, hh = r // 2, r % 2
        heads = [3 * hh + i for i in range(3)]
        cols = []
        for comp in (0, 384, 768):
            for h in heads:
                cols += list(range(comp + h * 64, comp + h * 64 + 64))
        cols += list(range(1152 + 32 * d, 1152 + 32 * d + 32))
        cols += list(range(1216 + 32 * d, 1216 + 32 * d + 32))
        cols = np.array(cols)
        hcols = np.concatenate([np.arange(h * 64, (h + 1) * 64) for h in heads])
        wa = np.ascontiguousarray(inp['w_in'][:depth][:, :, cols])
        cw = np.zeros((depth, 64, 33), np.float32); pv = np.zeros((depth, 64, 15), np.float32)
        for l in L:
            conv = inp['rwkv_conv'][l][:, cols]
            if d == 1:
                conv = conv[::-1]
            for ci in range(9):
                cw[l, :, ci * 3:ci * 3 + 3] = conv[:, ci * 64:(ci + 1) * 64].T
            cw[l, 0:32, 27:30] = conv[:, 576:608].T
            cw[l, 0:32, 30:33] = conv[:, 608:640].T
            for i, h in enumerate(heads):
                hs = slice(h * 64, (h + 1) * 64)
                pv[l, :, i * 5 + 0] = inp['decay_w0'][l][d, hs]
                pv[l, :, i * 5 + 1] = inp['iclr_a0'][l][d, hs]
                pv[l, :, i * 5 + 2] = inp['rwkv_k_k'][l][hs]
                pv[l, :, i * 5 + 3] = inp['rwkv_k_a'][l][hs]
                pv[l, :, i * 5 + 4] = inp['rwkv_r_k'][l][h]
        w2 = np.ascontiguousarray(inp['decay_w2'][:depth, d][:, :, hcols])
        a2 = np.ascontiguousarray(inp['iclr_a2'][:depth, d][:, :, hcols])
        own = slice(2048 * r, 2048 * r + 2048)
        jd = np.eye(128, dtype=np.float32)
        if d == 1:
            jd = np.ascontiguousarray(jd[::-1])
        dsel = np.zeros((128, 2), np.float32); dsel[:, d] = 1.0
        xs0 = np.zeros((2560, 1024), np.float32)
        lo = 2048 * r - 256
        for sr_ in range(2560):
            pass
        a0, a1 = max(lo, 0), min(lo + 2560, 8192)
        xs0[a0 - lo:a1 - lo] = inp['x'][b][a0:a1]
        m = dict(shared)
        m['xs0'] = xs0
        m.update(per_batch[b])
        m.update(wa=wa, cw=cw, pv=pv, w2=w2, a2=a2, nab=nabs[r],
                 cosQ=np.tile(cos[own], (1, 6)), sinQ=np.tile(sinS[own], (1, 6)), jd=jd, dsel=dsel)
        maps.append(m)
    return maps


from concourse.bass_utils import run_bass_kernel_spmd

_NC = {}


def kernel(**inputs):
    inp = {k: np.asarray(v, dtype=np.float32) for k, v in inputs.items()}
    if 'F' not in _NC:
        _NC['F'] = build_F(4)
    maps = host_F(inp, 4)
    res = run_bass_kernel_spmd(_NC['F'], maps, core_ids=list(range(8))).results
    x = np.stack([np.concatenate([res[b * 4 + r]["out"] for r in range(4)], axis=0) for b in range(2)])
    return np.ascontiguousarray(x.astype(np.float32))
```

```python
import contextlib
import numpy as np
import concourse.bass as bass
import concourse.mybir as mybir

F32 = mybir.dt.float32
BF16 = mybir.dt.bfloat16
AF = mybir.ActivationFunctionType
ALU = mybir.AluOpType
AX = mybir.AxisListType


class Tk:
    __slots__ = ("w", "r", "name", "excl", "acc")

    def __init__(self, name=""):
        self.w = None
        self.r = {}
        self.name = name
        self.excl = False
        self.acc = {}


class T:
    def __init__(self, S, t, name):
        self.t = t
        self.tk = Tk(name)
        self.name = name

    def __getitem__(self, idx):
        return V(self.t[idx], self.tk)


class V:
    __slots__ = ("ap", "tk")

    def __init__(self, ap, tk):
        self.ap = ap
        self.tk = tk


import threading


class _Worker(threading.Thread):
    def __init__(self, il, fn):
        super().__init__(daemon=True)
        self.il = il
        self.fn = fn
        self.go = threading.Event()
        self.done = False
        self.exc = None

    def run(self):
        self.go.wait(); self.go.clear()
        try:
            self.fn()
        except BaseException as e:
            self.exc = e
        self.done = True
        self.il.main_ev.set()

    def pause(self):
        self.il.main_ev.set()
        self.go.wait(); self.go.clear()


class Interleaver:
    def __init__(self, s):
        self.s = s
        self.main_ev = threading.Event()
        self.cur = None

    def run(self, fns, width):
        pending = list(fns)
        active = []
        self.s.yield_hook = self._hook
        try:
            while pending or active:
                while pending and len(active) < width:
                    w = _Worker(self, pending.pop(0)); w.start(); active.append(w)
                for w in list(active):
                    self.cur = w
                    self.main_ev.clear()
                    w.go.set()
                    self.main_ev.wait()
                    if w.exc is not None:
                        raise w.exc
                    if w.done:
                        active.remove(w)
        finally:
            self.s.yield_hook = None
            self.cur = None

    def _hook(self):
        w = self.cur
        if w is not None and threading.current_thread() is w and self.s.atomic_depth == 0:
            w.pause()


class S:
    ENG = ("pe", "act", "dve", "pool", "sp")
    yield_hook = None
    atomic_depth = 0

    @contextlib.contextmanager
    def atomic(self):
        self.atomic_depth += 1
        try:
            yield
        finally:
            self.atomic_depth -= 1
            if self.atomic_depth == 0 and self.yield_hook is not None:
                self.yield_hook()

    def __init__(self, nc):
        self.nc = nc
        self.es = contextlib.ExitStack()
        self.eng = {"pe": nc.tensor, "act": nc.scalar, "dve": nc.vector, "pool": nc.gpsimd, "sp": nc.sync}
        self.sem = {e: self.es.enter_context(nc.semaphore("s_" + e)) for e in self.ENG}
        self.cnt = {e: 0 for e in self.ENG}
        self.dq = {}
        for q in ("sp", "act", "pool"):
            sems = [self.es.enter_context(nc.semaphore("d_%s%d" % (q, i))) for i in range(8)]
            self.dq[q] = dict(sems=sems, cnt=[0] * 8, nxt=0)
            for i, s_ in enumerate(sems):
                self.sem[(q, i)] = s_
        self.waited = {}
        self.cur = self.es
        self.cc_keys = []
        self.n_tiles = 0
        self.n_instr = 0
        self.n_wait = 0

    def sb(self, shape, dt=F32, name=None):
        self.n_tiles += 1
        name = "%s_%d" % (name or "t", self.n_tiles)
        t = self.cur.enter_context(self.nc.sbuf_tensor("sb_" + name, list(shape), dt))
        return T(self, t, name)

    def dram(self, shape, name, dt=F32):
        self.n_tiles += 1
        t = self.nc.dram_tensor("%s_%d" % (name, self.n_tiles), list(shape), dt)
        return T(self, t.ap(), name)

    @contextlib.contextmanager
    def phase(self):
        prev = self.cur
        self.cur = contextlib.ExitStack()
        try:
            yield
        finally:
            self.barrier()
            self.cur.close()
            self.cur = prev

    def barrier(self):
        for e in self.ENG:
            for e2 in self.ENG:
                if e2 != e and self.cnt[e2] > 0:
                    self._wait(e, e2, self.cnt[e2])
            for q, d in self.dq.items():
                for i, c in enumerate(d["cnt"]):
                    if c > 0:
                        self._wait(e, (q, i), c)

    def allgather(self, src, dst, groups):
        if "cc" not in self.dq:
            sems = [self.es.enter_context(self.nc.semaphore("cc%d" % i)) for i in range(8)]
            self.dq["cc"] = dict(sems=sems, cnt=[0] * 8, nxt=0)
            for i, s_ in enumerate(sems):
                self.sem[("cc", i)] = s_
        d = self.dq["cc"]
        i = d["nxt"]; d["nxt"] = (i + 1) % 8
        key = ("cc", i)
        self._wait("pool", key, d["cnt"][i])
        self._deps("pool", [src], [dst])
        ins = self.nc.gpsimd.collective_compute("AllGather", mybir.AluOpType.bypass, replica_groups=groups,
                                                ins=[src.ap.opt()], outs=[dst.ap.opt()])
        d["cnt"][i] += 1
        ins.then_inc(self.sem[key])
        self._mark((key, d["cnt"][i]), [src], [dst])
        self.n_instr += 1
        return (key, d["cnt"][i])

    def ps(self, shape, dt=F32, name=None):
        self.n_tiles += 1
        name = name or "p%d" % self.n_tiles
        t = self.es.enter_context(self.nc.psum_tensor("ps_" + name, list(shape), dt))
        tt_ = T(self, t, name)
        tt_.tk.excl = True
        return tt_

    def close(self):
        self.es.close()

    def _wait(self, e, key, val):
        if val is None:
            return
        k = (e, key)
        if self.waited.get(k, 0) >= val:
            return
        self.waited[k] = val
        self.eng[e].wait_ge(self.sem[key], val)
        self.n_wait += 1

    def _deps(self, e, reads, writes, pe_acc=False):
        for v in list(reads) + list(writes):
            if v.tk.excl:
                for e2, n2 in v.tk.acc.items():
                    if e2 == e and e == "pe":
                        continue
                    self._wait(e, e2, n2)
        reads = [v for v in reads if not v.tk.excl]
        writes = [v for v in writes if not v.tk.excl]
        for v in reads:
            w = v.tk.w
            if w is not None:
                self._wait(e, w[0], w[1])
        for v in writes:
            tk = v.tk
            if tk.w is not None:
                if not (pe_acc and tk.w[0] == "pe" and e == "pe"):
                    self._wait(e, tk.w[0], tk.w[1])
            for re_, rn in tk.r.items():
                if re_ == e and e == "pe":
                    continue
                self._wait(e, re_, rn)

    def _mark(self, ticket, reads, writes):
        for v in list(reads) + list(writes):
            if v.tk.excl:
                v.tk.acc[ticket[0]] = ticket[1]
        reads = [v for v in reads if not v.tk.excl]
        writes = [v for v in writes if not v.tk.excl]
        for v in reads:
            v.tk.r[ticket[0]] = ticket[1]
        for v in writes:
            v.tk.w = ticket
            v.tk.r = {}

    def op(self, e, fn, reads, writes, pe_acc=False):
        reads = [v for v in reads if isinstance(v, V)]
        self._deps(e, reads, writes, pe_acc)
        ins = fn()
        self.cnt[e] += 1
        ins.then_inc(self.sem[e], 1)
        self._mark((e, self.cnt[e]), reads, writes)
        self.n_instr += 1
        if self.yield_hook is not None:
            self.yield_hook()
        return ins

    def dma(self, q, out, in_, **kw):
        d = self.dq[q]
        i = d["nxt"]
        d["nxt"] = (i + 1) % len(d["sems"])
        key = (q, i)
        self._wait(q, key, d["cnt"][i])
        reads = [in_] if isinstance(in_, V) else []
        writes = [out] if isinstance(out, V) else []
        self._deps(q, reads, writes)
        oa = out.ap if isinstance(out, V) else out
        ia = in_.ap if isinstance(in_, V) else in_
        ins = self.eng[q].dma_start(out=oa, in_=ia, **kw)
        d["cnt"][i] += 16
        ins.then_inc(self.sem[key], 16)
        self._mark((key, d["cnt"][i]), reads, writes)
        self.n_instr += 1
        if self.yield_hook is not None:
            self.yield_hook()
        return (key, d["cnt"][i])

    def wait_ticket(self, e, ticket):
        self._wait(e, ticket[0], ticket[1])

    def mm(self, out, lhsT, rhs, start=True, stop=True, **kw):
        return self.op("pe", lambda: self.nc.tensor.matmul(out.ap, lhsT.ap, rhs.ap, start=start, stop=stop, **kw),
                       [lhsT, rhs], [out], pe_acc=not start)

    def tr(self, out, in_, ident):
        return self.op("pe", lambda: self.nc.tensor.transpose(out.ap, in_.ap, ident.ap), [in_, ident], [out])

    def act(self, out, in_, func, bias=None, scale=None, accum_out=None, e="act"):
        kw = {}
        rd = [in_]
        if bias is not None:
            kw["bias"] = bias.ap if isinstance(bias, V) else bias
            rd.append(bias)
        if scale is not None:
            kw["scale"] = scale.ap if isinstance(scale, V) else scale
            rd.append(scale)
        wr = [out]
        if accum_out is not None:
            kw["accum_out"] = accum_out.ap
            wr.append(accum_out)
        return self.op("act", lambda: self.nc.scalar.activation(out.ap, in_.ap, func, **kw), rd, wr)

    def _ve(self, e):
        return {"dve": self.nc.vector, "pool": self.nc.gpsimd, "act": self.nc.scalar}[e]

    def tt(self, out, a, b, op, e="dve"):
        return self.op(e, lambda: self._ve(e).tensor_tensor(out.ap, a.ap, b.ap, op), [a, b], [out])

    def ts(self, out, a, s1, op0, s2=None, op1=None, e="dve", accum_out=None):
        rd = [a, s1, s2]
        a1 = s1.ap if isinstance(s1, V) else s1
        a2 = s2.ap if isinstance(s2, V) else s2
        kw = {}
        wr = [out]
        if op1 is not None:
            kw["op1"] = op1
        if accum_out is not None:
            kw["accum_out"] = accum_out.ap
            wr.append(accum_out)
        return self.op(e, lambda: self._ve(e).tensor_scalar(out.ap, a.ap, a1, a2, op0, **kw), rd, wr)

    def stt(self, out, a, s, b, op0, op1, e="dve"):
        sa = s.ap if isinstance(s, V) else s
        return self.op(e, lambda: self._ve(e).scalar_tensor_tensor(out.ap, a.ap, sa, b.ap, op0, op1), [a, s, b], [out])

    def cp(self, out, in_, e="dve"):
        if e == "act":
            return self.op("act", lambda: self.nc.scalar.copy(out.ap, in_.ap), [in_], [out])
        return self.op(e, lambda: self._ve(e).tensor_copy(out.ap, in_.ap), [in_], [out])

    def memset(self, out, val, e="pool"):
        return self.op(e, lambda: self._ve(e).memset(out.ap, val), [], [out])

    def red(self, out, in_, op, axis=AX.X, e="dve"):
        return self.op(e, lambda: self._ve(e).tensor_reduce(out.ap, in_.ap, axis, op), [in_], [out])

    def recip(self, out, in_):
        return self.op("dve", lambda: self.nc.vector.reciprocal(out.ap, in_.ap), [in_], [out])

    def finish(self, tickets):
        for t in tickets:
            self._wait("sp", t[0], t[1])


A_DEC = 0.6065306597126334
ALPHA = (2 * 4) ** 0.25
NOWN = 18
NKT = 66
NNT = 22
GROUPS = [[0, 1, 2, 3], [4, 5, 6, 7]]
XROWS = 8960


def na_chunks(i):
    if i == 0:
        return [(c, 128) for c in range(0, 6)]
    if i == 1:
        return [(c, 128) for c in range(1, 6)]
    if i == 15:
        return [(c, 128) for c in range(14, 19)] + [(19, 64)]
    return [(c, 128) for c in range(i, i + 4)] + [(i + 4, 64)]


def na_class(i):
    return {0: 0, 1: 1, 14: 3, 15: 4}.get(i, 2)


def lat_row(tau):
    rho, rem = divmod(tau, 2048)
    k, i = divmod(rem, 256)
    return 512 + 1024 * k + 256 * rho + i


class _A:
    def __init__(self, ap, tk):
        self.t = ap; self.tk = tk

    def __getitem__(self, idx):
        return V(self.t[idx], self.tk)


def build_F(depth=4):
    nc = bass.Bass("TRN2", target_bir_lowering=False)
    dt = nc.dram_tensor
    I = lambda n, sh: dt(n, sh, F32, kind="ExternalInput").ap()
    xa0 = I("xa0", [XROWS, 1024]); xs0 = I("xs0", [2560, 1024])
    cv = I("cv", [128, 16]); wmod = I("wmod", [depth, 1024, 3072]); bmod = I("bmod", [depth, 128, 3072])
    wb = I("wb", [depth, 1024, 2432]); wout = I("wout", [depth, 1024, 1024])
    wa = I("wa", [depth, 1024, 640]); cw = I("cw", [depth, 64, 33]); pvi = I("pv", [depth, 64, 15])
    w2 = I("w2", [depth, 32, 192]); a2 = I("a2", [depth, 32, 192])
    cstA = I("cstA", [64, 1664])
    cosK = I("cosK", [8192, 128]); sinK = I("sinK", [8192, 128])
    cosQ = I("cosQ", [2048, 384]); sinQ = I("sinQ", [2048, 384])
    dsel_in = I("dsel", [128, 2])
    gk = I("gk", [depth, 128, 128]); gq = I("gq", [depth, 128, 384])
    nab = I("nab", [depth, 5, 4, 128, 768])
    gng = I("gng", [depth, 128, 384]); gnb = I("gnb", [depth, 128, 384])
    lng = I("lng", [depth, 128, 1024]); lnb = I("lnb", [depth, 128, 1024])
    ident_in = I("ident", [128, 128]); jmat_in = I("jmat", [128, 128]); jd_in = I("jd", [128, 128])
    out = dt("out", [2048, 1024], F32, kind="ExternalOutput").ap()

    s = S(nc)
    tickets = []
    pb = [s.ps([128, 512], name="pb%d" % i) for i in range(8)]
    XA = [s.dram([XROWS, 1024], "XA%d" % i) for i in range(2)]
    YB = s.dram([8448, 384], "YB")
    YGc = s.dram([4 * 256, 384], "YGc")
    YGl = s.dram([16 * 4 * 512, 384], "YGl")
    XN = s.dram([2048, 1024], "XN")
    XS = s.dram([2560, 1024], "XS")
    YF = [s.dram([2048, 384], "YF%d" % i) for i in range(2)]
    YBk = [s.dram([2048, 384], "YBk%d" % i) for i in range(2)]
    xa0_T = _A(xa0, Tk("xa0")); xs0_T = _A(xs0, Tk("xs0"))

    _rd = {}

    def RR(q):
        if q not in _rd:
            pid = s.eng[q].partition_id()
            _rd[q] = pid % 4
        return _rd[q]

    dynq = ["sp", "act", "pool"]
    dync = [0]

    def dyndma(dst_v, src_fn):
        q = dynq[dync[0] % 3]; dync[0] += 1
        return s.dma(q, dst_v, src_fn(RR(q)))

    ident = s.sb([128, 128], name="ident"); s.dma("sp", ident[:], ident_in)
    jmat = s.sb([128, 128], name="jmat"); s.dma("act", jmat[:], jmat_in)
    jd = s.sb([128, 128], name="jd"); s.dma("sp", jd[:], jd_in)
    dsel = s.sb([128, 2], name="dsel"); s.dma("act", dsel[:], dsel_in)
    identb = s.sb([128, 128], BF16, name="identb"); s.cp(identb[:], ident[:])
    jdb = s.sb([128, 128], BF16, name="jdb"); s.cp(jdb[:], jd[:])
    ones = s.sb([128, 128], name="ones"); s.memset(ones[:], 1.0)
    cv_t = s.sb([128, 16], name="cv"); s.dma("sp", cv_t[:], cv)
    scv = s.sb([128, 16], name="scv"); s.act(scv[:], cv_t[:], AF.Silu)
    mod = [s.sb([128, 3072], name="mod%d" % j) for j in range(2)]
    for l in range(depth):
        Xc = xa0_T if l == 0 else XA[(l - 1) % 2]
        Xn = XA[l % 2]
        last = (l == depth - 1)

        if l == 0:
            XSc = xs0_T
        else:
            XSc = XS
            lat = Xc.t[512:8704, :]
            dyndma(V(XS.t[256:2304, :].rearrange("(o k i) c -> o k (i c)", o=1, k=8), XS.tk),
                   lambda r: V(lat.rearrange("(k rr i) c -> rr k (i c)", rr=4, i=256)[bass.ds(r, 1)], Xc.tk))
            units = lat.rearrange("(u i) c -> u (i c)", i=256)
            dyndma(V(XS.t[0:256, :].rearrange("(o i) c -> o (i c)", o=1), XS.tk),
                   lambda r: V(units[bass.ds(r + 27, 1), :], Xc.tk))
            dyndma(V(XS.t[2304:2560, :].rearrange("(o i) c -> o (i c)", o=1), XS.tk),
                   lambda r: V(units[bass.ds(r + 1, 1), :], Xc.tk))
        with s.phase():
            stage_ws = [s.sb([128, 8, 512], name="stage_w%d" % i) for i in range(2)]
            Rl = s.sb([128, 16, 128], name="Rl")
            bmod_ts = [s.sb([128, 512], name="bmodt%d" % i) for i in range(2)]
            for i in range(16):
                s.ts(Rl[:, i, :], ones[:], scv[:, i:i + 1], ALU.mult, e="dve" if i % 2 == 0 else "pool")
            for cb in range(6):
                stage_w = stage_ws[cb % 2]; bmod_t = bmod_ts[cb % 2]
                s.dma("sp", stage_w[:], wmod[l].rearrange("(k p) c -> p k c", p=128)[:, :, cb * 512:(cb + 1) * 512])
                s.dma("act", bmod_t[:], bmod[l][:, cb * 512:(cb + 1) * 512])
                for j in range(2):
                    ps = pb[(2 * cb + j) % 4]
                    for k in range(8):
                        s.mm(ps[:, :], Rl[:, 2 * k + j, :], stage_w[:, k, :], start=(k == 0), stop=(k == 7))
                    s.tt(mod[j][:, cb * 512:(cb + 1) * 512], ps[:, :], bmod_t[:], ALU.add, e="dve")
            for j in range(2):
                s.ts(mod[j][:, 1024:2048], mod[j][:, 1024:2048], 1.0, ALU.add, e="pool")

        with s.phase():
            cst_t = s.sb([64, 1664], name="cst"); s.dma("sp", cst_t[:], cstA)
            identA = cst_t[:, 0:64]
            mask3 = lambda h: cst_t[:, 64 + h * 320: 64 + (h + 1) * 320]
            rmask = cst_t[:, 1024:1280]
            idt3 = cst_t[:, 1280:1664]
            cw_t = s.sb([64, 33], name="cw"); s.dma("act", cw_t[:], cw[l])
            pv_t = s.sb([64, 16], name="pv"); s.dma("act", pv_t[:, 0:15], pvi[l])
            omk = s.sb([64, 3], name="omk")
            for h in range(3):
                s.ts(omk[:, h:h + 1], pv_t[:, h * 5 + 3:h * 5 + 4], -1.0, ALU.mult, 1.0, ALU.add)
            w2_t = s.sb([32, 192], name="w2"); s.dma("act", w2_t[:], w2[l])
            a2_t = s.sb([32, 192], name="a2"); s.dma("act", a2_t[:], a2[l])
            xt = [s.sb([128, 1024], name="xt%d" % i) for i in range(2)]
            ht = s.sb([128, 1024], name="ht")
            xr = s.sb([128, 1024], name="xr")
            hbA = s.sb([128, 1024], BF16, name="hbA")
            wab = s.sb([128, 8, 640], BF16, name="wab")
            for k in range(8):
                st_ = xt[k % 2]
                s.dma("sp" if k % 2 == 0 else "act", st_[:, 0:640], wa[l][k * 128:(k + 1) * 128, :])
                s.cp(wab[:, k, :], st_[:, 0:640], e="dve" if k % 2 == 0 else "pool")
            cts = [(i * 64, 64) for i in range(9)] + [(576, 32), (608, 32)]
            hg = [s.sb([128, 8, 258], BF16, name="hg%d" % i) for i in range(2)]
            for h_ in hg:
                s.memset(h_[:], 0.0)
            raw = [s.sb([64, 258], name="raw%d" % i) for i in range(2)]
            ctmp = [s.sb([64, 256], name="ctmp%d" % i) for i in range(2)]
            mk = lambda nm, shape=(64, 256), dt_=F32: [s.sb(list(shape), dt_, name="%s%d" % (nm, h)) for h in range(3)]
            uR, uK, uV = mk("uR"), mk("uK"), mk("uV", dt_=BF16)
            uD = s.sb([32, 256], name="uD"); uA = s.sb([32, 256], name="uA"); ddt = s.sb([32, 256], name="ddt")
            sg, ic, kk, tmp, kd, bd = mk("sg"), mk("ic"), mk("kk"), mk("tmp"), mk("kd"), mk("bd")
            cs, csx, csr = mk("cs"), mk("csx"), mk("csr")
            E1, E3 = mk("E1"), mk("E3")
            E4 = E1
            RH, KKH, kt, bt, kc, bc, rk = (mk("RH", dt_=BF16), mk("KKH", dt_=BF16), mk("kt", dt_=BF16), mk("bt", dt_=BF16),
                                           mk("kc", dt_=BF16), mk("bc", dt_=BF16), mk("rk", dt_=BF16))
            identAb = s.sb([64, 64], BF16, name="identAb"); s.cp(identAb[:], identA)
            onesb = s.sb([64, 1], BF16, name="onesb"); s.memset(onesb[:], 1.0)
            wc = mk("wc", (64, 4)); rn = tmp
            trT = [s.sb([64, 3, 256], BF16, name="trT%d" % i) for i in range(4)]
            scS = [s.sb([64, 3, 320], BF16, name="scS%d" % i) for i in range(4)]
            XYs = [[s.sb([64, 3, 128], BF16, name="XY%d_%d" % (c, i)) for i in range(2)] for c in range(4)]
            PQs = [[s.sb([64, 3, 128], BF16, name="PQ%d_%d" % (c, i)) for i in range(2)] for c in range(4)]
            KKpTs = [s.sb([64, 192], BF16, name="KKpT%d" % c) for c in range(4)]
            AVs = [s.sb([64, 192], BF16, name="AV%d" % c) for c in range(4)]
            Ulocs = [s.sb([64, 192], name="Uloc%d" % c) for c in range(4)]
            Us = [s.sb([64, 192], BF16, name="U%d" % c) for c in range(4)]
            STb = [s.sb([64, 192], BF16, name="STb%d" % i) for i in range(2)]
            bss = [s.sb([64, 4], name="bs%d" % c) for c in range(4)]
            il = Interleaver(s)
            ST = [s.sb([64, 192], name="ST%d" % i) for i in range(2)]
            YBuf = [s.sb([64, 4, 384], name="YBuf0")] * 2
            bs_ = s.sb([64, 4], name="bs")
            s.memset(ST[0][:], 0.0)
            s.memset(STb[0][:], 0.0)
            sti = 0
            acnt = [0]

            def frontA(g):
                hgt = hg[g % 2]
                for a in range(2):
                    u = 2 * g + a
                    i = acnt[0]; acnt[0] += 1
                    if u < 2:
                        bf_, br_ = 128 * u, 128 * (1 - u)
                        j = 1
                    else:
                        v = u - 2
                        bf_, br_ = lat_row(128 * v), lat_row(128 * (63 - v))
                        j = 0
                    x_ = xt[i % 2]
                    s.dma("sp", x_[:], V(Xc.t[bf_:bf_ + 128, :], Xc.tk))
                    s.dma("act", xr[:], V(Xc.t[br_:br_ + 128, :], Xc.tk))
                    s.act(x_[:], x_[:], AF.Identity, scale=dsel[:, 0:1])
                    s.stt(x_[:], xr[:], dsel[:, 1:2], x_[:], ALU.mult, ALU.add)
                    s.tt(ht[:], x_[:], mod[j][:, 1024:2048], ALU.mult, e="pool")
                    s.tt(hbA[:], ht[:], mod[j][:, 0:1024], ALU.add, e="dve")
                    for half in range(2):
                        p = pb[half]
                        with s.atomic():
                            pbf = p.t[:, 0:256].bitcast(BF16)
                            for k in range(4):
                                kk_ = half * 4 + k
                                s.tr(V(pbf[:, k * 128:(k + 1) * 128], p.tk), hbA[:, kk_ * 128:(kk_ + 1) * 128], jdb[:])
                            s.cp(hgt[:, half * 4:half * 4 + 4, 1 + 128 * a:1 + 128 * (a + 1)],
                                 V(pbf.rearrange("p (k t) -> p k t", k=4), p.tk), e="act" if half == 0 else "dve")

            NGRP = 33
            OPN = ('RH', 'KKH', 'kt', 'bt', 'kc', 'bc', 'rk', 'uV')
            opsets = [dict(RH=RH, KKH=KKH, kt=kt, bt=bt, kc=kc, bc=bc, rk=rk, uV=uV, wc=wc),
                      dict(RH=mk('RHb', dt_=BF16), KKH=mk('KKHb', dt_=BF16), kt=mk('ktb', dt_=BF16), bt=mk('btb', dt_=BF16),
                           kc=mk('kcb', dt_=BF16), bc=mk('bcb', dt_=BF16), rk=mk('rkb', dt_=BF16), uV=mk('uVb', dt_=BF16),
                           wc=mk('wcb', (64, 4)))]

            def pro1(g):
                O = opsets[g % 2]; uV = O['uV']
                first = g in (0, 1)
                lastg = g in (0, NGRP - 1)
                hgt = hg[g % 2]
                if g + 1 < NGRP:
                    frontA(g + 1)
                    hn_ = hg[(g + 1) % 2]
                    s.cp(hgt[:, :, 257:258], hn_[:, :, 1:2], e="pool")
                    s.cp(hn_[:, :, 0:1], hgt[:, :, 256:257], e="pool")
                for ci, (c0, M) in enumerate(cts):
                    pr = pb[ci % 2]
                    rw = raw[ci % 2]
                    with s.atomic():
                        for k in range(8):
                            s.mm(pr[0:M, 0:258], wab[:, k, c0:c0 + M], hgt[:, k, :], start=(k == 0), stop=(k == 7))
                        s.cp(rw[0:M, :], pr[0:M, 0:258], e="act" if ci % 2 == 0 else "dve")
                    if first:
                        s.memset(rw[0:M, 0:1], 0.0, e="pool")
                    if lastg:
                        s.memset(rw[0:M, 257:258], 0.0, e="pool")
                    dst = (uR, uK, uV)[ci // 3][ci % 3] if ci < 9 else (uD, uA)[ci - 9]
                    tm = ctmp[ci % 2]
                    s.act(tm[0:M, :], rw[0:M, 0:256], AF.Identity, scale=cw_t[0:M, ci * 3:ci * 3 + 1])
                    s.stt(tm[0:M, :], rw[0:M, 1:257], cw_t[0:M, ci * 3 + 1:ci * 3 + 2], tm[0:M, :], ALU.mult, ALU.add)
                    s.stt(dst[0:M, :], rw[0:M, 2:258], cw_t[0:M, ci * 3 + 2:ci * 3 + 3], tm[0:M, :], ALU.mult, ALU.add)
                s.act(ddt[:], uD[:], AF.Tanh)

            def prep_head(g, h):
                O = opsets[g % 2]
                RH, KKH, kt, bt, kc, bc, rk, wc = O['RH'], O['KKH'], O['kt'], O['bt'], O['kc'], O['bc'], O['rk'], O['wc']
                P = lambda i: pv_t[:, h * 5 + i:h * 5 + i + 1]
                pz = pb[2 + h]
                with s.atomic():
                    s.mm(pz[0:64, 0:256], w2_t[:, h * 64:(h + 1) * 64], ddt[:])
                    s.act(sg[h][:], pz[0:64, 0:256], AF.Sigmoid, bias=P(0))
                with s.atomic():
                    s.mm(pz[0:64, 256:512], a2_t[:, h * 64:(h + 1) * 64], uA[:])
                    s.act(ic[h][:], pz[0:64, 256:512], AF.Sigmoid, bias=P(1))
                s.act(kk[h][:], uK[h][:], AF.Identity, scale=P(2))
                s.act(tmp[h][:], kk[h][:], AF.Square)
                pss = pb[2 + h]
                with s.atomic():
                    s.mm(pss[0:64, 0:256], ones[0:64, 0:64], tmp[h][:])
                    s.ts(rn[h][:], pss[0:64, 0:256], 1e-12, ALU.max)
                s.act(rn[h][:], rn[h][:], AF.Sqrt)
                s.recip(rn[h][:], rn[h][:])
                s.tt(kk[h][:], kk[h][:], rn[h][:], ALU.mult)
                s.act(tmp[h][:], ic[h][:], AF.Identity, scale=P(3), bias=omk[:, h:h + 1])
                s.tt(kd[h][:], uK[h][:], tmp[h][:], ALU.mult, e="dve")
                s.tt(bd[h][:], kk[h][:], ic[h][:], ALU.mult, e="pool")
                s.op("dve", lambda h=h: nc.vector.tensor_tensor_scan(cs[h][:].ap, rmask.ap, sg[h][:].ap, 0.0, ALU.mult, ALU.add),
                     [rmask, sg[h][:]], [cs[h][:]])
                s.tt(csx[h][:], cs[h][:], sg[h][:], ALU.subtract)
                for c in range(4):
                    s.ts(csr[h][:, c * 64:(c + 1) * 64], cs[h][:, c * 64:(c + 1) * 64],
                         cs[h][:, c * 64 + 63:c * 64 + 64], ALU.subtract)
                s.act(wc[h][:], cs[h][:, 63::64], AF.Exp, scale=-A_DEC)
                s.act(E1[h][:], cs[h][:], AF.Exp, scale=-A_DEC)
                s.tt(RH[h][:], uR[h][:], E1[h][:], ALU.mult, e="dve")
                s.act(E1[h][:], csx[h][:], AF.Exp, scale=-A_DEC)
                s.tt(KKH[h][:], kk[h][:], E1[h][:], ALU.mult, e="pool")
                s.act(E3[h][:], cs[h][:], AF.Exp, scale=A_DEC)
                s.tt(kt[h][:], kd[h][:], E3[h][:], ALU.mult)
                s.tt(bt[h][:], bd[h][:], E3[h][:], ALU.mult, e="pool")
                s.act(E4[h][:], csr[h][:], AF.Exp, scale=A_DEC)
                s.tt(kc[h][:], kd[h][:], E4[h][:], ALU.mult)
                s.tt(bc[h][:], bd[h][:], E4[h][:], ALU.mult, e="pool")
                s.stt(rk[h][:], uR[h][:], P(4), kd[h][:], ALU.mult, ALU.mult)

            def chunk_body(g, c):
                n = 4 * g + c
                O = opsets[g % 2]
                RH, KKH, kt, bt, kc, bc, rk, uV = O['RH'], O['KKH'], O['kt'], O['bt'], O['kc'], O['bc'], O['rk'], O['uV']
                cc = slice(c * 64, (c + 1) * 64)
                tT = trT[c]; sS = scS[c]
                XY = XYs[c]; PQ = PQs[c]; KKpT = KKpTs[c]; AV = AVs[c]; Uloc = Ulocs[c]; U = Us[c]; bs_ = bss[c]
                p_ = c % 2
                for h in range(3):
                    ptr = pb[2 + p_]
                    with s.atomic():
                        ptrb = ptr.t[0:64, 0:128].bitcast(BF16)
                        for i, src_ in enumerate((KKH, bc, kc, uV)):
                            s.tr(V(ptrb[:, i * 64:(i + 1) * 64], ptr.tk), src_[h][:, cc], identAb[:])
                        s.cp(tT[:, h, :], V(ptrb[:, 0:256], ptr.tk), e="act")
                    psc = pb[4 + p_]
                    with s.atomic():
                        s.mm(psc[0:64, 0:64], kt[h][:, cc], RH[h][:, cc])
                        s.mm(psc[0:64, 64:128], kt[h][:, cc], KKH[h][:, cc])
                        s.mm(psc[0:64, 128:192], bt[h][:, cc], RH[h][:, cc])
                        s.mm(psc[0:64, 192:256], bt[h][:, cc], KKH[h][:, cc])
                        s.mm(psc[0:64, 256:320], KKH[h][:, cc], bt[h][:, cc])
                        s.tt(sS[:, h, :], psc[0:64, 0:320], mask3(h), ALU.mult)
                s.tt(PQ[0][:, :, :], sS[:, :, 192:320], V(idt3.ap.rearrange("p (h c) -> p h c", c=128), idt3.tk), ALU.add, e="pool")
                Xc_ = lambda lvl, h: (sS[:, h, 192:256] if lvl == 0 else XY[lvl % 2][:, h, 0:64])
                Yc_ = lambda lvl, h: (sS[:, h, 256:320] if lvl == 0 else XY[lvl % 2][:, h, 64:128])
                for lvl in range(5):
                    pn, pq = pb[4 + p_], pb[6 + p_]
                    nxt = XY[(lvl + 1) % 2]
                    with s.atomic():
                        for h in range(3):
                            s.mm(pn[0:64, h * 128:h * 128 + 64], Yc_(lvl, h), Xc_(lvl, h))
                            if lvl < 4:
                                s.mm(pn[0:64, h * 128 + 64:h * 128 + 128], Xc_(lvl, h), Yc_(lvl, h))
                        pn3 = pn.t[0:64, 0:384].rearrange("p (h c) -> p h c", c=128)
                        if lvl < 4:
                            s.cp(nxt[:, :, :], V(pn3, pn.tk), e="act")
                        else:
                            s.cp(nxt[:, :, 0:64], V(pn3[:, :, 0:64], pn.tk), e="act")
                    Pc, Pn = PQ[lvl % 2], PQ[(lvl + 1) % 2]
                    with s.atomic():
                        for h in range(3):
                            s.mm(pq[0:64, h * 128:h * 128 + 64], Pc[:, h, 64:128], nxt[:, h, 0:64])
                            if lvl < 4:
                                s.mm(pq[0:64, h * 128 + 64:h * 128 + 128], Pc[:, h, 0:64], nxt[:, h, 64:128])
                        pq3 = pq.t[0:64, 0:384].rearrange("p (h c) -> p h c", c=128)
                        if lvl < 4:
                            s.tt(Pn[:, :, :], V(pq3, pq.tk), Pc[:, :, :], ALU.add)
                        else:
                            s.tt(Pn[:, :, 0:64], V(pq3[:, :, 0:64], pq.tk), Pc[:, :, 0:64], ALU.add)
                TT = PQ[1]
                pk = pb[2 + p_]
                with s.atomic():
                    for h in range(3):
                        s.mm(pk[0:64, h * 64:(h + 1) * 64], tT[:, h, 0:64], TT[:, h, 0:64])
                        s.mm(pk[0:64, 192 + h * 64:192 + (h + 1) * 64], sS[:, h, 64:128], tT[:, h, 192:256])
                    s.cp(KKpT[:], pk[0:64, 0:192], e="act")
                    s.cp(AV[:], pk[0:64, 192:384], e="dve")
                pk3 = pb[6 + p_]
                with s.atomic():
                    for h in range(3):
                        s.mm(pk3[0:64, h * 64:(h + 1) * 64], TT[:, h, 0:64], AV[:, h * 64:(h + 1) * 64])
                    s.cp(Uloc[:], pk3[0:64, 0:192], e="act")

            def chunk_seq(g, c):
                n = 4 * g + c
                yb = YBuf[0]
                O = opsets[g % 2]
                RH, rk, wc = O['RH'], O['rk'], O['wc']
                cc = slice(c * 64, (c + 1) * 64)
                tT = trT[c]; sS = scS[c]
                KKpT = KKpTs[c]; Uloc = Ulocs[c]; U = Us[c]; bs_ = bss[c]
                Sc, Sn = ST[n % 2], ST[(n + 1) % 2]
                Scb, Snb = STb[n % 2], STb[(n + 1) % 2]
                pu = pb[0]
                with s.atomic():
                    for h in range(3):
                        s.mm(pu[0:64, h * 64:(h + 1) * 64], KKpT[:, h * 64:(h + 1) * 64], Scb[:, h * 64:(h + 1) * 64])
                    s.stt(U[:], pu[0:64, 0:192], -1.0, Uloc[:], ALU.mult, ALU.subtract)
                pS = pb[1]
                with s.atomic():
                    for h in range(3):
                        hs = slice(h * 64, (h + 1) * 64)
                        s.mm(pS[0:64, hs], tT[:, h, 128:192], tT[:, h, 192:256], start=True, stop=False)
                        s.mm(pS[0:64, hs], tT[:, h, 64:128], U[:, hs], start=False, stop=True)
                    for h in range(3):
                        hs = slice(h * 64, (h + 1) * 64)
                        s.stt(Sn[:, hs], Sc[:, hs], wc[h][:, c:c + 1], pS[0:64, hs], ALU.mult, ALU.add)
                    s.cp(Snb[:], Sn[:], e="act")
                py = pb[0]
                with s.atomic():
                    for h in range(3):
                        hs = slice(256 + h * 64, 256 + (h + 1) * 64)
                        hh = slice(h * 64, (h + 1) * 64)
                        s.mm(py[0:64, hs], RH[h][:, cc], Scb[:, hh], start=True, stop=False)
                        s.mm(py[0:64, hs], sS[:, h, 128:192], U[:, hh], start=False, stop=False)
                        s.mm(py[0:64, hs], sS[:, h, 0:64], tT[:, h, 192:256], start=False, stop=True)
                    s.cp(yb[:, c, 0:192], py[0:64, 256:448], e="act")
                pbn = pb[1]
                with s.atomic():
                    for h in range(3):
                        s.mm(pbn[0:64, 256 + h:256 + h + 1], rk[h][:, cc], onesb[:, 0:1])
                    s.cp(bs_[:, 0:3], pbn[0:64, 256:259], e="dve")
                for h in range(3):
                    s.ts(yb[:, c, 192 + h * 64:192 + (h + 1) * 64], tT[:, h, 192:256], bs_[:, h:h + 1], ALU.mult, e="pool")
            def seq_group(g):
                tbase = 256 * g
                for c in range(4):
                    chunk_seq(g, c)
                s.dma("sp" if g % 2 == 0 else "act",
                      V(YB.t[tbase:tbase + 256, :].rearrange("(c t) f -> t c f", t=64), YB.tk), YBuf[0][:, :, :])
                if g == 0:
                    s.allgather(YB[0:256, :], YGc[:, :], GROUPS)
                elif g % 2 == 0:
                    m = g // 2 - 1
                    s.allgather(YB[256 + 512 * m:256 + 512 * (m + 1), :], YGl[2048 * m:2048 * (m + 1), :], GROUPS)

            frontA(0)
            pro1(0)
            il.run([(lambda h=h: prep_head(0, h)) for h in range(3)], 3)
            for g in range(NGRP):
                wk1 = [(lambda c=c: chunk_body(g, c)) for c in range(4)]
                if g + 1 < NGRP:
                    wk1.append(lambda: pro1(g + 1))
                il.run(wk1, 5)
                wk2 = [lambda: seq_group(g)]
                if g + 1 < NGRP:
                    wk2 += [(lambda h=h: prep_head(g + 1, h)) for h in range(3)]
                il.run(wk2, 4)
        ygv = YGl.t.rearrange("(m sr i) c -> sr m (i c)", sr=4, i=512)
        for sr in range(2):
            dyndma(V(YF[sr].t.rearrange("(m i) c -> m (i c)", i=512), YF[sr].tk),
                   lambda r, sr=sr: V(ygv[sr][bass.ds(r * 4, 4), :], YGl.tk))
            dyndma(V(YBk[sr].t.rearrange("(m i) c -> m (i c)", i=512), YBk[sr].tk),
                   lambda r, sr=sr: V(ygv[2 + sr][bass.ds((3 - r) * 4, 4), :], YGl.tk))
        with s.phase():
            GK = s.sb([128, 128], name="GK"); s.dma("act", GK[:], gk[l])
            GQ = s.sb([128, 384], name="GQ"); s.dma("act", GQ[:], gq[l])
            GNG = s.sb([128, 384], name="GNG"); s.dma("act", GNG[:], gng[l])
            GNB = s.sb([128, 384], name="GNB"); s.dma("act", GNB[:], gnb[l])
            YAN = s.sb([128, 18, 640], BF16, name="YAN")
            wbuf = s.sb([128, 8, 1024], BF16, name="wbuf")
            woutb = s.sb([128, 8, 1024], BF16, name="woutb")
            xt = [s.sb([128, 1024], name="xt%d" % i) for i in range(2)]
            hts = [s.sb([128, 1024], name="ht%d" % i) for i in range(2)]
            ht = hts[0]
            pe = [s.sb([128, 1024], name="pe%d" % i) for i in range(2)]
            sqs = [pe[1], None]
            wst = xt
            hbs = []
            ilB = Interleaver(s)
            pjB = _A(YAN.t[:, 0:2, :].rearrange("p a c -> p (a c)").bitcast(F32), YAN.tk)
            sqs[1] = _A(YAN.t[:, 2:4, :].rearrange("p a c -> p (a c)").bitcast(F32), YAN.tk)

            def load_w(dst, src, c0, ncols, dcol=0):
                for k in range(8):
                    st_ = wst[k % 2]
                    s.dma("sp" if k % 2 == 0 else "act", st_[:, 0:ncols], src[k * 128:(k + 1) * 128, c0:c0 + ncols])
                    s.cp(dst[:, k, dcol:dcol + ncols], st_[:, 0:ncols], e="dve" if k % 2 == 0 else "pool")

            load_w(woutb, wout[l], 0, 1024)
            kT = s.sb([128, 8448], BF16, name="kT")
            Vg = s.sb([128, NKT, 2, 65], BF16, name="Vg")
            qT = [s.sb([128, 2304], BF16, name="qT%d" % i) for i in range(3)]
            nqT = [s.sb([128, 2304], BF16, name="nqT%d" % i) for i in range(2)]
            nkT = [s.sb([128, 2816], BF16, name="nkT%d" % i) for i in range(2)]
            Vn = s.sb([128, NNT, 4, 65], BF16, name="Vn")
            s.memset(Vg[:, :, :, 64:65], 1.0)
            s.memset(Vn[:, :, :, 64:65], 1.0)
            hT = [s.sb([128, 8, 128], BF16, name="hT%d" % i) for i in range(2)]
            tcs = [s.sb([128, 384], name="tcos%d" % i) for i in range(2)]
            tsns = [s.sb([128, 384], name="tsin%d" % i) for i in range(2)]
            sms = [s.sb([128, 64], name="sm%d" % i) for i in range(2)]
            tc_, tsn, sm = tcs[0], tsns[0], sms[0]
            cnt = [0]

            def front(srcT, row, j, w=None):
                if w is None:
                    i = cnt[0]; cnt[0] += 1
                    w = i % 2
                    banks = (pb[0], pb[1])
                else:
                    banks = (pb[w], pb[w])
                q = "sp" if w == 0 else "act"
                x_ = xt[w]
                ht_ = hts[w]
                s.dma(q, x_[:], V(srcT.t[row:row + 128, :], srcT.tk))
                hb_ = hbs[w]
                s.tt(ht_[:], x_[:], mod[j][:, 1024:2048], ALU.mult, e="pool")
                s.tt(hb_[:], ht_[:], mod[j][:, 0:1024], ALU.add, e="dve")
                h_ = hT[w]
                for half in range(2):
                    p = banks[half]
                    with s.atomic():
                        pbf = p.t[:, 0:256].bitcast(BF16)
                        for k in range(4):
                            kk_ = half * 4 + k
                            s.tr(V(pbf[:, k * 128:(k + 1) * 128], p.tk), hb_[:, kk_ * 128:(kk_ + 1) * 128], identb[:])
                        s.cp(h_[:, half * 4:half * 4 + 4, :], V(pbf.rearrange("p (k t) -> p k t", k=4), p.tk),
                             e="act" if half == 0 else "dve")
                return x_, h_

            def proj(h_, c0, ncols, dst, wsrc=None, w=None):
                wsrc = wsrc or wbuf
                o = 0
                bi = 2
                while o < ncols:
                    n = min(512, ncols - o)
                    p = pb[bi] if w is None else pb[2 + w]
                    with s.atomic():
                        for k in range(8):
                            s.mm(p[:, 0:n], h_[:, k, :], wsrc[:, k, c0 + o:c0 + o + n], start=(k == 0), stop=(k == 7))
                        s.cp(dst[:, o:o + n], p[:, 0:n], e="act" if bi == 2 else "dve")
                    o += n
                    bi = 5 - bi

            def rms_rope(src, H, gtab, scale_mode, rope, dst, w=0):
                sq = sqs[w]; sm = sms[w]; tc_ = tcs[w]; tsn = tsns[w]
                s.act(sq[:, 0:H * 64], src, AF.Square)
                s.red(sm[:, 0:H], V(sq.t[:, 0:H * 64].rearrange("p (h d) -> p h d", d=64), sq.tk), ALU.add)
                if scale_mode == "k":
                    s.ts(sm[:, 0:H], sm[:, 0:H], 1.0 / 64, ALU.mult, 1e-6, ALU.add)
                else:
                    s.ts(sm[:, 0:H], sm[:, 0:H], 64e-6, ALU.add)
                s.act(sm[:, 0:H], sm[:, 0:H], AF.Sqrt)
                s.recip(sm[:, 0:H], sm[:, 0:H])
                for h in range(H):
                    s.stt(V(dst.ap[:, h * 64:(h + 1) * 64], dst.tk), V(src.ap[:, h * 64:(h + 1) * 64], src.tk), sm[:, h:h + 1],
                          gtab[:, h * 64:(h + 1) * 64], ALU.mult, ALU.mult)
                if rope is not None:
                    cos_d, sin_d, rowfn = rope
                    s.dma("sp", tc_[:, 0:H * 64], cos_d[rowfn:rowfn + 128, :])
                    s.dma("act", tsn[:, 0:H * 64], sin_d[rowfn:rowfn + 128, :])
                    t1 = sq
                    v4 = lambda ap: ap.rearrange("p (g a d) -> p g a d", a=2, d=16)
                    d4 = v4(dst.ap); s4 = v4(tsn.t[:, 0:H * 64]); t4 = v4(t1.t[:, 0:H * 64])
                    s.tt(V(t4[:, :, 0, :], t1.tk), V(d4[:, :, 1, :], dst.tk), V(s4[:, :, 0, :], tsn.tk), ALU.mult, e="pool")
                    s.tt(V(t4[:, :, 1, :], t1.tk), V(d4[:, :, 0, :], dst.tk), V(s4[:, :, 1, :], tsn.tk), ALU.mult, e="pool")
                    s.tt(dst, dst, tc_[:, 0:H * 64], ALU.mult)
                    s.tt(dst, dst, t1[:, 0:H * 64], ALU.add)

            pt = [s.sb([128, 512], BF16, name="pt%d" % i) for i in range(3)]
            rcp = s.sb([128, 8], name="rcp")
            bias_t = [s.sb([128, 768], name="bias%d" % i) for i in range(2)]
            hbs.extend([_A(bias_t[i].t[:, 0:512].bitcast(BF16), bias_t[i].tk) for i in range(2)])
            ptc = [0]

            load_w(wbuf, wb[l], 768, 256)

            def k_body(t):
                w = t % 2
                j = 1 if t < 2 else 0
                x_, h_ = front(Xc, 128 * t if t < 2 else lat_row(128 * (t - 2)), j, w)
                pj = pe[0] if w == 0 else pjB
                proj(h_, 0, 256, pj, w=w)
                rope = None if t < 2 else (cosK, sinK, (t - 2) * 128)
                kr = hts[w]
                rms_rope(pj[:, 0:128], 2, GK, "k", rope, kr[:, 0:128], w)
                p = pb[6 + w]
                with s.atomic():
                    s.tr(p[:, 0:128], kr[:, 0:128], ident[:])
                    s.cp(kT[:, t * 128:(t + 1) * 128], p[:, 0:128], e="act")
                s.cp(Vg[:, t, :, 0:64], V(pj.t[:, 128:256].rearrange("p (g d) -> p g d", d=64), pj.tk), e="pool")

            ilB.run([(lambda t=t: k_body(t)) for t in range(NKT)], 2)
            load_w(wbuf, wb[l], 1664, 512)

            def n_body(t):
                w = t % 2
                j = 1 if t >= 20 else 0
                if t >= 20:
                    x_, h_ = front(Xc, 128 * (t - 20), j, w)
                else:
                    x_, h_ = front(XSc, 128 * t, j, w)
                pj = pe[0] if w == 0 else pjB
                proj(h_, 0, 512, pj, w=w)
                for pr_ in range(2):
                    p = pb[6 + w]
                    with s.atomic():
                        s.tr(p[:, 0:128], pj[:, pr_ * 128:(pr_ + 1) * 128], ident[:])
                        s.cp(nkT[pr_][:, t * 128:(t + 1) * 128], p[:, 0:128], e="act" if pr_ == 0 else "dve")
                s.cp(Vn[:, t, :, 0:64], V(pj.t[:, 256:512].rearrange("p (g d) -> p g d", d=64), pj.tk), e="pool")

            ilB.run([(lambda t=t: n_body(t)) for t in range(NNT)], 2)
            for sl in range(6):
                hh_ = (sl // 2) + 3 * (sl % 2)
                load_w(wbuf, wb[l], 384 + hh_ * 64, 64, dcol=sl * 64)
            load_w(wbuf, wb[l], 1408, 256, dcol=384)
            own_src = lambda t: ((Xc, 128 * (t - 16)) if t >= 16 else (XSc, 256 + 128 * t))

            def q_body(t):
                w = t % 2
                j = 1 if t >= 16 else 0
                x_, h_ = front(*own_src(t), j, w)
                pj = pe[0] if w == 0 else pjB
                proj(h_, 0, 640, pj, w=w)
                rope = None if t >= 16 else (cosQ, sinQ, 128 * t)
                qr = hts[w]
                rms_rope(pj[:, 0:384], 6, GQ, "q", rope, qr[:, 0:384], w)
                for pr_ in range(3):
                    p = pb[6 + w]
                    with s.atomic():
                        s.tr(p[:, 0:128], qr[:, pr_ * 128:(pr_ + 1) * 128], ident[:])
                        s.cp(qT[pr_][:, t * 128:(t + 1) * 128], p[:, 0:128], e="act" if pr_ % 2 == 0 else "dve")
                s.ts(qr[:, 384:640], pj[:, 384:640], 0.125, ALU.mult, e="pool")
                for pr_ in range(2):
                    p = pb[6 + w]
                    with s.atomic():
                        s.tr(p[:, 0:128], qr[:, 384 + pr_ * 128:384 + (pr_ + 1) * 128], ident[:])
                        s.cp(nqT[pr_][:, t * 128:(t + 1) * 128], p[:, 0:128], e="act" if pr_ == 0 else "dve")

            ilB.run([(lambda t=t: q_body(t)) for t in range(NOWN)], 2)

            def attend(qsrc, g, qc0, nq, chunks, ksrc, vsrc_fn, bias_fn, dst_fn):
                nqs = nq // 128
                lo, hi = 64 * g, 64 * g + 64
                n_ = len(chunks)
                for ci in range(n_ + 1):
                    if ci < n_:
                        kc0, nk, cid = chunks[ci]
                        ps = pb[ci % 3]
                        bsrc = bias_fn(cid, nk) if bias_fn else None
                        s.mm(ps[0:nk, 0:nq], ksrc[lo:hi, kc0:kc0 + nk], qsrc[lo:hi, qc0:qc0 + nq], start=True, stop=(bsrc is None))
                        if bsrc is not None:
                            s.mm(ps[0:nk, 0:nq], bsrc, ident[:, 0:nq], start=False, stop=True)
                    if ci >= 1:
                        kc0p, nkp, cidp = chunks[ci - 1]
                        p_ = pt[ptc[0] % 3]; ptc[0] += 1
                        s.act(p_[0:nkp, 0:nq], pb[(ci - 1) % 3][0:nkp, 0:nq], AF.Exp)
                        for qs in range(nqs):
                            s.mm(pb[4 + qs][:, 0:65], p_[0:nkp, qs * 128:(qs + 1) * 128], vsrc_fn(cidp, nkp),
                                 start=(ci == 1), stop=(ci == n_))
                for qs in range(nqs):
                    s.recip(rcp[:, qs:qs + 1], pb[4 + qs][:, 64:65])
                    s.ts(dst_fn(qs), pb[4 + qs][:, 0:64], rcp[:, qs:qs + 1], ALU.mult)

            oT = _A(pe[0].t[0:65, 0:512], pe[0].tk)
            fin = [0]

            def attend_T(qsrc, g, qc0, nq, chunks, ksrc, vsrc_fn, dst_fn):
                nqs = nq // 128
                lo, hi = 64 * g, 64 * g + 64
                po = pb[4]
                n_ = len(chunks)
                for ci in range(n_ + 1):
                    if ci < n_:
                        kc0, nk, cid = chunks[ci]
                        s.mm(pb[ci % 3][0:nk, 0:nq], ksrc[lo:hi, kc0:kc0 + nk], qsrc[lo:hi, qc0:qc0 + nq])
                    if ci >= 1:
                        kc0p, nkp, cidp = chunks[ci - 1]
                        p_ = pt[ptc[0] % 3]; ptc[0] += 1
                        s.act(p_[0:nkp, 0:nq], pb[(ci - 1) % 3][0:nkp, 0:nq], AF.Exp)
                        s.mm(po[0:65, 0:nq], vsrc_fn(cidp, nkp), p_[0:nkp, 0:nq], start=(ci == 1), stop=(ci == n_))
                s.cp(oT[:, 0:nq], po[0:65, 0:nq], e="dve")
                for qs in range(nqs):
                    pf = pb[5 + fin[0] % 3]; fin[0] += 1
                    s.tr(pf[:, 0:65], oT[:, qs * 128:(qs + 1) * 128], ident[0:65, 0:65])
                    s.recip(rcp[:, qs:qs + 1], pf[:, 64:65])
                    s.ts(dst_fn(qs), pf[:, 0:64], rcp[:, qs:qs + 1], ALU.mult)

            for qg in range(4):
                for h in range(6):
                    pr_, g = h % 3, h // 3
                    attend_T(qT[pr_], g, qg * 512, 512, [(c * 128, 128, c) for c in range(NKT)], kT,
                             lambda cid, nk, g=g: Vg[0:nk, cid, g, :],
                             lambda qs, qg=qg, h=h: YAN[:, qg * 4 + qs, h * 64:(h + 1) * 64])
            for h in range(6):
                pr_, g = h % 3, h // 3
                attend_T(qT[pr_], g, 2048, 256, [(c * 128, 128, c) for c in range(2)], kT,
                         lambda cid, nk, g=g: Vg[0:nk, cid, g, :],
                         lambda qs, h=h: YAN[:, 16 + qs, h * 64:(h + 1) * 64])
            bc_ = [0]
            for i in range(16):
                chs = na_chunks(i)
                base = chs[0][0] * 128
                cls = na_class(i)
                for hn in range(4):
                    pr_, g = hn // 2, hn % 2
                    bt_ = bias_t[bc_[0] % 2]; bc_[0] += 1
                    nkeys = sum(nk for _, nk in chs)
                    s.dma("sp" if hn % 2 == 0 else "act", bt_[:, 0:nkeys], nab[l, cls, hn, :, 0:nkeys])
                    chunks = [(c * 128, nk, c) for c, nk in chs] + [(20 * 128, 128, 20), (21 * 128, 128, 21)]
                    attend(nqT[pr_], g, i * 128, 128, chunks, nkT[pr_],
                           lambda cid, nk, hn=hn: Vn[0:nk, cid, hn, :],
                           lambda cid, nk, bt_=bt_, base=base: (None if cid >= 20 else bt_[:, cid * 128 - base:cid * 128 - base + nk]),
                           lambda qs, i=i, hn=hn: YAN[:, i, 384 + hn * 64:384 + (hn + 1) * 64])
            for hn in range(4):
                pr_, g = hn // 2, hn % 2
                attend(nqT[pr_], g, 2048, 256, [(20 * 128, 128, 20), (21 * 128, 128, 21)], nkT[pr_],
                       lambda cid, nk, hn=hn: Vn[0:nk, cid, hn, :], None,
                       lambda qs, hn=hn: YAN[:, 16 + qs, 384 + hn * 64:384 + (hn + 1) * 64])
            load_w(wbuf, wb[l], 0, 384)
            load_w(wbuf, wb[l], 1024, 384, dcol=384)
            load_w(wbuf, wb[l], 2176, 256, dcol=768)
            LNG = _A(kT.t[:, 0:2048].bitcast(F32), kT.tk); s.dma("sp", LNG[:], lng[l])
            LNB = _A(kT.t[:, 2048:4096].bitcast(F32), kT.tk); s.dma("act", LNB[:], lnb[l])
            Ff = _A(kT.t[:, 4096:5632].bitcast(F32).rearrange("p (s c) -> p s c", s=2), kT.tk)
            Bk = _A(kT.t[:, 5632:7168].bitcast(F32), kT.tk)
            cen = _A(kT.t[:, 7168:7936].bitcast(F32), kT.tk)
            ysb = _A(qT[1].t[:, 0:768].bitcast(F32), qT[1].tk)
            bsb = _A(qT[1].t[:, 768:1536].bitcast(F32), qT[1].tk)
            sqb = _A(qT[2].t[:, 0:768].bitcast(F32), qT[2].tk)
            YgT = _A(qT[0].t[:, 0:1024].rearrange("p (k t) -> p k t", k=8), qT[0].tk)
            for t in range(NOWN):
                j = 1 if t >= 16 else 0
                x_, h_ = front(*own_src(t), j)
                G = pe[0]
                proj(h_, 0, 1024, G)
                s.act(G[:], G[:], AF.Silu)
                for sr in range(2):
                    q = "sp" if sr == 0 else "act"
                    if t >= 16:
                        s.dma(q, Ff[:, sr, :], YGc[sr * 256 + 128 * (t - 16):sr * 256 + 128 * (t - 16) + 128, :])
                        rb = (2 + sr) * 256 + 128 - 128 * (t - 16)
                        s.dma(q, Bk[:, sr * 384:(sr + 1) * 384], YGc[rb:rb + 128, :])
                    else:
                        s.dma(q, Ff[:, sr, :], YF[sr][128 * t:128 * t + 128, :])
                        s.dma(q, Bk[:, sr * 384:(sr + 1) * 384], YBk[sr][1920 - 128 * t:1920 - 128 * t + 128, :])
                for sr in range(2):
                    s.mm(pb[6 + sr][:, 0:384], jmat[:], Bk[:, sr * 384:(sr + 1) * 384])
                v2 = lambda A_: V(A_.t[:, 0:384].rearrange("p (s c) -> p s c", s=2), A_.tk)
                for sr in range(2):
                    s.tt(ysb[:, sr * 192:(sr + 1) * 192], Ff[:, sr, 0:192], pb[6 + sr][:, 0:192], ALU.add)
                    s.tt(bsb[:, sr * 192:(sr + 1) * 192], Ff[:, sr, 192:384], pb[6 + sr][:, 192:384], ALU.add)
                v3 = lambda A_: V(A_.t[:, 0:384].rearrange("p (h d) -> p h d", d=64), A_.tk)
                s.red(sm[:, 0:6], v3(ysb), ALU.add)
                s.ts(sm[:, 0:6], sm[:, 0:6], 1.0 / 64, ALU.mult)
                s.tt(v3(cen), v3(ysb), V(sm.t[:, 0:6].unsqueeze(2).to_broadcast([128, 6, 64]), sm.tk), ALU.subtract)
                s.tt(sqb[:], cen[:], cen[:], ALU.mult, e="pool")
                s.red(sm[:, 8:14], v3(sqb), ALU.add)
                s.ts(sm[:, 8:14], sm[:, 8:14], 1.0 / 64, ALU.mult, 64e-5, ALU.add)
                s.act(sm[:, 8:14], sm[:, 8:14], AF.Sqrt)
                s.recip(sm[:, 8:14], sm[:, 8:14])
                s.tt(v3(cen), v3(cen), V(sm.t[:, 8:14].unsqueeze(2).to_broadcast([128, 6, 64]), sm.tk), ALU.mult)
                s.tt(cen[:], cen[:], GNG[:], ALU.mult)
                s.tt(cen[:], cen[:], GNB[:], ALU.add)
                s.tt(cen[:], cen[:], bsb[:], ALU.add)
                Yg = pe[1]
                s.tt(Yg[:, 0:384], cen[:], G[:, 0:384], ALU.mult)
                s.tt(Yg[:, 384:1024], YAN[:, t, :], G[:, 384:1024], ALU.mult)
                for half in range(2):
                    p = pb[half]
                    for k in range(4):
                        kk_ = half * 4 + k
                        s.tr(p[:, k * 128:(k + 1) * 128], Yg[:, kk_ * 128:(kk_ + 1) * 128], ident[:])
                    s.cp(YgT[:, half * 4:half * 4 + 4, :], V(p.t[:, :].rearrange("p (k t) -> p k t", k=4), p.tk),
                         e="act" if half == 0 else "dve")
                yo = pe[0]
                proj(YgT, 0, 1024, yo, wsrc=woutb)
                s.tt(yo[:], yo[:], mod[j][:, 2048:3072], ALU.mult)
                z = pe[1]
                s.stt(z[:], x_[:], ALPHA, yo[:], ALU.mult, ALU.add)
                s.red(sm[:, 16:17], z[:], ALU.add)
                s.ts(sm[:, 16:17], sm[:, 16:17], 1.0 / 1024, ALU.mult)
                s.ts(z[:], z[:], sm[:, 16:17], ALU.subtract)
                s.tt(yo[:], z[:], z[:], ALU.mult, e="pool")
                s.red(sm[:, 17:18], yo[:], ALU.add)
                s.ts(sm[:, 17:18], sm[:, 17:18], 1.0 / 1024, ALU.mult, 1e-5, ALU.add)
                s.act(sm[:, 17:18], sm[:, 17:18], AF.Sqrt)
                s.recip(sm[:, 17:18], sm[:, 17:18])
                s.stt(z[:], z[:], sm[:, 17:18], LNG[:], ALU.mult, ALU.mult)
                s.tt(z[:], z[:], LNB[:], ALU.add)
                q = "sp" if t % 2 == 0 else "act"
                if t < 16:
                    if last:
                        tickets.append(s.dma(q, out[128 * t:128 * t + 128, :], z[:]))
                    else:
                        s.dma(q, XN[128 * t:128 * t + 128, :], z[:])
                        if t % 2 == 1:
                            k = t // 2
                            s.allgather(XN[256 * k:256 * (k + 1), :], Xn[512 + 1024 * k:512 + 1024 * (k + 1), :], GROUPS)
                elif not last:
                    s.dma(q, Xn[128 * (t - 16):128 * (t - 16) + 128, :], z[:])
    s.finish(tickets)
    s.close()
    return nc


def rope_tables():
    t = np.arange(8192)
    row = (t // 64).astype(np.float32); col = (t % 64).astype(np.float32)
    inv = (10000.0 ** (-np.arange(16, dtype=np.float32) / 16)).astype(np.float32)
    ar = row[:, None] * inv; ac = col[:, None] * inv
    ang = np.concatenate([ar, ar, ac, ac], axis=-1).astype(np.float32)
    cos = np.cos(ang).astype(np.float32); sin = np.sin(ang).astype(np.float32)
    sgn = np.concatenate([-np.ones(16), np.ones(16), -np.ones(16), np.ones(16)]).astype(np.float32)
    return cos, sin * sgn


def na_bias(rpb, j):
    NEG = -30000.0
    out = np.full((5, 4, 128, 768), NEG, np.float32)
    tiles = {0: 0, 1: 1, 2: 7, 3: 14, 4: 15}
    for cls, i in tiles.items():
        chs = na_chunks(i)
        srow0 = chs[0][0] * 2
        nkeys = sum(nk for _, nk in chs)
        qrow_l = np.repeat(np.array([2 * i, 2 * i + 1]), 64)
        qcol = np.tile(np.arange(64), 2)
        r = 32 * j + qrow_l
        r_start = np.clip(r - 4, 0, 120)
        c_start = np.clip(qcol - 8, 0, 48)
        key = np.arange(nkeys)
        krow = (32 * j - 4) + srow0 + key // 64
        kcol = key % 64
        dr = krow[None, :] - r[:, None] + 7
        dc = kcol[None, :] - qcol[:, None] + 15
        inwin = ((krow[None, :] >= r_start[:, None]) & (krow[None, :] < r_start[:, None] + 8) &
                 (kcol[None, :] >= c_start[:, None]) & (kcol[None, :] < c_start[:, None] + 16))
        drc = np.clip(dr, 0, 14); dcc = np.clip(dc, 0, 30)
        for h in range(4):
            vals = rpb[h][drc, dcc]
            out[cls, h, :, 0:nkeys] = np.where(inwin, vals, NEG)
    return out


def consts_A():
    idx = np.arange(64)
    inclT = (idx[:, None] <= idx[None, :]).astype(np.float32)
    strictT = (idx[:, None] < idx[None, :]).astype(np.float32)
    strict = strictT.T.copy()
    mask = np.concatenate([inclT, strictT, inclT, -strictT, -strict], axis=1)
    rmask = np.ones((64, 256), np.float32); rmask[:, 0::64] = 0.0
    ident = np.eye(64, dtype=np.float32)
    return np.concatenate([ident, mask, mask, mask, rmask] + [ident] * 6, axis=1).astype(np.float32)


def host_F(inp, depth=4):
    cos, sinS = rope_tables()
    bc = lambda v: np.ascontiguousarray(np.broadcast_to(v[None, :], (128, v.shape[0]))).astype(np.float32)
    L = range(depth)
    shared = dict(
        wmod=np.ascontiguousarray(inp['w_mod'][:depth]),
        bmod=np.stack([bc(inp['b_mod'][l]) for l in L]),
        wb=np.ascontiguousarray(inp['w_in'][:depth, :, 1280:]),
        wout=np.ascontiguousarray(inp['w_out'][:depth]),
        cstA=consts_A(),
        cosK=np.tile(cos, (1, 2)), sinK=np.tile(sinS, (1, 2)),
        gk=np.stack([bc(np.tile(inp['gqa_k_norm'][l], 2)) for l in L]),
        gq=np.stack([bc(np.tile(inp['gqa_q_norm'][l], 6)) for l in L]),
        gng=np.stack([bc(inp['rwkv_gn_g'][l]) for l in L]), gnb=np.stack([bc(inp['rwkv_gn_b'][l]) for l in L]),
        lng=np.stack([bc(inp['ln_g'][l]) for l in L]), lnb=np.stack([bc(inp['ln_b'][l]) for l in L]),
        ident=np.eye(128, dtype=np.float32), jmat=np.ascontiguousarray(np.eye(128, dtype=np.float32)[::-1]),
    )
    per_batch = []
    for b in range(2):
        xa0 = np.zeros((XROWS, 1024), np.float32)
        xa0[0:256] = inp['ctx'][b]
        xa0[512:8704] = inp['x'][b].reshape(4, 8, 256, 1024).transpose(1, 0, 2, 3).reshape(8192, 1024)
        cvec = np.stack([inp['c'][b], inp['c_ctx']], axis=1)
        cv = np.ascontiguousarray(cvec.reshape(8, 128, 2).transpose(1, 0, 2).reshape(128, 16))
        per_batch.append(dict(xa0=xa0, cv=cv))
    nabs = [np.stack([na_bias(inp['na_rpb'][l], j) for l in L]) for j in range(4)]
    maps = []
    for c in range(8):
        b, r = c // 4, c % 4
        d, hh = r // 2, r % 2
        heads = [3 * hh + i for i in range(3)]
        cols = []
        for comp in (0, 384, 768):
            for h in heads:
                cols += list(range(comp + h * 64, comp + h * 64 + 64))
        cols += list(range(1152 + 32 * d, 1152 + 32 * d + 32))
        cols += list(range(1216 + 32 * d, 1216 + 32 * d + 32))
        cols = np.array(cols)
        hcols = np.concatenate([np.arange(h * 64, (h + 1) * 64) for h in heads])
        wa = np.ascontiguousarray(inp['w_in'][:depth][:, :, cols])
        cw = np.zeros((depth, 64, 33), np.float32); pv = np.zeros((depth, 64, 15), np.float32)
        for l in L:
            conv = inp['rwkv_conv'][l][:, cols]
            if d == 1:
                conv = conv[::-1]
            for ci in range(9):
                cw[l, :, ci * 3:ci * 3 + 3] = conv[:, ci * 64:(ci + 1) * 64].T
            cw[l, 0:32, 27:30] = conv[:, 576:608].T
            cw[l, 0:32, 30:33] = conv[:, 608:640].T
            for i, h in enumerate(heads):
                hs = slice(h * 64, (h + 1) * 64)
                pv[l, :, i * 5 + 0] = inp['decay_w0'][l][d, hs]
                pv[l, :, i * 5 + 1] = inp['iclr_a0'][l][d, hs]
                pv[l, :, i * 5 + 2] = inp['rwkv_k_k'][l][hs]
                pv[l, :, i * 5 + 3] = inp['rwkv_k_a'][l][hs]
                pv[l, :, i * 5 + 4] = inp['rwkv_r_k'][l][h]
        w2 = np.ascontiguousarray(inp['decay_w2'][:depth, d][:, :, hcols])
        a2 = np.ascontiguousarray(inp['iclr_a2'][:depth, d][:, :, hcols])
        own = slice(2048 * r, 2048 * r + 2048)
        jd = np.eye(128, dtype=np.float32)
        if d == 1:
            jd = np.ascontiguousarray(jd[::-1])
        dsel = np.zeros((128, 2), np.float32); dsel[:, d] = 1.0
        xs0 = np.zeros((2560, 1024), np.float32)
        lo = 2048 * r - 256
        for sr_ in range(2560):
            pass
        a0, a1 = max(lo, 0), min(lo + 2560, 8192)
        xs0[a0 - lo:a1 - lo] = inp['x'][b][a0:a1]
        m = dict(shared)
        m['xs0'] = xs0
        m.update(per_batch[b])
        m.update(wa=wa, cw=cw, pv=pv, w2=w2, a2=a2, nab=nabs[r],
                 cosQ=np.tile(cos[own], (1, 6)), sinQ=np.tile(sinS[own], (1, 6)), jd=jd, dsel=dsel)
        maps.append(m)
    return maps


from concourse.bass_utils import run_bass_kernel_spmd

_NC = {}


def kernel(**inputs):
    inp = {k: np.asarray(v, dtype=np.float32) for k, v in inputs.items()}
    if 'F' not in _NC:
        _NC['F'] = build_F(4)
    maps = host_F(inp, 4)
    res = run_bass_kernel_spmd(_NC['F'], maps, core_ids=list(range(8))).results
    x = np.stack([np.concatenate([res[b * 4 + r]["out"] for r in range(4)], axis=0) for b in range(2)])
    return np.ascontiguousarray(x.astype(np.float32))
```

```python
import contextlib
import numpy as np
import concourse.bass as bass
import concourse.mybir as mybir

F32 = mybir.dt.float32
BF16 = mybir.dt.bfloat16
AF = mybir.ActivationFunctionType
ALU = mybir.AluOpType
AX = mybir.AxisListType


class Tk:
    __slots__ = ("w", "r", "name", "excl", "acc")

    def __init__(self, name=""):
        self.w = None
        self.r = {}
        self.name = name
        self.excl = False
        self.acc = {}


class T:
    def __init__(self, S, t, name):
        self.t = t
        self.tk = Tk(name)
        self.name = name

    def __getitem__(self, idx):
        return V(self.t[idx], self.tk)


class V:
    __slots__ = ("ap", "tk")

    def __init__(self, ap, tk):
        self.ap = ap
        self.tk = tk


import threading


class _Worker(threading.Thread):
    def __init__(self, il, fn):
        super().__init__(daemon=True)
        self.il = il
        self.fn = fn
        self.go = threading.Event()
        self.done = False
        self.exc = None

    def run(self):
        self.go.wait(); self.go.clear()
        try:
            self.fn()
        except BaseException as e:
            self.exc = e
        self.done = True
        self.il.main_ev.set()

    def pause(self):
        self.il.main_ev.set()
        self.go.wait(); self.go.clear()


class Interleaver:
    def __init__(self, s):
        self.s = s
        self.main_ev = threading.Event()
        self.cur = None

    def run(self, fns, width):
        pending = list(fns)
        active = []
        self.s.yield_hook = self._hook
        try:
            while pending or active:
                while pending and len(active) < width:
                    w = _Worker(self, pending.pop(0)); w.start(); active.append(w)
                for w in list(active):
                    self.cur = w
                    self.main_ev.clear()
                    w.go.set()
                    self.main_ev.wait()
                    if w.exc is not None:
                        raise w.exc
                    if w.done:
                        active.remove(w)
        finally:
            self.s.yield_hook = None
            self.cur = None

    def _hook(self):
        w = self.cur
        if w is not None and threading.current_thread() is w and self.s.atomic_depth == 0:
            w.pause()


class S:
    ENG = ("pe", "act", "dve", "pool", "sp")
    yield_hook = None
    atomic_depth = 0

    @contextlib.contextmanager
    def atomic(self):
        self.atomic_depth += 1
        try:
            yield
        finally:
            self.atomic_depth -= 1
            if self.atomic_depth == 0 and self.yield_hook is not None:
                self.yield_hook()

    def __init__(self, nc):
        self.nc = nc
        self.es = contextlib.ExitStack()
        self.eng = {"pe": nc.tensor, "act": nc.scalar, "dve": nc.vector, "pool": nc.gpsimd, "sp": nc.sync}
        self.sem = {e: self.es.enter_context(nc.semaphore("s_" + e)) for e in self.ENG}
        self.cnt = {e: 0 for e in self.ENG}
        self.dq = {}
        for q in ("sp", "act", "pool"):
            sems = [self.es.enter_context(nc.semaphore("d_%s%d" % (q, i))) for i in range(8)]
            self.dq[q] = dict(sems=sems, cnt=[0] * 8, nxt=0)
            for i, s_ in enumerate(sems):
                self.sem[(q, i)] = s_
        self.waited = {}
        self.cur = self.es
        self.cc_keys = []
        self.n_tiles = 0
        self.n_instr = 0
        self.n_wait = 0

    def sb(self, shape, dt=F32, name=None):
        self.n_tiles += 1
        name = "%s_%d" % (name or "t", self.n_tiles)
        t = self.cur.enter_context(self.nc.sbuf_tensor("sb_" + name, list(shape), dt))
        return T(self, t, name)

    def dram(self, shape, name, dt=F32):
        self.n_tiles += 1
        t = self.nc.dram_tensor("%s_%d" % (name, self.n_tiles), list(shape), dt)
        return T(self, t.ap(), name)

    @contextlib.contextmanager
    def phase(self):
        prev = self.cur
        self.cur = contextlib.ExitStack()
        try:
            yield
        finally:
            self.barrier()
            self.cur.close()
            self.cur = prev

    def barrier(self):
        for e in self.ENG:
            for e2 in self.ENG:
                if e2 != e and self.cnt[e2] > 0:
                    self._wait(e, e2, self.cnt[e2])
            for q, d in self.dq.items():
                for i, c in enumerate(d["cnt"]):
                    if c > 0:
                        self._wait(e, (q, i), c)

    def allgather(self, src, dst, groups):
        if "cc" not in self.dq:
            sems = [self.es.enter_context(self.nc.semaphore("cc%d" % i)) for i in range(8)]
            self.dq["cc"] = dict(sems=sems, cnt=[0] * 8, nxt=0)
            for i, s_ in enumerate(sems):
                self.sem[("cc", i)] = s_
        d = self.dq["cc"]
        i = d["nxt"]; d["nxt"] = (i + 1) % 8
        key = ("cc", i)
        self._wait("pool", key, d["cnt"][i])
        self._deps("pool", [src], [dst])
        ins = self.nc.gpsimd.collective_compute("AllGather", mybir.AluOpType.bypass, replica_groups=groups,
                                                ins=[src.ap.opt()], outs=[dst.ap.opt()])
        d["cnt"][i] += 1
        ins.then_inc(self.sem[key])
        self._mark((key, d["cnt"][i]), [src], [dst])
        self.n_instr += 1
        return (key, d["cnt"][i])

    def ps(self, shape, dt=F32, name=None):
        self.n_tiles += 1
        name = name or "p%d" % self.n_tiles
        t = self.es.enter_context(self.nc.psum_tensor("ps_" + name, list(shape), dt))
        tt_ = T(self, t, name)
        tt_.tk.excl = True
        return tt_

    def close(self):
        self.es.close()

    def _wait(self, e, key, val):
        if val is None:
            return
        k = (e, key)
        if self.waited.get(k, 0) >= val:
            return
        self.waited[k] = val
        self.eng[e].wait_ge(self.sem[key], val)
        self.n_wait += 1

    def _deps(self, e, reads, writes, pe_acc=False):
        for v in list(reads) + list(writes):
            if v.tk.excl:
                for e2, n2 in v.tk.acc.items():
                    if e2 == e and e == "pe":
                        continue
                    self._wait(e, e2, n2)
        reads = [v for v in reads if not v.tk.excl]
        writes = [v for v in writes if not v.tk.excl]
        for v in reads:
            w = v.tk.w
            if w is not None:
                self._wait(e, w[0], w[1])
        for v in writes:
            tk = v.tk
            if tk.w is not None:
                if not (pe_acc and tk.w[0] == "pe" and e == "pe"):
                    self._wait(e, tk.w[0], tk.w[1])
            for re_, rn in tk.r.items():
                if re_ == e and e == "pe":
                    continue
                self._wait(e, re_, rn)

    def _mark(self, ticket, reads, writes):
        for v in list(reads) + list(writes):
            if v.tk.excl:
                v.tk.acc[ticket[0]] = ticket[1]
        reads = [v for v in reads if not v.tk.excl]
        writes = [v for v in writes if not v.tk.excl]
        for v in reads:
            v.tk.r[ticket[0]] = ticket[1]
        for v in writes:
            v.tk.w = ticket
            v.tk.r = {}

    def op(self, e, fn, reads, writes, pe_acc=False):
        reads = [v for v in reads if isinstance(v, V)]
        self._deps(e, reads, writes, pe_acc)
        ins = fn()
        self.cnt[e] += 1
        ins.then_inc(self.sem[e], 1)
        self._mark((e, self.cnt[e]), reads, writes)
        self.n_instr += 1
        if self.yield_hook is not None:
            self.yield_hook()
        return ins

    def dma(self, q, out, in_, **kw):
        d = self.dq[q]
        i = d["nxt"]
        d["nxt"] = (i + 1) % len(d["sems"])
        key = (q, i)
        self._wait(q, key, d["cnt"][i])
        reads = [in_] if isinstance(in_, V) else []
        writes = [out] if isinstance(out, V) else []
        self._deps(q, reads, writes)
        oa = out.ap if isinstance(out, V) else out
        ia = in_.ap if isinstance(in_, V) else in_
        ins = self.eng[q].dma_start(out=oa, in_=ia, **kw)
        d["cnt"][i] += 16
        ins.then_inc(self.sem[key], 16)
        self._mark((key, d["cnt"][i]), reads, writes)
        self.n_instr += 1
        if self.yield_hook is not None:
            self.yield_hook()
        return (key, d["cnt"][i])

    def wait_ticket(self, e, ticket):
        self._wait(e, ticket[0], ticket[1])

    def mm(self, out, lhsT, rhs, start=True, stop=True, **kw):
        return self.op("pe", lambda: self.nc.tensor.matmul(out.ap, lhsT.ap, rhs.ap, start=start, stop=stop, **kw),
                       [lhsT, rhs], [out], pe_acc=not start)

    def tr(self, out, in_, ident):
        return self.op("pe", lambda: self.nc.tensor.transpose(out.ap, in_.ap, ident.ap), [in_, ident], [out])

    def act(self, out, in_, func, bias=None, scale=None, accum_out=None, e="act"):
        kw = {}
        rd = [in_]
        if bias is not None:
            kw["bias"] = bias.ap if isinstance(bias, V) else bias
            rd.append(bias)
        if scale is not None:
            kw["scale"] = scale.ap if isinstance(scale, V) else scale
            rd.append(scale)
        wr = [out]
        if accum_out is not None:
            kw["accum_out"] = accum_out.ap
            wr.append(accum_out)
        return self.op("act", lambda: self.nc.scalar.activation(out.ap, in_.ap, func, **kw), rd, wr)

    def _ve(self, e):
        return {"dve": self.nc.vector, "pool": self.nc.gpsimd, "act": self.nc.scalar}[e]

    def tt(self, out, a, b, op, e="dve"):
        return self.op(e, lambda: self._ve(e).tensor_tensor(out.ap, a.ap, b.ap, op), [a, b], [out])

    def ts(self, out, a, s1, op0, s2=None, op1=None, e="dve", accum_out=None):
        rd = [a, s1, s2]
        a1 = s1.ap if isinstance(s1, V) else s1
        a2 = s2.ap if isinstance(s2, V) else s2
        kw = {}
        wr = [out]
        if op1 is not None:
            kw["op1"] = op1
        if accum_out is not None:
            kw["accum_out"] = accum_out.ap
            wr.append(accum_out)
        return self.op(e, lambda: self._ve(e).tensor_scalar(out.ap, a.ap, a1, a2, op0, **kw), rd, wr)

    def stt(self, out, a, s, b, op0, op1, e="dve"):
        sa = s.ap if isinstance(s, V) else s
        return self.op(e, lambda: self._ve(e).scalar_tensor_tensor(out.ap, a.ap, sa, b.ap, op0, op1), [a, s, b], [out])

    def cp(self, out, in_, e="dve"):
        if e == "act":
            return self.op("act", lambda: self.nc.scalar.copy(out.ap, in_.ap), [in_], [out])
        return self.op(e, lambda: self._ve(e).tensor_copy(out.ap, in_.ap), [in_], [out])

    def memset(self, out, val, e="pool"):
        return self.op(e, lambda: self._ve(e).memset(out.ap, val), [], [out])

    def red(self, out, in_, op, axis=AX.X, e="dve"):
        return self.op(e, lambda: self._ve(e).tensor_reduce(out.ap, in_.ap, axis, op), [in_], [out])

    def recip(self, out, in_):
        return self.op("dve", lambda: self.nc.vector.reciprocal(out.ap, in_.ap), [in_], [out])

    def finish(self, tickets):
        for t in tickets:
            self._wait("sp", t[0], t[1])


A_DEC = 0.6065306597126334
ALPHA = (2 * 4) ** 0.25
NOWN = 18
NKT = 66
NNT = 22
GROUPS = [[0, 1, 2, 3], [4, 5, 6, 7]]
XROWS = 8960


def na_chunks(i):
    if i == 0:
        return [(c, 128) for c in range(0, 6)]
    if i == 1:
        return [(c, 128) for c in range(1, 6)]
    if i == 15:
        return [(c, 128) for c in range(14, 19)] + [(19, 64)]
    return [(c, 128) for c in range(i, i + 4)] + [(i + 4, 64)]


def na_class(i):
    return {0: 0, 1: 1, 14: 3, 15: 4}.get(i, 2)


def lat_row(tau):
    rho, rem = divmod(tau, 2048)
    k, i = divmod(rem, 256)
    return 512 + 1024 * k + 256 * rho + i


class _A:
    def __init__(self, ap, tk):
        self.t = ap; self.tk = tk

    def __getitem__(self, idx):
        return V(self.t[idx], self.tk)


def build_F(depth=4):
    nc = bass.Bass("TRN2", target_bir_lowering=False)
    dt = nc.dram_tensor
    I = lambda n, sh: dt(n, sh, F32, kind="ExternalInput").ap()
    xa0 = I("xa0", [XROWS, 1024]); xs0 = I("xs0", [2560, 1024])
    cv = I("cv", [128, 16]); wmod = I("wmod", [depth, 1024, 3072]); bmod = I("bmod", [depth, 128, 3072])
    wb = I("wb", [depth, 1024, 2432]); wout = I("wout", [depth, 1024, 1024])
    wa = I("wa", [depth, 1024, 640]); cw = I("cw", [depth, 64, 33]); pvi = I("pv", [depth, 64, 15])
    w2 = I("w2", [depth, 32, 192]); a2 = I("a2", [depth, 32, 192])
    cstA = I("cstA", [64, 1664])
    cosK = I("cosK", [8192, 128]); sinK = I("sinK", [8192, 128])
    cosQ = I("cosQ", [2048, 384]); sinQ = I("sinQ", [2048, 384])
    dsel_in = I("dsel", [128, 2])
    gk = I("gk", [depth, 128, 128]); gq = I("gq", [depth, 128, 384])
    nab = I("nab", [depth, 5, 4, 128, 768])
    gng = I("gng", [depth, 128, 384]); gnb = I("gnb", [depth, 128, 384])
    lng = I("lng", [depth, 128, 1024]); lnb = I("lnb", [depth, 128, 1024])
    ident_in = I("ident", [128, 128]); jmat_in = I("jmat", [128, 128]); jd_in = I("jd", [128, 128])
    out = dt("out", [2048, 1024], F32, kind="ExternalOutput").ap()

    s = S(nc)
    tickets = []
    pb = [s.ps([128, 512], name="pb%d" % i) for i in range(8)]
    XA = [s.dram([XROWS, 1024], "XA%d" % i) for i in range(2)]
    YB = s.dram([8448, 384], "YB")
    YGc = s.dram([4 * 256, 384], "YGc")
    YGl = s.dram([16 * 4 * 512, 384], "YGl")
    XN = s.dram([2048, 1024], "XN")
    XS = s.dram([2560, 1024], "XS")
    YF = [s.dram([2048, 384], "YF%d" % i) for i in range(2)]
    YBk = [s.dram([2048, 384], "YBk%d" % i) for i in range(2)]
    xa0_T = _A(xa0, Tk("xa0")); xs0_T = _A(xs0, Tk("xs0"))

    _rd = {}

    def RR(q):
        if q not in _rd:
            pid = s.eng[q].partition_id()
            _rd[q] = pid % 4
        return _rd[q]

    dynq = ["sp", "act", "pool"]
    dync = [0]

    def dyndma(dst_v, src_fn):
        q = dynq[dync[0] % 3]; dync[0] += 1
        return s.dma(q, dst_v, src_fn(RR(q)))

    ident = s.sb([128, 128], name="ident"); s.dma("sp", ident[:], ident_in)
    jmat = s.sb([128, 128], name="jmat"); s.dma("act", jmat[:], jmat_in)
    jd = s.sb([128, 128], name="jd"); s.dma("sp", jd[:], jd_in)
    dsel = s.sb([128, 2], name="dsel"); s.dma("act", dsel[:], dsel_in)
    identb = s.sb([128, 128], BF16, name="identb"); s.cp(identb[:], ident[:])
    jdb = s.sb([128, 128], BF16, name="jdb"); s.cp(jdb[:], jd[:])
    ones = s.sb([128, 128], name="ones"); s.memset(ones[:], 1.0)
    cv_t = s.sb([128, 16], name="cv"); s.dma("sp", cv_t[:], cv)
    scv = s.sb([128, 16], name="scv"); s.act(scv[:], cv_t[:], AF.Silu)
    mod = [s.sb([128, 3072], name="mod%d" % j) for j in range(2)]
    for l in range(depth):
        Xc = xa0_T if l == 0 else XA[(l - 1) % 2]
        Xn = XA[l % 2]
        last = (l == depth - 1)

        if l == 0:
            XSc = xs0_T
        else:
            XSc = XS
            lat = Xc.t[512:8704, :]
            dyndma(V(XS.t[256:2304, :].rearrange("(o k i) c -> o k (i c)", o=1, k=8), XS.tk),
                   lambda r: V(lat.rearrange("(k rr i) c -> rr k (i c)", rr=4, i=256)[bass.ds(r, 1)], Xc.tk))
            units = lat.rearrange("(u i) c -> u (i c)", i=256)
            dyndma(V(XS.t[0:256, :].rearrange("(o i) c -> o (i c)", o=1), XS.tk),
                   lambda r: V(units[bass.ds(r + 27, 1), :], Xc.tk))
            dyndma(V(XS.t[2304:2560, :].rearrange("(o i) c -> o (i c)", o=1), XS.tk),
                   lambda r: V(units[bass.ds(r + 1, 1), :], Xc.tk))
        with s.phase():
            stage_ws = [s.sb([128, 8, 512], name="stage_w%d" % i) for i in range(2)]
            Rl = s.sb([128, 16, 128], name="Rl")
            bmod_ts = [s.sb([128, 512], name="bmodt%d" % i) for i in range(2)]
            for i in range(16):
                s.ts(Rl[:, i, :], ones[:], scv[:, i:i + 1], ALU.mult, e="dve" if i % 2 == 0 else "pool")
            for cb in range(6):
                stage_w = stage_ws[cb % 2]; bmod_t = bmod_ts[cb % 2]
                s.dma("sp", stage_w[:], wmod[l].rearrange("(k p) c -> p k c", p=128)[:, :, cb * 512:(cb + 1) * 512])
                s.dma("act", bmod_t[:], bmod[l][:, cb * 512:(cb + 1) * 512])
                for j in range(2):
                    ps = pb[(2 * cb + j) % 4]
                    for k in range(8):
                        s.mm(ps[:, :], Rl[:, 2 * k + j, :], stage_w[:, k, :], start=(k == 0), stop=(k == 7))
                    s.tt(mod[j][:, cb * 512:(cb + 1) * 512], ps[:, :], bmod_t[:], ALU.add, e="dve")
            for j in range(2):
                s.ts(mod[j][:, 1024:2048], mod[j][:, 1024:2048], 1.0, ALU.add, e="pool")

        with s.phase():
            cst_t = s.sb([64, 1664], name="cst"); s.dma("sp", cst_t[:], cstA)
            identA = cst_t[:, 0:64]
            mask3 = lambda h: cst_t[:, 64 + h * 320: 64 + (h + 1) * 320]
            rmask = cst_t[:, 1024:1280]
            idt3 = cst_t[:, 1280:1664]
            cw_t = s.sb([64, 33], name="cw"); s.dma("act", cw_t[:], cw[l])
            pv_t = s.sb([64, 16], name="pv"); s.dma("act", pv_t[:, 0:15], pvi[l])
            omk = s.sb([64, 3], name="omk")
            for h in range(3):
                s.ts(omk[:, h:h + 1], pv_t[:, h * 5 + 3:h * 5 + 4], -1.0, ALU.mult, 1.0, ALU.add)
            w2_t = s.sb([32, 192], name="w2"); s.dma("act", w2_t[:], w2[l])
            a2_t = s.sb([32, 192], name="a2"); s.dma("act", a2_t[:], a2[l])
            xt = [s.sb([128, 1024], name="xt%d" % i) for i in range(2)]
            ht = s.sb([128, 1024], name="ht")
            xr = s.sb([128, 1024], name="xr")
            hbA = s.sb([128, 1024], BF16, name="hbA")
            wab = s.sb([128, 8, 640], BF16, name="wab")
            for k in range(8):
                st_ = xt[k % 2]
                s.dma("sp" if k % 2 == 0 else "act", st_[:, 0:640], wa[l][k * 128:(k + 1) * 128, :])
                s.cp(wab[:, k, :], st_[:, 0:640], e="dve" if k % 2 == 0 else "pool")
            cts = [(i * 64, 64) for i in range(9)] + [(576, 32), (608, 32)]
            hg = [s.sb([128, 8, 258], BF16, name="hg%d" % i) for i in range(2)]
            for h_ in hg:
                s.memset(h_[:], 0.0)
            raw = [s.sb([64, 258], name="raw%d" % i) for i in range(2)]
            ctmp = [s.sb([64, 256], name="ctmp%d" % i) for i in range(2)]
            mk = lambda nm, shape=(64, 256), dt_=F32: [s.sb(list(shape), dt_, name="%s%d" % (nm, h)) for h in range(3)]
            uR, uK, uV = mk("uR"), mk("uK"), mk("uV", dt_=BF16)
            uD = s.sb([32, 256], name="uD"); uA = s.sb([32, 256], name="uA"); ddt = s.sb([32, 256], name="ddt")
            sg, ic, kk, tmp, kd, bd = mk("sg"), mk("ic"), mk("kk"), mk("tmp"), mk("kd"), mk("bd")
            cs, csx, csr = mk("cs"), mk("csx"), mk("csr")
            E1, E3 = mk("E1"), mk("E3")
            E4 = E1
            RH, KKH, kt, bt, kc, bc, rk = (mk("RH", dt_=BF16), mk("KKH", dt_=BF16), mk("kt", dt_=BF16), mk("bt", dt_=BF16),
                                           mk("kc", dt_=BF16), mk("bc", dt_=BF16), mk("rk", dt_=BF16))
            identAb = s.sb([64, 64], BF16, name="identAb"); s.cp(identAb[:], identA)
            onesb = s.sb([64, 1], BF16, name="onesb"); s.memset(onesb[:], 1.0)
            wc = mk("wc", (64, 4)); rn = tmp
            trT = [s.sb([64, 3, 256], BF16, name="trT%d" % i) for i in range(4)]
            scS = [s.sb([64, 3, 320], BF16, name="scS%d" % i) for i in range(4)]
            XYs = [[s.sb([64, 3, 128], BF16, name="XY%d_%d" % (c, i)) for i in range(2)] for c in range(4)]
            PQs = [[s.sb([64, 3, 128], BF16, name="PQ%d_%d" % (c, i)) for i in range(2)] for c in range(4)]
            KKpTs = [s.sb([64, 192], BF16, name="KKpT%d" % c) for c in range(4)]
            AVs = [s.sb([64, 192], BF16, name="AV%d" % c) for c in range(4)]
            Ulocs = [s.sb([64, 192], name="Uloc%d" % c) for c in range(4)]
            Us = [s.sb([64, 192], BF16, name="U%d" % c) for c in range(4)]
            STb = [s.sb([64, 192], BF16, name="STb%d" % i) for i in range(2)]
            bss = [s.sb([64, 4], name="bs%d" % c) for c in range(4)]
            il = Interleaver(s)
            ST = [s.sb([64, 192], name="ST%d" % i) for i in range(2)]
            YBuf = [s.sb([64, 4, 384], name="YBuf0")] * 2
            bs_ = s.sb([64, 4], name="bs")
            s.memset(ST[0][:], 0.0)
            s.memset(STb[0][:], 0.0)
            sti = 0
            acnt = [0]

            def frontA(g):
                hgt = hg[g % 2]
                for a in range(2):
                    u = 2 * g + a
                    i = acnt[0]; acnt[0] += 1
                    if u < 2:
                        bf_, br_ = 128 * u, 128 * (1 - u)
                        j = 1
                    else:
                        v = u - 2
                        bf_, br_ = lat_row(128 * v), lat_row(128 * (63 - v))
                        j = 0
                    x_ = xt[i % 2]
                    s.dma("sp", x_[:], V(Xc.t[bf_:bf_ + 128, :], Xc.tk))
                    s.dma("act", xr[:], V(Xc.t[br_:br_ + 128, :], Xc.tk))
                    s.act(x_[:], x_[:], AF.Identity, scale=dsel[:, 0:1])
                    s.stt(x_[:], xr[:], dsel[:, 1:2], x_[:], ALU.mult, ALU.add)
                    s.tt(ht[:], x_[:], mod[j][:, 1024:2048], ALU.mult, e="pool")
                    s.tt(hbA[:], ht[:], mod[j][:, 0:1024], ALU.add, e="dve")
                    for half in range(2):
                        p = pb[half]
                        with s.atomic():
                            pbf = p.t[:, 0:256].bitcast(BF16)
                            for k in range(4):
                                kk_ = half * 4 + k
                                s.tr(V(pbf[:, k * 128:(k + 1) * 128], p.tk), hbA[:, kk_ * 128:(kk_ + 1) * 128], jdb[:])
                            s.cp(hgt[:, half * 4:half * 4 + 4, 1 + 128 * a:1 + 128 * (a + 1)],
                                 V(pbf.rearrange("p (k t) -> p k t", k=4), p.tk), e="act" if half == 0 else "dve")

            NGRP = 33
            OPN = ('RH', 'KKH', 'kt', 'bt', 'kc', 'bc', 'rk', 'uV')
            opsets = [dict(RH=RH, KKH=KKH, kt=kt, bt=bt, kc=kc, bc=bc, rk=rk, uV=uV, wc=wc),
                      dict(RH=mk('RHb', dt_=BF16), KKH=mk('KKHb', dt_=BF16), kt=mk('ktb', dt_=BF16), bt=mk('btb', dt_=BF16),
                           kc=mk('kcb', dt_=BF16), bc=mk('bcb', dt_=BF16), rk=mk('rkb', dt_=BF16), uV=mk('uVb', dt_=BF16),
                           wc=mk('wcb', (64, 4)))]

            def pro1(g):
                O = opsets[g % 2]; uV = O['uV']
                first = g in (0, 1)
                lastg = g in (0, NGRP - 1)
                hgt = hg[g % 2]
                if g + 1 < NGRP:
                    frontA(g + 1)
                    hn_ = hg[(g + 1) % 2]
                    s.cp(hgt[:, :, 257:258], hn_[:, :, 1:2], e="pool")
                    s.cp(hn_[:, :, 0:1], hgt[:, :, 256:257], e="pool")
                for ci, (c0, M) in enumerate(cts):
                    pr = pb[ci % 2]
                    rw = raw[ci % 2]
                    with s.atomic():
                        for k in range(8):
                            s.mm(pr[0:M, 0:258], wab[:, k, c0:c0 + M], hgt[:, k, :], start=(k == 0), stop=(k == 7))
                        s.cp(rw[0:M, :], pr[0:M, 0:258], e="act" if ci % 2 == 0 else "dve")
                    if first:
                        s.memset(rw[0:M, 0:1], 0.0, e="pool")
                    if lastg:
                        s.memset(rw[0:M, 257:258], 0.0, e="pool")
                    dst = (uR, uK, uV)[ci // 3][ci % 3] if ci < 9 else (uD, uA)[ci - 9]
                    tm = ctmp[ci % 2]
                    s.act(tm[0:M, :], rw[0:M, 0:256], AF.Identity, scale=cw_t[0:M, ci * 3:ci * 3 + 1])
                    s.stt(tm[0:M, :], rw[0:M, 1:257], cw_t[0:M, ci * 3 + 1:ci * 3 + 2], tm[0:M, :], ALU.mult, ALU.add)
                    s.stt(dst[0:M, :], rw[0:M, 2:258], cw_t[0:M, ci * 3 + 2:ci * 3 + 3], tm[0:M, :], ALU.mult, ALU.add)
                s.act(ddt[:], uD[:], AF.Tanh)

            def prep_head(g, h):
                O = opsets[g % 2]
                RH, KKH, kt, bt, kc, bc, rk, wc = O['RH'], O['KKH'], O['kt'], O['bt'], O['kc'], O['bc'], O['rk'], O['wc']
                P = lambda i: pv_t[:, h * 5 + i:h * 5 + i + 1]
                pz = pb[2 + h]
                with s.atomic():
                    s.mm(pz[0:64, 0:256], w2_t[:, h * 64:(h + 1) * 64], ddt[:])
                    s.act(sg[h][:], pz[0:64, 0:256], AF.Sigmoid, bias=P(0))
                with s.atomic():
                    s.mm(pz[0:64, 256:512], a2_t[:, h * 64:(h + 1) * 64], uA[:])
                    s.act(ic[h][:], pz[0:64, 256:512], AF.Sigmoid, bias=P(1))
                s.act(kk[h][:], uK[h][:], AF.Identity, scale=P(2))
                s.act(tmp[h][:], kk[h][:], AF.Square)
                pss = pb[2 + h]
                with s.atomic():
                    s.mm(pss[0:64, 0:256], ones[0:64, 0:64], tmp[h][:])
                    s.ts(rn[h][:], pss[0:64, 0:256], 1e-12, ALU.max)
                s.act(rn[h][:], rn[h][:], AF.Sqrt)
                s.recip(rn[h][:], rn[h][:])
                s.tt(kk[h][:], kk[h][:], rn[h][:], ALU.mult)
                s.act(tmp[h][:], ic[h][:], AF.Identity, scale=P(3), bias=omk[:, h:h + 1])
                s.tt(kd[h][:], uK[h][:], tmp[h][:], ALU.mult, e="dve")
                s.tt(bd[h][:], kk[h][:], ic[h][:], ALU.mult, e="pool")
                s.op("dve", lambda h=h: nc.vector.tensor_tensor_scan(cs[h][:].ap, rmask.ap, sg[h][:].ap, 0.0, ALU.mult, ALU.add),
                     [rmask, sg[h][:]], [cs[h][:]])
                s.tt(csx[h][:], cs[h][:], sg[h][:], ALU.subtract)
                for c in range(4):
                    s.ts(csr[h][:, c * 64:(c + 1) * 64], cs[h][:, c * 64:(c + 1) * 64],
                         cs[h][:, c * 64 + 63:c * 64 + 64], ALU.subtract)
                s.act(wc[h][:], cs[h][:, 63::64], AF.Exp, scale=-A_DEC)
                s.act(E1[h][:], cs[h][:], AF.Exp, scale=-A_DEC)
                s.tt(RH[h][:], uR[h][:], E1[h][:], ALU.mult, e="dve")
                s.act(E1[h][:], csx[h][:], AF.Exp, scale=-A_DEC)
                s.tt(KKH[h][:], kk[h][:], E1[h][:], ALU.mult, e="pool")
                s.act(E3[h][:], cs[h][:], AF.Exp, scale=A_DEC)
                s.tt(kt[h][:], kd[h][:], E3[h][:], ALU.mult)
                s.tt(bt[h][:], bd[h][:], E3[h][:], ALU.mult, e="pool")
                s.act(E4[h][:], csr[h][:], AF.Exp, scale=A_DEC)
                s.tt(kc[h][:], kd[h][:], E4[h][:], ALU.mult)
                s.tt(bc[h][:], bd[h][:], E4[h][:], ALU.mult, e="pool")
                s.stt(rk[h][:], uR[h][:], P(4), kd[h][:], ALU.mult, ALU.mult)

            def chunk_body(g, c):
                n = 4 * g + c
                O = opsets[g % 2]
                RH, KKH, kt, bt, kc, bc, rk, uV = O['RH'], O['KKH'], O['kt'], O['bt'], O['kc'], O['bc'], O['rk'], O['uV']
                cc = slice(c * 64, (c + 1) * 64)
                tT = trT[c]; sS = scS[c]
                XY = XYs[c]; PQ = PQs[c]; KKpT = KKpTs[c]; AV = AVs[c]; Uloc = Ulocs[c]; U = Us[c]; bs_ = bss[c]
                p_ = c % 2
                for h in range(3):
                    ptr = pb[2 + p_]
                    with s.atomic():
                        ptrb = ptr.t[0:64, 0:128].bitcast(BF16)
                        for i, src_ in enumerate((KKH, bc, kc, uV)):
                            s.tr(V(ptrb[:, i * 64:(i + 1) * 64], ptr.tk), src_[h][:, cc], identAb[:])
                        s.cp(tT[:, h, :], V(ptrb[:, 0:256], ptr.tk), e="act")
                    psc = pb[4 + p_]
                    with s.atomic():
                        s.mm(psc[0:64, 0:64], kt[h][:, cc], RH[h][:, cc])
                        s.mm(psc[0:64, 64:128], kt[h][:, cc], KKH[h][:, cc])
                        s.mm(psc[0:64, 128:192], bt[h][:, cc], RH[h][:, cc])
                        s.mm(psc[0:64, 192:256], bt[h][:, cc], KKH[h][:, cc])
                        s.mm(psc[0:64, 256:320], KKH[h][:, cc], bt[h][:, cc])
                        s.tt(sS[:, h, :], psc[0:64, 0:320], mask3(h), ALU.mult)
                s.tt(PQ[0][:, :, :], sS[:, :, 192:320], V(idt3.ap.rearrange("p (h c) -> p h c", c=128), idt3.tk), ALU.add, e="pool")
                Xc_ = lambda lvl, h: (sS[:, h, 192:256] if lvl == 0 else XY[lvl % 2][:, h, 0:64])
                Yc_ = lambda lvl, h: (sS[:, h, 256:320] if lvl == 0 else XY[lvl % 2][:, h, 64:128])
                for lvl in range(5):
                    pn, pq = pb[4 + p_], pb[6 + p_]
                    nxt = XY[(lvl + 1) % 2]
                    with s.atomic():
                        for h in range(3):
                            s.mm(pn[0:64, h * 128:h * 128 + 64], Yc_(lvl, h), Xc_(lvl, h))
                            if lvl < 4:
                                s.mm(pn[0:64, h * 128 + 64:h * 128 + 128], Xc_(lvl, h), Yc_(lvl, h))
                        pn3 = pn.t[0:64, 0:384].rearrange("p (h c) -> p h c", c=128)
                        if lvl < 4:
                            s.cp(nxt[:, :, :], V(pn3, pn.tk), e="act")
                        else:
                            s.cp(nxt[:, :, 0:64], V(pn3[:, :, 0:64], pn.tk), e="act")
                    Pc, Pn = PQ[lvl % 2], PQ[(lvl + 1) % 2]
                    with s.atomic():
                        for h in range(3):
                            s.mm(pq[0:64, h * 128:h * 128 + 64], Pc[:, h, 64:128], nxt[:, h, 0:64])
                            if lvl < 4:
                                s.mm(pq[0:64, h * 128 + 64:h * 128 + 128], Pc[:, h, 0:64], nxt[:, h, 64:128])
                        pq3 = pq.t[0:64, 0:384].rearrange("p (h c) -> p h c", c=128)
                        if lvl < 4:
                            s.tt(Pn[:, :, :], V(pq3, pq.tk), Pc[:, :, :], ALU.add)
                        else:
                            s.tt(Pn[:, :, 0:64], V(pq3[:, :, 0:64], pq.tk), Pc[:, :, 0:64], ALU.add)
                TT = PQ[1]
                pk = pb[2 + p_]
                with s.atomic():
                    for h in range(3):
                        s.mm(pk[0:64, h * 64:(h + 1) * 64], tT[:, h, 0:64], TT[:, h, 0:64])
                        s.mm(pk[0:64, 192 + h * 64:192 + (h + 1) * 64], sS[:, h, 64:128], tT[:, h, 192:256])
                    s.cp(KKpT[:], pk[0:64, 0:192], e="act")
                    s.cp(AV[:], pk[0:64, 192:384], e="dve")
                pk3 = pb[6 + p_]
                with s.atomic():
                    for h in range(3):
                        s.mm(pk3[0:64, h * 64:(h + 1) * 64], TT[:, h, 0:64], AV[:, h * 64:(h + 1) * 64])
                    s.cp(Uloc[:], pk3[0:64, 0:192], e="act")

            def chunk_seq(g, c):
                n = 4 * g + c
                yb = YBuf[0]
                O = opsets[g % 2]
                RH, rk, wc = O['RH'], O['rk'], O['wc']
                cc = slice(c * 64, (c + 1) * 64)
                tT = trT[c]; sS = scS[c]
                KKpT = KKpTs[c]; Uloc = Ulocs[c]; U = Us[c]; bs_ = bss[c]
                Sc, Sn = ST[n % 2], ST[(n + 1) % 2]
                Scb, Snb = STb[n % 2], STb[(n + 1) % 2]
                pu = pb[0]
                with s.atomic():
                    for h in range(3):
                        s.mm(pu[0:64, h * 64:(h + 1) * 64], KKpT[:, h * 64:(h + 1) * 64], Scb[:, h * 64:(h + 1) * 64])
                    s.stt(U[:], pu[0:64, 0:192], -1.0, Uloc[:], ALU.mult, ALU.subtract)
                pS = pb[1]
                with s.atomic():
                    for h in range(3):
                        hs = slice(h * 64, (h + 1) * 64)
                        s.mm(pS[0:64, hs], tT[:, h, 128:192], tT[:, h, 192:256], start=True, stop=False)
                        s.mm(pS[0:64, hs], tT[:, h, 64:128], U[:, hs], start=False, stop=True)
                    for h in range(3):
                        hs = slice(h * 64, (h + 1) * 64)
                        s.stt(Sn[:, hs], Sc[:, hs], wc[h][:, c:c + 1], pS[0:64, hs], ALU.mult, ALU.add)
                    s.cp(Snb[:], Sn[:], e="act")
                py = pb[0]
                with s.atomic():
                    for h in range(3):
                        hs = slice(256 + h * 64, 256 + (h + 1) * 64)
                        hh = slice(h * 64, (h + 1) * 64)
                        s.mm(py[0:64, hs], RH[h][:, cc], Scb[:, hh], start=True, stop=False)
                        s.mm(py[0:64, hs], sS[:, h, 128:192], U[:, hh], start=False, stop=False)
                        s.mm(py[0:64, hs], sS[:, h, 0:64], tT[:, h, 192:256], start=False, stop=True)
                    s.cp(yb[:, c, 0:192], py[0:64, 256:448], e="act")
                pbn = pb[1]
                with s.atomic():
                    for h in range(3):
                        s.mm(pbn[0:64, 256 + h:256 + h + 1], rk[h][:, cc], onesb[:, 0:1])
                    s.cp(bs_[:, 0:3], pbn[0:64, 256:259], e="dve")
                for h in range(3):
                    s.ts(yb[:, c, 192 + h * 64:192 + (h + 1) * 64], tT[:, h, 192:256], bs_[:, h:h + 1], ALU.mult, e="pool")
            def seq_group(g):
                tbase = 256 * g
                for c in range(4):
                    chunk_seq(g, c)
                s.dma("sp" if g % 2 == 0 else "act",
                      V(YB.t[tbase:tbase + 256, :].rearrange("(c t) f -> t c f", t=64), YB.tk), YBuf[0][:, :, :])
                if g == 0:
                    s.allgather(YB[0:256, :], YGc[:, :], GROUPS)
                elif g % 2 == 0:
                    m = g // 2 - 1
                    s.allgather(YB[256 + 512 * m:256 + 512 * (m + 1), :], YGl[2048 * m:2048 * (m + 1), :], GROUPS)

            frontA(0)
            pro1(0)
            il.run([(lambda h=h: prep_head(0, h)) for h in range(3)], 3)
            for g in range(NGRP):
                wk1 = [(lambda c=c: chunk_body(g, c)) for c in range(4)]
                if g + 1 < NGRP:
                    wk1.append(lambda: pro1(g + 1))
                il.run(wk1, 5)
                wk2 = [lambda: seq_group(g)]
                if g + 1 < NGRP:
                    wk2 += [(lambda h=h: prep_head(g + 1, h)) for h in range(3)]
                il.run(wk2, 4)
        ygv = YGl.t.rearrange("(m sr i) c -> sr m (i c)", sr=4, i=512)
        for sr in range(2):
            dyndma(V(YF[sr].t.rearrange("(m i) c -> m (i c)", i=512), YF[sr].tk),
                   lambda r, sr=sr: V(ygv[sr][bass.ds(r * 4, 4), :], YGl.tk))
            dyndma(V(YBk[sr].t.rearrange("(m i) c -> m (i c)", i=512), YBk[sr].tk),
                   lambda r, sr=sr: V(ygv[2 + sr][bass.ds((3 - r) * 4, 4), :], YGl.tk))
        with s.phase():
            GK = s.sb([128, 128], name="GK"); s.dma("act", GK[:], gk[l])
            GQ = s.sb([128, 384], name="GQ"); s.dma("act", GQ[:], gq[l])
            GNG = s.sb([128, 384], name="GNG"); s.dma("act", GNG[:], gng[l])
            GNB = s.sb([128, 384], name="GNB"); s.dma("act", GNB[:], gnb[l])
            YAN = s.sb([128, 18, 640], BF16, name="YAN")
            wbuf = s.sb([128, 8, 1024], BF16, name="wbuf")
            woutb = s.sb([128, 8, 1024], BF16, name="woutb")
            xt = [s.sb([128, 1024], name="xt%d" % i) for i in range(2)]
            hts = [s.sb([128, 1024], name="ht%d" % i) for i in range(2)]
            ht = hts[0]
            pe = [s.sb([128, 1024], name="pe%d" % i) for i in range(2)]
            sqs = [pe[1], None]
            wst = xt
            hbs = []
            ilB = Interleaver(s)
            pjB = _A(YAN.t[:, 0:2, :].rearrange("p a c -> p (a c)").bitcast(F32), YAN.tk)
            sqs[1] = _A(YAN.t[:, 2:4, :].rearrange("p a c -> p (a c)").bitcast(F32), YAN.tk)

            def load_w(dst, src, c0, ncols, dcol=0):
                for k in range(8):
                    st_ = wst[k % 2]
                    s.dma("sp" if k % 2 == 0 else "act", st_[:, 0:ncols], src[k * 128:(k + 1) * 128, c0:c0 + ncols])
                    s.cp(dst[:, k, dcol:dcol + ncols], st_[:, 0:ncols], e="dve" if k % 2 == 0 else "pool")

            load_w(woutb, wout[l], 0, 1024)
            kT = s.sb([128, 8448], BF16, name="kT")
            Vg = s.sb([128, NKT, 2, 65], BF16, name="Vg")
            qT = [s.sb([128, 2304], BF16, name="qT%d" % i) for i in range(3)]
            nqT = [s.sb([128, 2304], BF16, name="nqT%d" % i) for i in range(2)]
            nkT = [s.sb([128, 2816], BF16, name="nkT%d" % i) for i in range(2)]
            Vn = s.sb([128, NNT, 4, 65], BF16, name="Vn")
            s.memset(Vg[:, :, :, 64:65], 1.0)
            s.memset(Vn[:, :, :, 64:65], 1.0)
            hT = [s.sb([128, 8, 128], BF16, name="hT%d" % i) for i in range(2)]
            tcs = [s.sb([128, 384], name="tcos%d" % i) for i in range(2)]
            tsns = [s.sb([128, 384], name="tsin%d" % i) for i in range(2)]
            sms = [s.sb([128, 64], name="sm%d" % i) for i in range(2)]
            tc_, tsn, sm = tcs[0], tsns[0], sms[0]
            cnt = [0]

            def front(srcT, row, j, w=None):
                if w is None:
                    i = cnt[0]; cnt[0] += 1
                    w = i % 2
                    banks = (pb[0], pb[1])
                else:
                    banks = (pb[w], pb[w])
                q = "sp" if w == 0 else "act"
                x_ = xt[w]
                ht_ = hts[w]
                s.dma(q, x_[:], V(srcT.t[row:row + 128, :], srcT.tk))
                hb_ = hbs[w]
                s.tt(ht_[:], x_[:], mod[j][:, 1024:2048], ALU.mult, e="pool")
                s.tt(hb_[:], ht_[:], mod[j][:, 0:1024], ALU.add, e="dve")
                h_ = hT[w]
                for half in range(2):
                    p = banks[half]
                    with s.atomic():
                        pbf = p.t[:, 0:256].bitcast(BF16)
                        for k in range(4):
                            kk_ = half * 4 + k
                            s.tr(V(pbf[:, k * 128:(k + 1) * 128], p.tk), hb_[:, kk_ * 128:(kk_ + 1) * 128], identb[:])
                        s.cp(h_[:, half * 4:half * 4 + 4, :], V(pbf.rearrange("p (k t) -> p k t", k=4), p.tk),
                             e="act" if half == 0 else "dve")
                return x_, h_

            def proj(h_, c0, ncols, dst, wsrc=None, w=None):
                wsrc = wsrc or wbuf
                o = 0
                bi = 2
                while o < ncols:
                    n = min(512, ncols - o)
                    p = pb[bi] if w is None else pb[2 + w]
                    with s.atomic():
                        for k in range(8):
                            s.mm(p[:, 0:n], h_[:, k, :], wsrc[:, k, c0 + o:c0 + o + n], start=(k == 0), stop=(k == 7))
                        s.cp(dst[:, o:o + n], p[:, 0:n], e="act" if bi == 2 else "dve")
                    o += n
                    bi = 5 - bi

            def rms_rope(src, H, gtab, scale_mode, rope, dst, w=0):
                sq = sqs[w]; sm = sms[w]; tc_ = tcs[w]; tsn = tsns[w]
                s.act(sq[:, 0:H * 64], src, AF.Square)
                s.red(sm[:, 0:H], V(sq.t[:, 0:H * 64].rearrange("p (h d) -> p h d", d=64), sq.tk), ALU.add)
                if scale_mode == "k":
                    s.ts(sm[:, 0:H], sm[:, 0:H], 1.0 / 64, ALU.mult, 1e-6, ALU.add)
                else:
                    s.ts(sm[:, 0:H], sm[:, 0:H], 64e-6, ALU.add)
                s.act(sm[:, 0:H], sm[:, 0:H], AF.Sqrt)
                s.recip(sm[:, 0:H], sm[:, 0:H])
                for h in range(H):
                    s.stt(V(dst.ap[:, h * 64:(h + 1) * 64], dst.tk), V(src.ap[:, h * 64:(h + 1) * 64], src.tk), sm[:, h:h + 1],
                          gtab[:, h * 64:(h + 1) * 64], ALU.mult, ALU.mult)
                if rope is not None:
                    cos_d, sin_d, rowfn = rope
                    s.dma("sp", tc_[:, 0:H * 64], cos_d[rowfn:rowfn + 128, :])
                    s.dma("act", tsn[:, 0:H * 64], sin_d[rowfn:rowfn + 128, :])
                    t1 = sq
                    v4 = lambda ap: ap.rearrange("p (g a d) -> p g a d", a=2, d=16)
                    d4 = v4(dst.ap); s4 = v4(tsn.t[:, 0:H * 64]); t4 = v4(t1.t[:, 0:H * 64])
                    s.tt(V(t4[:, :, 0, :], t1.tk), V(d4[:, :, 1, :], dst.tk), V(s4[:, :, 0, :], tsn.tk), ALU.mult, e="pool")
                    s.tt(V(t4[:, :, 1, :], t1.tk), V(d4[:, :, 0, :], dst.tk), V(s4[:, :, 1, :], tsn.tk), ALU.mult, e="pool")
                    s.tt(dst, dst, tc_[:, 0:H * 64], ALU.mult)
                    s.tt(dst, dst, t1[:, 0:H * 64], ALU.add)

            pt = [s.sb([128, 512], BF16, name="pt%d" % i) for i in range(3)]
            rcp = s.sb([128, 8], name="rcp")
            bias_t = [s.sb([128, 768], name="bias%d" % i) for i in range(2)]
            hbs.extend([_A(bias_t[i].t[:, 0:512].bitcast(BF16), bias_t[i].tk) for i in range(2)])
            ptc = [0]

            load_w(wbuf, wb[l], 768, 256)

            def k_body(t):
                w = t % 2
                j = 1 if t < 2 else 0
                x_, h_ = front(Xc, 128 * t if t < 2 else lat_row(128 * (t - 2)), j, w)
                pj = pe[0] if w == 0 else pjB
                proj(h_, 0, 256, pj, w=w)
                rope = None if t < 2 else (cosK, sinK, (t - 2) * 128)
                kr = hts[w]
                rms_rope(pj[:, 0:128], 2, GK, "k", rope, kr[:, 0:128], w)
                p = pb[6 + w]
                with s.atomic():
                    s.tr(p[:, 0:128], kr[:, 0:128], ident[:])
                    s.cp(kT[:, t * 128:(t + 1) * 128], p[:, 0:128], e="act")
                s.cp(Vg[:, t, :, 0:64], V(pj.t[:, 128:256].rearrange("p (g d) -> p g d", d=64), pj.tk), e="pool")

            ilB.run([(lambda t=t: k_body(t)) for t in range(NKT)], 2)
            load_w(wbuf, wb[l], 1664, 512)

            def n_body(t):
                w = t % 2
                j = 1 if t >= 20 else 0
                if t >= 20:
                    x_, h_ = front(Xc, 128 * (t - 20), j, w)
                else:
                    x_, h_ = front(XSc, 128 * t, j, w)
                pj = pe[0] if w == 0 else pjB
                proj(h_, 0, 512, pj, w=w)
                for pr_ in range(2):
                    p = pb[6 + w]
                    with s.atomic():
                        s.tr(p[:, 0:128], pj[:, pr_ * 128:(pr_ + 1) * 128], ident[:])
                        s.cp(nkT[pr_][:, t * 128:(t + 1) * 128], p[:, 0:128], e="act" if pr_ == 0 else "dve")
                s.cp(Vn[:, t, :, 0:64], V(pj.t[:, 256:512].rearrange("p (g d) -> p g d", d=64), pj.tk), e="pool")

            ilB.run([(lambda t=t: n_body(t)) for t in range(NNT)], 2)
            for sl in range(6):
                hh_ = (sl // 2) + 3 * (sl % 2)
                load_w(wbuf, wb[l], 384 + hh_ * 64, 64, dcol=sl * 64)
            load_w(wbuf, wb[l], 1408, 256, dcol=384)
            own_src = lambda t: ((Xc, 128 * (t - 16)) if t >= 16 else (XSc, 256 + 128 * t))

            def q_body(t):
                w = t % 2
                j = 1 if t >= 16 else 0
                x_, h_ = front(*own_src(t), j, w)
                pj = pe[0] if w == 0 else pjB
                proj(h_, 0, 640, pj, w=w)
                rope = None if t >= 16 else (cosQ, sinQ, 128 * t)
                qr = hts[w]
                rms_rope(pj[:, 0:384], 6, GQ, "q", rope, qr[:, 0:384], w)
                for pr_ in range(3):
                    p = pb[6 + w]
                    with s.atomic():
                        s.tr(p[:, 0:128], qr[:, pr_ * 128:(pr_ + 1) * 128], ident[:])
                        s.cp(qT[pr_][:, t * 128:(t + 1) * 128], p[:, 0:128], e="act" if pr_ % 2 == 0 else "dve")
                s.ts(qr[:, 384:640], pj[:, 384:640], 0.125, ALU.mult, e="pool")
                for pr_ in range(2):
                    p = pb[6 + w]
                    with s.atomic():
                        s.tr(p[:, 0:128], qr[:, 384 + pr_ * 128:384 + (pr_ + 1) * 128], ident[:])
                        s.cp(nqT[pr_][:, t * 128:(t + 1) * 128], p[:, 0:128], e="act" if pr_ == 0 else "dve")

            ilB.run([(lambda t=t: q_body(t)) for t in range(NOWN)], 2)

            def attend(qsrc, g, qc0, nq, chunks, ksrc, vsrc_fn, bias_fn, dst_fn):
                nqs = nq // 128
                lo, hi = 64 * g, 64 * g + 64
                n_ = len(chunks)
                for ci in range(n_ + 1):
                    if ci < n_:
                        kc0, nk, cid = chunks[ci]
                        ps = pb[ci % 3]
                        bsrc = bias_fn(cid, nk) if bias_fn else None
                        s.mm(ps[0:nk, 0:nq], ksrc[lo:hi, kc0:kc0 + nk], qsrc[lo:hi, qc0:qc0 + nq], start=True, stop=(bsrc is None))
                        if bsrc is not None:
                            s.mm(ps[0:nk, 0:nq], bsrc, ident[:, 0:nq], start=False, stop=True)
                    if ci >= 1:
                        kc0p, nkp, cidp = chunks[ci - 1]
                        p_ = pt[ptc[0] % 3]; ptc[0] += 1
                        s.act(p_[0:nkp, 0:nq], pb[(ci - 1) % 3][0:nkp, 0:nq], AF.Exp)
                        for qs in range(nqs):
                            s.mm(pb[4 + qs][:, 0:65], p_[0:nkp, qs * 128:(qs + 1) * 128], vsrc_fn(cidp, nkp),
                                 start=(ci == 1), stop=(ci == n_))
                for qs in range(nqs):
                    s.recip(rcp[:, qs:qs + 1], pb[4 + qs][:, 64:65])
                    s.ts(dst_fn(qs), pb[4 + qs][:, 0:64], rcp[:, qs:qs + 1], ALU.mult)

            oT = _A(pe[0].t[0:65, 0:512], pe[0].tk)
            fin = [0]

            def attend_T(qsrc, g, qc0, nq, chunks, ksrc, vsrc_fn, dst_fn):
                nqs = nq // 128
                lo, hi = 64 * g, 64 * g + 64
                po = pb[4]
                n_ = len(chunks)
                for ci in range(n_ + 1):
                    if ci < n_:
                        kc0, nk, cid = chunks[ci]
                        s.mm(pb[ci % 3][0:nk, 0:nq], ksrc[lo:hi, kc0:kc0 + nk], qsrc[lo:hi, qc0:qc0 + nq])
                    if ci >= 1:
                        kc0p, nkp, cidp = chunks[ci - 1]
                        p_ = pt[ptc[0] % 3]; ptc[0] += 1
                        s.act(p_[0:nkp, 0:nq], pb[(ci - 1) % 3][0:nkp, 0:nq], AF.Exp)
                        s.mm(po[0:65, 0:nq], vsrc_fn(cidp, nkp), p_[0:nkp, 0:nq], start=(ci == 1), stop=(ci == n_))
                s.cp(oT[:, 0:nq], po[0:65, 0:nq], e="dve")
                for qs in range(nqs):
                    pf = pb[5 + fin[0] % 3]; fin[0] += 1
                    s.tr(pf[:, 0:65], oT[:, qs * 128:(qs + 1) * 128], ident[0:65, 0:65])
                    s.recip(rcp[:, qs:qs + 1], pf[:, 64:65])
                    s.ts(dst_fn(qs), pf[:, 0:64], rcp[:, qs:qs + 1], ALU.mult)

            def attend_T2(qsrc, qc0, nq, chunks, ksrc, vsrc_fn, dst_fn):
                nqs = nq // 128
                n_ = len(chunks)
                po = [pb[4], pb[5]]
                for ci in range(n_ + 1):
                    if ci < n_:
                        kc0, nk, cid = chunks[ci]
                        for g in range(2):
                            lo, hi = 64 * g, 64 * g + 64
                            s.mm(pb[(2 * ci + g) % 4][0:nk, 0:nq], ksrc[lo:hi, kc0:kc0 + nk], qsrc[lo:hi, qc0:qc0 + nq])
                    if ci >= 1:
                        kc0p, nkp, cidp = chunks[ci - 1]
                        for g in range(2):
                            p_ = pt[ptc[0] % 3]; ptc[0] += 1
                            s.act(p_[0:nkp, 0:nq], pb[(2 * (ci - 1) + g) % 4][0:nkp, 0:nq], AF.Exp)
                            s.mm(po[g][0:65, 0:nq], vsrc_fn(cidp, nkp, g), p_[0:nkp, 0:nq], start=(ci == 1), stop=(ci == n_))
                for g in range(2):
                    s.cp(oT[:, 0:nq], po[g][0:65, 0:nq], e="dve")
                    for qs in range(nqs):
                        pf = pb[6 + fin[0] % 2]; fin[0] += 1
                        s.tr(pf[:, 0:65], oT[:, qs * 128:(qs + 1) * 128], ident[0:65, 0:65])
                        s.recip(rcp[:, qs:qs + 1], pf[:, 64:65])
                        s.ts(dst_fn(qs, g), pf[:, 0:64], rcp[:, qs:qs + 1], ALU.mult)

            for qg in range(4):
                for pr_ in range(3):
                    attend_T2(qT[pr_], qg * 512, 512, [(c * 128, 128, c) for c in range(NKT)], kT,
                              lambda cid, nk, g: Vg[0:nk, cid, g, :],
                              lambda qs, g, qg=qg, pr_=pr_: YAN[:, qg * 4 + qs, (pr_ + 3 * g) * 64:(pr_ + 3 * g + 1) * 64])
            for h in range(6):
                pr_, g = h % 3, h // 3
                attend_T(qT[pr_], g, 2048, 256, [(c * 128, 128, c) for c in range(2)], kT,
                         lambda cid, nk, g=g: Vg[0:nk, cid, g, :],
                         lambda qs, h=h: YAN[:, 16 + qs, h * 64:(h + 1) * 64])
            bc_ = [0]
            for i in range(16):
                chs = na_chunks(i)
                base = chs[0][0] * 128
                cls = na_class(i)
                for hn in range(4):
                    pr_, g = hn // 2, hn % 2
                    bt_ = bias_t[bc_[0] % 2]; bc_[0] += 1
                    nkeys = sum(nk for _, nk in chs)
                    s.dma("sp" if hn % 2 == 0 else "act", bt_[:, 0:nkeys], nab[l, cls, hn, :, 0:nkeys])
                    chunks = [(c * 128, nk, c) for c, nk in chs] + [(20 * 128, 128, 20), (21 * 128, 128, 21)]
                    attend(nqT[pr_], g, i * 128, 128, chunks, nkT[pr_],
                           lambda cid, nk, hn=hn: Vn[0:nk, cid, hn, :],
                           lambda cid, nk, bt_=bt_, base=base: (None if cid >= 20 else bt_[:, cid * 128 - base:cid * 128 - base + nk]),
                           lambda qs, i=i, hn=hn: YAN[:, i, 384 + hn * 64:384 + (hn + 1) * 64])
            for hn in range(4):
                pr_, g = hn // 2, hn % 2
                attend(nqT[pr_], g, 2048, 256, [(20 * 128, 128, 20), (21 * 128, 128, 21)], nkT[pr_],
                       lambda cid, nk, hn=hn: Vn[0:nk, cid, hn, :], None,
                       lambda qs, hn=hn: YAN[:, 16 + qs, 384 + hn * 64:384 + (hn + 1) * 64])
            load_w(wbuf, wb[l], 0, 384)
            load_w(wbuf, wb[l], 1024, 384, dcol=384)
            load_w(wbuf, wb[l], 2176, 256, dcol=768)
            LNG = _A(kT.t[:, 0:2048].bitcast(F32), kT.tk); s.dma("sp", LNG[:], lng[l])
            LNB = _A(kT.t[:, 2048:4096].bitcast(F32), kT.tk); s.dma("act", LNB[:], lnb[l])
            Ff = _A(kT.t[:, 4096:5632].bitcast(F32).rearrange("p (s c) -> p s c", s=2), kT.tk)
            Bk = _A(kT.t[:, 5632:7168].bitcast(F32), kT.tk)
            cen = _A(kT.t[:, 7168:7936].bitcast(F32), kT.tk)
            ysb = _A(qT[1].t[:, 0:768].bitcast(F32), qT[1].tk)
            bsb = _A(qT[1].t[:, 768:1536].bitcast(F32), qT[1].tk)
            sqb = _A(qT[2].t[:, 0:768].bitcast(F32), qT[2].tk)
            YgT = _A(qT[0].t[:, 0:1024].rearrange("p (k t) -> p k t", k=8), qT[0].tk)
            for t in range(NOWN):
                j = 1 if t >= 16 else 0
                x_, h_ = front(*own_src(t), j)
                G = pe[0]
                proj(h_, 0, 1024, G)
                s.act(G[:], G[:], AF.Silu)
                for sr in range(2):
                    q = "sp" if sr == 0 else "act"
                    if t >= 16:
                        s.dma(q, Ff[:, sr, :], YGc[sr * 256 + 128 * (t - 16):sr * 256 + 128 * (t - 16) + 128, :])
                        rb = (2 + sr) * 256 + 128 - 128 * (t - 16)
                        s.dma(q, Bk[:, sr * 384:(sr + 1) * 384], YGc[rb:rb + 128, :])
                    else:
                        s.dma(q, Ff[:, sr, :], YF[sr][128 * t:128 * t + 128, :])
                        s.dma(q, Bk[:, sr * 384:(sr + 1) * 384], YBk[sr][1920 - 128 * t:1920 - 128 * t + 128, :])
                for sr in range(2):
                    s.mm(pb[6 + sr][:, 0:384], jmat[:], Bk[:, sr * 384:(sr + 1) * 384])
                v2 = lambda A_: V(A_.t[:, 0:384].rearrange("p (s c) -> p s c", s=2), A_.tk)
                for sr in range(2):
                    s.tt(ysb[:, sr * 192:(sr + 1) * 192], Ff[:, sr, 0:192], pb[6 + sr][:, 0:192], ALU.add)
                    s.tt(bsb[:, sr * 192:(sr + 1) * 192], Ff[:, sr, 192:384], pb[6 + sr][:, 192:384], ALU.add)
                v3 = lambda A_: V(A_.t[:, 0:384].rearrange("p (h d) -> p h d", d=64), A_.tk)
                s.red(sm[:, 0:6], v3(ysb), ALU.add)
                s.ts(sm[:, 0:6], sm[:, 0:6], 1.0 / 64, ALU.mult)
                s.tt(v3(cen), v3(ysb), V(sm.t[:, 0:6].unsqueeze(2).to_broadcast([128, 6, 64]), sm.tk), ALU.subtract)
                s.tt(sqb[:], cen[:], cen[:], ALU.mult, e="pool")
                s.red(sm[:, 8:14], v3(sqb), ALU.add)
                s.ts(sm[:, 8:14], sm[:, 8:14], 1.0 / 64, ALU.mult, 64e-5, ALU.add)
                s.act(sm[:, 8:14], sm[:, 8:14], AF.Sqrt)
                s.recip(sm[:, 8:14], sm[:, 8:14])
                s.tt(v3(cen), v3(cen), V(sm.t[:, 8:14].unsqueeze(2).to_broadcast([128, 6, 64]), sm.tk), ALU.mult)
                s.tt(cen[:], cen[:], GNG[:], ALU.mult)
                s.tt(cen[:], cen[:], GNB[:], ALU.add)
                s.tt(cen[:], cen[:], bsb[:], ALU.add)
                Yg = pe[1]
                s.tt(Yg[:, 0:384], cen[:], G[:, 0:384], ALU.mult)
                s.tt(Yg[:, 384:1024], YAN[:, t, :], G[:, 384:1024], ALU.mult)
                for half in range(2):
                    p = pb[half]
                    for k in range(4):
                        kk_ = half * 4 + k
                        s.tr(p[:, k * 128:(k + 1) * 128], Yg[:, kk_ * 128:(kk_ + 1) * 128], ident[:])
                    s.cp(YgT[:, half * 4:half * 4 + 4, :], V(p.t[:, :].rearrange("p (k t) -> p k t", k=4), p.tk),
                         e="act" if half == 0 else "dve")
                yo = pe[0]
                proj(YgT, 0, 1024, yo, wsrc=woutb)
                s.tt(yo[:], yo[:], mod[j][:, 2048:3072], ALU.mult)
                z = pe[1]
                s.stt(z[:], x_[:], ALPHA, yo[:], ALU.mult, ALU.add)
                s.red(sm[:, 16:17], z[:], ALU.add)
                s.ts(sm[:, 16:17], sm[:, 16:17], 1.0 / 1024, ALU.mult)
                s.ts(z[:], z[:], sm[:, 16:17], ALU.subtract)
                s.tt(yo[:], z[:], z[:], ALU.mult, e="pool")
                s.red(sm[:, 17:18], yo[:], ALU.add)
                s.ts(sm[:, 17:18], sm[:, 17:18], 1.0 / 1024, ALU.mult, 1e-5, ALU.add)
                s.act(sm[:, 17:18], sm[:, 17:18], AF.Sqrt)
                s.recip(sm[:, 17:18], sm[:, 17:18])
                s.stt(z[:], z[:], sm[:, 17:18], LNG[:], ALU.mult, ALU.mult)
                s.tt(z[:], z[:], LNB[:], ALU.add)
                q = "sp" if t % 2 == 0 else "act"
                if t < 16:
                    if last:
                        tickets.append(s.dma(q, out[128 * t:128 * t + 128, :], z[:]))
                    else:
                        s.dma(q, XN[128 * t:128 * t + 128, :], z[:])
                        if t % 2 == 1:
                            k = t // 2
                            s.allgather(XN[256 * k:256 * (k + 1), :], Xn[512 + 1024 * k:512 + 1024 * (k + 1), :], GROUPS)
                elif not last:
                    s.dma(q, Xn[128 * (t - 16):128 * (t - 16) + 128, :], z[:])
    s.finish(tickets)
    s.close()
    return nc


def rope_tables():
    t = np.arange(8192)
    row = (t // 64).astype(np.float32); col = (t % 64).astype(np.float32)
    inv = (10000.0 ** (-np.arange(16, dtype=np.float32) / 16)).astype(np.float32)
    ar = row[:, None] * inv; ac = col[:, None] * inv
    ang = np.concatenate([ar, ar, ac, ac], axis=-1).astype(np.float32)
    cos = np.cos(ang).astype(np.float32); sin = np.sin(ang).astype(np.float32)
    sgn = np.concatenate([-np.ones(16), np.ones(16), -np.ones(16), np.ones(16)]).astype(np.float32)
    return cos, sin * sgn


def na_bias(rpb, j):
    NEG = -30000.0
    out = np.full((5, 4, 128, 768), NEG, np.float32)
    tiles = {0: 0, 1: 1, 2: 7, 3: 14, 4: 15}
    for cls, i in tiles.items():
        chs = na_chunks(i)
        srow0 = chs[0][0] * 2
        nkeys = sum(nk for _, nk in chs)
        qrow_l = np.repeat(np.array([2 * i, 2 * i + 1]), 64)
        qcol = np.tile(np.arange(64), 2)
        r = 32 * j + qrow_l
        r_start = np.clip(r - 4, 0, 120)
        c_start = np.clip(qcol - 8, 0, 48)
        key = np.arange(nkeys)
        krow = (32 * j - 4) + srow0 + key // 64
        kcol = key % 64
        dr = krow[None, :] - r[:, None] + 7
        dc = kcol[None, :] - qcol[:, None] + 15
        inwin = ((krow[None, :] >= r_start[:, None]) & (krow[None, :] < r_start[:, None] + 8) &
                 (kcol[None, :] >= c_start[:, None]) & (kcol[None, :] < c_start[:, None] + 16))
        drc = np.clip(dr, 0, 14); dcc = np.clip(dc, 0, 30)
        for h in range(4):
            vals = rpb[h][drc, dcc]
            out[cls, h, :, 0:nkeys] = np.where(inwin, vals, NEG)
    return out


def consts_A():
    idx = np.arange(64)
    inclT = (idx[:, None] <= idx[None, :]).astype(np.float32)
    strictT = (idx[:, None] < idx[None, :]).astype(np.float32)
    strict = strictT.T.copy()
    mask = np.concatenate([inclT, strictT, inclT, -strictT, -strict], axis=1)
    rmask = np.ones((64, 256), np.float32); rmask[:, 0::64] = 0.0
    ident = np.eye(64, dtype=np.float32)
    return np.concatenate([ident, mask, mask, mask, rmask] + [ident] * 6, axis=1).astype(np.float32)


def host_F(inp, depth=4):
    cos, sinS = rope_tables()
    bc = lambda v: np.ascontiguousarray(np.broadcast_to(v[None, :], (128, v.shape[0]))).astype(np.float32)
    L = range(depth)
    shared = dict(
        wmod=np.ascontiguousarray(inp['w_mod'][:depth]),
        bmod=np.stack([bc(inp['b_mod'][l]) for l in L]),
        wb=np.ascontiguousarray(inp['w_in'][:depth, :, 1280:]),
        wout=np.ascontiguousarray(inp['w_out'][:depth]),
        cstA=consts_A(),
        cosK=np.tile(cos, (1, 2)), sinK=np.tile(sinS, (1, 2)),
        gk=np.stack([bc(np.tile(inp['gqa_k_norm'][l], 2)) for l in L]),
        gq=np.stack([bc(np.tile(inp['gqa_q_norm'][l], 6)) for l in L]),
        gng=np.stack([bc(inp['rwkv_gn_g'][l]) for l in L]), gnb=np.stack([bc(inp['rwkv_gn_b'][l]) for l in L]),
        lng=np.stack([bc(inp['ln_g'][l]) for l in L]), lnb=np.stack([bc(inp['ln_b'][l]) for l in L]),
        ident=np.eye(128, dtype=np.float32), jmat=np.ascontiguousarray(np.eye(128, dtype=np.float32)[::-1]),
    )
    per_batch = []
    for b in range(2):
        xa0 = np.zeros((XROWS, 1024), np.float32)
        xa0[0:256] = inp['ctx'][b]
        xa0[512:8704] = inp['x'][b].reshape(4, 8, 256, 1024).transpose(1, 0, 2, 3).reshape(8192, 1024)
        cvec = np.stack([inp['c'][b], inp['c_ctx']], axis=1)
        cv = np.ascontiguousarray(cvec.reshape(8, 128, 2).transpose(1, 0, 2).reshape(128, 16))
        per_batch.append(dict(xa0=xa0, cv=cv))
    nabs = [np.stack([na_bias(inp['na_rpb'][l], j) for l in L]) for j in range(4)]
    maps = []
    for c in range(8):
        b, r = c // 4, c % 4
        d, hh = r // 2, r % 2
        heads = [3 * hh + i for i in range(3)]
        cols = []
        for comp in (0, 384, 768):
            for h in heads:
                cols += list(range(comp + h * 64, comp + h * 64 + 64))
        cols += list(range(1152 + 32 * d, 1152 + 32 * d + 32))
        cols += list(range(1216 + 32 * d, 1216 + 32 * d + 32))
        cols = np.array(cols)
        hcols = np.concatenate([np.arange(h * 64, (h + 1) * 64) for h in heads])
        wa = np.ascontiguousarray(inp['w_in'][:depth][:, :, cols])
        cw = np.zeros((depth, 64, 33), np.float32); pv = np.zeros((depth, 64, 15), np.float32)
        for l in L:
            conv = inp['rwkv_conv'][l][:, cols]
            if d == 1:
                conv = conv[::-1]
            for ci in range(9):
                cw[l, :, ci * 3:ci * 3 + 3] = conv[:, ci * 64:(ci + 1) * 64].T
            cw[l, 0:32, 27:30] = conv[:, 576:608].T
            cw[l, 0:32, 30:33] = conv[:, 608:640].T
            for i, h in enumerate(heads):
                hs = slice(h * 64, (h + 1) * 64)
                pv[l, :, i * 5 + 0] = inp['decay_w0'][l][d, hs]
                pv[l, :, i * 5 + 1] = inp['iclr_a0'][l][d, hs]
                pv[l, :, i * 5 + 2] = inp['rwkv_k_k'][l][hs]
                pv[l, :, i * 5 + 3] = inp['rwkv_k_a'][l][hs]
                pv[l, :, i * 5 + 4] = inp['rwkv_r_k'][l][h]
        w2 = np.ascontiguousarray(inp['decay_w2'][:depth, d][:, :, hcols])
        a2 = np.ascontiguousarray(inp['iclr_a2'][:depth, d][:, :, hcols])
        own = slice(2048 * r, 2048 * r + 2048)
        jd = np.eye(128, dtype=np.float32)
        if d == 1:
            jd = np.ascontiguousarray(jd[::-1])
        dsel = np.zeros((128, 2), np.float32); dsel[:, d] = 1.0
        xs0 = np.zeros((2560, 1024), np.float32)
        lo = 2048 * r - 256
        for sr_ in range(2560):
            pass
        a0, a1 = max(lo, 0), min(lo + 2560, 8192)
        xs0[a0 - lo:a1 - lo] = inp['x'][b][a0:a1]
        m = dict(shared)
        m['xs0'] = xs0
        m.update(per_batch[b])
        m.update(wa=wa, cw=cw, pv=pv, w2=w2, a2=a2, nab=nabs[r],
                 cosQ=np.tile(cos[own], (1, 6)), sinQ=np.tile(sinS[own], (1, 6)), jd=jd, dsel=dsel)
        maps.append(m)
    return maps


from concourse.bass_utils import run_bass_kernel_spmd

_NC = {}


def kernel(**inputs):
    inp = {k: np.asarray(v, dtype=np.float32) for k, v in inputs.items()}
    if 'F' not in _NC:
        _NC['F'] = build_F(4)
    maps = host_F(inp, 4)
    res = run_bass_kernel_spmd(_NC['F'], maps, core_ids=list(range(8))).results
    x = np.stack([np.concatenate([res[b * 4 + r]["out"] for r in range(4)], axis=0) for b in range(2)])
    return np.ascontiguousarray(x.astype(np.float32))
```

```python
import contextlib
import numpy as np
import concourse.bass as bass
import concourse.mybir as mybir

F32 = mybir.dt.float32
BF16 = mybir.dt.bfloat16
AF = mybir.ActivationFunctionType
ALU = mybir.AluOpType
AX = mybir.AxisListType


class Tk:
    __slots__ = ("w", "r", "name", "excl", "acc")

    def __init__(self, name=""):
        self.w = None
        self.r = {}
        self.name = name
        self.excl = False
        self.acc = {}


class T:
    def __init__(self, S, t, name):
        self.t = t
        self.tk = Tk(name)
        self.name = name

    def __getitem__(self, idx):
        return V(self.t[idx], self.tk)


class V:
    __slots__ = ("ap", "tk")

    def __init__(self, ap, tk):
        self.ap = ap
        self.tk = tk


import threading


class _Worker(threading.Thread):
    def __init__(self, il, fn):
        super().__init__(daemon=True)
        self.il = il
        self.fn = fn
        self.go = threading.Event()
        self.done = False
        self.exc = None

    def run(self):
        self.go.wait(); self.go.clear()
        try:
            self.fn()
        except BaseException as e:
            self.exc = e
        self.done = True
        self.il.main_ev.set()

    def pause(self):
        self.il.main_ev.set()
        self.go.wait(); self.go.clear()


class Interleaver:
    def __init__(self, s):
        self.s = s
        self.main_ev = threading.Event()
        self.cur = None

    def run(self, fns, width):
        pending = list(fns)
        active = []
        self.s.yield_hook = self._hook
        try:
            while pending or active:
                while pending and len(active) < width:
                    w = _Worker(self, pending.pop(0)); w.start(); active.append(w)
                for w in list(active):
                    self.cur = w
                    self.main_ev.clear()
                    w.go.set()
                    self.main_ev.wait()
                    if w.exc is not None:
                        raise w.exc
                    if w.done:
                        active.remove(w)
        finally:
            self.s.yield_hook = None
            self.cur = None

    def _hook(self):
        w = self.cur
        if w is not None and threading.current_thread() is w and self.s.atomic_depth == 0:
            w.pause()


class S:
    ENG = ("pe", "act", "dve", "pool", "sp")
    yield_hook = None
    atomic_depth = 0

    @contextlib.contextmanager
    def atomic(self):
        self.atomic_depth += 1
        try:
            yield
        finally:
            self.atomic_depth -= 1
            if self.atomic_depth == 0 and self.yield_hook is not None:
                self.yield_hook()

    def __init__(self, nc):
        self.nc = nc
        self.es = contextlib.ExitStack()
        self.eng = {"pe": nc.tensor, "act": nc.scalar, "dve": nc.vector, "pool": nc.gpsimd, "sp": nc.sync}
        self.sem = {e: self.es.enter_context(nc.semaphore("s_" + e)) for e in self.ENG}
        self.cnt = {e: 0 for e in self.ENG}
        self.dq = {}
        for q in ("sp", "act", "pool"):
            sems = [self.es.enter_context(nc.semaphore("d_%s%d" % (q, i))) for i in range(8)]
            self.dq[q] = dict(sems=sems, cnt=[0] * 8, nxt=0)
            for i, s_ in enumerate(sems):
                self.sem[(q, i)] = s_
        self.waited = {}
        self.cur = self.es
        self.cc_keys = []
        self.n_tiles = 0
        self.n_instr = 0
        self.n_wait = 0

    def sb(self, shape, dt=F32, name=None):
        self.n_tiles += 1
        name = "%s_%d" % (name or "t", self.n_tiles)
        t = self.cur.enter_context(self.nc.sbuf_tensor("sb_" + name, list(shape), dt))
        return T(self, t, name)

    def dram(self, shape, name, dt=F32):
        self.n_tiles += 1
        t = self.nc.dram_tensor("%s_%d" % (name, self.n_tiles), list(shape), dt)
        return T(self, t.ap(), name)

    @contextlib.contextmanager
    def phase(self):
        prev = self.cur
        self.cur = contextlib.ExitStack()
        try:
            yield
        finally:
            self.barrier()
            self.cur.close()
            self.cur = prev

    def barrier(self):
        for e in self.ENG:
            for e2 in self.ENG:
                if e2 != e and self.cnt[e2] > 0:
                    self._wait(e, e2, self.cnt[e2])
            for q, d in self.dq.items():
                for i, c in enumerate(d["cnt"]):
                    if c > 0:
                        self._wait(e, (q, i), c)

    def allgather(self, src, dst, groups):
        if "cc" not in self.dq:
            sems = [self.es.enter_context(self.nc.semaphore("cc%d" % i)) for i in range(8)]
            self.dq["cc"] = dict(sems=sems, cnt=[0] * 8, nxt=0)
            for i, s_ in enumerate(sems):
                self.sem[("cc", i)] = s_
        d = self.dq["cc"]
        i = d["nxt"]; d["nxt"] = (i + 1) % 8
        key = ("cc", i)
        self._wait("pool", key, d["cnt"][i])
        self._deps("pool", [src], [dst])
        ins = self.nc.gpsimd.collective_compute("AllGather", mybir.AluOpType.bypass, replica_groups=groups,
                                                ins=[src.ap.opt()], outs=[dst.ap.opt()])
        d["cnt"][i] += 1
        ins.then_inc(self.sem[key])
        self._mark((key, d["cnt"][i]), [src], [dst])
        self.n_instr += 1
        return (key, d["cnt"][i])

    def ps(self, shape, dt=F32, name=None):
        self.n_tiles += 1
        name = name or "p%d" % self.n_tiles
        t = self.es.enter_context(self.nc.psum_tensor("ps_" + name, list(shape), dt))
        tt_ = T(self, t, name)
        tt_.tk.excl = True
        return tt_

    def close(self):
        self.es.close()

    def _wait(self, e, key, val):
        if val is None:
            return
        k = (e, key)
        if self.waited.get(k, 0) >= val:
            return
        self.waited[k] = val
        self.eng[e].wait_ge(self.sem[key], val)
        self.n_wait += 1

    def _deps(self, e, reads, writes, pe_acc=False):
        for v in list(reads) + list(writes):
            if v.tk.excl:
                for e2, n2 in v.tk.acc.items():
                    if e2 == e and e == "pe":
                        continue
                    self._wait(e, e2, n2)
        reads = [v for v in reads if not v.tk.excl]
        writes = [v for v in writes if not v.tk.excl]
        for v in reads:
            w = v.tk.w
            if w is not None:
                self._wait(e, w[0], w[1])
        for v in writes:
            tk = v.tk
            if tk.w is not None:
                if not (pe_acc and tk.w[0] == "pe" and e == "pe"):
                    self._wait(e, tk.w[0], tk.w[1])
            for re_, rn in tk.r.items():
                if re_ == e and e == "pe":
                    continue
                self._wait(e, re_, rn)

    def _mark(self, ticket, reads, writes):
        for v in list(reads) + list(writes):
            if v.tk.excl:
                v.tk.acc[ticket[0]] = ticket[1]
        reads = [v for v in reads if not v.tk.excl]
        writes = [v for v in writes if not v.tk.excl]
        for v in reads:
            v.tk.r[ticket[0]] = ticket[1]
        for v in writes:
            v.tk.w = ticket
            v.tk.r = {}

    def op(self, e, fn, reads, writes, pe_acc=False):
        reads = [v for v in reads if isinstance(v, V)]
        self._deps(e, reads, writes, pe_acc)
        ins = fn()
        self.cnt[e] += 1
        ins.then_inc(self.sem[e], 1)
        self._mark((e, self.cnt[e]), reads, writes)
        self.n_instr += 1
        if self.yield_hook is not None:
            self.yield_hook()
        return ins

    def dma(self, q, out, in_, **kw):
        d = self.dq[q]
        i = d["nxt"]
        d["nxt"] = (i + 1) % len(d["sems"])
        key = (q, i)
        self._wait(q, key, d["cnt"][i])
        reads = [in_] if isinstance(in_, V) else []
        writes = [out] if isinstance(out, V) else []
        self._deps(q, reads, writes)
        oa = out.ap if isinstance(out, V) else out
        ia = in_.ap if isinstance(in_, V) else in_
        ins = self.eng[q].dma_start(out=oa, in_=ia, **kw)
        d["cnt"][i] += 16
        ins.then_inc(self.sem[key], 16)
        self._mark((key, d["cnt"][i]), reads, writes)
        self.n_instr += 1
        if self.yield_hook is not None:
            self.yield_hook()
        return (key, d["cnt"][i])

    def wait_ticket(self, e, ticket):
        self._wait(e, ticket[0], ticket[1])

    def mm(self, out, lhsT, rhs, start=True, stop=True, **kw):
        return self.op("pe", lambda: self.nc.tensor.matmul(out.ap, lhsT.ap, rhs.ap, start=start, stop=stop, **kw),
                       [lhsT, rhs], [out], pe_acc=not start)

    def tr(self, out, in_, ident):
        return self.op("pe", lambda: self.nc.tensor.transpose(out.ap, in_.ap, ident.ap), [in_, ident], [out])

    def act(self, out, in_, func, bias=None, scale=None, accum_out=None, e="act"):
        kw = {}
        rd = [in_]
        if bias is not None:
            kw["bias"] = bias.ap if isinstance(bias, V) else bias
            rd.append(bias)
        if scale is not None:
            kw["scale"] = scale.ap if isinstance(scale, V) else scale
            rd.append(scale)
        wr = [out]
        if accum_out is not None:
            kw["accum_out"] = accum_out.ap
            wr.append(accum_out)
        return self.op("act", lambda: self.nc.scalar.activation(out.ap, in_.ap, func, **kw), rd, wr)

    def _ve(self, e):
        return {"dve": self.nc.vector, "pool": self.nc.gpsimd, "act": self.nc.scalar}[e]

    def tt(self, out, a, b, op, e="dve"):
        return self.op(e, lambda: self._ve(e).tensor_tensor(out.ap, a.ap, b.ap, op), [a, b], [out])

    def ts(self, out, a, s1, op0, s2=None, op1=None, e="dve", accum_out=None):
        rd = [a, s1, s2]
        a1 = s1.ap if isinstance(s1, V) else s1
        a2 = s2.ap if isinstance(s2, V) else s2
        kw = {}
        wr = [out]
        if op1 is not None:
            kw["op1"] = op1
        if accum_out is not None:
            kw["accum_out"] = accum_out.ap
            wr.append(accum_out)
        return self.op(e, lambda: self._ve(e).tensor_scalar(out.ap, a.ap, a1, a2, op0, **kw), rd, wr)

    def stt(self, out, a, s, b, op0, op1, e="dve"):
        sa = s.ap if isinstance(s, V) else s
        return self.op(e, lambda: self._ve(e).scalar_tensor_tensor(out.ap, a.ap, sa, b.ap, op0, op1), [a, s, b], [out])

    def cp(self, out, in_, e="dve"):
        if e == "act":
            return self.op("act", lambda: self.nc.scalar.copy(out.ap, in_.ap), [in_], [out])
        return self.op(e, lambda: self._ve(e).tensor_copy(out.ap, in_.ap), [in_], [out])

    def memset(self, out, val, e="pool"):
        return self.op(e, lambda: self._ve(e).memset(out.ap, val), [], [out])

    def red(self, out, in_, op, axis=AX.X, e="dve"):
        return self.op(e, lambda: self._ve(e).tensor_reduce(out.ap, in_.ap, axis, op), [in_], [out])

    def recip(self, out, in_):
        return self.op("dve", lambda: self.nc.vector.reciprocal(out.ap, in_.ap), [in_], [out])

    def finish(self, tickets):
        for t in tickets:
            self._wait("sp", t[0], t[1])


A_DEC = 0.6065306597126334
ALPHA = (2 * 4) ** 0.25
NOWN = 18
NKT = 66
NNT = 22
GROUPS = [[0, 1, 2, 3], [4, 5, 6, 7]]
XROWS = 8960


def na_chunks(i):
    if i == 0:
        return [(c, 128) for c in range(0, 6)]
    if i == 1:
        return [(c, 128) for c in range(1, 6)]
    if i == 15:
        return [(c, 128) for c in range(14, 19)] + [(19, 64)]
    return [(c, 128) for c in range(i, i + 4)] + [(i + 4, 64)]


def na_class(i):
    return {0: 0, 1: 1, 14: 3, 15: 4}.get(i, 2)


def lat_row(tau):
    rho, rem = divmod(tau, 2048)
    k, i = divmod(rem, 256)
    return 512 + 1024 * k + 256 * rho + i


class _A:
    def __init__(self, ap, tk):
        self.t = ap; self.tk = tk

    def __getitem__(self, idx):
        return V(self.t[idx], self.tk)


def build_F(depth=4):
    nc = bass.Bass("TRN2", target_bir_lowering=False)
    dt = nc.dram_tensor
    I = lambda n, sh: dt(n, sh, F32, kind="ExternalInput").ap()
    xa0 = I("xa0", [XROWS, 1024]); xs0 = I("xs0", [2560, 1024])
    cv = I("cv", [128, 16]); wmod = I("wmod", [depth, 1024, 3072]); bmod = I("bmod", [depth, 128, 3072])
    wb = I("wb", [depth, 1024, 2432]); wout = I("wout", [depth, 1024, 1024])
    wa = I("wa", [depth, 1024, 640]); cw = I("cw", [depth, 64, 33]); pvi = I("pv", [depth, 64, 15])
    w2 = I("w2", [depth, 32, 192]); a2 = I("a2", [depth, 32, 192])
    cstA = I("cstA", [64, 1664])
    cosK = I("cosK", [8192, 128]); sinK = I("sinK", [8192, 128])
    cosQ = I("cosQ", [2048, 384]); sinQ = I("sinQ", [2048, 384])
    dsel_in = I("dsel", [128, 2])
    gk = I("gk", [depth, 128, 128]); gq = I("gq", [depth, 128, 384])
    nab = I("nab", [depth, 5, 4, 128, 768])
    gng = I("gng", [depth, 128, 384]); gnb = I("gnb", [depth, 128, 384])
    lng = I("lng", [depth, 128, 1024]); lnb = I("lnb", [depth, 128, 1024])
    ident_in = I("ident", [128, 128]); jmat_in = I("jmat", [128, 128]); jd_in = I("jd", [128, 128])
    out = dt("out", [2048, 1024], F32, kind="ExternalOutput").ap()

    s = S(nc)
    tickets = []
    pb = [s.ps([128, 512], name="pb%d" % i) for i in range(8)]
    XA = [s.dram([XROWS, 1024], "XA%d" % i) for i in range(2)]
    YB = s.dram([8448, 384], "YB")
    YGc = s.dram([4 * 256, 384], "YGc")
    YGl = s.dram([16 * 4 * 512, 384], "YGl")
    XN = s.dram([2048, 1024], "XN")
    XS = s.dram([2560, 1024], "XS")
    KTO = s.dram([128, 2048], "KTO", BF16); KTG = s.dram([512, 2048], "KTG", BF16)
    VOd = s.dram([2048, 130], "VOd", BF16); VGd = s.dram([8192, 130], "VGd", BF16)
    YF = [s.dram([2048, 384], "YF%d" % i) for i in range(2)]
    YBk = [s.dram([2048, 384], "YBk%d" % i) for i in range(2)]
    xa0_T = _A(xa0, Tk("xa0")); xs0_T = _A(xs0, Tk("xs0"))

    _rd = {}

    def RR(q):
        if q not in _rd:
            pid = s.eng[q].partition_id()
            _rd[q] = pid % 4
        return _rd[q]

    dynq = ["sp", "act", "pool"]
    dync = [0]

    def dyndma(dst_v, src_fn):
        q = dynq[dync[0] % 3]; dync[0] += 1
        return s.dma(q, dst_v, src_fn(RR(q)))

    ident = s.sb([128, 128], name="ident"); s.dma("sp", ident[:], ident_in)
    jmat = s.sb([128, 128], name="jmat"); s.dma("act", jmat[:], jmat_in)
    jd = s.sb([128, 128], name="jd"); s.dma("sp", jd[:], jd_in)
    dsel = s.sb([128, 2], name="dsel"); s.dma("act", dsel[:], dsel_in)
    identb = s.sb([128, 128], BF16, name="identb"); s.cp(identb[:], ident[:])
    jdb = s.sb([128, 128], BF16, name="jdb"); s.cp(jdb[:], jd[:])
    ones = s.sb([128, 128], name="ones"); s.memset(ones[:], 1.0)
    cv_t = s.sb([128, 16], name="cv"); s.dma("sp", cv_t[:], cv)
    scv = s.sb([128, 16], name="scv"); s.act(scv[:], cv_t[:], AF.Silu)
    mod = [s.sb([128, 3072], name="mod%d" % j) for j in range(2)]
    for l in range(depth):
        Xc = xa0_T if l == 0 else XA[(l - 1) % 2]
        Xn = XA[l % 2]
        last = (l == depth - 1)

        if l == 0:
            XSc = xs0_T
        else:
            XSc = XS
            lat = Xc.t[512:8704, :]
            dyndma(V(XS.t[256:2304, :].rearrange("(o k i) c -> o k (i c)", o=1, k=8), XS.tk),
                   lambda r: V(lat.rearrange("(k rr i) c -> rr k (i c)", rr=4, i=256)[bass.ds(r, 1)], Xc.tk))
            units = lat.rearrange("(u i) c -> u (i c)", i=256)
            dyndma(V(XS.t[0:256, :].rearrange("(o i) c -> o (i c)", o=1), XS.tk),
                   lambda r: V(units[bass.ds(r + 27, 1), :], Xc.tk))
            dyndma(V(XS.t[2304:2560, :].rearrange("(o i) c -> o (i c)", o=1), XS.tk),
                   lambda r: V(units[bass.ds(r + 1, 1), :], Xc.tk))
        with s.phase():
            stage_ws = [s.sb([128, 8, 512], name="stage_w%d" % i) for i in range(2)]
            Rl = s.sb([128, 16, 128], name="Rl")
            bmod_ts = [s.sb([128, 512], name="bmodt%d" % i) for i in range(2)]
            for i in range(16):
                s.ts(Rl[:, i, :], ones[:], scv[:, i:i + 1], ALU.mult, e="dve" if i % 2 == 0 else "pool")
            for cb in range(6):
                stage_w = stage_ws[cb % 2]; bmod_t = bmod_ts[cb % 2]
                s.dma("sp", stage_w[:], wmod[l].rearrange("(k p) c -> p k c", p=128)[:, :, cb * 512:(cb + 1) * 512])
                s.dma("act", bmod_t[:], bmod[l][:, cb * 512:(cb + 1) * 512])
                for j in range(2):
                    ps = pb[(2 * cb + j) % 4]
                    for k in range(8):
                        s.mm(ps[:, :], Rl[:, 2 * k + j, :], stage_w[:, k, :], start=(k == 0), stop=(k == 7))
                    s.tt(mod[j][:, cb * 512:(cb + 1) * 512], ps[:, :], bmod_t[:], ALU.add, e="dve")
            for j in range(2):
                s.ts(mod[j][:, 1024:2048], mod[j][:, 1024:2048], 1.0, ALU.add, e="pool")

        with s.phase():
            cst_t = s.sb([64, 1664], name="cst"); s.dma("sp", cst_t[:], cstA)
            identA = cst_t[:, 0:64]
            mask3 = lambda h: cst_t[:, 64 + h * 320: 64 + (h + 1) * 320]
            rmask = cst_t[:, 1024:1280]
            idt3 = cst_t[:, 1280:1664]
            cw_t = s.sb([64, 33], name="cw"); s.dma("act", cw_t[:], cw[l])
            pv_t = s.sb([64, 16], name="pv"); s.dma("act", pv_t[:, 0:15], pvi[l])
            omk = s.sb([64, 3], name="omk")
            for h in range(3):
                s.ts(omk[:, h:h + 1], pv_t[:, h * 5 + 3:h * 5 + 4], -1.0, ALU.mult, 1.0, ALU.add)
            w2_t = s.sb([32, 192], name="w2"); s.dma("act", w2_t[:], w2[l])
            a2_t = s.sb([32, 192], name="a2"); s.dma("act", a2_t[:], a2[l])
            xt = [s.sb([128, 1024], name="xt%d" % i) for i in range(2)]
            ht = s.sb([128, 1024], name="ht")
            xr = s.sb([128, 1024], name="xr")
            hbA = s.sb([128, 1024], BF16, name="hbA")
            wab = s.sb([128, 8, 640], BF16, name="wab")
            for k in range(8):
                st_ = xt[k % 2]
                s.dma("sp" if k % 2 == 0 else "act", st_[:, 0:640], wa[l][k * 128:(k + 1) * 128, :])
                s.cp(wab[:, k, :], st_[:, 0:640], e="dve" if k % 2 == 0 else "pool")
            cts = [(i * 64, 64) for i in range(9)] + [(576, 32), (608, 32)]
            hg = [s.sb([128, 8, 258], BF16, name="hg%d" % i) for i in range(2)]
            for h_ in hg:
                s.memset(h_[:], 0.0)
            raw = [s.sb([64, 258], name="raw%d" % i) for i in range(2)]
            ctmp = [s.sb([64, 256], name="ctmp%d" % i) for i in range(2)]
            mk = lambda nm, shape=(64, 256), dt_=F32: [s.sb(list(shape), dt_, name="%s%d" % (nm, h)) for h in range(3)]
            uR, uK, uV = mk("uR"), mk("uK"), mk("uV", dt_=BF16)
            uD = s.sb([32, 256], name="uD"); uA = s.sb([32, 256], name="uA"); ddt = s.sb([32, 256], name="ddt")
            sg, ic, kk, tmp, kd, bd = mk("sg"), mk("ic"), mk("kk"), mk("tmp"), mk("kd"), mk("bd")
            cs, csx, csr = mk("cs"), mk("csx"), mk("csr")
            E1, E3 = mk("E1"), mk("E3")
            E4 = E1
            RH, KKH, kt, bt, kc, bc, rk = (mk("RH", dt_=BF16), mk("KKH", dt_=BF16), mk("kt", dt_=BF16), mk("bt", dt_=BF16),
                                           mk("kc", dt_=BF16), mk("bc", dt_=BF16), mk("rk", dt_=BF16))
            identAb = s.sb([64, 64], BF16, name="identAb"); s.cp(identAb[:], identA)
            onesb = s.sb([64, 1], BF16, name="onesb"); s.memset(onesb[:], 1.0)
            wc = mk("wc", (64, 4)); rn = tmp
            trT = [s.sb([64, 3, 256], BF16, name="trT%d" % i) for i in range(4)]
            scS = [s.sb([64, 3, 320], BF16, name="scS%d" % i) for i in range(4)]
            XYs = [[s.sb([64, 3, 128], BF16, name="XY%d_%d" % (c, i)) for i in range(2)] for c in range(4)]
            PQs = [[s.sb([64, 3, 128], BF16, name="PQ%d_%d" % (c, i)) for i in range(2)] for c in range(4)]
            KKpTs = [s.sb([64, 192], BF16, name="KKpT%d" % c) for c in range(4)]
            AVs = [s.sb([64, 192], BF16, name="AV%d" % c) for c in range(4)]
            Ulocs = [s.sb([64, 192], name="Uloc%d" % c) for c in range(4)]
            Us = [s.sb([64, 192], BF16, name="U%d" % c) for c in range(4)]
            STb = [s.sb([64, 192], BF16, name="STb%d" % i) for i in range(2)]
            bss = [s.sb([64, 4], name="bs%d" % c) for c in range(4)]
            il = Interleaver(s)
            ST = [s.sb([64, 192], name="ST%d" % i) for i in range(2)]
            YBuf = [s.sb([64, 4, 384], name="YBuf0")] * 2
            bs_ = s.sb([64, 4], name="bs")
            s.memset(ST[0][:], 0.0)
            s.memset(STb[0][:], 0.0)
            sti = 0
            acnt = [0]

            def frontA(g):
                hgt = hg[g % 2]
                for a in range(2):
                    u = 2 * g + a
                    i = acnt[0]; acnt[0] += 1
                    if u < 2:
                        bf_, br_ = 128 * u, 128 * (1 - u)
                        j = 1
                    else:
                        v = u - 2
                        bf_, br_ = lat_row(128 * v), lat_row(128 * (63 - v))
                        j = 0
                    x_ = xt[i % 2]
                    s.dma("sp", x_[:], V(Xc.t[bf_:bf_ + 128, :], Xc.tk))
                    s.dma("act", xr[:], V(Xc.t[br_:br_ + 128, :], Xc.tk))
                    s.act(x_[:], x_[:], AF.Identity, scale=dsel[:, 0:1])
                    s.stt(x_[:], xr[:], dsel[:, 1:2], x_[:], ALU.mult, ALU.add)
                    s.tt(ht[:], x_[:], mod[j][:, 1024:2048], ALU.mult, e="pool")
                    s.tt(hbA[:], ht[:], mod[j][:, 0:1024], ALU.add, e="dve")
                    for half in range(2):
                        p = pb[half]
                        with s.atomic():
                            pbf = p.t[:, 0:256].bitcast(BF16)
                            for k in range(4):
                                kk_ = half * 4 + k
                                s.tr(V(pbf[:, k * 128:(k + 1) * 128], p.tk), hbA[:, kk_ * 128:(kk_ + 1) * 128], jdb[:])
                            s.cp(hgt[:, half * 4:half * 4 + 4, 1 + 128 * a:1 + 128 * (a + 1)],
                                 V(pbf.rearrange("p (k t) -> p k t", k=4), p.tk), e="act" if half == 0 else "dve")

            NGRP = 33
            OPN = ('RH', 'KKH', 'kt', 'bt', 'kc', 'bc', 'rk', 'uV')
            opsets = [dict(RH=RH, KKH=KKH, kt=kt, bt=bt, kc=kc, bc=bc, rk=rk, uV=uV, wc=wc),
                      dict(RH=mk('RHb', dt_=BF16), KKH=mk('KKHb', dt_=BF16), kt=mk('ktb', dt_=BF16), bt=mk('btb', dt_=BF16),
                           kc=mk('kcb', dt_=BF16), bc=mk('bcb', dt_=BF16), rk=mk('rkb', dt_=BF16), uV=mk('uVb', dt_=BF16),
                           wc=mk('wcb', (64, 4)))]

            def pro1(g):
                O = opsets[g % 2]; uV = O['uV']
                first = g in (0, 1)
                lastg = g in (0, NGRP - 1)
                hgt = hg[g % 2]
                if g + 1 < NGRP:
                    frontA(g + 1)
                    hn_ = hg[(g + 1) % 2]
                    s.cp(hgt[:, :, 257:258], hn_[:, :, 1:2], e="pool")
                    s.cp(hn_[:, :, 0:1], hgt[:, :, 256:257], e="pool")
                for ci, (c0, M) in enumerate(cts):
                    pr = pb[ci % 2]
                    rw = raw[ci % 2]
                    with s.atomic():
                        for k in range(8):
                            s.mm(pr[0:M, 0:258], wab[:, k, c0:c0 + M], hgt[:, k, :], start=(k == 0), stop=(k == 7))
                        s.cp(rw[0:M, :], pr[0:M, 0:258], e="act" if ci % 2 == 0 else "dve")
                    if first:
                        s.memset(rw[0:M, 0:1], 0.0, e="pool")
                    if lastg:
                        s.memset(rw[0:M, 257:258], 0.0, e="pool")
                    dst = (uR, uK, uV)[ci // 3][ci % 3] if ci < 9 else (uD, uA)[ci - 9]
                    tm = ctmp[ci % 2]
                    s.act(tm[0:M, :], rw[0:M, 0:256], AF.Identity, scale=cw_t[0:M, ci * 3:ci * 3 + 1])
                    s.stt(tm[0:M, :], rw[0:M, 1:257], cw_t[0:M, ci * 3 + 1:ci * 3 + 2], tm[0:M, :], ALU.mult, ALU.add)
                    s.stt(dst[0:M, :], rw[0:M, 2:258], cw_t[0:M, ci * 3 + 2:ci * 3 + 3], tm[0:M, :], ALU.mult, ALU.add)
                s.act(ddt[:], uD[:], AF.Tanh)

            def prep_head(g, h):
                O = opsets[g % 2]
                RH, KKH, kt, bt, kc, bc, rk, wc = O['RH'], O['KKH'], O['kt'], O['bt'], O['kc'], O['bc'], O['rk'], O['wc']
                P = lambda i: pv_t[:, h * 5 + i:h * 5 + i + 1]
                pz = pb[2 + h]
                with s.atomic():
                    s.mm(pz[0:64, 0:256], w2_t[:, h * 64:(h + 1) * 64], ddt[:])
                    s.act(sg[h][:], pz[0:64, 0:256], AF.Sigmoid, bias=P(0))
                with s.atomic():
                    s.mm(pz[0:64, 256:512], a2_t[:, h * 64:(h + 1) * 64], uA[:])
                    s.act(ic[h][:], pz[0:64, 256:512], AF.Sigmoid, bias=P(1))
                s.act(kk[h][:], uK[h][:], AF.Identity, scale=P(2))
                s.act(tmp[h][:], kk[h][:], AF.Square)
                pss = pb[2 + h]
                with s.atomic():
                    s.mm(pss[0:64, 0:256], ones[0:64, 0:64], tmp[h][:])
                    s.ts(rn[h][:], pss[0:64, 0:256], 1e-12, ALU.max)
                s.act(rn[h][:], rn[h][:], AF.Sqrt)
                s.recip(rn[h][:], rn[h][:])
                s.tt(kk[h][:], kk[h][:], rn[h][:], ALU.mult)
                s.act(tmp[h][:], ic[h][:], AF.Identity, scale=P(3), bias=omk[:, h:h + 1])
                s.tt(kd[h][:], uK[h][:], tmp[h][:], ALU.mult, e="dve")
                s.tt(bd[h][:], kk[h][:], ic[h][:], ALU.mult, e="pool")
                s.op("dve", lambda h=h: nc.vector.tensor_tensor_scan(cs[h][:].ap, rmask.ap, sg[h][:].ap, 0.0, ALU.mult, ALU.add),
                     [rmask, sg[h][:]], [cs[h][:]])
                s.tt(csx[h][:], cs[h][:], sg[h][:], ALU.subtract)
                for c in range(4):
                    s.ts(csr[h][:, c * 64:(c + 1) * 64], cs[h][:, c * 64:(c + 1) * 64],
                         cs[h][:, c * 64 + 63:c * 64 + 64], ALU.subtract)
                s.act(wc[h][:], cs[h][:, 63::64], AF.Exp, scale=-A_DEC)
                s.act(E1[h][:], cs[h][:], AF.Exp, scale=-A_DEC)
                s.tt(RH[h][:], uR[h][:], E1[h][:], ALU.mult, e="dve")
                s.act(E1[h][:], csx[h][:], AF.Exp, scale=-A_DEC)
                s.tt(KKH[h][:], kk[h][:], E1[h][:], ALU.mult, e="pool")
                s.act(E3[h][:], cs[h][:], AF.Exp, scale=A_DEC)
                s.tt(kt[h][:], kd[h][:], E3[h][:], ALU.mult)
                s.tt(bt[h][:], bd[h][:], E3[h][:], ALU.mult, e="pool")
                s.act(E4[h][:], csr[h][:], AF.Exp, scale=A_DEC)
                s.tt(kc[h][:], kd[h][:], E4[h][:], ALU.mult)
                s.tt(bc[h][:], bd[h][:], E4[h][:], ALU.mult, e="pool")
                s.stt(rk[h][:], uR[h][:], P(4), kd[h][:], ALU.mult, ALU.mult)

            def chunk_body(g, c):
                n = 4 * g + c
                O = opsets[g % 2]
                RH, KKH, kt, bt, kc, bc, rk, uV = O['RH'], O['KKH'], O['kt'], O['bt'], O['kc'], O['bc'], O['rk'], O['uV']
                cc = slice(c * 64, (c + 1) * 64)
                tT = trT[c]; sS = scS[c]
                XY = XYs[c]; PQ = PQs[c]; KKpT = KKpTs[c]; AV = AVs[c]; Uloc = Ulocs[c]; U = Us[c]; bs_ = bss[c]
                p_ = c % 2
                for h in range(3):
                    ptr = pb[2 + p_]
                    with s.atomic():
                        ptrb = ptr.t[0:64, 0:128].bitcast(BF16)
                        for i, src_ in enumerate((KKH, bc, kc, uV)):
                            s.tr(V(ptrb[:, i * 64:(i + 1) * 64], ptr.tk), src_[h][:, cc], identAb[:])
                        s.cp(tT[:, h, :], V(ptrb[:, 0:256], ptr.tk), e="act")
                    psc = pb[4 + p_]
                    with s.atomic():
                        s.mm(psc[0:64, 0:64], kt[h][:, cc], RH[h][:, cc])
                        s.mm(psc[0:64, 64:128], kt[h][:, cc], KKH[h][:, cc])
                        s.mm(psc[0:64, 128:192], bt[h][:, cc], RH[h][:, cc])
                        s.mm(psc[0:64, 192:256], bt[h][:, cc], KKH[h][:, cc])
                        s.mm(psc[0:64, 256:320], KKH[h][:, cc], bt[h][:, cc])
                        s.tt(sS[:, h, :], psc[0:64, 0:320], mask3(h), ALU.mult)
                s.tt(PQ[0][:, :, :], sS[:, :, 192:320], V(idt3.ap.rearrange("p (h c) -> p h c", c=128), idt3.tk), ALU.add, e="pool")
                Xc_ = lambda lvl, h: (sS[:, h, 192:256] if lvl == 0 else XY[lvl % 2][:, h, 0:64])
                Yc_ = lambda lvl, h: (sS[:, h, 256:320] if lvl == 0 else XY[lvl % 2][:, h, 64:128])
                for lvl in range(5):
                    pn, pq = pb[4 + p_], pb[6 + p_]
                    nxt = XY[(lvl + 1) % 2]
                    with s.atomic():
                        for h in range(3):
                            s.mm(pn[0:64, h * 128:h * 128 + 64], Yc_(lvl, h), Xc_(lvl, h))
                            if lvl < 4:
                                s.mm(pn[0:64, h * 128 + 64:h * 128 + 128], Xc_(lvl, h), Yc_(lvl, h))
                        pn3 = pn.t[0:64, 0:384].rearrange("p (h c) -> p h c", c=128)
                        if lvl < 4:
                            s.cp(nxt[:, :, :], V(pn3, pn.tk), e="act")
                        else:
                            s.cp(nxt[:, :, 0:64], V(pn3[:, :, 0:64], pn.tk), e="act")
                    Pc, Pn = PQ[lvl % 2], PQ[(lvl + 1) % 2]
                    with s.atomic():
                        for h in range(3):
                            s.mm(pq[0:64, h * 128:h * 128 + 64], Pc[:, h, 64:128], nxt[:, h, 0:64])
                            if lvl < 4:
                                s.mm(pq[0:64, h * 128 + 64:h * 128 + 128], Pc[:, h, 0:64], nxt[:, h, 64:128])
                        pq3 = pq.t[0:64, 0:384].rearrange("p (h c) -> p h c", c=128)
                        if lvl < 4:
                            s.tt(Pn[:, :, :], V(pq3, pq.tk), Pc[:, :, :], ALU.add)
                        else:
                            s.tt(Pn[:, :, 0:64], V(pq3[:, :, 0:64], pq.tk), Pc[:, :, 0:64], ALU.add)
                TT = PQ[1]
                pk = pb[2 + p_]
                with s.atomic():
                    for h in range(3):
                        s.mm(pk[0:64, h * 64:(h + 1) * 64], tT[:, h, 0:64], TT[:, h, 0:64])
                        s.mm(pk[0:64, 192 + h * 64:192 + (h + 1) * 64], sS[:, h, 64:128], tT[:, h, 192:256])
                    s.cp(KKpT[:], pk[0:64, 0:192], e="act")
                    s.cp(AV[:], pk[0:64, 192:384], e="dve")
                pk3 = pb[6 + p_]
                with s.atomic():
                    for h in range(3):
                        s.mm(pk3[0:64, h * 64:(h + 1) * 64], TT[:, h, 0:64], AV[:, h * 64:(h + 1) * 64])
                    s.cp(Uloc[:], pk3[0:64, 0:192], e="act")

            def chunk_seq(g, c):
                n = 4 * g + c
                yb = YBuf[0]
                O = opsets[g % 2]
                RH, rk, wc = O['RH'], O['rk'], O['wc']
                cc = slice(c * 64, (c + 1) * 64)
                tT = trT[c]; sS = scS[c]
                KKpT = KKpTs[c]; Uloc = Ulocs[c]; U = Us[c]; bs_ = bss[c]
                Sc, Sn = ST[n % 2], ST[(n + 1) % 2]
                Scb, Snb = STb[n % 2], STb[(n + 1) % 2]
                pu = pb[0]
                with s.atomic():
                    for h in range(3):
                        s.mm(pu[0:64, h * 64:(h + 1) * 64], KKpT[:, h * 64:(h + 1) * 64], Scb[:, h * 64:(h + 1) * 64])
                    s.stt(U[:], pu[0:64, 0:192], -1.0, Uloc[:], ALU.mult, ALU.subtract)
                pS = pb[1]
                with s.atomic():
                    for h in range(3):
                        hs = slice(h * 64, (h + 1) * 64)
                        s.mm(pS[0:64, hs], tT[:, h, 128:192], tT[:, h, 192:256], start=True, stop=False)
                        s.mm(pS[0:64, hs], tT[:, h, 64:128], U[:, hs], start=False, stop=True)
                    for h in range(3):
                        hs = slice(h * 64, (h + 1) * 64)
                        s.stt(Sn[:, hs], Sc[:, hs], wc[h][:, c:c + 1], pS[0:64, hs], ALU.mult, ALU.add)
                    s.cp(Snb[:], Sn[:], e="act")
                py = pb[0]
                with s.atomic():
                    for h in range(3):
                        hs = slice(256 + h * 64, 256 + (h + 1) * 64)
                        hh = slice(h * 64, (h + 1) * 64)
                        s.mm(py[0:64, hs], RH[h][:, cc], Scb[:, hh], start=True, stop=False)
                        s.mm(py[0:64, hs], sS[:, h, 128:192], U[:, hh], start=False, stop=False)
                        s.mm(py[0:64, hs], sS[:, h, 0:64], tT[:, h, 192:256], start=False, stop=True)
                    s.cp(yb[:, c, 0:192], py[0:64, 256:448], e="act")
                pbn = pb[1]
                with s.atomic():
                    for h in range(3):
                        s.mm(pbn[0:64, 256 + h:256 + h + 1], rk[h][:, cc], onesb[:, 0:1])
                    s.cp(bs_[:, 0:3], pbn[0:64, 256:259], e="dve")
                for h in range(3):
                    s.ts(yb[:, c, 192 + h * 64:192 + (h + 1) * 64], tT[:, h, 192:256], bs_[:, h:h + 1], ALU.mult, e="pool")
            def seq_group(g):
                tbase = 256 * g
                for c in range(4):
                    chunk_seq(g, c)
                s.dma("sp" if g % 2 == 0 else "act",
                      V(YB.t[tbase:tbase + 256, :].rearrange("(c t) f -> t c f", t=64), YB.tk), YBuf[0][:, :, :])
                if g == 0:
                    s.allgather(YB[0:256, :], YGc[:, :], GROUPS)
                elif g % 2 == 0:
                    m = g // 2 - 1
                    s.allgather(YB[256 + 512 * m:256 + 512 * (m + 1), :], YGl[2048 * m:2048 * (m + 1), :], GROUPS)

            frontA(0)
            pro1(0)
            il.run([(lambda h=h: prep_head(0, h)) for h in range(3)], 3)
            for g in range(NGRP):
                wk1 = [(lambda c=c: chunk_body(g, c)) for c in range(4)]
                if g + 1 < NGRP:
                    wk1.append(lambda: pro1(g + 1))
                il.run(wk1, 5)
                wk2 = [lambda: seq_group(g)]
                if g + 1 < NGRP:
                    wk2 += [(lambda h=h: prep_head(g + 1, h)) for h in range(3)]
                il.run(wk2, 4)
        ygv = YGl.t.rearrange("(m sr i) c -> sr m (i c)", sr=4, i=512)
        for sr in range(2):
            dyndma(V(YF[sr].t.rearrange("(m i) c -> m (i c)", i=512), YF[sr].tk),
                   lambda r, sr=sr: V(ygv[sr][bass.ds(r * 4, 4), :], YGl.tk))
            dyndma(V(YBk[sr].t.rearrange("(m i) c -> m (i c)", i=512), YBk[sr].tk),
                   lambda r, sr=sr: V(ygv[2 + sr][bass.ds((3 - r) * 4, 4), :], YGl.tk))
        with s.phase():
            GK = s.sb([128, 128], name="GK"); s.dma("act", GK[:], gk[l])
            GQ = s.sb([128, 384], name="GQ"); s.dma("act", GQ[:], gq[l])
            GNG = s.sb([128, 384], name="GNG"); s.dma("act", GNG[:], gng[l])
            GNB = s.sb([128, 384], name="GNB"); s.dma("act", GNB[:], gnb[l])
            YAN = s.sb([128, 18, 640], BF16, name="YAN")
            wbuf = s.sb([128, 8, 1024], BF16, name="wbuf")
            woutb = s.sb([128, 8, 1024], BF16, name="woutb")
            xt = [s.sb([128, 1024], name="xt%d" % i) for i in range(2)]
            hts = [s.sb([128, 1024], name="ht%d" % i) for i in range(2)]
            ht = hts[0]
            pe = [s.sb([128, 1024], name="pe%d" % i) for i in range(2)]
            sqs = [pe[1], None]
            wst = xt
            hbs = []
            ilB = Interleaver(s)
            pjB = _A(YAN.t[:, 0:2, :].rearrange("p a c -> p (a c)").bitcast(F32), YAN.tk)
            sqs[1] = _A(YAN.t[:, 2:4, :].rearrange("p a c -> p (a c)").bitcast(F32), YAN.tk)

            def load_w(dst, src, c0, ncols, dcol=0):
                for k in range(8):
                    st_ = wst[k % 2]
                    s.dma("sp" if k % 2 == 0 else "act", st_[:, 0:ncols], src[k * 128:(k + 1) * 128, c0:c0 + ncols])
                    s.cp(dst[:, k, dcol:dcol + ncols], st_[:, 0:ncols], e="dve" if k % 2 == 0 else "pool")

            load_w(woutb, wout[l], 0, 1024)
            kT = s.sb([128, 8448], BF16, name="kT")
            Vg = s.sb([128, NKT, 2, 65], BF16, name="Vg")
            qT = [s.sb([128, 2304], BF16, name="qT%d" % i) for i in range(3)]
            nqT = [s.sb([128, 2304], BF16, name="nqT%d" % i) for i in range(2)]
            nkT = [s.sb([128, 2816], BF16, name="nkT%d" % i) for i in range(2)]
            Vn = s.sb([128, NNT, 4, 65], BF16, name="Vn")
            s.memset(Vg[:, :, :, 64:65], 1.0)
            s.memset(Vn[:, :, :, 64:65], 1.0)
            hT = [s.sb([128, 8, 128], BF16, name="hT%d" % i) for i in range(2)]
            tcs = [s.sb([128, 384], name="tcos%d" % i) for i in range(2)]
            tsns = [s.sb([128, 384], name="tsin%d" % i) for i in range(2)]
            sms = [s.sb([128, 64], name="sm%d" % i) for i in range(2)]
            tc_, tsn, sm = tcs[0], tsns[0], sms[0]
            cnt = [0]

            def front(srcT, row, j, w=None):
                if w is None:
                    i = cnt[0]; cnt[0] += 1
                    w = i % 2
                    banks = (pb[0], pb[1])
                else:
                    banks = (pb[w], pb[w])
                q = "sp" if w == 0 else "act"
                x_ = xt[w]
                ht_ = hts[w]
                s.dma(q, x_[:], V(srcT.t[row:row + 128, :], srcT.tk))
                hb_ = hbs[w]
                s.tt(ht_[:], x_[:], mod[j][:, 1024:2048], ALU.mult, e="pool")
                s.tt(hb_[:], ht_[:], mod[j][:, 0:1024], ALU.add, e="dve")
                h_ = hT[w]
                for half in range(2):
                    p = banks[half]
                    with s.atomic():
                        pbf = p.t[:, 0:256].bitcast(BF16)
                        for k in range(4):
                            kk_ = half * 4 + k
                            s.tr(V(pbf[:, k * 128:(k + 1) * 128], p.tk), hb_[:, kk_ * 128:(kk_ + 1) * 128], identb[:])
                        s.cp(h_[:, half * 4:half * 4 + 4, :], V(pbf.rearrange("p (k t) -> p k t", k=4), p.tk),
                             e="act" if half == 0 else "dve")
                return x_, h_

            def proj(h_, c0, ncols, dst, wsrc=None, w=None):
                wsrc = wsrc or wbuf
                o = 0
                bi = 2
                while o < ncols:
                    n = min(512, ncols - o)
                    p = pb[bi] if w is None else pb[2 + w]
                    with s.atomic():
                        for k in range(8):
                            s.mm(p[:, 0:n], h_[:, k, :], wsrc[:, k, c0 + o:c0 + o + n], start=(k == 0), stop=(k == 7))
                        s.cp(dst[:, o:o + n], p[:, 0:n], e="act" if bi == 2 else "dve")
                    o += n
                    bi = 5 - bi

            def rms_rope(src, H, gtab, scale_mode, rope, dst, w=0):
                sq = sqs[w]; sm = sms[w]; tc_ = tcs[w]; tsn = tsns[w]
                s.act(sq[:, 0:H * 64], src, AF.Square)
                s.red(sm[:, 0:H], V(sq.t[:, 0:H * 64].rearrange("p (h d) -> p h d", d=64), sq.tk), ALU.add)
                if scale_mode == "k":
                    s.ts(sm[:, 0:H], sm[:, 0:H], 1.0 / 64, ALU.mult, 1e-6, ALU.add)
                else:
                    s.ts(sm[:, 0:H], sm[:, 0:H], 64e-6, ALU.add)
                s.act(sm[:, 0:H], sm[:, 0:H], AF.Sqrt)
                s.recip(sm[:, 0:H], sm[:, 0:H])
                for h in range(H):
                    s.stt(V(dst.ap[:, h * 64:(h + 1) * 64], dst.tk), V(src.ap[:, h * 64:(h + 1) * 64], src.tk), sm[:, h:h + 1],
                          gtab[:, h * 64:(h + 1) * 64], ALU.mult, ALU.mult)
                if rope is not None:
                    cos_d, sin_d, rowfn = rope
                    s.dma("sp", tc_[:, 0:H * 64], cos_d[rowfn:rowfn + 128, :])
                    s.dma("act", tsn[:, 0:H * 64], sin_d[rowfn:rowfn + 128, :])
                    t1 = sq
                    v4 = lambda ap: ap.rearrange("p (g a d) -> p g a d", a=2, d=16)
                    d4 = v4(dst.ap); s4 = v4(tsn.t[:, 0:H * 64]); t4 = v4(t1.t[:, 0:H * 64])
                    s.tt(V(t4[:, :, 0, :], t1.tk), V(d4[:, :, 1, :], dst.tk), V(s4[:, :, 0, :], tsn.tk), ALU.mult, e="pool")
                    s.tt(V(t4[:, :, 1, :], t1.tk), V(d4[:, :, 0, :], dst.tk), V(s4[:, :, 1, :], tsn.tk), ALU.mult, e="pool")
                    s.tt(dst, dst, tc_[:, 0:H * 64], ALU.mult)
                    s.tt(dst, dst, t1[:, 0:H * 64], ALU.add)

            pt = [s.sb([128, 512], BF16, name="pt%d" % i) for i in range(3)]
            rcp = s.sb([128, 8], name="rcp")
            bias_t = [s.sb([128, 768], name="bias%d" % i) for i in range(2)]
            hbs.extend([_A(bias_t[i].t[:, 0:512].bitcast(BF16), bias_t[i].tk) for i in range(2)])
            ptc = [0]

            load_w(wbuf, wb[l], 768, 256)
            kst = [_A(YAN.t[:, 4 + i, 0:128], YAN.tk) for i in range(2)]
            vst = [_A(YAN.t[:, 6 + i, 0:130].rearrange("p (g d) -> p g d", d=65), YAN.tk) for i in range(2)]
            for v_ in vst:
                s.memset(v_[:, :, 64:65], 1.0)

            def k_body(t):
                w = t % 2
                ctx_t = t < 2
                j = 1 if ctx_t else 0
                if ctx_t:
                    x_, h_ = front(Xc, 128 * t, j, w)
                else:
                    x_, h_ = front(XSc, 256 + 128 * (t - 2), j, w)
                pj = pe[0] if w == 0 else pjB
                proj(h_, 0, 256, pj, w=w)
                rope = None if ctx_t else (cosQ[:, 0:128], sinQ[:, 0:128], (t - 2) * 128)
                kr = hts[w]
                rms_rope(pj[:, 0:128], 2, GK, "k", rope, kr[:, 0:128], w)
                p = pb[6 + w]
                vsrc = V(pj.t[:, 128:256].rearrange("p (g d) -> p g d", d=64), pj.tk)
                if ctx_t:
                    with s.atomic():
                        s.tr(p[:, 0:128], kr[:, 0:128], ident[:])
                        s.cp(kT[:, t * 128:(t + 1) * 128], p[:, 0:128], e="act")
                    s.cp(Vg[:, t, :, 0:64], vsrc, e="pool")
                else:
                    i_ = t - 2
                    with s.atomic():
                        s.tr(p[:, 0:128], kr[:, 0:128], ident[:])
                        s.cp(kst[w][:], p[:, 0:128], e="act")
                    s.dma("sp" if w == 0 else "act", KTO[:, 128 * i_:128 * (i_ + 1)], kst[w][:])
                    s.cp(vst[w][:, :, 0:64], vsrc, e="pool")
                    s.dma("sp" if w == 0 else "act", VOd[128 * i_:128 * (i_ + 1), :], vst[w][:, :, :])

            ilB.run([(lambda t=t: k_body(t)) for t in range(18)], 2)
            s.allgather(KTO[:, :], KTG[:, :], GROUPS)
            s.allgather(VOd[:, :], VGd[:, :], GROUPS)
            for rho in range(4):
                s.dma("sp" if rho % 2 == 0 else "act", kT[:, 256 + 2048 * rho:256 + 2048 * (rho + 1)], KTG[128 * rho:128 * (rho + 1), :])
                s.dma("act" if rho % 2 == 0 else "sp",
                      V(Vg.t[:, 2 + 16 * rho:2 + 16 * (rho + 1), :, :].rearrange("p i g d -> p i (g d)"), Vg.tk),
                      V(VGd.t[2048 * rho:2048 * (rho + 1), :].rearrange("(i p) c -> p i c", p=128), VGd.tk))
            load_w(wbuf, wb[l], 1664, 512)

            def n_body(t):
                w = t % 2
                j = 1 if t >= 20 else 0
                if t >= 20:
                    x_, h_ = front(Xc, 128 * (t - 20), j, w)
                else:
                    x_, h_ = front(XSc, 128 * t, j, w)
                pj = pe[0] if w == 0 else pjB
                proj(h_, 0, 512, pj, w=w)
                for pr_ in range(2):
                    p = pb[6 + w]
                    with s.atomic():
                        s.tr(p[:, 0:128], pj[:, pr_ * 128:(pr_ + 1) * 128], ident[:])
                        s.cp(nkT[pr_][:, t * 128:(t + 1) * 128], p[:, 0:128], e="act" if pr_ == 0 else "dve")
                s.cp(Vn[:, t, :, 0:64], V(pj.t[:, 256:512].rearrange("p (g d) -> p g d", d=64), pj.tk), e="pool")

            ilB.run([(lambda t=t: n_body(t)) for t in range(NNT)], 2)
            for sl in range(6):
                hh_ = (sl // 2) + 3 * (sl % 2)
                load_w(wbuf, wb[l], 384 + hh_ * 64, 64, dcol=sl * 64)
            load_w(wbuf, wb[l], 1408, 256, dcol=384)
            own_src = lambda t: ((Xc, 128 * (t - 16)) if t >= 16 else (XSc, 256 + 128 * t))

            def q_body(t):
                w = t % 2
                j = 1 if t >= 16 else 0
                x_, h_ = front(*own_src(t), j, w)
                pj = pe[0] if w == 0 else pjB
                proj(h_, 0, 640, pj, w=w)
                rope = None if t >= 16 else (cosQ, sinQ, 128 * t)
                qr = hts[w]
                rms_rope(pj[:, 0:384], 6, GQ, "q", rope, qr[:, 0:384], w)
                for pr_ in range(3):
                    p = pb[6 + w]
                    with s.atomic():
                        s.tr(p[:, 0:128], qr[:, pr_ * 128:(pr_ + 1) * 128], ident[:])
                        s.cp(qT[pr_][:, t * 128:(t + 1) * 128], p[:, 0:128], e="act" if pr_ % 2 == 0 else "dve")
                s.ts(qr[:, 384:640], pj[:, 384:640], 0.125, ALU.mult, e="pool")
                for pr_ in range(2):
                    p = pb[6 + w]
                    with s.atomic():
                        s.tr(p[:, 0:128], qr[:, 384 + pr_ * 128:384 + (pr_ + 1) * 128], ident[:])
                        s.cp(nqT[pr_][:, t * 128:(t + 1) * 128], p[:, 0:128], e="act" if pr_ == 0 else "dve")

            ilB.run([(lambda t=t: q_body(t)) for t in range(NOWN)], 2)

            def attend(qsrc, g, qc0, nq, chunks, ksrc, vsrc_fn, bias_fn, dst_fn):
                nqs = nq // 128
                lo, hi = 64 * g, 64 * g + 64
                n_ = len(chunks)
                for ci in range(n_ + 1):
                    if ci < n_:
                        kc0, nk, cid = chunks[ci]
                        ps = pb[ci % 3]
                        bsrc = bias_fn(cid, nk) if bias_fn else None
                        s.mm(ps[0:nk, 0:nq], ksrc[lo:hi, kc0:kc0 + nk], qsrc[lo:hi, qc0:qc0 + nq], start=True, stop=(bsrc is None))
                        if bsrc is not None:
                            s.mm(ps[0:nk, 0:nq], bsrc, ident[:, 0:nq], start=False, stop=True)
                    if ci >= 1:
                        kc0p, nkp, cidp = chunks[ci - 1]
                        p_ = pt[ptc[0] % 3]; ptc[0] += 1
                        s.act(p_[0:nkp, 0:nq], pb[(ci - 1) % 3][0:nkp, 0:nq], AF.Exp)
                        for qs in range(nqs):
                            s.mm(pb[4 + qs][:, 0:65], p_[0:nkp, qs * 128:(qs + 1) * 128], vsrc_fn(cidp, nkp),
                                 start=(ci == 1), stop=(ci == n_))
                for qs in range(nqs):
                    s.recip(rcp[:, qs:qs + 1], pb[4 + qs][:, 64:65])
                    s.ts(dst_fn(qs), pb[4 + qs][:, 0:64], rcp[:, qs:qs + 1], ALU.mult)

            oT = _A(pe[0].t[0:65, 0:512], pe[0].tk)
            fin = [0]

            def attend_T(qsrc, g, qc0, nq, chunks, ksrc, vsrc_fn, dst_fn):
                nqs = nq // 128
                lo, hi = 64 * g, 64 * g + 64
                po = pb[4]
                n_ = len(chunks)
                for ci in range(n_ + 1):
                    if ci < n_:
                        kc0, nk, cid = chunks[ci]
                        s.mm(pb[ci % 3][0:nk, 0:nq], ksrc[lo:hi, kc0:kc0 + nk], qsrc[lo:hi, qc0:qc0 + nq])
                    if ci >= 1:
                        kc0p, nkp, cidp = chunks[ci - 1]
                        p_ = pt[ptc[0] % 3]; ptc[0] += 1
                        s.act(p_[0:nkp, 0:nq], pb[(ci - 1) % 3][0:nkp, 0:nq], AF.Exp)
                        s.mm(po[0:65, 0:nq], vsrc_fn(cidp, nkp), p_[0:nkp, 0:nq], start=(ci == 1), stop=(ci == n_))
                s.cp(oT[:, 0:nq], po[0:65, 0:nq], e="dve")
                for qs in range(nqs):
                    pf = pb[5 + fin[0] % 3]; fin[0] += 1
                    s.tr(pf[:, 0:65], oT[:, qs * 128:(qs + 1) * 128], ident[0:65, 0:65])
                    s.recip(rcp[:, qs:qs + 1], pf[:, 64:65])
                    s.ts(dst_fn(qs), pf[:, 0:64], rcp[:, qs:qs + 1], ALU.mult)

            def attend_T2(qsrc, qc0, nq, chunks, ksrc, vsrc_fn, dst_fn):
                nqs = nq // 128
                n_ = len(chunks)
                po = [pb[4], pb[5]]
                for ci in range(n_ + 1):
                    if ci < n_:
                        kc0, nk, cid = chunks[ci]
                        for g in range(2):
                            lo, hi = 64 * g, 64 * g + 64
                            s.mm(pb[(2 * ci + g) % 4][0:nk, 0:nq], ksrc[lo:hi, kc0:kc0 + nk], qsrc[lo:hi, qc0:qc0 + nq])
                    if ci >= 1:
                        kc0p, nkp, cidp = chunks[ci - 1]
                        for g in range(2):
                            p_ = pt[ptc[0] % 3]; ptc[0] += 1
                            s.act(p_[0:nkp, 0:nq], pb[(2 * (ci - 1) + g) % 4][0:nkp, 0:nq], AF.Exp)
                            s.mm(po[g][0:65, 0:nq], vsrc_fn(cidp, nkp, g), p_[0:nkp, 0:nq], start=(ci == 1), stop=(ci == n_))
                for g in range(2):
                    s.cp(oT[:, 0:nq], po[g][0:65, 0:nq], e="dve")
                    for qs in range(nqs):
                        pf = pb[6 + fin[0] % 2]; fin[0] += 1
                        s.tr(pf[:, 0:65], oT[:, qs * 128:(qs + 1) * 128], ident[0:65, 0:65])
                        s.recip(rcp[:, qs:qs + 1], pf[:, 64:65])
                        s.ts(dst_fn(qs, g), pf[:, 0:64], rcp[:, qs:qs + 1], ALU.mult)

            for qg in range(4):
                for pr_ in range(3):
                    attend_T2(qT[pr_], qg * 512, 512, [(c * 128, 128, c) for c in range(NKT)], kT,
                              lambda cid, nk, g: Vg[0:nk, cid, g, :],
                              lambda qs, g, qg=qg, pr_=pr_: YAN[:, qg * 4 + qs, (pr_ + 3 * g) * 64:(pr_ + 3 * g + 1) * 64])
            for h in range(6):
                pr_, g = h % 3, h // 3
                attend_T(qT[pr_], g, 2048, 256, [(c * 128, 128, c) for c in range(2)], kT,
                         lambda cid, nk, g=g: Vg[0:nk, cid, g, :],
                         lambda qs, h=h: YAN[:, 16 + qs, h * 64:(h + 1) * 64])
            bc_ = [0]
            for i in range(16):
                chs = na_chunks(i)
                base = chs[0][0] * 128
                cls = na_class(i)
                for hn in range(4):
                    pr_, g = hn // 2, hn % 2
                    bt_ = bias_t[bc_[0] % 2]; bc_[0] += 1
                    nkeys = sum(nk for _, nk in chs)
                    s.dma("sp" if hn % 2 == 0 else "act", bt_[:, 0:nkeys], nab[l, cls, hn, :, 0:nkeys])
                    chunks = [(c * 128, nk, c) for c, nk in chs] + [(20 * 128, 128, 20), (21 * 128, 128, 21)]
                    attend(nqT[pr_], g, i * 128, 128, chunks, nkT[pr_],
                           lambda cid, nk, hn=hn: Vn[0:nk, cid, hn, :],
                           lambda cid, nk, bt_=bt_, base=base: (None if cid >= 20 else bt_[:, cid * 128 - base:cid * 128 - base + nk]),
                           lambda qs, i=i, hn=hn: YAN[:, i, 384 + hn * 64:384 + (hn + 1) * 64])
            for hn in range(4):
                pr_, g = hn // 2, hn % 2
                attend(nqT[pr_], g, 2048, 256, [(20 * 128, 128, 20), (21 * 128, 128, 21)], nkT[pr_],
                       lambda cid, nk, hn=hn: Vn[0:nk, cid, hn, :], None,
                       lambda qs, hn=hn: YAN[:, 16 + qs, 384 + hn * 64:384 + (hn + 1) * 64])
            load_w(wbuf, wb[l], 0, 384)
            load_w(wbuf, wb[l], 1024, 384, dcol=384)
            load_w(wbuf, wb[l], 2176, 256, dcol=768)
            LNG = _A(kT.t[:, 0:2048].bitcast(F32), kT.tk); s.dma("sp", LNG[:], lng[l])
            LNB = _A(kT.t[:, 2048:4096].bitcast(F32), kT.tk); s.dma("act", LNB[:], lnb[l])
            Ff = _A(kT.t[:, 4096:5632].bitcast(F32).rearrange("p (s c) -> p s c", s=2), kT.tk)
            Bk = _A(kT.t[:, 5632:7168].bitcast(F32), kT.tk)
            cen = _A(kT.t[:, 7168:7936].bitcast(F32), kT.tk)
            ysb = _A(qT[1].t[:, 0:768].bitcast(F32), qT[1].tk)
            bsb = _A(qT[1].t[:, 768:1536].bitcast(F32), qT[1].tk)
            sqb = _A(qT[2].t[:, 0:768].bitcast(F32), qT[2].tk)
            YgT = _A(qT[0].t[:, 0:1024].rearrange("p (k t) -> p k t", k=8), qT[0].tk)
            for t in range(NOWN):
                j = 1 if t >= 16 else 0
                x_, h_ = front(*own_src(t), j)
                G = pe[0]
                proj(h_, 0, 1024, G)
                s.act(G[:], G[:], AF.Silu)
                for sr in range(2):
                    q = "sp" if sr == 0 else "act"
                    if t >= 16:
                        s.dma(q, Ff[:, sr, :], YGc[sr * 256 + 128 * (t - 16):sr * 256 + 128 * (t - 16) + 128, :])
                        rb = (2 + sr) * 256 + 128 - 128 * (t - 16)
                        s.dma(q, Bk[:, sr * 384:(sr + 1) * 384], YGc[rb:rb + 128, :])
                    else:
                        s.dma(q, Ff[:, sr, :], YF[sr][128 * t:128 * t + 128, :])
                        s.dma(q, Bk[:, sr * 384:(sr + 1) * 384], YBk[sr][1920 - 128 * t:1920 - 128 * t + 128, :])
                for sr in range(2):
                    s.mm(pb[6 + sr][:, 0:384], jmat[:], Bk[:, sr * 384:(sr + 1) * 384])
                v2 = lambda A_: V(A_.t[:, 0:384].rearrange("p (s c) -> p s c", s=2), A_.tk)
                for sr in range(2):
                    s.tt(ysb[:, sr * 192:(sr + 1) * 192], Ff[:, sr, 0:192], pb[6 + sr][:, 0:192], ALU.add)
                    s.tt(bsb[:, sr * 192:(sr + 1) * 192], Ff[:, sr, 192:384], pb[6 + sr][:, 192:384], ALU.add)
                v3 = lambda A_: V(A_.t[:, 0:384].rearrange("p (h d) -> p h d", d=64), A_.tk)
                s.red(sm[:, 0:6], v3(ysb), ALU.add)
                s.ts(sm[:, 0:6], sm[:, 0:6], 1.0 / 64, ALU.mult)
                s.tt(v3(cen), v3(ysb), V(sm.t[:, 0:6].unsqueeze(2).to_broadcast([128, 6, 64]), sm.tk), ALU.subtract)
                s.tt(sqb[:], cen[:], cen[:], ALU.mult, e="pool")
                s.red(sm[:, 8:14], v3(sqb), ALU.add)
                s.ts(sm[:, 8:14], sm[:, 8:14], 1.0 / 64, ALU.mult, 64e-5, ALU.add)
                s.act(sm[:, 8:14], sm[:, 8:14], AF.Sqrt)
                s.recip(sm[:, 8:14], sm[:, 8:14])
                s.tt(v3(cen), v3(cen), V(sm.t[:, 8:14].unsqueeze(2).to_broadcast([128, 6, 64]), sm.tk), ALU.mult)
                s.tt(cen[:], cen[:], GNG[:], ALU.mult)
                s.tt(cen[:], cen[:], GNB[:], ALU.add)
                s.tt(cen[:], cen[:], bsb[:], ALU.add)
                Yg = pe[1]
                s.tt(Yg[:, 0:384], cen[:], G[:, 0:384], ALU.mult)
                s.tt(Yg[:, 384:1024], YAN[:, t, :], G[:, 384:1024], ALU.mult)
                for half in range(2):
                    p = pb[half]
                    for k in range(4):
                        kk_ = half * 4 + k
                        s.tr(p[:, k * 128:(k + 1) * 128], Yg[:, kk_ * 128:(kk_ + 1) * 128], ident[:])
                    s.cp(YgT[:, half * 4:half * 4 + 4, :], V(p.t[:, :].rearrange("p (k t) -> p k t", k=4), p.tk),
                         e="act" if half == 0 else "dve")
                yo = pe[0]
                proj(YgT, 0, 1024, yo, wsrc=woutb)
                s.tt(yo[:], yo[:], mod[j][:, 2048:3072], ALU.mult)
                z = pe[1]
                s.stt(z[:], x_[:], ALPHA, yo[:], ALU.mult, ALU.add)
                s.red(sm[:, 16:17], z[:], ALU.add)
                s.ts(sm[:, 16:17], sm[:, 16:17], 1.0 / 1024, ALU.mult)
                s.ts(z[:], z[:], sm[:, 16:17], ALU.subtract)
                s.tt(yo[:], z[:], z[:], ALU.mult, e="pool")
                s.red(sm[:, 17:18], yo[:], ALU.add)
                s.ts(sm[:, 17:18], sm[:, 17:18], 1.0 / 1024, ALU.mult, 1e-5, ALU.add)
                s.act(sm[:, 17:18], sm[:, 17:18], AF.Sqrt)
                s.recip(sm[:, 17:18], sm[:, 17:18])
                s.stt(z[:], z[:], sm[:, 17:18], LNG[:], ALU.mult, ALU.mult)
                s.tt(z[:], z[:], LNB[:], ALU.add)
                q = "sp" if t % 2 == 0 else "act"
                if t < 16:
                    if last:
                        tickets.append(s.dma(q, out[128 * t:128 * t + 128, :], z[:]))
                    else:
                        s.dma(q, XN[128 * t:128 * t + 128, :], z[:])
                        if t % 2 == 1:
                            k = t // 2
                            s.allgather(XN[256 * k:256 * (k + 1), :], Xn[512 + 1024 * k:512 + 1024 * (k + 1), :], GROUPS)
                elif not last:
                    s.dma(q, Xn[128 * (t - 16):128 * (t - 16) + 128, :], z[:])
    s.finish(tickets)
    s.close()
    return nc


def rope_tables():
    t = np.arange(8192)
    row = (t // 64).astype(np.float32); col = (t % 64).astype(np.float32)
    inv = (10000.0 ** (-np.arange(16, dtype=np.float32) / 16)).astype(np.float32)
    ar = row[:, None] * inv; ac = col[:, None] * inv
    ang = np.concatenate([ar, ar, ac, ac], axis=-1).astype(np.float32)
    cos = np.cos(ang).astype(np.float32); sin = np.sin(ang).astype(np.float32)
    sgn = np.concatenate([-np.ones(16), np.ones(16), -np.ones(16), np.ones(16)]).astype(np.float32)
    return cos, sin * sgn


def na_bias(rpb, j):
    NEG = -30000.0
    out = np.full((5, 4, 128, 768), NEG, np.float32)
    tiles = {0: 0, 1: 1, 2: 7, 3: 14, 4: 15}
    for cls, i in tiles.items():
        chs = na_chunks(i)
        srow0 = chs[0][0] * 2
        nkeys = sum(nk for _, nk in chs)
        qrow_l = np.repeat(np.array([2 * i, 2 * i + 1]), 64)
        qcol = np.tile(np.arange(64), 2)
        r = 32 * j + qrow_l
        r_start = np.clip(r - 4, 0, 120)
        c_start = np.clip(qcol - 8, 0, 48)
        key = np.arange(nkeys)
        krow = (32 * j - 4) + srow0 + key // 64
        kcol = key % 64
        dr = krow[None, :] - r[:, None] + 7
        dc = kcol[None, :] - qcol[:, None] + 15
        inwin = ((krow[None, :] >= r_start[:, None]) & (krow[None, :] < r_start[:, None] + 8) &
                 (kcol[None, :] >= c_start[:, None]) & (kcol[None, :] < c_start[:, None] + 16))
        drc = np.clip(dr, 0, 14); dcc = np.clip(dc, 0, 30)
        for h in range(4):
            vals = rpb[h][drc, dcc]
            out[cls, h, :, 0:nkeys] = np.where(inwin, vals, NEG)
    return out


def consts_A():
    idx = np.arange(64)
    inclT = (idx[:, None] <= idx[None, :]).astype(np.float32)
    strictT = (idx[:, None] < idx[None, :]).astype(np.float32)
    strict = strictT.T.copy()
    mask = np.concatenate([inclT, strictT, inclT, -strictT, -strict], axis=1)
    rmask = np.ones((64, 256), np.float32); rmask[:, 0::64] = 0.0
    ident = np.eye(64, dtype=np.float32)
    return np.concatenate([ident, mask, mask, mask, rmask] + [ident] * 6, axis=1).astype(np.float32)


def host_F(inp, depth=4):
    cos, sinS = rope_tables()
    bc = lambda v: np.ascontiguousarray(np.broadcast_to(v[None, :], (128, v.shape[0]))).astype(np.float32)
    L = range(depth)
    shared = dict(
        wmod=np.ascontiguousarray(inp['w_mod'][:depth]),
        bmod=np.stack([bc(inp['b_mod'][l]) for l in L]),
        wb=np.ascontiguousarray(inp['w_in'][:depth, :, 1280:]),
        wout=np.ascontiguousarray(inp['w_out'][:depth]),
        cstA=consts_A(),
        cosK=np.tile(cos, (1, 2)), sinK=np.tile(sinS, (1, 2)),
        gk=np.stack([bc(np.tile(inp['gqa_k_norm'][l], 2)) for l in L]),
        gq=np.stack([bc(np.tile(inp['gqa_q_norm'][l], 6)) for l in L]),
        gng=np.stack([bc(inp['rwkv_gn_g'][l]) for l in L]), gnb=np.stack([bc(inp['rwkv_gn_b'][l]) for l in L]),
        lng=np.stack([bc(inp['ln_g'][l]) for l in L]), lnb=np.stack([bc(inp['ln_b'][l]) for l in L]),
        ident=np.eye(128, dtype=np.float32), jmat=np.ascontiguousarray(np.eye(128, dtype=np.float32)[::-1]),
    )
    per_batch = []
    for b in range(2):
        xa0 = np.zeros((XROWS, 1024), np.float32)
        xa0[0:256] = inp['ctx'][b]
        xa0[512:8704] = inp['x'][b].reshape(4, 8, 256, 1024).transpose(1, 0, 2, 3).reshape(8192, 1024)
        cvec = np.stack([inp['c'][b], inp['c_ctx']], axis=1)
        cv = np.ascontiguousarray(cvec.reshape(8, 128, 2).transpose(1, 0, 2).reshape(128, 16))
        per_batch.append(dict(xa0=xa0, cv=cv))
    nabs = [np.stack([na_bias(inp['na_rpb'][l], j) for l in L]) for j in range(4)]
    maps = []
    for c in range(8):
        b, r = c // 4, c % 4
        d, hh = r // 2, r % 2
        heads = [3 * hh + i for i in range(3)]
        cols = []
        for comp in (0, 384, 768):
            for h in heads:
                cols += list(range(comp + h * 64, comp + h * 64 + 64))
        cols += list(range(1152 + 32 * d, 1152 + 32 * d + 32))
        cols += list(range(1216 + 32 * d, 1216 + 32 * d + 32))
        cols = np.array(cols)
        hcols = np.concatenate([np.arange(h * 64, (h + 1) * 64) for h in heads])
        wa = np.ascontiguousarray(inp['w_in'][:depth][:, :, cols])
        cw = np.zeros((depth, 64, 33), np.float32); pv = np.zeros((depth, 64, 15), np.float32)
        for l in L:
            conv = inp['rwkv_conv'][l][:, cols]
            if d == 1:
                conv = conv[::-1]
            for ci in range(9):
                cw[l, :, ci * 3:ci * 3 + 3] = conv[:, ci * 64:(ci + 1) * 64].T
            cw[l, 0:32, 27:30] = conv[:, 576:608].T
            cw[l, 0:32, 30:33] = conv[:, 608:640].T
            for i, h in enumerate(heads):
                hs = slice(h * 64, (h + 1) * 64)
                pv[l, :, i * 5 + 0] = inp['decay_w0'][l][d, hs]
                pv[l, :, i * 5 + 1] = inp['iclr_a0'][l][d, hs]
                pv[l, :, i * 5 + 2] = inp['rwkv_k_k'][l][hs]
                pv[l, :, i * 5 + 3] = inp['rwkv_k_a'][l][hs]
                pv[l, :, i * 5 + 4] = inp['rwkv_r_k'][l][h]
        w2 = np.ascontiguousarray(inp['decay_w2'][:depth, d][:, :, hcols])
        a2 = np.ascontiguousarray(inp['iclr_a2'][:depth, d][:, :, hcols])
        own = slice(2048 * r, 2048 * r + 2048)
        jd = np.eye(128, dtype=np.float32)
        if d == 1:
            jd = np.ascontiguousarray(jd[::-1])
        dsel = np.zeros((128, 2), np.float32); dsel[:, d] = 1.0
        xs0 = np.zeros((2560, 1024), np.float32)
        lo = 2048 * r - 256
        for sr_ in range(2560):
            pass
        a0, a1 = max(lo, 0), min(lo + 2560, 8192)
        xs0[a0 - lo:a1 - lo] = inp['x'][b][a0:a1]
        m = dict(shared)
        m['xs0'] = xs0
        m.update(per_batch[b])
        m.update(wa=wa, cw=cw, pv=pv, w2=w2, a2=a2, nab=nabs[r],
                 cosQ=np.tile(cos[own], (1, 6)), sinQ=np.tile(sinS[own], (1, 6)), jd=jd, dsel=dsel)
        maps.append(m)
    return maps


from concourse.bass_utils import run_bass_kernel_spmd

_NC = {}


def kernel(**inputs):
    inp = {k: np.asarray(v, dtype=np.float32) for k, v in inputs.items()}
    if 'F' not in _NC:
        _NC['F'] = build_F(4)
    maps = host_F(inp, 4)
    res = run_bass_kernel_spmd(_NC['F'], maps, core_ids=list(range(8))).results
    x = np.stack([np.concatenate([res[b * 4 + r]["out"] for r in range(4)], axis=0) for b in range(2)])
    return np.ascontiguousarray(x.astype(np.float32))
```

```python
import contextlib
import numpy as np
import concourse.bass as bass
import concourse.mybir as mybir

F32 = mybir.dt.float32
BF16 = mybir.dt.bfloat16
AF = mybir.ActivationFunctionType
ALU = mybir.AluOpType
AX = mybir.AxisListType


class Tk:
    __slots__ = ("w", "r", "name", "excl", "acc")

    def __init__(self, name=""):
        self.w = None
        self.r = {}
        self.name = name
        self.excl = False
        self.acc = {}


class T:
    def __init__(self, S, t, name):
        self.t = t
        self.tk = Tk(name)
        self.name = name

    def __getitem__(self, idx):
        return V(self.t[idx], self.tk)


class V:
    __slots__ = ("ap", "tk")

    def __init__(self, ap, tk):
        self.ap = ap
        self.tk = tk


import threading


class _Worker(threading.Thread):
    def __init__(self, il, fn):
        super().__init__(daemon=True)
        self.il = il
        self.fn = fn
        self.go = threading.Event()
        self.done = False
        self.exc = None

    def run(self):
        self.go.wait(); self.go.clear()
        try:
            self.fn()
        except BaseException as e:
            self.exc = e
        self.done = True
        self.il.main_ev.set()

    def pause(self):
        self.il.main_ev.set()
        self.go.wait(); self.go.clear()


class Interleaver:
    def __init__(self, s):
        self.s = s
        self.main_ev = threading.Event()
        self.cur = None

    def run(self, fns, width):
        pending = list(fns)
        active = []
        self.s.yield_hook = self._hook
        try:
            while pending or active:
                while pending and len(active) < width:
                    w = _Worker(self, pending.pop(0)); w.start(); active.append(w)
                for w in list(active):
                    self.cur = w
                    self.main_ev.clear()
                    w.go.set()
                    self.main_ev.wait()
                    if w.exc is not None:
                        raise w.exc
                    if w.done:
                        active.remove(w)
        finally:
            self.s.yield_hook = None
            self.cur = None

    def _hook(self):
        w = self.cur
        if w is not None and threading.current_thread() is w and self.s.atomic_depth == 0:
            w.pause()


class S:
    ENG = ("pe", "act", "dve", "pool", "sp")
    yield_hook = None
    atomic_depth = 0

    @contextlib.contextmanager
    def atomic(self):
        self.atomic_depth += 1
        try:
            yield
        finally:
            self.atomic_depth -= 1
            if self.atomic_depth == 0 and self.yield_hook is not None:
                self.yield_hook()

    def __init__(self, nc):
        self.nc = nc
        self.es = contextlib.ExitStack()
        self.eng = {"pe": nc.tensor, "act": nc.scalar, "dve": nc.vector, "pool": nc.gpsimd, "sp": nc.sync}
        self.sem = {e: self.es.enter_context(nc.semaphore("s_" + e)) for e in self.ENG}
        self.cnt = {e: 0 for e in self.ENG}
        self.dq = {}
        for q in ("sp", "act", "pool"):
            sems = [self.es.enter_context(nc.semaphore("d_%s%d" % (q, i))) for i in range(8)]
            self.dq[q] = dict(sems=sems, cnt=[0] * 8, nxt=0)
            for i, s_ in enumerate(sems):
                self.sem[(q, i)] = s_
        self.waited = {}
        self.cur = self.es
        self.cc_keys = []
        self.n_tiles = 0
        self.n_instr = 0
        self.n_wait = 0

    def sb(self, shape, dt=F32, name=None):
        self.n_tiles += 1
        name = "%s_%d" % (name or "t", self.n_tiles)
        t = self.cur.enter_context(self.nc.sbuf_tensor("sb_" + name, list(shape), dt))
        return T(self, t, name)

    def dram(self, shape, name, dt=F32):
        self.n_tiles += 1
        t = self.nc.dram_tensor("%s_%d" % (name, self.n_tiles), list(shape), dt)
        return T(self, t.ap(), name)

    @contextlib.contextmanager
    def phase(self):
        prev = self.cur
        self.cur = contextlib.ExitStack()
        try:
            yield
        finally:
            self.barrier()
            self.cur.close()
            self.cur = prev

    def barrier(self):
        for e in self.ENG:
            for e2 in self.ENG:
                if e2 != e and self.cnt[e2] > 0:
                    self._wait(e, e2, self.cnt[e2])
            for q, d in self.dq.items():
                for i, c in enumerate(d["cnt"]):
                    if c > 0:
                        self._wait(e, (q, i), c)

    def allgather(self, src, dst, groups):
        if "cc" not in self.dq:
            sems = [self.es.enter_context(self.nc.semaphore("cc%d" % i)) for i in range(8)]
            self.dq["cc"] = dict(sems=sems, cnt=[0] * 8, nxt=0)
            for i, s_ in enumerate(sems):
                self.sem[("cc", i)] = s_
        d = self.dq["cc"]
        i = d["nxt"]; d["nxt"] = (i + 1) % 8
        key = ("cc", i)
        self._wait("pool", key, d["cnt"][i])
        self._deps("pool", [src], [dst])
        ins = self.nc.gpsimd.collective_compute("AllGather", mybir.AluOpType.bypass, replica_groups=groups,
                                                ins=[src.ap.opt()], outs=[dst.ap.opt()])
        d["cnt"][i] += 1
        ins.then_inc(self.sem[key])
        self._mark((key, d["cnt"][i]), [src], [dst])
        self.n_instr += 1
        return (key, d["cnt"][i])

    def ps(self, shape, dt=F32, name=None):
        self.n_tiles += 1
        name = name or "p%d" % self.n_tiles
        t = self.es.enter_context(self.nc.psum_tensor("ps_" + name, list(shape), dt))
        tt_ = T(self, t, name)
        tt_.tk.excl = True
        return tt_

    def close(self):
        self.es.close()

    def _wait(self, e, key, val):
        if val is None:
            return
        k = (e, key)
        if self.waited.get(k, 0) >= val:
            return
        self.waited[k] = val
        self.eng[e].wait_ge(self.sem[key], val)
        self.n_wait += 1

    def _deps(self, e, reads, writes, pe_acc=False):
        for v in list(reads) + list(writes):
            if v.tk.excl:
                for e2, n2 in v.tk.acc.items():
                    if e2 == e and e == "pe":
                        continue
                    self._wait(e, e2, n2)
        reads = [v for v in reads if not v.tk.excl]
        writes = [v for v in writes if not v.tk.excl]
        for v in reads:
            w = v.tk.w
            if w is not None:
                self._wait(e, w[0], w[1])
        for v in writes:
            tk = v.tk
            if tk.w is not None:
                if not (pe_acc and tk.w[0] == "pe" and e == "pe"):
                    self._wait(e, tk.w[0], tk.w[1])
            for re_, rn in tk.r.items():
                if re_ == e and e == "pe":
                    continue
                self._wait(e, re_, rn)

    def _mark(self, ticket, reads, writes):
        for v in list(reads) + list(writes):
            if v.tk.excl:
                v.tk.acc[ticket[0]] = ticket[1]
        reads = [v for v in reads if not v.tk.excl]
        writes = [v for v in writes if not v.tk.excl]
        for v in reads:
            v.tk.r[ticket[0]] = ticket[1]
        for v in writes:
            v.tk.w = ticket
            v.tk.r = {}

    def op(self, e, fn, reads, writes, pe_acc=False):
        reads = [v for v in reads if isinstance(v, V)]
        self._deps(e, reads, writes, pe_acc)
        ins = fn()
        self.cnt[e] += 1
        ins.then_inc(self.sem[e], 1)
        self._mark((e, self.cnt[e]), reads, writes)
        self.n_instr += 1
        if self.yield_hook is not None:
            self.yield_hook()
        return ins

    def dma(self, q, out, in_, **kw):
        d = self.dq[q]
        i = d["nxt"]
        d["nxt"] = (i + 1) % len(d["sems"])
        key = (q, i)
        self._wait(q, key, d["cnt"][i])
        reads = [in_] if isinstance(in_, V) else []
        writes = [out] if isinstance(out, V) else []
        self._deps(q, reads, writes)
        oa = out.ap if isinstance(out, V) else out
        ia = in_.ap if isinstance(in_, V) else in_
        ins = self.eng[q].dma_start(out=oa, in_=ia, **kw)
        d["cnt"][i] += 16
        ins.then_inc(self.sem[key], 16)
        self._mark((key, d["cnt"][i]), reads, writes)
        self.n_instr += 1
        if self.yield_hook is not None:
            self.yield_hook()
        return (key, d["cnt"][i])

    def wait_ticket(self, e, ticket):
        self._wait(e, ticket[0], ticket[1])

    def mm(self, out, lhsT, rhs, start=True, stop=True, **kw):
        return self.op("pe", lambda: self.nc.tensor.matmul(out.ap, lhsT.ap, rhs.ap, start=start, stop=stop, **kw),
                       [lhsT, rhs], [out], pe_acc=not start)

    def tr(self, out, in_, ident):
        return self.op("pe", lambda: self.nc.tensor.transpose(out.ap, in_.ap, ident.ap), [in_, ident], [out])

    def act(self, out, in_, func, bias=None, scale=None, accum_out=None, e="act"):
        kw = {}
        rd = [in_]
        if bias is not None:
            kw["bias"] = bias.ap if isinstance(bias, V) else bias
            rd.append(bias)
        if scale is not None:
            kw["scale"] = scale.ap if isinstance(scale, V) else scale
            rd.append(scale)
        wr = [out]
        if accum_out is not None:
            kw["accum_out"] = accum_out.ap
            wr.append(accum_out)
        return self.op("act", lambda: self.nc.scalar.activation(out.ap, in_.ap, func, **kw), rd, wr)

    def _ve(self, e):
        return {"dve": self.nc.vector, "pool": self.nc.gpsimd, "act": self.nc.scalar}[e]

    def tt(self, out, a, b, op, e="dve"):
        return self.op(e, lambda: self._ve(e).tensor_tensor(out.ap, a.ap, b.ap, op), [a, b], [out])

    def ts(self, out, a, s1, op0, s2=None, op1=None, e="dve", accum_out=None):
        rd = [a, s1, s2]
        a1 = s1.ap if isinstance(s1, V) else s1
        a2 = s2.ap if isinstance(s2, V) else s2
        kw = {}
        wr = [out]
        if op1 is not None:
            kw["op1"] = op1
        if accum_out is not None:
            kw["accum_out"] = accum_out.ap
            wr.append(accum_out)
        return self.op(e, lambda: self._ve(e).tensor_scalar(out.ap, a.ap, a1, a2, op0, **kw), rd, wr)

    def stt(self, out, a, s, b, op0, op1, e="dve"):
        sa = s.ap if isinstance(s, V) else s
        return self.op(e, lambda: self._ve(e).scalar_tensor_tensor(out.ap, a.ap, sa, b.ap, op0, op1), [a, s, b], [out])

    def cp(self, out, in_, e="dve"):
        if e == "act":
            return self.op("act", lambda: self.nc.scalar.copy(out.ap, in_.ap), [in_], [out])
        return self.op(e, lambda: self._ve(e).tensor_copy(out.ap, in_.ap), [in_], [out])

    def memset(self, out, val, e="pool"):
        return self.op(e, lambda: self._ve(e).memset(out.ap, val), [], [out])

    def red(self, out, in_, op, axis=AX.X, e="dve"):
        return self.op(e, lambda: self._ve(e).tensor_reduce(out.ap, in_.ap, axis, op), [in_], [out])

    def recip(self, out, in_):
        return self.op("dve", lambda: self.nc.vector.reciprocal(out.ap, in_.ap), [in_], [out])

    def finish(self, tickets):
        for t in tickets:
            self._wait("sp", t[0], t[1])


A_DEC = 0.6065306597126334
ALPHA = (2 * 4) ** 0.25
NOWN = 18
NKT = 66
NNT = 22
GROUPS = [[0, 1, 2, 3], [4, 5, 6, 7]]
XROWS = 8960


def na_chunks(i):
    if i == 0:
        return [(c, 128) for c in range(0, 6)]
    if i == 1:
        return [(c, 128) for c in range(1, 6)]
    if i == 15:
        return [(c, 128) for c in range(14, 19)] + [(19, 64)]
    return [(c, 128) for c in range(i, i + 4)] + [(i + 4, 64)]


def na_class(i):
    return {0: 0, 1: 1, 14: 3, 15: 4}.get(i, 2)


def lat_row(tau):
    rho, rem = divmod(tau, 2048)
    k, i = divmod(rem, 256)
    return 512 + 1024 * k + 256 * rho + i


class _A:
    def __init__(self, ap, tk):
        self.t = ap; self.tk = tk

    def __getitem__(self, idx):
        return V(self.t[idx], self.tk)


def build_F(depth=4):
    nc = bass.Bass("TRN2", target_bir_lowering=False)
    dt = nc.dram_tensor
    I = lambda n, sh: dt(n, sh, F32, kind="ExternalInput").ap()
    xa0 = I("xa0", [XROWS, 1024]); xs0 = I("xs0", [2560, 1024])
    cv = I("cv", [128, 16]); wmod = I("wmod", [depth, 1024, 3072]); bmod = I("bmod", [depth, 128, 3072])
    wb = I("wb", [depth, 1024, 2432]); wout = I("wout", [depth, 1024, 1024])
    wa = I("wa", [depth, 1024, 640]); cw = I("cw", [depth, 64, 33]); pvi = I("pv", [depth, 64, 15])
    w2 = I("w2", [depth, 32, 192]); a2 = I("a2", [depth, 32, 192])
    cstA = I("cstA", [64, 1664])
    cosK = I("cosK", [8192, 128]); sinK = I("sinK", [8192, 128])
    cosQ = I("cosQ", [2048, 384]); sinQ = I("sinQ", [2048, 384])
    dsel_in = I("dsel", [128, 2])
    gk = I("gk", [depth, 128, 128]); gq = I("gq", [depth, 128, 384])
    nab = I("nab", [depth, 5, 4, 128, 768])
    gng = I("gng", [depth, 128, 384]); gnb = I("gnb", [depth, 128, 384])
    lng = I("lng", [depth, 128, 1024]); lnb = I("lnb", [depth, 128, 1024])
    ident_in = I("ident", [128, 128]); jmat_in = I("jmat", [128, 128]); jd_in = I("jd", [128, 128])
    out = dt("out", [2048, 1024], F32, kind="ExternalOutput").ap()

    s = S(nc)
    tickets = []
    pb = [s.ps([128, 512], name="pb%d" % i) for i in range(8)]
    XA = [s.dram([XROWS, 1024], "XA%d" % i) for i in range(2)]
    YB = s.dram([8448, 384], "YB")
    YGc = s.dram([4 * 256, 384], "YGc")
    YGl = s.dram([16 * 4 * 512, 384], "YGl")
    XN = s.dram([2048, 1024], "XN")
    XS = s.dram([2560, 1024], "XS")
    KTO = s.dram([128, 2048], "KTO", BF16); KTG = s.dram([512, 2048], "KTG", BF16)
    VOd = s.dram([2048, 130], "VOd", BF16); VGd = s.dram([8192, 130], "VGd", BF16)
    YF = [s.dram([2048, 384], "YF%d" % i) for i in range(2)]
    YBk = [s.dram([2048, 384], "YBk%d" % i) for i in range(2)]
    xa0_T = _A(xa0, Tk("xa0")); xs0_T = _A(xs0, Tk("xs0"))

    _rd = {}

    def RR(q):
        if q not in _rd:
            pid = s.eng[q].partition_id()
            _rd[q] = pid % 4
        return _rd[q]

    dynq = ["sp", "act", "pool"]
    dync = [0]

    def dyndma(dst_v, src_fn):
        q = dynq[dync[0] % 3]; dync[0] += 1
        return s.dma(q, dst_v, src_fn(RR(q)))

    ident = s.sb([128, 128], name="ident"); s.dma("sp", ident[:], ident_in)
    jmat = s.sb([128, 128], name="jmat"); s.dma("act", jmat[:], jmat_in)
    jd = s.sb([128, 128], name="jd"); s.dma("sp", jd[:], jd_in)
    dsel = s.sb([128, 2], name="dsel"); s.dma("act", dsel[:], dsel_in)
    identb = s.sb([128, 128], BF16, name="identb"); s.cp(identb[:], ident[:])
    jdb = s.sb([128, 128], BF16, name="jdb"); s.cp(jdb[:], jd[:])
    ones = s.sb([128, 128], name="ones"); s.memset(ones[:], 1.0)
    cv_t = s.sb([128, 16], name="cv"); s.dma("sp", cv_t[:], cv)
    scv = s.sb([128, 16], name="scv"); s.act(scv[:], cv_t[:], AF.Silu)
    mod = [s.sb([128, 3072], name="mod%d" % j) for j in range(2)]
    for l in range(depth):
        Xc = xa0_T if l == 0 else XA[(l - 1) % 2]
        Xn = XA[l % 2]
        last = (l == depth - 1)

        if l == 0:
            XSc = xs0_T
        else:
            XSc = XS
            lat = Xc.t[512:8704, :]
            dyndma(V(XS.t[256:2304, :].rearrange("(o k i) c -> o k (i c)", o=1, k=8), XS.tk),
                   lambda r: V(lat.rearrange("(k rr i) c -> rr k (i c)", rr=4, i=256)[bass.ds(r, 1)], Xc.tk))
            units = lat.rearrange("(u i) c -> u (i c)", i=256)
            dyndma(V(XS.t[0:256, :].rearrange("(o i) c -> o (i c)", o=1), XS.tk),
                   lambda r: V(units[bass.ds(r + 27, 1), :], Xc.tk))
            dyndma(V(XS.t[2304:2560, :].rearrange("(o i) c -> o (i c)", o=1), XS.tk),
                   lambda r: V(units[bass.ds(r + 1, 1), :], Xc.tk))
        with s.phase():
            stage_ws = [s.sb([128, 8, 512], name="stage_w%d" % i) for i in range(2)]
            Rl = s.sb([128, 16, 128], name="Rl")
            bmod_ts = [s.sb([128, 512], name="bmodt%d" % i) for i in range(2)]
            for i in range(16):
                s.ts(Rl[:, i, :], ones[:], scv[:, i:i + 1], ALU.mult, e="dve" if i % 2 == 0 else "pool")
            for cb in range(6):
                stage_w = stage_ws[cb % 2]; bmod_t = bmod_ts[cb % 2]
                s.dma("sp", stage_w[:], wmod[l].rearrange("(k p) c -> p k c", p=128)[:, :, cb * 512:(cb + 1) * 512])
                s.dma("act", bmod_t[:], bmod[l][:, cb * 512:(cb + 1) * 512])
                for j in range(2):
                    ps = pb[(2 * cb + j) % 4]
                    for k in range(8):
                        s.mm(ps[:, :], Rl[:, 2 * k + j, :], stage_w[:, k, :], start=(k == 0), stop=(k == 7))
                    s.tt(mod[j][:, cb * 512:(cb + 1) * 512], ps[:, :], bmod_t[:], ALU.add, e="dve")
            for j in range(2):
                s.ts(mod[j][:, 1024:2048], mod[j][:, 1024:2048], 1.0, ALU.add, e="pool")

        with s.phase():
            cst_t = s.sb([64, 1664], name="cst"); s.dma("sp", cst_t[:], cstA)
            identA = cst_t[:, 0:64]
            mask3 = lambda h: cst_t[:, 64 + h * 320: 64 + (h + 1) * 320]
            rmask = cst_t[:, 1024:1280]
            idt3 = cst_t[:, 1280:1664]
            cw_t = s.sb([64, 33], name="cw"); s.dma("act", cw_t[:], cw[l])
            pv_t = s.sb([64, 16], name="pv"); s.dma("act", pv_t[:, 0:15], pvi[l])
            omk = s.sb([64, 3], name="omk")
            for h in range(3):
                s.ts(omk[:, h:h + 1], pv_t[:, h * 5 + 3:h * 5 + 4], -1.0, ALU.mult, 1.0, ALU.add)
            w2_t = s.sb([32, 192], name="w2"); s.dma("act", w2_t[:], w2[l])
            a2_t = s.sb([32, 192], name="a2"); s.dma("act", a2_t[:], a2[l])
            xt = [s.sb([128, 1024], name="xt%d" % i) for i in range(2)]
            ht = s.sb([128, 1024], name="ht")
            xr = s.sb([128, 1024], name="xr")
            hbA = s.sb([128, 1024], BF16, name="hbA")
            wab = s.sb([128, 8, 640], BF16, name="wab")
            for k in range(8):
                st_ = xt[k % 2]
                s.dma("sp" if k % 2 == 0 else "act", st_[:, 0:640], wa[l][k * 128:(k + 1) * 128, :])
                s.cp(wab[:, k, :], st_[:, 0:640], e="dve" if k % 2 == 0 else "pool")
            cts = [(i * 64, 64) for i in range(9)] + [(576, 32), (608, 32)]
            hg = [s.sb([128, 8, 258], BF16, name="hg%d" % i) for i in range(2)]
            for h_ in hg:
                s.memset(h_[:], 0.0)
            raw = [s.sb([64, 258], name="raw%d" % i) for i in range(2)]
            ctmp = [s.sb([64, 256], name="ctmp%d" % i) for i in range(2)]
            mk = lambda nm, shape=(64, 256), dt_=F32: [s.sb(list(shape), dt_, name="%s%d" % (nm, h)) for h in range(3)]
            uR, uK, uV = mk("uR"), mk("uK"), mk("uV", dt_=BF16)
            uD = s.sb([32, 256], name="uD"); uA = s.sb([32, 256], name="uA"); ddt = s.sb([32, 256], name="ddt")
            sg, ic, kk, tmp, kd, bd = mk("sg"), mk("ic"), mk("kk"), mk("tmp"), mk("kd"), mk("bd")
            cs, csx, csr = mk("cs"), mk("csx"), mk("csr")
            E1, E3 = mk("E1"), mk("E3")
            E4 = E1
            RH, KKH, kt, bt, kc, bc, rk = (mk("RH", dt_=BF16), mk("KKH", dt_=BF16), mk("kt", dt_=BF16), mk("bt", dt_=BF16),
                                           mk("kc", dt_=BF16), mk("bc", dt_=BF16), mk("rk", dt_=BF16))
            identAb = s.sb([64, 64], BF16, name="identAb"); s.cp(identAb[:], identA)
            onesb = s.sb([64, 1], BF16, name="onesb"); s.memset(onesb[:], 1.0)
            wc = mk("wc", (64, 4)); rn = tmp
            trT = [s.sb([64, 3, 256], BF16, name="trT%d" % i) for i in range(4)]
            scS = [s.sb([64, 3, 320], BF16, name="scS%d" % i) for i in range(4)]
            XYs = [[s.sb([64, 3, 128], BF16, name="XY%d_%d" % (c, i)) for i in range(2)] for c in range(4)]
            PQs = [[s.sb([64, 3, 128], BF16, name="PQ%d_%d" % (c, i)) for i in range(2)] for c in range(4)]
            KKpTs = [s.sb([64, 192], BF16, name="KKpT%d" % c) for c in range(4)]
            AVs = [s.sb([64, 192], BF16, name="AV%d" % c) for c in range(4)]
            Ulocs = [s.sb([64, 192], name="Uloc%d" % c) for c in range(4)]
            Us = [s.sb([64, 192], BF16, name="U%d" % c) for c in range(4)]
            STb = [s.sb([64, 192], BF16, name="STb%d" % i) for i in range(2)]
            bss = [s.sb([64, 4], name="bs%d" % c) for c in range(4)]
            il = Interleaver(s)
            ST = [s.sb([64, 192], name="ST%d" % i) for i in range(2)]
            YBuf = [s.sb([64, 4, 384], name="YBuf0")] * 2
            bs_ = s.sb([64, 4], name="bs")
            s.memset(ST[0][:], 0.0)
            s.memset(STb[0][:], 0.0)
            sti = 0
            acnt = [0]

            def frontA(g):
                hgt = hg[g % 2]
                for a in range(2):
                    u = 2 * g + a
                    i = acnt[0]; acnt[0] += 1
                    if u < 2:
                        bf_, br_ = 128 * u, 128 * (1 - u)
                        j = 1
                    else:
                        v = u - 2
                        bf_, br_ = lat_row(128 * v), lat_row(128 * (63 - v))
                        j = 0
                    x_ = xt[i % 2]
                    s.dma("sp", x_[:], V(Xc.t[bf_:bf_ + 128, :], Xc.tk))
                    s.dma("act", xr[:], V(Xc.t[br_:br_ + 128, :], Xc.tk))
                    s.act(x_[:], x_[:], AF.Identity, scale=dsel[:, 0:1])
                    s.stt(x_[:], xr[:], dsel[:, 1:2], x_[:], ALU.mult, ALU.add)
                    s.tt(ht[:], x_[:], mod[j][:, 1024:2048], ALU.mult, e="pool")
                    s.tt(hbA[:], ht[:], mod[j][:, 0:1024], ALU.add, e="dve")
                    for half in range(2):
                        p = pb[half]
                        with s.atomic():
                            pbf = p.t[:, 0:256].bitcast(BF16)
                            for k in range(4):
                                kk_ = half * 4 + k
                                s.tr(V(pbf[:, k * 128:(k + 1) * 128], p.tk), hbA[:, kk_ * 128:(kk_ + 1) * 128], jdb[:])
                            s.cp(hgt[:, half * 4:half * 4 + 4, 1 + 128 * a:1 + 128 * (a + 1)],
                                 V(pbf.rearrange("p (k t) -> p k t", k=4), p.tk), e="act" if half == 0 else "dve")

            NGRP = 33
            OPN = ('RH', 'KKH', 'kt', 'bt', 'kc', 'bc', 'rk', 'uV')
            opsets = [dict(RH=RH, KKH=KKH, kt=kt, bt=bt, kc=kc, bc=bc, rk=rk, uV=uV, wc=wc),
                      dict(RH=mk('RHb', dt_=BF16), KKH=mk('KKHb', dt_=BF16), kt=mk('ktb', dt_=BF16), bt=mk('btb', dt_=BF16),
                           kc=mk('kcb', dt_=BF16), bc=mk('bcb', dt_=BF16), rk=mk('rkb', dt_=BF16), uV=mk('uVb', dt_=BF16),
                           wc=mk('wcb', (64, 4)))]

            def pro1(g):
                O = opsets[g % 2]; uV = O['uV']
                first = g in (0, 1)
                lastg = g in (0, NGRP - 1)
                hgt = hg[g % 2]
                if g + 1 < NGRP:
                    frontA(g + 1)
                    hn_ = hg[(g + 1) % 2]
                    s.cp(hgt[:, :, 257:258], hn_[:, :, 1:2], e="pool")
                    s.cp(hn_[:, :, 0:1], hgt[:, :, 256:257], e="pool")
                for ci, (c0, M) in enumerate(cts):
                    pr = pb[ci % 2]
                    rw = raw[ci % 2]
                    with s.atomic():
                        for k in range(8):
                            s.mm(pr[0:M, 0:258], wab[:, k, c0:c0 + M], hgt[:, k, :], start=(k == 0), stop=(k == 7))
                        s.cp(rw[0:M, :], pr[0:M, 0:258], e="act" if ci % 2 == 0 else "dve")
                    if first:
                        s.memset(rw[0:M, 0:1], 0.0, e="pool")
                    if lastg:
                        s.memset(rw[0:M, 257:258], 0.0, e="pool")
                    dst = (uR, uK, uV)[ci // 3][ci % 3] if ci < 9 else (uD, uA)[ci - 9]
                    tm = ctmp[ci % 2]
                    s.act(tm[0:M, :], rw[0:M, 0:256], AF.Identity, scale=cw_t[0:M, ci * 3:ci * 3 + 1])
                    s.stt(tm[0:M, :], rw[0:M, 1:257], cw_t[0:M, ci * 3 + 1:ci * 3 + 2], tm[0:M, :], ALU.mult, ALU.add)
                    s.stt(dst[0:M, :], rw[0:M, 2:258], cw_t[0:M, ci * 3 + 2:ci * 3 + 3], tm[0:M, :], ALU.mult, ALU.add)
                s.act(ddt[:], uD[:], AF.Tanh)

            def prep_head(g, h):
                O = opsets[g % 2]
                RH, KKH, kt, bt, kc, bc, rk, wc = O['RH'], O['KKH'], O['kt'], O['bt'], O['kc'], O['bc'], O['rk'], O['wc']
                P = lambda i: pv_t[:, h * 5 + i:h * 5 + i + 1]
                pz = pb[2 + h]
                with s.atomic():
                    s.mm(pz[0:64, 0:256], w2_t[:, h * 64:(h + 1) * 64], ddt[:])
                    s.act(sg[h][:], pz[0:64, 0:256], AF.Sigmoid, bias=P(0))
                with s.atomic():
                    s.mm(pz[0:64, 256:512], a2_t[:, h * 64:(h + 1) * 64], uA[:])
                    s.act(ic[h][:], pz[0:64, 256:512], AF.Sigmoid, bias=P(1))
                s.act(kk[h][:], uK[h][:], AF.Identity, scale=P(2))
                s.act(tmp[h][:], kk[h][:], AF.Square)
                pss = pb[2 + h]
                with s.atomic():
                    s.mm(pss[0:64, 0:256], ones[0:64, 0:64], tmp[h][:])
                    s.ts(rn[h][:], pss[0:64, 0:256], 1e-12, ALU.max)
                s.act(rn[h][:], rn[h][:], AF.Sqrt)
                s.recip(rn[h][:], rn[h][:])
                s.tt(kk[h][:], kk[h][:], rn[h][:], ALU.mult)
                s.act(tmp[h][:], ic[h][:], AF.Identity, scale=P(3), bias=omk[:, h:h + 1])
                s.tt(kd[h][:], uK[h][:], tmp[h][:], ALU.mult, e="dve")
                s.tt(bd[h][:], kk[h][:], ic[h][:], ALU.mult, e="pool")
                s.op("dve", lambda h=h: nc.vector.tensor_tensor_scan(cs[h][:].ap, rmask.ap, sg[h][:].ap, 0.0, ALU.mult, ALU.add),
                     [rmask, sg[h][:]], [cs[h][:]])
                s.tt(csx[h][:], cs[h][:], sg[h][:], ALU.subtract)
                for c in range(4):
                    s.ts(csr[h][:, c * 64:(c + 1) * 64], cs[h][:, c * 64:(c + 1) * 64],
                         cs[h][:, c * 64 + 63:c * 64 + 64], ALU.subtract)
                s.act(wc[h][:], cs[h][:, 63::64], AF.Exp, scale=-A_DEC)
                s.act(E1[h][:], cs[h][:], AF.Exp, scale=-A_DEC)
                s.tt(RH[h][:], uR[h][:], E1[h][:], ALU.mult, e="dve")
                s.act(E1[h][:], csx[h][:], AF.Exp, scale=-A_DEC)
                s.tt(KKH[h][:], kk[h][:], E1[h][:], ALU.mult, e="pool")
                s.act(E3[h][:], cs[h][:], AF.Exp, scale=A_DEC)
                s.tt(kt[h][:], kd[h][:], E3[h][:], ALU.mult)
                s.tt(bt[h][:], bd[h][:], E3[h][:], ALU.mult, e="pool")
                s.act(E4[h][:], csr[h][:], AF.Exp, scale=A_DEC)
                s.tt(kc[h][:], kd[h][:], E4[h][:], ALU.mult)
                s.tt(bc[h][:], bd[h][:], E4[h][:], ALU.mult, e="pool")
                s.stt(rk[h][:], uR[h][:], P(4), kd[h][:], ALU.mult, ALU.mult)

            def chunk_body(g, c):
                n = 4 * g + c
                O = opsets[g % 2]
                RH, KKH, kt, bt, kc, bc, rk, uV = O['RH'], O['KKH'], O['kt'], O['bt'], O['kc'], O['bc'], O['rk'], O['uV']
                cc = slice(c * 64, (c + 1) * 64)
                tT = trT[c]; sS = scS[c]
                XY = XYs[c]; PQ = PQs[c]; KKpT = KKpTs[c]; AV = AVs[c]; Uloc = Ulocs[c]; U = Us[c]; bs_ = bss[c]
                p_ = c % 2
                for h in range(3):
                    ptr = pb[2 + p_]
                    with s.atomic():
                        ptrb = ptr.t[0:64, 0:128].bitcast(BF16)
                        for i, src_ in enumerate((KKH, bc, kc, uV)):
                            s.tr(V(ptrb[:, i * 64:(i + 1) * 64], ptr.tk), src_[h][:, cc], identAb[:])
                        s.cp(tT[:, h, :], V(ptrb[:, 0:256], ptr.tk), e="act")
                    psc = pb[4 + p_]
                    with s.atomic():
                        s.mm(psc[0:64, 0:64], kt[h][:, cc], RH[h][:, cc])
                        s.mm(psc[0:64, 64:128], kt[h][:, cc], KKH[h][:, cc])
                        s.mm(psc[0:64, 128:192], bt[h][:, cc], RH[h][:, cc])
                        s.mm(psc[0:64, 192:256], bt[h][:, cc], KKH[h][:, cc])
                        s.mm(psc[0:64, 256:320], KKH[h][:, cc], bt[h][:, cc])
                        s.tt(sS[:, h, :], psc[0:64, 0:320], mask3(h), ALU.mult)
                s.tt(PQ[0][:, :, :], sS[:, :, 192:320], V(idt3.ap.rearrange("p (h c) -> p h c", c=128), idt3.tk), ALU.add, e="pool")
                Xc_ = lambda lvl, h: (sS[:, h, 192:256] if lvl == 0 else XY[lvl % 2][:, h, 0:64])
                Yc_ = lambda lvl, h: (sS[:, h, 256:320] if lvl == 0 else XY[lvl % 2][:, h, 64:128])
                for lvl in range(5):
                    pn, pq = pb[4 + p_], pb[6 + p_]
                    nxt = XY[(lvl + 1) % 2]
                    with s.atomic():
                        for h in range(3):
                            s.mm(pn[0:64, h * 128:h * 128 + 64], Yc_(lvl, h), Xc_(lvl, h))
                            if lvl < 4:
                                s.mm(pn[0:64, h * 128 + 64:h * 128 + 128], Xc_(lvl, h), Yc_(lvl, h))
                        pn3 = pn.t[0:64, 0:384].rearrange("p (h c) -> p h c", c=128)
                        if lvl < 4:
                            s.cp(nxt[:, :, :], V(pn3, pn.tk), e="act")
                        else:
                            s.cp(nxt[:, :, 0:64], V(pn3[:, :, 0:64], pn.tk), e="act")
                    Pc, Pn = PQ[lvl % 2], PQ[(lvl + 1) % 2]
                    with s.atomic():
                        for h in range(3):
                            s.mm(pq[0:64, h * 128:h * 128 + 64], Pc[:, h, 64:128], nxt[:, h, 0:64])
                            if lvl < 4:
                                s.mm(pq[0:64, h * 128 + 64:h * 128 + 128], Pc[:, h, 0:64], nxt[:, h, 64:128])
                        pq3 = pq.t[0:64, 0:384].rearrange("p (h c) -> p h c", c=128)
                        if lvl < 4:
                            s.tt(Pn[:, :, :], V(pq3, pq.tk), Pc[:, :, :], ALU.add)
                        else:
                            s.tt(Pn[:, :, 0:64], V(pq3[:, :, 0:64], pq.tk), Pc[:, :, 0:64], ALU.add)
                TT = PQ[1]
                pk = pb[2 + p_]
                with s.atomic():
                    for h in range(3):
                        s.mm(pk[0:64, h * 64:(h + 1) * 64], tT[:, h, 0:64], TT[:, h, 0:64])
                        s.mm(pk[0:64, 192 + h * 64:192 + (h + 1) * 64], sS[:, h, 64:128], tT[:, h, 192:256])
                    s.cp(KKpT[:], pk[0:64, 0:192], e="act")
                    s.cp(AV[:], pk[0:64, 192:384], e="dve")
                pk3 = pb[6 + p_]
                with s.atomic():
                    for h in range(3):
                        s.mm(pk3[0:64, h * 64:(h + 1) * 64], TT[:, h, 0:64], AV[:, h * 64:(h + 1) * 64])
                    s.cp(Uloc[:], pk3[0:64, 0:192], e="act")

            def chunk_seq(g, c):
                n = 4 * g + c
                yb = YBuf[0]
                O = opsets[g % 2]
                RH, rk, wc = O['RH'], O['rk'], O['wc']
                cc = slice(c * 64, (c + 1) * 64)
                tT = trT[c]; sS = scS[c]
                KKpT = KKpTs[c]; Uloc = Ulocs[c]; U = Us[c]; bs_ = bss[c]
                Sc, Sn = ST[n % 2], ST[(n + 1) % 2]
                Scb, Snb = STb[n % 2], STb[(n + 1) % 2]
                pu = pb[0]
                with s.atomic():
                    for h in range(3):
                        s.mm(pu[0:64, h * 64:(h + 1) * 64], KKpT[:, h * 64:(h + 1) * 64], Scb[:, h * 64:(h + 1) * 64])
                    s.stt(U[:], pu[0:64, 0:192], -1.0, Uloc[:], ALU.mult, ALU.subtract)
                pS = pb[1]
                with s.atomic():
                    for h in range(3):
                        hs = slice(h * 64, (h + 1) * 64)
                        s.mm(pS[0:64, hs], tT[:, h, 128:192], tT[:, h, 192:256], start=True, stop=False)
                        s.mm(pS[0:64, hs], tT[:, h, 64:128], U[:, hs], start=False, stop=True)
                    for h in range(3):
                        hs = slice(h * 64, (h + 1) * 64)
                        s.stt(Sn[:, hs], Sc[:, hs], wc[h][:, c:c + 1], pS[0:64, hs], ALU.mult, ALU.add)
                    s.cp(Snb[:], Sn[:], e="act")
                py = pb[0]
                with s.atomic():
                    for h in range(3):
                        hs = slice(256 + h * 64, 256 + (h + 1) * 64)
                        hh = slice(h * 64, (h + 1) * 64)
                        s.mm(py[0:64, hs], RH[h][:, cc], Scb[:, hh], start=True, stop=False)
                        s.mm(py[0:64, hs], sS[:, h, 128:192], U[:, hh], start=False, stop=False)
                        s.mm(py[0:64, hs], sS[:, h, 0:64], tT[:, h, 192:256], start=False, stop=True)
                    s.cp(yb[:, c, 0:192], py[0:64, 256:448], e="act")
                pbn = pb[1]
                with s.atomic():
                    for h in range(3):
                        s.mm(pbn[0:64, 256 + h:256 + h + 1], rk[h][:, cc], onesb[:, 0:1])
                    s.cp(bs_[:, 0:3], pbn[0:64, 256:259], e="dve")
                for h in range(3):
                    s.ts(yb[:, c, 192 + h * 64:192 + (h + 1) * 64], tT[:, h, 192:256], bs_[:, h:h + 1], ALU.mult, e="pool")
            def seq_group(g):
                tbase = 256 * g
                for c in range(4):
                    chunk_seq(g, c)
                s.dma("sp" if g % 2 == 0 else "act",
                      V(YB.t[tbase:tbase + 256, :].rearrange("(c t) f -> t c f", t=64), YB.tk), YBuf[0][:, :, :])
                if g == 0:
                    s.allgather(YB[0:256, :], YGc[:, :], GROUPS)
                elif g % 2 == 0:
                    m = g // 2 - 1
                    s.allgather(YB[256 + 512 * m:256 + 512 * (m + 1), :], YGl[2048 * m:2048 * (m + 1), :], GROUPS)

            frontA(0)
            pro1(0)
            il.run([(lambda h=h: prep_head(0, h)) for h in range(3)], 3)
            for g in range(NGRP):
                wk1 = [(lambda c=c: chunk_body(g, c)) for c in range(4)]
                if g + 1 < NGRP:
                    wk1.append(lambda: pro1(g + 1))
                il.run(wk1, 5)
                wk2 = [lambda: seq_group(g)]
                if g + 1 < NGRP:
                    wk2 += [(lambda h=h: prep_head(g + 1, h)) for h in range(3)]
                il.run(wk2, 4)
        ygv = YGl.t.rearrange("(m sr i) c -> sr m (i c)", sr=4, i=512)
        for sr in range(2):
            dyndma(V(YF[sr].t.rearrange("(m i) c -> m (i c)", i=512), YF[sr].tk),
                   lambda r, sr=sr: V(ygv[sr][bass.ds(r * 4, 4), :], YGl.tk))
            dyndma(V(YBk[sr].t.rearrange("(m i) c -> m (i c)", i=512), YBk[sr].tk),
                   lambda r, sr=sr: V(ygv[2 + sr][bass.ds((3 - r) * 4, 4), :], YGl.tk))
        with s.phase():
            GK = s.sb([128, 128], name="GK"); s.dma("act", GK[:], gk[l])
            GQ = s.sb([128, 384], name="GQ"); s.dma("act", GQ[:], gq[l])
            GNG = s.sb([128, 384], name="GNG"); s.dma("act", GNG[:], gng[l])
            GNB = s.sb([128, 384], name="GNB"); s.dma("act", GNB[:], gnb[l])
            YAN = s.sb([128, 18, 640], BF16, name="YAN")
            wbuf = s.sb([128, 8, 1024], BF16, name="wbuf")
            woutb = s.sb([128, 8, 1024], BF16, name="woutb")
            xt = [s.sb([128, 1024], name="xt%d" % i) for i in range(2)]
            hts = [s.sb([128, 1024], name="ht%d" % i) for i in range(2)]
            ht = hts[0]
            pe = [s.sb([128, 1024], name="pe%d" % i) for i in range(2)]
            sqs = [pe[1], None]
            wst = xt
            hbs = []
            ilB = Interleaver(s)
            pjB = _A(YAN.t[:, 0:2, :].rearrange("p a c -> p (a c)").bitcast(F32), YAN.tk)
            sqs[1] = _A(YAN.t[:, 2:4, :].rearrange("p a c -> p (a c)").bitcast(F32), YAN.tk)

            def load_w(dst, src, c0, ncols, dcol=0):
                for k in range(8):
                    st_ = wst[k % 2]
                    s.dma("sp" if k % 2 == 0 else "act", st_[:, 0:ncols], src[k * 128:(k + 1) * 128, c0:c0 + ncols])
                    s.cp(dst[:, k, dcol:dcol + ncols], st_[:, 0:ncols], e="dve" if k % 2 == 0 else "pool")

            load_w(woutb, wout[l], 0, 1024)
            kT = s.sb([128, 8448], BF16, name="kT")
            Vg = s.sb([128, NKT, 2, 65], BF16, name="Vg")
            qT = [s.sb([128, 2304], BF16, name="qT%d" % i) for i in range(3)]
            nqT = [s.sb([128, 2304], BF16, name="nqT%d" % i) for i in range(2)]
            nkT = [s.sb([128, 2816], BF16, name="nkT%d" % i) for i in range(2)]
            Vn = s.sb([128, NNT, 4, 65], BF16, name="Vn")
            s.memset(Vg[:, :, :, 64:65], 1.0)
            s.memset(Vn[:, :, :, 64:65], 1.0)
            hT = [s.sb([128, 8, 128], BF16, name="hT%d" % i) for i in range(2)]
            tcs = [s.sb([128, 384], name="tcos%d" % i) for i in range(2)]
            tsns = [s.sb([128, 384], name="tsin%d" % i) for i in range(2)]
            sms = [s.sb([128, 64], name="sm%d" % i) for i in range(2)]
            tc_, tsn, sm = tcs[0], tsns[0], sms[0]
            cnt = [0]

            def front(srcT, row, j, w=None):
                if w is None:
                    i = cnt[0]; cnt[0] += 1
                    w = i % 2
                    banks = (pb[0], pb[1])
                else:
                    banks = (pb[w], pb[w])
                q = "sp" if w == 0 else "act"
                x_ = xt[w]
                ht_ = hts[w]
                s.dma(q, x_[:], V(srcT.t[row:row + 128, :], srcT.tk))
                hb_ = hbs[w]
                s.tt(ht_[:], x_[:], mod[j][:, 1024:2048], ALU.mult, e="pool")
                s.tt(hb_[:], ht_[:], mod[j][:, 0:1024], ALU.add, e="dve")
                h_ = hT[w]
                for half in range(2):
                    p = banks[half]
                    with s.atomic():
                        pbf = p.t[:, 0:256].bitcast(BF16)
                        for k in range(4):
                            kk_ = half * 4 + k
                            s.tr(V(pbf[:, k * 128:(k + 1) * 128], p.tk), hb_[:, kk_ * 128:(kk_ + 1) * 128], identb[:])
                        s.cp(h_[:, half * 4:half * 4 + 4, :], V(pbf.rearrange("p (k t) -> p k t", k=4), p.tk),
                             e="act" if half == 0 else "dve")
                return x_, h_

            def proj(h_, c0, ncols, dst, wsrc=None, w=None):
                wsrc = wsrc or wbuf
                o = 0
                bi = 2
                while o < ncols:
                    n = min(512, ncols - o)
                    p = pb[bi] if w is None else pb[2 + w]
                    with s.atomic():
                        for k in range(8):
                            s.mm(p[:, 0:n], h_[:, k, :], wsrc[:, k, c0 + o:c0 + o + n], start=(k == 0), stop=(k == 7))
                        s.cp(dst[:, o:o + n], p[:, 0:n], e="act" if bi == 2 else "dve")
                    o += n
                    bi = 5 - bi

            def rms_rope(src, H, gtab, scale_mode, rope, dst, w=0):
                sq = sqs[w]; sm = sms[w]; tc_ = tcs[w]; tsn = tsns[w]
                s.act(sq[:, 0:H * 64], src, AF.Square)
                s.red(sm[:, 0:H], V(sq.t[:, 0:H * 64].rearrange("p (h d) -> p h d", d=64), sq.tk), ALU.add)
                if scale_mode == "k":
                    s.ts(sm[:, 0:H], sm[:, 0:H], 1.0 / 64, ALU.mult, 1e-6, ALU.add)
                else:
                    s.ts(sm[:, 0:H], sm[:, 0:H], 64e-6, ALU.add)
                s.act(sm[:, 0:H], sm[:, 0:H], AF.Sqrt)
                s.recip(sm[:, 0:H], sm[:, 0:H])
                for h in range(H):
                    s.stt(V(dst.ap[:, h * 64:(h + 1) * 64], dst.tk), V(src.ap[:, h * 64:(h + 1) * 64], src.tk), sm[:, h:h + 1],
                          gtab[:, h * 64:(h + 1) * 64], ALU.mult, ALU.mult)
                if rope is not None:
                    cos_d, sin_d, rowfn = rope
                    s.dma("sp", tc_[:, 0:H * 64], cos_d[rowfn:rowfn + 128, :])
                    s.dma("act", tsn[:, 0:H * 64], sin_d[rowfn:rowfn + 128, :])
                    t1 = sq
                    v4 = lambda ap: ap.rearrange("p (g a d) -> p g a d", a=2, d=16)
                    d4 = v4(dst.ap); s4 = v4(tsn.t[:, 0:H * 64]); t4 = v4(t1.t[:, 0:H * 64])
                    s.tt(V(t4[:, :, 0, :], t1.tk), V(d4[:, :, 1, :], dst.tk), V(s4[:, :, 0, :], tsn.tk), ALU.mult, e="pool")
                    s.tt(V(t4[:, :, 1, :], t1.tk), V(d4[:, :, 0, :], dst.tk), V(s4[:, :, 1, :], tsn.tk), ALU.mult, e="pool")
                    s.tt(dst, dst, tc_[:, 0:H * 64], ALU.mult)
                    s.tt(dst, dst, t1[:, 0:H * 64], ALU.add)

            pt = [s.sb([128, 512], BF16, name="pt%d" % i) for i in range(3)]
            rcp = s.sb([128, 8], name="rcp")
            bias_t = [s.sb([128, 768], name="bias%d" % i) for i in range(2)]
            hbs.extend([_A(bias_t[i].t[:, 0:512].bitcast(BF16), bias_t[i].tk) for i in range(2)])
            ptc = [0]

            load_w(wbuf, wb[l], 768, 256)
            kst = [_A(YAN.t[:, 4 + i, 0:128], YAN.tk) for i in range(2)]
            vst = [_A(YAN.t[:, 6 + i, 0:130].rearrange("p (g d) -> p g d", d=65), YAN.tk) for i in range(2)]
            for v_ in vst:
                s.memset(v_[:, :, 64:65], 1.0)

            def k_body(t):
                w = t % 2
                ctx_t = t < 2
                j = 1 if ctx_t else 0
                if ctx_t:
                    x_, h_ = front(Xc, 128 * t, j, w)
                else:
                    x_, h_ = front(XSc, 256 + 128 * (t - 2), j, w)
                pj = pe[0] if w == 0 else pjB
                proj(h_, 0, 256, pj, w=w)
                rope = None if ctx_t else (cosQ[:, 0:128], sinQ[:, 0:128], (t - 2) * 128)
                kr = hts[w]
                rms_rope(pj[:, 0:128], 2, GK, "k", rope, kr[:, 0:128], w)
                p = pb[6 + w]
                vsrc = V(pj.t[:, 128:256].rearrange("p (g d) -> p g d", d=64), pj.tk)
                if ctx_t:
                    with s.atomic():
                        s.tr(p[:, 0:128], kr[:, 0:128], ident[:])
                        s.cp(kT[:, t * 128:(t + 1) * 128], p[:, 0:128], e="act")
                    s.cp(Vg[:, t, :, 0:64], vsrc, e="pool")
                else:
                    i_ = t - 2
                    with s.atomic():
                        s.tr(p[:, 0:128], kr[:, 0:128], ident[:])
                        s.cp(kst[w][:], p[:, 0:128], e="act")
                    s.dma("sp" if w == 0 else "act", KTO[:, 128 * i_:128 * (i_ + 1)], kst[w][:])
                    s.cp(vst[w][:, :, 0:64], vsrc, e="pool")
                    s.dma("sp" if w == 0 else "act", VOd[128 * i_:128 * (i_ + 1), :], vst[w][:, :, :])

            ilB.run([(lambda t=t: k_body(t)) for t in range(18)], 2)
            s.allgather(KTO[:, :], KTG[:, :], GROUPS)
            s.allgather(VOd[:, :], VGd[:, :], GROUPS)
            for rho in range(4):
                s.dma("sp" if rho % 2 == 0 else "act", kT[:, 256 + 2048 * rho:256 + 2048 * (rho + 1)], KTG[128 * rho:128 * (rho + 1), :])
                s.dma("act" if rho % 2 == 0 else "sp",
                      V(Vg.t[:, 2 + 16 * rho:2 + 16 * (rho + 1), :, :].rearrange("p i g d -> p i (g d)"), Vg.tk),
                      V(VGd.t[2048 * rho:2048 * (rho + 1), :].rearrange("(i p) c -> p i c", p=128), VGd.tk))
            load_w(wbuf, wb[l], 1664, 512)

            def n_body(t):
                w = t % 2
                j = 1 if t >= 20 else 0
                if t >= 20:
                    x_, h_ = front(Xc, 128 * (t - 20), j, w)
                else:
                    x_, h_ = front(XSc, 128 * t, j, w)
                pj = pe[0] if w == 0 else pjB
                proj(h_, 0, 512, pj, w=w)
                for pr_ in range(2):
                    p = pb[6 + w]
                    with s.atomic():
                        s.tr(p[:, 0:128], pj[:, pr_ * 128:(pr_ + 1) * 128], ident[:])
                        s.cp(nkT[pr_][:, t * 128:(t + 1) * 128], p[:, 0:128], e="act" if pr_ == 0 else "dve")
                s.cp(Vn[:, t, :, 0:64], V(pj.t[:, 256:512].rearrange("p (g d) -> p g d", d=64), pj.tk), e="pool")

            ilB.run([(lambda t=t: n_body(t)) for t in range(NNT)], 2)
            for sl in range(6):
                hh_ = (sl // 2) + 3 * (sl % 2)
                load_w(wbuf, wb[l], 384 + hh_ * 64, 64, dcol=sl * 64)
            load_w(wbuf, wb[l], 1408, 256, dcol=384)
            own_src = lambda t: ((Xc, 128 * (t - 16)) if t >= 16 else (XSc, 256 + 128 * t))

            def q_body(t):
                w = t % 2
                j = 1 if t >= 16 else 0
                x_, h_ = front(*own_src(t), j, w)
                pj = pe[0] if w == 0 else pjB
                proj(h_, 0, 640, pj, w=w)
                rope = None if t >= 16 else (cosQ, sinQ, 128 * t)
                qr = hts[w]
                rms_rope(pj[:, 0:384], 6, GQ, "q", rope, qr[:, 0:384], w)
                for pr_ in range(3):
                    p = pb[6 + w]
                    with s.atomic():
                        s.tr(p[:, 0:128], qr[:, pr_ * 128:(pr_ + 1) * 128], ident[:])
                        s.cp(qT[pr_][:, t * 128:(t + 1) * 128], p[:, 0:128], e="act" if pr_ % 2 == 0 else "dve")
                s.ts(qr[:, 384:640], pj[:, 384:640], 0.125, ALU.mult, e="pool")
                for pr_ in range(2):
                    p = pb[6 + w]
                    with s.atomic():
                        s.tr(p[:, 0:128], qr[:, 384 + pr_ * 128:384 + (pr_ + 1) * 128], ident[:])
                        s.cp(nqT[pr_][:, t * 128:(t + 1) * 128], p[:, 0:128], e="act" if pr_ == 0 else "dve")

            ilB.run([(lambda t=t: q_body(t)) for t in range(NOWN)], 2)

            def attend(qsrc, g, qc0, nq, chunks, ksrc, vsrc_fn, bias_fn, dst_fn):
                nqs = nq // 128
                lo, hi = 64 * g, 64 * g + 64
                n_ = len(chunks)
                for ci in range(n_ + 1):
                    if ci < n_:
                        kc0, nk, cid = chunks[ci]
                        ps = pb[ci % 3]
                        bsrc = bias_fn(cid, nk) if bias_fn else None
                        s.mm(ps[0:nk, 0:nq], ksrc[lo:hi, kc0:kc0 + nk], qsrc[lo:hi, qc0:qc0 + nq], start=True, stop=(bsrc is None))
                        if bsrc is not None:
                            s.mm(ps[0:nk, 0:nq], bsrc, ident[:, 0:nq], start=False, stop=True)
                    if ci >= 1:
                        kc0p, nkp, cidp = chunks[ci - 1]
                        p_ = pt[ptc[0] % 3]; ptc[0] += 1
                        s.act(p_[0:nkp, 0:nq], pb[(ci - 1) % 3][0:nkp, 0:nq], AF.Exp)
                        for qs in range(nqs):
                            s.mm(pb[4 + qs][:, 0:65], p_[0:nkp, qs * 128:(qs + 1) * 128], vsrc_fn(cidp, nkp),
                                 start=(ci == 1), stop=(ci == n_))
                for qs in range(nqs):
                    s.recip(rcp[:, qs:qs + 1], pb[4 + qs][:, 64:65])
                    s.ts(dst_fn(qs), pb[4 + qs][:, 0:64], rcp[:, qs:qs + 1], ALU.mult)

            oT = _A(pe[0].t[0:65, 0:512], pe[0].tk)
            fin = [0]

            def attend_T(qsrc, g, qc0, nq, chunks, ksrc, vsrc_fn, dst_fn):
                nqs = nq // 128
                lo, hi = 64 * g, 64 * g + 64
                po = pb[4]
                n_ = len(chunks)
                for ci in range(n_ + 1):
                    if ci < n_:
                        kc0, nk, cid = chunks[ci]
                        s.mm(pb[ci % 3][0:nk, 0:nq], ksrc[lo:hi, kc0:kc0 + nk], qsrc[lo:hi, qc0:qc0 + nq])
                    if ci >= 1:
                        kc0p, nkp, cidp = chunks[ci - 1]
                        p_ = pt[ptc[0] % 3]; ptc[0] += 1
                        s.act(p_[0:nkp, 0:nq], pb[(ci - 1) % 3][0:nkp, 0:nq], AF.Exp)
                        s.mm(po[0:65, 0:nq], vsrc_fn(cidp, nkp), p_[0:nkp, 0:nq], start=(ci == 1), stop=(ci == n_))
                s.cp(oT[:, 0:nq], po[0:65, 0:nq], e="dve")
                for qs in range(nqs):
                    pf = pb[5 + fin[0] % 3]; fin[0] += 1
                    s.tr(pf[:, 0:65], oT[:, qs * 128:(qs + 1) * 128], ident[0:65, 0:65])
                    s.recip(rcp[:, qs:qs + 1], pf[:, 64:65])
                    s.ts(dst_fn(qs), pf[:, 0:64], rcp[:, qs:qs + 1], ALU.mult)

            def attend_T2(qsrc, qc0, nq, chunks, ksrc, vsrc_fn, dst_fn):
                nqs = nq // 128
                n_ = len(chunks)
                po = [pb[4], pb[5]]
                for ci in range(n_ + 1):
                    if ci < n_:
                        kc0, nk, cid = chunks[ci]
                        for g in range(2):
                            lo, hi = 64 * g, 64 * g + 64
                            s.mm(pb[(2 * ci + g) % 4][0:nk, 0:nq], ksrc[lo:hi, kc0:kc0 + nk], qsrc[lo:hi, qc0:qc0 + nq])
                    if ci >= 1:
                        kc0p, nkp, cidp = chunks[ci - 1]
                        for g in range(2):
                            p_ = pt[ptc[0] % 3]; ptc[0] += 1
                            s.act(p_[0:nkp, 0:nq], pb[(2 * (ci - 1) + g) % 4][0:nkp, 0:nq], AF.Exp)
                            s.mm(po[g][0:65, 0:nq], vsrc_fn(cidp, nkp, g), p_[0:nkp, 0:nq], start=(ci == 1), stop=(ci == n_))
                for g in range(2):
                    s.cp(oT[:, 0:nq], po[g][0:65, 0:nq], e="dve")
                    for qs in range(nqs):
                        pf = pb[6 + fin[0] % 2]; fin[0] += 1
                        s.tr(pf[:, 0:65], oT[:, qs * 128:(qs + 1) * 128], ident[0:65, 0:65])
                        s.recip(rcp[:, qs:qs + 1], pf[:, 64:65])
                        s.ts(dst_fn(qs, g), pf[:, 0:64], rcp[:, qs:qs + 1], ALU.mult)

            for qg in range(4):
                for pr_ in range(3):
                    attend_T2(qT[pr_], qg * 512, 512, [(c * 128, 128, c) for c in range(NKT)], kT,
                              lambda cid, nk, g: Vg[0:nk, cid, g, :],
                              lambda qs, g, qg=qg, pr_=pr_: YAN[:, qg * 4 + qs, (pr_ + 3 * g) * 64:(pr_ + 3 * g + 1) * 64])
            for h in range(6):
                pr_, g = h % 3, h // 3
                attend_T(qT[pr_], g, 2048, 256, [(c * 128, 128, c) for c in range(2)], kT,
                         lambda cid, nk, g=g: Vg[0:nk, cid, g, :],
                         lambda qs, h=h: YAN[:, 16 + qs, h * 64:(h + 1) * 64])
            def attend_na2(pr_, i, chunks, bts, base):
                qsrc, ksrc = nqT[pr_], nkT[pr_]
                qc0 = i * 128
                n_ = len(chunks)
                po = [pb[4], pb[5]]
                for ci in range(n_ + 1):
                    if ci < n_:
                        kc0, nk, cid = chunks[ci]
                        for g in range(2):
                            lo, hi = 64 * g, 64 * g + 64
                            ps = pb[(2 * ci + g) % 4]
                            has_b = cid < 20
                            s.mm(ps[0:nk, 0:128], ksrc[lo:hi, kc0:kc0 + nk], qsrc[lo:hi, qc0:qc0 + 128], start=True, stop=not has_b)
                            if has_b:
                                s.mm(ps[0:nk, 0:128], bts[g][:, cid * 128 - base:cid * 128 - base + nk], ident[:, 0:128],
                                     start=False, stop=True)
                    if ci >= 1:
                        kc0p, nkp, cidp = chunks[ci - 1]
                        for g in range(2):
                            hn = 2 * pr_ + g
                            p_ = pt[ptc[0] % 3]; ptc[0] += 1
                            s.act(p_[0:nkp, 0:128], pb[(2 * (ci - 1) + g) % 4][0:nkp, 0:128], AF.Exp)
                            s.mm(po[g][:, 0:65], p_[0:nkp, 0:128], Vn[0:nkp, cidp, hn, :], start=(ci == 1), stop=(ci == n_))
                for g in range(2):
                    hn = 2 * pr_ + g
                    s.recip(rcp[:, g:g + 1], po[g][:, 64:65])
                    s.ts(YAN[:, i, 384 + hn * 64:384 + (hn + 1) * 64], po[g][:, 0:64], rcp[:, g:g + 1], ALU.mult)

            nab_t = [s.sb([128, 768], name="nabx%d" % i) for i in range(2)] if False else None
            for i in range(16):
                chs = na_chunks(i)
                base = chs[0][0] * 128
                cls = na_class(i)
                nkeys = sum(nk for _, nk in chs)
                chunks = [(c * 128, nk, c) for c, nk in chs] + [(20 * 128, 128, 20), (21 * 128, 128, 21)]
                for pr_ in range(2):
                    bts = []
                    for g in range(2):
                        hn = 2 * pr_ + g
                        bt_ = bias_t[g]
                        s.dma("sp" if g == 0 else "act", bt_[:, 0:nkeys], nab[l, cls, hn, :, 0:nkeys])
                        bts.append(bt_)
                    attend_na2(pr_, i, chunks, bts, base)
            for hn in range(4):
                pr_, g = hn // 2, hn % 2
                attend(nqT[pr_], g, 2048, 256, [(20 * 128, 128, 20), (21 * 128, 128, 21)], nkT[pr_],
                       lambda cid, nk, hn=hn: Vn[0:nk, cid, hn, :], None,
                       lambda qs, hn=hn: YAN[:, 16 + qs, 384 + hn * 64:384 + (hn + 1) * 64])
            load_w(wbuf, wb[l], 0, 384)
            load_w(wbuf, wb[l], 1024, 384, dcol=384)
            load_w(wbuf, wb[l], 2176, 256, dcol=768)
            LNG = _A(kT.t[:, 0:2048].bitcast(F32), kT.tk); s.dma("sp", LNG[:], lng[l])
            LNB = _A(kT.t[:, 2048:4096].bitcast(F32), kT.tk); s.dma("act", LNB[:], lnb[l])
            Ff = _A(kT.t[:, 4096:5632].bitcast(F32).rearrange("p (s c) -> p s c", s=2), kT.tk)
            Bk = _A(kT.t[:, 5632:7168].bitcast(F32), kT.tk)
            cen = _A(kT.t[:, 7168:7936].bitcast(F32), kT.tk)
            ysb = _A(qT[1].t[:, 0:768].bitcast(F32), qT[1].tk)
            bsb = _A(qT[1].t[:, 768:1536].bitcast(F32), qT[1].tk)
            sqb = _A(qT[2].t[:, 0:768].bitcast(F32), qT[2].tk)
            YgT = _A(qT[0].t[:, 0:1024].rearrange("p (k t) -> p k t", k=8), qT[0].tk)
            for t in range(NOWN):
                j = 1 if t >= 16 else 0
                x_, h_ = front(*own_src(t), j)
                G = pe[0]
                proj(h_, 0, 1024, G)
                s.act(G[:], G[:], AF.Silu)
                for sr in range(2):
                    q = "sp" if sr == 0 else "act"
                    if t >= 16:
                        s.dma(q, Ff[:, sr, :], YGc[sr * 256 + 128 * (t - 16):sr * 256 + 128 * (t - 16) + 128, :])
                        rb = (2 + sr) * 256 + 128 - 128 * (t - 16)
                        s.dma(q, Bk[:, sr * 384:(sr + 1) * 384], YGc[rb:rb + 128, :])
                    else:
                        s.dma(q, Ff[:, sr, :], YF[sr][128 * t:128 * t + 128, :])
                        s.dma(q, Bk[:, sr * 384:(sr + 1) * 384], YBk[sr][1920 - 128 * t:1920 - 128 * t + 128, :])
                for sr in range(2):
                    s.mm(pb[6 + sr][:, 0:384], jmat[:], Bk[:, sr * 384:(sr + 1) * 384])
                v2 = lambda A_: V(A_.t[:, 0:384].rearrange("p (s c) -> p s c", s=2), A_.tk)
                for sr in range(2):
                    s.tt(ysb[:, sr * 192:(sr + 1) * 192], Ff[:, sr, 0:192], pb[6 + sr][:, 0:192], ALU.add)
                    s.tt(bsb[:, sr * 192:(sr + 1) * 192], Ff[:, sr, 192:384], pb[6 + sr][:, 192:384], ALU.add)
                v3 = lambda A_: V(A_.t[:, 0:384].rearrange("p (h d) -> p h d", d=64), A_.tk)
                s.red(sm[:, 0:6], v3(ysb), ALU.add)
                s.ts(sm[:, 0:6], sm[:, 0:6], 1.0 / 64, ALU.mult)
                s.tt(v3(cen), v3(ysb), V(sm.t[:, 0:6].unsqueeze(2).to_broadcast([128, 6, 64]), sm.tk), ALU.subtract)
                s.tt(sqb[:], cen[:], cen[:], ALU.mult, e="pool")
                s.red(sm[:, 8:14], v3(sqb), ALU.add)
                s.ts(sm[:, 8:14], sm[:, 8:14], 1.0 / 64, ALU.mult, 64e-5, ALU.add)
                s.act(sm[:, 8:14], sm[:, 8:14], AF.Sqrt)
                s.recip(sm[:, 8:14], sm[:, 8:14])
                s.tt(v3(cen), v3(cen), V(sm.t[:, 8:14].unsqueeze(2).to_broadcast([128, 6, 64]), sm.tk), ALU.mult)
                s.tt(cen[:], cen[:], GNG[:], ALU.mult)
                s.tt(cen[:], cen[:], GNB[:], ALU.add)
                s.tt(cen[:], cen[:], bsb[:], ALU.add)
                Yg = pe[1]
                s.tt(Yg[:, 0:384], cen[:], G[:, 0:384], ALU.mult)
                s.tt(Yg[:, 384:1024], YAN[:, t, :], G[:, 384:1024], ALU.mult)
                for half in range(2):
                    p = pb[half]
                    for k in range(4):
                        kk_ = half * 4 + k
                        s.tr(p[:, k * 128:(k + 1) * 128], Yg[:, kk_ * 128:(kk_ + 1) * 128], ident[:])
                    s.cp(YgT[:, half * 4:half * 4 + 4, :], V(p.t[:, :].rearrange("p (k t) -> p k t", k=4), p.tk),
                         e="act" if half == 0 else "dve")
                yo = pe[0]
                proj(YgT, 0, 1024, yo, wsrc=woutb)
                s.tt(yo[:], yo[:], mod[j][:, 2048:3072], ALU.mult)
                z = pe[1]
                s.stt(z[:], x_[:], ALPHA, yo[:], ALU.mult, ALU.add)
                s.red(sm[:, 16:17], z[:], ALU.add)
                s.ts(sm[:, 16:17], sm[:, 16:17], 1.0 / 1024, ALU.mult)
                s.ts(z[:], z[:], sm[:, 16:17], ALU.subtract)
                s.tt(yo[:], z[:], z[:], ALU.mult, e="pool")
                s.red(sm[:, 17:18], yo[:], ALU.add)
                s.ts(sm[:, 17:18], sm[:, 17:18], 1.0 / 1024, ALU.mult, 1e-5, ALU.add)
                s.act(sm[:, 17:18], sm[:, 17:18], AF.Sqrt)
                s.recip(sm[:, 17:18], sm[:, 17:18])
                s.stt(z[:], z[:], sm[:, 17:18], LNG[:], ALU.mult, ALU.mult)
                s.tt(z[:], z[:], LNB[:], ALU.add)
                q = "sp" if t % 2 == 0 else "act"
                if t < 16:
                    if last:
                        tickets.append(s.dma(q, out[128 * t:128 * t + 128, :], z[:]))
                    else:
                        s.dma(q, XN[128 * t:128 * t + 128, :], z[:])
                        if t % 2 == 1:
                            k = t // 2
                            s.allgather(XN[256 * k:256 * (k + 1), :], Xn[512 + 1024 * k:512 + 1024 * (k + 1), :], GROUPS)
                elif not last:
                    s.dma(q, Xn[128 * (t - 16):128 * (t - 16) + 128, :], z[:])
    s.finish(tickets)
    s.close()
    return nc


def rope_tables():
    t = np.arange(8192)
    row = (t // 64).astype(np.float32); col = (t % 64).astype(np.float32)
    inv = (10000.0 ** (-np.arange(16, dtype=np.float32) / 16)).astype(np.float32)
    ar = row[:, None] * inv; ac = col[:, None] * inv
    ang = np.concatenate([ar, ar, ac, ac], axis=-1).astype(np.float32)
    cos = np.cos(ang).astype(np.float32); sin = np.sin(ang).astype(np.float32)
    sgn = np.concatenate([-np.ones(16), np.ones(16), -np.ones(16), np.ones(16)]).astype(np.float32)
    return cos, sin * sgn


def na_bias(rpb, j):
    NEG = -30000.0
    out = np.full((5, 4, 128, 768), NEG, np.float32)
    tiles = {0: 0, 1: 1, 2: 7, 3: 14, 4: 15}
    for cls, i in tiles.items():
        chs = na_chunks(i)
        srow0 = chs[0][0] * 2
        nkeys = sum(nk for _, nk in chs)
        qrow_l = np.repeat(np.array([2 * i, 2 * i + 1]), 64)
        qcol = np.tile(np.arange(64), 2)
        r = 32 * j + qrow_l
        r_start = np.clip(r - 4, 0, 120)
        c_start = np.clip(qcol - 8, 0, 48)
        key = np.arange(nkeys)
        krow = (32 * j - 4) + srow0 + key // 64
        kcol = key % 64
        dr = krow[None, :] - r[:, None] + 7
        dc = kcol[None, :] - qcol[:, None] + 15
        inwin = ((krow[None, :] >= r_start[:, None]) & (krow[None, :] < r_start[:, None] + 8) &
                 (kcol[None, :] >= c_start[:, None]) & (kcol[None, :] < c_start[:, None] + 16))
        drc = np.clip(dr, 0, 14); dcc = np.clip(dc, 0, 30)
        for h in range(4):
            vals = rpb[h][drc, dcc]
            out[cls, h, :, 0:nkeys] = np.where(inwin, vals, NEG)
    return out


def consts_A():
    idx = np.arange(64)
    inclT = (idx[:, None] <= idx[None, :]).astype(np.float32)
    strictT = (idx[:, None] < idx[None, :]).astype(np.float32)
    strict = strictT.T.copy()
    mask = np.concatenate([inclT, strictT, inclT, -strictT, -strict], axis=1)
    rmask = np.ones((64, 256), np.float32); rmask[:, 0::64] = 0.0
    ident = np.eye(64, dtype=np.float32)
    return np.concatenate([ident, mask, mask, mask, rmask] + [ident] * 6, axis=1).astype(np.float32)


def host_F(inp, depth=4):
    cos, sinS = rope_tables()
    bc = lambda v: np.ascontiguousarray(np.broadcast_to(v[None, :], (128, v.shape[0]))).astype(np.float32)
    L = range(depth)
    shared = dict(
        wmod=np.ascontiguousarray(inp['w_mod'][:depth]),
        bmod=np.stack([bc(inp['b_mod'][l]) for l in L]),
        wb=np.ascontiguousarray(inp['w_in'][:depth, :, 1280:]),
        wout=np.ascontiguousarray(inp['w_out'][:depth]),
        cstA=consts_A(),
        cosK=np.tile(cos, (1, 2)), sinK=np.tile(sinS, (1, 2)),
        gk=np.stack([bc(np.tile(inp['gqa_k_norm'][l], 2)) for l in L]),
        gq=np.stack([bc(np.tile(inp['gqa_q_norm'][l], 6)) for l in L]),
        gng=np.stack([bc(inp['rwkv_gn_g'][l]) for l in L]), gnb=np.stack([bc(inp['rwkv_gn_b'][l]) for l in L]),
        lng=np.stack([bc(inp['ln_g'][l]) for l in L]), lnb=np.stack([bc(inp['ln_b'][l]) for l in L]),
        ident=np.eye(128, dtype=np.float32), jmat=np.ascontiguousarray(np.eye(128, dtype=np.float32)[::-1]),
    )
    per_batch = []
    for b in range(2):
        xa0 = np.zeros((XROWS, 1024), np.float32)
        xa0[0:256] = inp['ctx'][b]
        xa0[512:8704] = inp['x'][b].reshape(4, 8, 256, 1024).transpose(1, 0, 2, 3).reshape(8192, 1024)
        cvec = np.stack([inp['c'][b], inp['c_ctx']], axis=1)
        cv = np.ascontiguousarray(cvec.reshape(8, 128, 2).transpose(1, 0, 2).reshape(128, 16))
        per_batch.append(dict(xa0=xa0, cv=cv))
    nabs = [np.stack([na_bias(inp['na_rpb'][l], j) for l in L]) for j in range(4)]
    maps = []
    for c in range(8):
        b, r = c // 4, c % 4
        d, hh = r // 2, r % 2
        heads = [3 * hh + i for i in range(3)]
        cols = []
        for comp in (0, 384, 768):
            for h in heads:
                cols += list(range(comp + h * 64, comp + h * 64 + 64))
        cols += list(range(1152 + 32 * d, 1152 + 32 * d + 32))
        cols += list(range(1216 + 32 * d, 1216 + 32 * d + 32))
        cols = np.array(cols)
        hcols = np.concatenate([np.arange(h * 64, (h + 1) * 64) for h in heads])
        wa = np.ascontiguousarray(inp['w_in'][:depth][:, :, cols])
        cw = np.zeros((depth, 64, 33), np.float32); pv = np.zeros((depth, 64, 15), np.float32)
        for l in L:
            conv = inp['rwkv_conv'][l][:, cols]
            if d == 1:
                conv = conv[::-1]
            for ci in range(9):
                cw[l, :, ci * 3:ci * 3 + 3] = conv[:, ci * 64:(ci + 1) * 64].T
            cw[l, 0:32, 27:30] = conv[:, 576:608].T
            cw[l, 0:32, 30:33] = conv[:, 608:640].T
            for i, h in enumerate(heads):
                hs = slice(h * 64, (h + 1) * 64)
                pv[l, :, i * 5 + 0] = inp['decay_w0'][l][d, hs]
                pv[l, :, i * 5 + 1] = inp['iclr_a0'][l][d, hs]
                pv[l, :, i * 5 + 2] = inp['rwkv_k_k'][l][hs]
                pv[l, :, i * 5 + 3] = inp['rwkv_k_a'][l][hs]
                pv[l, :, i * 5 + 4] = inp['rwkv_r_k'][l][h]
        w2 = np.ascontiguousarray(inp['decay_w2'][:depth, d][:, :, hcols])
        a2 = np.ascontiguousarray(inp['iclr_a2'][:depth, d][:, :, hcols])
        own = slice(2048 * r, 2048 * r + 2048)
        jd = np.eye(128, dtype=np.float32)
        if d == 1:
            jd = np.ascontiguousarray(jd[::-1])
        dsel = np.zeros((128, 2), np.float32); dsel[:, d] = 1.0
        xs0 = np.zeros((2560, 1024), np.float32)
        lo = 2048 * r - 256
        for sr_ in range(2560):
            pass
        a0, a1 = max(lo, 0), min(lo + 2560, 8192)
        xs0[a0 - lo:a1 - lo] = inp['x'][b][a0:a1]
        m = dict(shared)
        m['xs0'] = xs0
        m.update(per_batch[b])
        m.update(wa=wa, cw=cw, pv=pv, w2=w2, a2=a2, nab=nabs[r],
                 cosQ=np.tile(cos[own], (1, 6)), sinQ=np.tile(sinS[own], (1, 6)), jd=jd, dsel=dsel)
        maps.append(m)
    return maps


from concourse.bass_utils import run_bass_kernel_spmd

_NC = {}


def kernel(**inputs):
    inp = {k: np.asarray(v, dtype=np.float32) for k, v in inputs.items()}
    if 'F' not in _NC:
        _NC['F'] = build_F(4)
    maps = host_F(inp, 4)
    res = run_bass_kernel_spmd(_NC['F'], maps, core_ids=list(range(8))).results
    x = np.stack([np.concatenate([res[b * 4 + r]["out"] for r in range(4)], axis=0) for b in range(2)])
    return np.ascontiguousarray(x.astype(np.float32))
```

```python
import contextlib
import numpy as np
import concourse.bass as bass
import concourse.mybir as mybir

F32 = mybir.dt.float32
BF16 = mybir.dt.bfloat16
AF = mybir.ActivationFunctionType
ALU = mybir.AluOpType
AX = mybir.AxisListType


class Tk:
    __slots__ = ("w", "r", "name", "excl", "acc")

    def __init__(self, name=""):
        self.w = None
        self.r = {}
        self.name = name
        self.excl = False
        self.acc = {}


class T:
    def __init__(self, S, t, name):
        self.t = t
        self.tk = Tk(name)
        self.name = name

    def __getitem__(self, idx):
        return V(self.t[idx], self.tk)


class V:
    __slots__ = ("ap", "tk")

    def __init__(self, ap, tk):
        self.ap = ap
        self.tk = tk


import threading


class _Worker(threading.Thread):
    def __init__(self, il, fn):
        super().__init__(daemon=True)
        self.il = il
        self.fn = fn
        self.go = threading.Event()
        self.done = False
        self.exc = None

    def run(self):
        self.go.wait(); self.go.clear()
        try:
            self.fn()
        except BaseException as e:
            self.exc = e
        self.done = True
        self.il.main_ev.set()

    def pause(self):
        self.il.main_ev.set()
        self.go.wait(); self.go.clear()


class Interleaver:
    def __init__(self, s):
        self.s = s
        self.main_ev = threading.Event()
        self.cur = None

    def run(self, fns, width):
        pending = list(fns)
        active = []
        self.s.yield_hook = self._hook
        try:
            while pending or active:
                while pending and len(active) < width:
                    w = _Worker(self, pending.pop(0)); w.start(); active.append(w)
                for w in list(active):
                    self.cur = w
                    self.main_ev.clear()
                    w.go.set()
                    self.main_ev.wait()
                    if w.exc is not None:
                        raise w.exc
                    if w.done:
                        active.remove(w)
        finally:
            self.s.yield_hook = None
            self.cur = None

    def _hook(self):
        w = self.cur
        if w is not None and threading.current_thread() is w and self.s.atomic_depth == 0:
            w.pause()


class S:
    ENG = ("pe", "act", "dve", "pool", "sp")
    yield_hook = None
    atomic_depth = 0

    @contextlib.contextmanager
    def atomic(self):
        self.atomic_depth += 1
        try:
            yield
        finally:
            self.atomic_depth -= 1
            if self.atomic_depth == 0 and self.yield_hook is not None:
                self.yield_hook()

    def __init__(self, nc):
        self.nc = nc
        self.es = contextlib.ExitStack()
        self.eng = {"pe": nc.tensor, "act": nc.scalar, "dve": nc.vector, "pool": nc.gpsimd, "sp": nc.sync}
        self.sem = {e: self.es.enter_context(nc.semaphore("s_" + e)) for e in self.ENG}
        self.cnt = {e: 0 for e in self.ENG}
        self.dq = {}
        for q in ("sp", "act", "pool"):
            sems = [self.es.enter_context(nc.semaphore("d_%s%d" % (q, i))) for i in range(8)]
            self.dq[q] = dict(sems=sems, cnt=[0] * 8, nxt=0)
            for i, s_ in enumerate(sems):
                self.sem[(q, i)] = s_
        self.waited = {}
        self.cur = self.es
        self.cc_keys = []
        self.n_tiles = 0
        self.n_instr = 0
        self.n_wait = 0

    def sb(self, shape, dt=F32, name=None):
        self.n_tiles += 1
        name = "%s_%d" % (name or "t", self.n_tiles)
        t = self.cur.enter_context(self.nc.sbuf_tensor("sb_" + name, list(shape), dt))
        return T(self, t, name)

    def dram(self, shape, name, dt=F32):
        self.n_tiles += 1
        t = self.nc.dram_tensor("%s_%d" % (name, self.n_tiles), list(shape), dt)
        return T(self, t.ap(), name)

    @contextlib.contextmanager
    def phase(self):
        prev = self.cur
        self.cur = contextlib.ExitStack()
        try:
            yield
        finally:
            self.barrier()
            self.cur.close()
            self.cur = prev

    def barrier(self):
        for e in self.ENG:
            for e2 in self.ENG:
                if e2 != e and self.cnt[e2] > 0:
                    self._wait(e, e2, self.cnt[e2])
            for q, d in self.dq.items():
                for i, c in enumerate(d["cnt"]):
                    if c > 0:
                        self._wait(e, (q, i), c)

    def allgather(self, src, dst, groups):
        if "cc" not in self.dq:
            sems = [self.es.enter_context(self.nc.semaphore("cc%d" % i)) for i in range(8)]
            self.dq["cc"] = dict(sems=sems, cnt=[0] * 8, nxt=0)
            for i, s_ in enumerate(sems):
                self.sem[("cc", i)] = s_
        d = self.dq["cc"]
        i = d["nxt"]; d["nxt"] = (i + 1) % 8
        key = ("cc", i)
        self._wait("pool", key, d["cnt"][i])
        self._deps("pool", [src], [dst])
        ins = self.nc.gpsimd.collective_compute("AllGather", mybir.AluOpType.bypass, replica_groups=groups,
                                                ins=[src.ap.opt()], outs=[dst.ap.opt()])
        d["cnt"][i] += 1
        ins.then_inc(self.sem[key])
        self._mark((key, d["cnt"][i]), [src], [dst])
        self.n_instr += 1
        return (key, d["cnt"][i])

    def ps(self, shape, dt=F32, name=None):
        self.n_tiles += 1
        name = name or "p%d" % self.n_tiles
        t = self.es.enter_context(self.nc.psum_tensor("ps_" + name, list(shape), dt))
        tt_ = T(self, t, name)
        tt_.tk.excl = True
        return tt_

    def close(self):
        self.es.close()

    def _wait(self, e, key, val):
        if val is None:
            return
        k = (e, key)
        if self.waited.get(k, 0) >= val:
            return
        self.waited[k] = val
        self.eng[e].wait_ge(self.sem[key], val)
        self.n_wait += 1

    def _deps(self, e, reads, writes, pe_acc=False):
        for v in list(reads) + list(writes):
            if v.tk.excl:
                for e2, n2 in v.tk.acc.items():
                    if e2 == e and e == "pe":
                        continue
                    self._wait(e, e2, n2)
        reads = [v for v in reads if not v.tk.excl]
        writes = [v for v in writes if not v.tk.excl]
        for v in reads:
            w = v.tk.w
            if w is not None:
                self._wait(e, w[0], w[1])
        for v in writes:
            tk = v.tk
            if tk.w is not None:
                if not (pe_acc and tk.w[0] == "pe" and e == "pe"):
                    self._wait(e, tk.w[0], tk.w[1])
            for re_, rn in tk.r.items():
                if re_ == e and e == "pe":
                    continue
                self._wait(e, re_, rn)

    def _mark(self, ticket, reads, writes):
        for v in list(reads) + list(writes):
            if v.tk.excl:
                v.tk.acc[ticket[0]] = ticket[1]
        reads = [v for v in reads if not v.tk.excl]
        writes = [v for v in writes if not v.tk.excl]
        for v in reads:
            v.tk.r[ticket[0]] = ticket[1]
        for v in writes:
            v.tk.w = ticket
            v.tk.r = {}

    def op(self, e, fn, reads, writes, pe_acc=False):
        reads = [v for v in reads if isinstance(v, V)]
        self._deps(e, reads, writes, pe_acc)
        ins = fn()
        self.cnt[e] += 1
        ins.then_inc(self.sem[e], 1)
        self._mark((e, self.cnt[e]), reads, writes)
        self.n_instr += 1
        if self.yield_hook is not None:
            self.yield_hook()
        return ins

    def dma(self, q, out, in_, **kw):
        d = self.dq[q]
        i = d["nxt"]
        d["nxt"] = (i + 1) % len(d["sems"])
        key = (q, i)
        self._wait(q, key, d["cnt"][i])
        reads = [in_] if isinstance(in_, V) else []
        writes = [out] if isinstance(out, V) else []
        self._deps(q, reads, writes)
        oa = out.ap if isinstance(out, V) else out
        ia = in_.ap if isinstance(in_, V) else in_
        ins = self.eng[q].dma_start(out=oa, in_=ia, **kw)
        d["cnt"][i] += 16
        ins.then_inc(self.sem[key], 16)
        self._mark((key, d["cnt"][i]), reads, writes)
        self.n_instr += 1
        if self.yield_hook is not None:
            self.yield_hook()
        return (key, d["cnt"][i])

    def wait_ticket(self, e, ticket):
        self._wait(e, ticket[0], ticket[1])

    def mm(self, out, lhsT, rhs, start=True, stop=True, **kw):
        return self.op("pe", lambda: self.nc.tensor.matmul(out.ap, lhsT.ap, rhs.ap, start=start, stop=stop, **kw),
                       [lhsT, rhs], [out], pe_acc=not start)

    def tr(self, out, in_, ident):
        return self.op("pe", lambda: self.nc.tensor.transpose(out.ap, in_.ap, ident.ap), [in_, ident], [out])

    def act(self, out, in_, func, bias=None, scale=None, accum_out=None, e="act"):
        kw = {}
        rd = [in_]
        if bias is not None:
            kw["bias"] = bias.ap if isinstance(bias, V) else bias
            rd.append(bias)
        if scale is not None:
            kw["scale"] = scale.ap if isinstance(scale, V) else scale
            rd.append(scale)
        wr = [out]
        if accum_out is not None:
            kw["accum_out"] = accum_out.ap
            wr.append(accum_out)
        return self.op("act", lambda: self.nc.scalar.activation(out.ap, in_.ap, func, **kw), rd, wr)

    def _ve(self, e):
        return {"dve": self.nc.vector, "pool": self.nc.gpsimd, "act": self.nc.scalar}[e]

    def tt(self, out, a, b, op, e="dve"):
        return self.op(e, lambda: self._ve(e).tensor_tensor(out.ap, a.ap, b.ap, op), [a, b], [out])

    def ts(self, out, a, s1, op0, s2=None, op1=None, e="dve", accum_out=None):
        rd = [a, s1, s2]
        a1 = s1.ap if isinstance(s1, V) else s1
        a2 = s2.ap if isinstance(s2, V) else s2
        kw = {}
        wr = [out]
        if op1 is not None:
            kw["op1"] = op1
        if accum_out is not None:
            kw["accum_out"] = accum_out.ap
            wr.append(accum_out)
        return self.op(e, lambda: self._ve(e).tensor_scalar(out.ap, a.ap, a1, a2, op0, **kw), rd, wr)

    def stt(self, out, a, s, b, op0, op1, e="dve"):
        sa = s.ap if isinstance(s, V) else s
        return self.op(e, lambda: self._ve(e).scalar_tensor_tensor(out.ap, a.ap, sa, b.ap, op0, op1), [a, s, b], [out])

    def cp(self, out, in_, e="dve"):
        if e == "act":
            return self.op("act", lambda: self.nc.scalar.copy(out.ap, in_.ap), [in_], [out])
        return self.op(e, lambda: self._ve(e).tensor_copy(out.ap, in_.ap), [in_], [out])

    def memset(self, out, val, e="pool"):
        return self.op(e, lambda: self._ve(e).memset(out.ap, val), [], [out])

    def red(self, out, in_, op, axis=AX.X, e="dve"):
        return self.op(e, lambda: self._ve(e).tensor_reduce(out.ap, in_.ap, axis, op), [in_], [out])

    def recip(self, out, in_):
        return self.op("dve", lambda: self.nc.vector.reciprocal(out.ap, in_.ap), [in_], [out])

    def finish(self, tickets):
        for t in tickets:
            self._wait("sp", t[0], t[1])


A_DEC = 0.6065306597126334
ALPHA = (2 * 4) ** 0.25
NOWN = 18
NKT = 66
NNT = 22
GROUPS = [[0, 1, 2, 3], [4, 5, 6, 7]]
XROWS = 8960


def na_chunks(i):
    if i == 0:
        return [(c, 128) for c in range(0, 6)]
    if i == 1:
        return [(c, 128) for c in range(1, 6)]
    if i == 15:
        return [(c, 128) for c in range(14, 19)] + [(19, 64)]
    return [(c, 128) for c in range(i, i + 4)] + [(i + 4, 64)]


def na_class(i):
    return {0: 0, 1: 1, 14: 3, 15: 4}.get(i, 2)


def lat_row(tau):
    rho, rem = divmod(tau, 2048)
    k, i = divmod(rem, 256)
    return 512 + 1024 * k + 256 * rho + i


class _A:
    def __init__(self, ap, tk):
        self.t = ap; self.tk = tk

    def __getitem__(self, idx):
        return V(self.t[idx], self.tk)


def build_F(depth=4):
    nc = bass.Bass("TRN2", target_bir_lowering=False)
    dt = nc.dram_tensor
    I = lambda n, sh: dt(n, sh, F32, kind="ExternalInput").ap()
    xa0 = I("xa0", [XROWS, 1024]); xs0 = I("xs0", [2560, 1024])
    cv = I("cv", [128, 16]); wmod = I("wmod", [depth, 1024, 3072]); bmod = I("bmod", [depth, 128, 3072])
    wb = I("wb", [depth, 1024, 2432]); wout = I("wout", [depth, 1024, 1024])
    wa = I("wa", [depth, 1024, 640]); cw = I("cw", [depth, 64, 33]); pvi = I("pv", [depth, 64, 15])
    w2 = I("w2", [depth, 32, 192]); a2 = I("a2", [depth, 32, 192])
    cstA = I("cstA", [64, 1664])
    cosK = I("cosK", [8192, 128]); sinK = I("sinK", [8192, 128])
    cosQ = I("cosQ", [2048, 384]); sinQ = I("sinQ", [2048, 384])
    dsel_in = I("dsel", [128, 2])
    gk = I("gk", [depth, 128, 128]); gq = I("gq", [depth, 128, 384])
    nab = I("nab", [depth, 5, 4, 128, 768])
    gng = I("gng", [depth, 128, 384]); gnb = I("gnb", [depth, 128, 384])
    lng = I("lng", [depth, 128, 1024]); lnb = I("lnb", [depth, 128, 1024])
    ident_in = I("ident", [128, 128]); jmat_in = I("jmat", [128, 128]); jd_in = I("jd", [128, 128])
    out = dt("out", [2048, 1024], F32, kind="ExternalOutput").ap()

    s = S(nc)
    tickets = []
    pb = [s.ps([128, 512], name="pb%d" % i) for i in range(8)]
    XA = [s.dram([XROWS, 1024], "XA%d" % i) for i in range(2)]
    YB = s.dram([8448, 384], "YB")
    YGc = s.dram([4 * 256, 384], "YGc")
    YGl = s.dram([16 * 4 * 512, 384], "YGl")
    XN = s.dram([2048, 1024], "XN")
    XS = s.dram([2560, 1024], "XS")
    KTO = s.dram([128, 2048], "KTO", BF16); KTG = s.dram([512, 2048], "KTG", BF16)
    VOd = s.dram([2048, 130], "VOd", BF16); VGd = s.dram([8192, 130], "VGd", BF16)
    YF = [s.dram([2048, 384], "YF%d" % i) for i in range(2)]
    YBk = [s.dram([2048, 384], "YBk%d" % i) for i in range(2)]
    xa0_T = _A(xa0, Tk("xa0")); xs0_T = _A(xs0, Tk("xs0"))

    _rd = {}

    def RR(q):
        if q not in _rd:
            pid = s.eng[q].partition_id()
            _rd[q] = pid % 4
        return _rd[q]

    dynq = ["sp", "act", "pool"]
    dync = [0]

    def dyndma(dst_v, src_fn):
        q = dynq[dync[0] % 3]; dync[0] += 1
        return s.dma(q, dst_v, src_fn(RR(q)))

    ident = s.sb([128, 128], name="ident"); s.dma("sp", ident[:], ident_in)
    jmat = s.sb([128, 128], name="jmat"); s.dma("act", jmat[:], jmat_in)
    jd = s.sb([128, 128], name="jd"); s.dma("sp", jd[:], jd_in)
    dsel = s.sb([128, 2], name="dsel"); s.dma("act", dsel[:], dsel_in)
    identb = s.sb([128, 128], BF16, name="identb"); s.cp(identb[:], ident[:])
    jdb = s.sb([128, 128], BF16, name="jdb"); s.cp(jdb[:], jd[:])
    ones = s.sb([128, 128], name="ones"); s.memset(ones[:], 1.0)
    cv_t = s.sb([128, 16], name="cv"); s.dma("sp", cv_t[:], cv)
    scv = s.sb([128, 16], name="scv"); s.act(scv[:], cv_t[:], AF.Silu)
    mod = [s.sb([128, 3072], name="mod%d" % j) for j in range(2)]
    for l in range(depth):
        Xc = xa0_T if l == 0 else XA[(l - 1) % 2]
        Xn = XA[l % 2]
        last = (l == depth - 1)

        if l == 0:
            XSc = xs0_T
        else:
            XSc = XS
            lat = Xc.t[512:8704, :]
            dyndma(V(XS.t[256:2304, :].rearrange("(o k i) c -> o k (i c)", o=1, k=8), XS.tk),
                   lambda r: V(lat.rearrange("(k rr i) c -> rr k (i c)", rr=4, i=256)[bass.ds(r, 1)], Xc.tk))
            units = lat.rearrange("(u i) c -> u (i c)", i=256)
            dyndma(V(XS.t[0:256, :].rearrange("(o i) c -> o (i c)", o=1), XS.tk),
                   lambda r: V(units[bass.ds(r + 27, 1), :], Xc.tk))
            dyndma(V(XS.t[2304:2560, :].rearrange("(o i) c -> o (i c)", o=1), XS.tk),
                   lambda r: V(units[bass.ds(r + 1, 1), :], Xc.tk))
        with s.phase():
            stage_ws = [s.sb([128, 8, 512], name="stage_w%d" % i) for i in range(2)]
            Rl = s.sb([128, 16, 128], name="Rl")
            bmod_ts = [s.sb([128, 512], name="bmodt%d" % i) for i in range(2)]
            for i in range(16):
                s.ts(Rl[:, i, :], ones[:], scv[:, i:i + 1], ALU.mult, e="dve" if i % 2 == 0 else "pool")
            for cb in range(6):
                stage_w = stage_ws[cb % 2]; bmod_t = bmod_ts[cb % 2]
                s.dma("sp", stage_w[:], wmod[l].rearrange("(k p) c -> p k c", p=128)[:, :, cb * 512:(cb + 1) * 512])
                s.dma("act", bmod_t[:], bmod[l][:, cb * 512:(cb + 1) * 512])
                for j in range(2):
                    ps = pb[(2 * cb + j) % 4]
                    for k in range(8):
                        s.mm(ps[:, :], Rl[:, 2 * k + j, :], stage_w[:, k, :], start=(k == 0), stop=(k == 7))
                    s.tt(mod[j][:, cb * 512:(cb + 1) * 512], ps[:, :], bmod_t[:], ALU.add, e="dve")
            for j in range(2):
                s.ts(mod[j][:, 1024:2048], mod[j][:, 1024:2048], 1.0, ALU.add, e="pool")

        with s.phase():
            cst_t = s.sb([64, 1664], name="cst"); s.dma("sp", cst_t[:], cstA)
            identA = cst_t[:, 0:64]
            mask3 = lambda h: cst_t[:, 64 + h * 320: 64 + (h + 1) * 320]
            rmask = cst_t[:, 1024:1280]
            idt3 = cst_t[:, 1280:1664]
            cw_t = s.sb([64, 33], name="cw"); s.dma("act", cw_t[:], cw[l])
            pv_t = s.sb([64, 16], name="pv"); s.dma("act", pv_t[:, 0:15], pvi[l])
            omk = s.sb([64, 3], name="omk")
            for h in range(3):
                s.ts(omk[:, h:h + 1], pv_t[:, h * 5 + 3:h * 5 + 4], -1.0, ALU.mult, 1.0, ALU.add)
            w2_t = s.sb([32, 192], name="w2"); s.dma("act", w2_t[:], w2[l])
            a2_t = s.sb([32, 192], name="a2"); s.dma("act", a2_t[:], a2[l])
            xt = [s.sb([128, 1024], name="xt%d" % i) for i in range(2)]
            ht = s.sb([128, 1024], name="ht")
            xr = s.sb([128, 1024], name="xr")
            hbA = s.sb([128, 1024], BF16, name="hbA")
            wab = s.sb([128, 8, 640], BF16, name="wab")
            for k in range(8):
                st_ = xt[k % 2]
                s.dma("sp" if k % 2 == 0 else "act", st_[:, 0:640], wa[l][k * 128:(k + 1) * 128, :])
                s.cp(wab[:, k, :], st_[:, 0:640], e="dve" if k % 2 == 0 else "pool")
            cts = [(i * 64, 64) for i in range(9)] + [(576, 32), (608, 32)]
            hg = [s.sb([128, 8, 258], BF16, name="hg%d" % i) for i in range(2)]
            for h_ in hg:
                s.memset(h_[:], 0.0)
            raw = [s.sb([64, 258], name="raw%d" % i) for i in range(2)]
            ctmp = [s.sb([64, 256], name="ctmp%d" % i) for i in range(2)]
            mk = lambda nm, shape=(64, 256), dt_=F32: [s.sb(list(shape), dt_, name="%s%d" % (nm, h)) for h in range(3)]
            uR, uK, uV = mk("uR"), mk("uK"), mk("uV", dt_=BF16)
            uD = s.sb([32, 256], name="uD"); uA = s.sb([32, 256], name="uA"); ddt = s.sb([32, 256], name="ddt")
            sg, ic, kk, tmp, kd, bd = mk("sg"), mk("ic"), mk("kk"), mk("tmp"), mk("kd"), mk("bd")
            cs, csx, csr = mk("cs"), mk("csx"), mk("csr")
            E1, E3 = mk("E1"), mk("E3")
            E4 = E1
            RH, KKH, kt, bt, kc, bc, rk = (mk("RH", dt_=BF16), mk("KKH", dt_=BF16), mk("kt", dt_=BF16), mk("bt", dt_=BF16),
                                           mk("kc", dt_=BF16), mk("bc", dt_=BF16), mk("rk", dt_=BF16))
            identAb = s.sb([64, 64], BF16, name="identAb"); s.cp(identAb[:], identA)
            onesb = s.sb([64, 1], BF16, name="onesb"); s.memset(onesb[:], 1.0)
            wc = mk("wc", (64, 4)); rn = tmp
            trT = [s.sb([64, 3, 256], BF16, name="trT%d" % i) for i in range(4)]
            scS = [s.sb([64, 3, 320], BF16, name="scS%d" % i) for i in range(4)]
            XYs = [[s.sb([64, 3, 128], BF16, name="XY%d_%d" % (c, i)) for i in range(2)] for c in range(4)]
            PQs = [[s.sb([64, 3, 128], BF16, name="PQ%d_%d" % (c, i)) for i in range(2)] for c in range(4)]
            KKpTs = [s.sb([64, 192], BF16, name="KKpT%d" % c) for c in range(4)]
            AVs = [s.sb([64, 192], BF16, name="AV%d" % c) for c in range(4)]
            Ulocs = [s.sb([64, 192], name="Uloc%d" % c) for c in range(4)]
            Us = [s.sb([64, 192], BF16, name="U%d" % c) for c in range(4)]
            STb = [s.sb([64, 192], BF16, name="STb%d" % i) for i in range(2)]
            bss = [s.sb([64, 4], name="bs%d" % c) for c in range(4)]
            il = Interleaver(s)
            ST = [s.sb([64, 192], name="ST%d" % i) for i in range(2)]
            YBuf = [s.sb([64, 4, 384], name="YBuf0")] * 2
            bs_ = s.sb([64, 4], name="bs")
            s.memset(ST[0][:], 0.0)
            s.memset(STb[0][:], 0.0)
            sti = 0
            acnt = [0]

            def frontA(g):
                hgt = hg[g % 2]
                for a in range(2):
                    u = 2 * g + a
                    i = acnt[0]; acnt[0] += 1
                    if u < 2:
                        bf_, br_ = 128 * u, 128 * (1 - u)
                        j = 1
                    else:
                        v = u - 2
                        bf_, br_ = lat_row(128 * v), lat_row(128 * (63 - v))
                        j = 0
                    x_ = xt[i % 2]
                    s.dma("sp", x_[:], V(Xc.t[bf_:bf_ + 128, :], Xc.tk))
                    s.dma("act", xr[:], V(Xc.t[br_:br_ + 128, :], Xc.tk))
                    s.act(x_[:], x_[:], AF.Identity, scale=dsel[:, 0:1])
                    s.stt(x_[:], xr[:], dsel[:, 1:2], x_[:], ALU.mult, ALU.add)
                    s.tt(ht[:], x_[:], mod[j][:, 1024:2048], ALU.mult, e="pool")
                    s.tt(hbA[:], ht[:], mod[j][:, 0:1024], ALU.add, e="dve")
                    for half in range(2):
                        p = pb[half]
                        with s.atomic():
                            pbf = p.t[:, 0:256].bitcast(BF16)
                            for k in range(4):
                                kk_ = half * 4 + k
                                s.tr(V(pbf[:, k * 128:(k + 1) * 128], p.tk), hbA[:, kk_ * 128:(kk_ + 1) * 128], jdb[:])
                            s.cp(hgt[:, half * 4:half * 4 + 4, 1 + 128 * a:1 + 128 * (a + 1)],
                                 V(pbf.rearrange("p (k t) -> p k t", k=4), p.tk), e="act" if half == 0 else "dve")

            NGRP = 33
            OPN = ('RH', 'KKH', 'kt', 'bt', 'kc', 'bc', 'rk', 'uV')
            opsets = [dict(RH=RH, KKH=KKH, kt=kt, bt=bt, kc=kc, bc=bc, rk=rk, uV=uV, wc=wc),
                      dict(RH=mk('RHb', dt_=BF16), KKH=mk('KKHb', dt_=BF16), kt=mk('ktb', dt_=BF16), bt=mk('btb', dt_=BF16),
                           kc=mk('kcb', dt_=BF16), bc=mk('bcb', dt_=BF16), rk=mk('rkb', dt_=BF16), uV=mk('uVb', dt_=BF16),
                           wc=mk('wcb', (64, 4)))]

            def pro1(g):
                O = opsets[g % 2]; uV = O['uV']
                first = g in (0, 1)
                lastg = g in (0, NGRP - 1)
                hgt = hg[g % 2]
                if g + 1 < NGRP:
                    frontA(g + 1)
                    hn_ = hg[(g + 1) % 2]
                    s.cp(hgt[:, :, 257:258], hn_[:, :, 1:2], e="pool")
                    s.cp(hn_[:, :, 0:1], hgt[:, :, 256:257], e="pool")
                for ci, (c0, M) in enumerate(cts):
                    pr = pb[ci % 2]
                    rw = raw[ci % 2]
                    with s.atomic():
                        for k in range(8):
                            s.mm(pr[0:M, 0:258], wab[:, k, c0:c0 + M], hgt[:, k, :], start=(k == 0), stop=(k == 7))
                        s.cp(rw[0:M, :], pr[0:M, 0:258], e="act" if ci % 2 == 0 else "dve")
                    if first:
                        s.memset(rw[0:M, 0:1], 0.0, e="pool")
                    if lastg:
                        s.memset(rw[0:M, 257:258], 0.0, e="pool")
                    dst = (uR, uK, uV)[ci // 3][ci % 3] if ci < 9 else (uD, uA)[ci - 9]
                    tm = ctmp[ci % 2]
                    s.act(tm[0:M, :], rw[0:M, 0:256], AF.Identity, scale=cw_t[0:M, ci * 3:ci * 3 + 1])
                    s.stt(tm[0:M, :], rw[0:M, 1:257], cw_t[0:M, ci * 3 + 1:ci * 3 + 2], tm[0:M, :], ALU.mult, ALU.add)
                    s.stt(dst[0:M, :], rw[0:M, 2:258], cw_t[0:M, ci * 3 + 2:ci * 3 + 3], tm[0:M, :], ALU.mult, ALU.add)
                s.act(ddt[:], uD[:], AF.Tanh)

            def prep_head(g, h):
                O = opsets[g % 2]
                RH, KKH, kt, bt, kc, bc, rk, wc = O['RH'], O['KKH'], O['kt'], O['bt'], O['kc'], O['bc'], O['rk'], O['wc']
                P = lambda i: pv_t[:, h * 5 + i:h * 5 + i + 1]
                pz = pb[2 + h]
                with s.atomic():
                    s.mm(pz[0:64, 0:256], w2_t[:, h * 64:(h + 1) * 64], ddt[:])
                    s.act(sg[h][:], pz[0:64, 0:256], AF.Sigmoid, bias=P(0))
                with s.atomic():
                    s.mm(pz[0:64, 256:512], a2_t[:, h * 64:(h + 1) * 64], uA[:])
                    s.act(ic[h][:], pz[0:64, 256:512], AF.Sigmoid, bias=P(1))
                s.act(kk[h][:], uK[h][:], AF.Identity, scale=P(2))
                s.act(tmp[h][:], kk[h][:], AF.Square)
                pss = pb[2 + h]
                with s.atomic():
                    s.mm(pss[0:64, 0:256], ones[0:64, 0:64], tmp[h][:])
                    s.ts(rn[h][:], pss[0:64, 0:256], 1e-12, ALU.max)
                s.act(rn[h][:], rn[h][:], AF.Sqrt)
                s.recip(rn[h][:], rn[h][:])
                s.tt(kk[h][:], kk[h][:], rn[h][:], ALU.mult)
                s.act(tmp[h][:], ic[h][:], AF.Identity, scale=P(3), bias=omk[:, h:h + 1])
                s.tt(kd[h][:], uK[h][:], tmp[h][:], ALU.mult, e="dve")
                s.tt(bd[h][:], kk[h][:], ic[h][:], ALU.mult, e="pool")
                s.op("dve", lambda h=h: nc.vector.tensor_tensor_scan(cs[h][:].ap, rmask.ap, sg[h][:].ap, 0.0, ALU.mult, ALU.add),
                     [rmask, sg[h][:]], [cs[h][:]])
                s.tt(csx[h][:], cs[h][:], sg[h][:], ALU.subtract)
                for c in range(4):
                    s.ts(csr[h][:, c * 64:(c + 1) * 64], cs[h][:, c * 64:(c + 1) * 64],
                         cs[h][:, c * 64 + 63:c * 64 + 64], ALU.subtract)
                s.act(wc[h][:], cs[h][:, 63::64], AF.Exp, scale=-A_DEC)
                s.act(E1[h][:], cs[h][:], AF.Exp, scale=-A_DEC)
                s.tt(RH[h][:], uR[h][:], E1[h][:], ALU.mult, e="dve")
                s.act(E1[h][:], csx[h][:], AF.Exp, scale=-A_DEC)
                s.tt(KKH[h][:], kk[h][:], E1[h][:], ALU.mult, e="pool")
                s.act(E3[h][:], cs[h][:], AF.Exp, scale=A_DEC)
                s.tt(kt[h][:], kd[h][:], E3[h][:], ALU.mult)
                s.tt(bt[h][:], bd[h][:], E3[h][:], ALU.mult, e="pool")
                s.act(E4[h][:], csr[h][:], AF.Exp, scale=A_DEC)
                s.tt(kc[h][:], kd[h][:], E4[h][:], ALU.mult)
                s.tt(bc[h][:], bd[h][:], E4[h][:], ALU.mult, e="pool")
                s.stt(rk[h][:], uR[h][:], P(4), kd[h][:], ALU.mult, ALU.mult)

            def chunk_body(g, c):
                n = 4 * g + c
                O = opsets[g % 2]
                RH, KKH, kt, bt, kc, bc, rk, uV = O['RH'], O['KKH'], O['kt'], O['bt'], O['kc'], O['bc'], O['rk'], O['uV']
                cc = slice(c * 64, (c + 1) * 64)
                tT = trT[c]; sS = scS[c]
                XY = XYs[c]; PQ = PQs[c]; KKpT = KKpTs[c]; AV = AVs[c]; Uloc = Ulocs[c]; U = Us[c]; bs_ = bss[c]
                p_ = c % 2
                for h in range(3):
                    ptr = pb[2 + p_]
                    with s.atomic():
                        ptrb = ptr.t[0:64, 0:128].bitcast(BF16)
                        for i, src_ in enumerate((KKH, bc, kc, uV)):
                            s.tr(V(ptrb[:, i * 64:(i + 1) * 64], ptr.tk), src_[h][:, cc], identAb[:])
                        s.cp(tT[:, h, :], V(ptrb[:, 0:256], ptr.tk), e="act")
                    psc = pb[4 + p_]
                    with s.atomic():
                        s.mm(psc[0:64, 0:64], kt[h][:, cc], RH[h][:, cc])
                        s.mm(psc[0:64, 64:128], kt[h][:, cc], KKH[h][:, cc])
                        s.mm(psc[0:64, 128:192], bt[h][:, cc], RH[h][:, cc])
                        s.mm(psc[0:64, 192:256], bt[h][:, cc], KKH[h][:, cc])
                        s.mm(psc[0:64, 256:320], KKH[h][:, cc], bt[h][:, cc])
                        s.tt(sS[:, h, :], psc[0:64, 0:320], mask3(h), ALU.mult)
                s.tt(PQ[0][:, :, :], sS[:, :, 192:320], V(idt3.ap.rearrange("p (h c) -> p h c", c=128), idt3.tk), ALU.add, e="pool")
                Xc_ = lambda lvl, h: (sS[:, h, 192:256] if lvl == 0 else XY[lvl % 2][:, h, 0:64])
                Yc_ = lambda lvl, h: (sS[:, h, 256:320] if lvl == 0 else XY[lvl % 2][:, h, 64:128])
                for lvl in range(5):
                    pn, pq = pb[4 + p_], pb[6 + p_]
                    nxt = XY[(lvl + 1) % 2]
                    with s.atomic():
                        for h in range(3):
                            s.mm(pn[0:64, h * 128:h * 128 + 64], Yc_(lvl, h), Xc_(lvl, h))
                            if lvl < 4:
                                s.mm(pn[0:64, h * 128 + 64:h * 128 + 128], Xc_(lvl, h), Yc_(lvl, h))
                        pn3 = pn.t[0:64, 0:384].rearrange("p (h c) -> p h c", c=128)
                        if lvl < 4:
                            s.cp(nxt[:, :, :], V(pn3, pn.tk), e="act")
                        else:
                            s.cp(nxt[:, :, 0:64], V(pn3[:, :, 0:64], pn.tk), e="act")
                    Pc, Pn = PQ[lvl % 2], PQ[(lvl + 1) % 2]
                    with s.atomic():
                        for h in range(3):
                            s.mm(pq[0:64, h * 128:h * 128 + 64], Pc[:, h, 64:128], nxt[:, h, 0:64])
                            if lvl < 4:
                                s.mm(pq[0:64, h * 128 + 64:h * 128 + 128], Pc[:, h, 0:64], nxt[:, h, 64:128])
                        pq3 = pq.t[0:64, 0:384].rearrange("p (h c) -> p h c", c=128)
                        if lvl < 4:
                            s.tt(Pn[:, :, :], V(pq3, pq.tk), Pc[:, :, :], ALU.add)
                        else:
                            s.tt(Pn[:, :, 0:64], V(pq3[:, :, 0:64], pq.tk), Pc[:, :, 0:64], ALU.add)
                TT = PQ[1]
                pk = pb[2 + p_]
                with s.atomic():
                    for h in range(3):
                        s.mm(pk[0:64, h * 64:(h + 1) * 64], tT[:, h, 0:64], TT[:, h, 0:64])
                        s.mm(pk[0:64, 192 + h * 64:192 + (h + 1) * 64], sS[:, h, 64:128], tT[:, h, 192:256])
                    s.cp(KKpT[:], pk[0:64, 0:192], e="act")
                    s.cp(AV[:], pk[0:64, 192:384], e="dve")
                pk3 = pb[6 + p_]
                with s.atomic():
                    for h in range(3):
                        s.mm(pk3[0:64, h * 64:(h + 1) * 64], TT[:, h, 0:64], AV[:, h * 64:(h + 1) * 64])
                    s.cp(Uloc[:], pk3[0:64, 0:192], e="act")

            def chunk_seq(g, c):
                n = 4 * g + c
                yb = YBuf[0]
                O = opsets[g % 2]
                RH, rk, wc = O['RH'], O['rk'], O['wc']
                cc = slice(c * 64, (c + 1) * 64)
                tT = trT[c]; sS = scS[c]
                KKpT = KKpTs[c]; Uloc = Ulocs[c]; U = Us[c]; bs_ = bss[c]
                Sc, Sn = ST[n % 2], ST[(n + 1) % 2]
                Scb, Snb = STb[n % 2], STb[(n + 1) % 2]
                pu = pb[0]
                with s.atomic():
                    for h in range(3):
                        s.mm(pu[0:64, h * 64:(h + 1) * 64], KKpT[:, h * 64:(h + 1) * 64], Scb[:, h * 64:(h + 1) * 64])
                    s.stt(U[:], pu[0:64, 0:192], -1.0, Uloc[:], ALU.mult, ALU.subtract)
                pS = pb[1]
                with s.atomic():
                    for h in range(3):
                        hs = slice(h * 64, (h + 1) * 64)
                        s.mm(pS[0:64, hs], tT[:, h, 128:192], tT[:, h, 192:256], start=True, stop=False)
                        s.mm(pS[0:64, hs], tT[:, h, 64:128], U[:, hs], start=False, stop=True)
                    for h in range(3):
                        hs = slice(h * 64, (h + 1) * 64)
                        s.stt(Sn[:, hs], Sc[:, hs], wc[h][:, c:c + 1], pS[0:64, hs], ALU.mult, ALU.add)
                    s.cp(Snb[:], Sn[:], e="act")
                py = pb[0]
                with s.atomic():
                    for h in range(3):
                        hs = slice(256 + h * 64, 256 + (h + 1) * 64)
                        hh = slice(h * 64, (h + 1) * 64)
                        s.mm(py[0:64, hs], RH[h][:, cc], Scb[:, hh], start=True, stop=False)
                        s.mm(py[0:64, hs], sS[:, h, 128:192], U[:, hh], start=False, stop=False)
                        s.mm(py[0:64, hs], sS[:, h, 0:64], tT[:, h, 192:256], start=False, stop=True)
                    s.cp(yb[:, c, 0:192], py[0:64, 256:448], e="act")
                pbn = pb[1]
                with s.atomic():
                    for h in range(3):
                        s.mm(pbn[0:64, 256 + h:256 + h + 1], rk[h][:, cc], onesb[:, 0:1])
                    s.cp(bs_[:, 0:3], pbn[0:64, 256:259], e="dve")
                for h in range(3):
                    s.ts(yb[:, c, 192 + h * 64:192 + (h + 1) * 64], tT[:, h, 192:256], bs_[:, h:h + 1], ALU.mult, e="pool")
            def seq_group(g):
                tbase = 256 * g
                for c in range(4):
                    chunk_seq(g, c)
                s.dma("sp" if g % 2 == 0 else "act",
                      V(YB.t[tbase:tbase + 256, :].rearrange("(c t) f -> t c f", t=64), YB.tk), YBuf[0][:, :, :])
                if g == 0:
                    s.allgather(YB[0:256, :], YGc[:, :], GROUPS)
                elif g % 2 == 0:
                    m = g // 2 - 1
                    s.allgather(YB[256 + 512 * m:256 + 512 * (m + 1), :], YGl[2048 * m:2048 * (m + 1), :], GROUPS)

            frontA(0)
            pro1(0)
            il.run([(lambda h=h: prep_head(0, h)) for h in range(3)], 3)
            for g in range(NGRP):
                wk1 = [(lambda c=c: chunk_body(g, c)) for c in range(4)]
                if g + 1 < NGRP:
                    wk1.append(lambda: pro1(g + 1))
                il.run(wk1, 5)
                wk2 = [lambda: seq_group(g)]
                if g + 1 < NGRP:
                    wk2 += [(lambda h=h: prep_head(g + 1, h)) for h in range(3)]
                il.run(wk2, 4)
        ygv = YGl.t.rearrange("(m sr i) c -> sr m (i c)", sr=4, i=512)
        for sr in range(2):
            dyndma(V(YF[sr].t.rearrange("(m i) c -> m (i c)", i=512), YF[sr].tk),
                   lambda r, sr=sr: V(ygv[sr][bass.ds(r * 4, 4), :], YGl.tk))
            dyndma(V(YBk[sr].t.rearrange("(m i) c -> m (i c)", i=512), YBk[sr].tk),
                   lambda r, sr=sr: V(ygv[2 + sr][bass.ds((3 - r) * 4, 4), :], YGl.tk))
        with s.phase():
            GK = s.sb([128, 128], name="GK"); s.dma("act", GK[:], gk[l])
            GQ = s.sb([128, 384], name="GQ"); s.dma("act", GQ[:], gq[l])
            GNG = s.sb([128, 384], name="GNG"); s.dma("act", GNG[:], gng[l])
            GNB = s.sb([128, 384], name="GNB"); s.dma("act", GNB[:], gnb[l])
            YAN = s.sb([128, 18, 640], BF16, name="YAN")
            wbuf = s.sb([128, 8, 1024], BF16, name="wbuf")
            woutb = s.sb([128, 8, 1024], BF16, name="woutb")
            xt = [s.sb([128, 1024], name="xt%d" % i) for i in range(2)]
            hts = [s.sb([128, 1024], name="ht%d" % i) for i in range(2)]
            ht = hts[0]
            pe = [s.sb([128, 1024], name="pe%d" % i) for i in range(2)]
            sqs = [pe[1], None]
            wst = xt
            hbs = []
            ilB = Interleaver(s)
            pjB = _A(YAN.t[:, 0:3, :].rearrange("p a c -> p (a c)").bitcast(F32), YAN.tk)
            sqs[1] = _A(YAN.t[:, 3:5, :].rearrange("p a c -> p (a c)").bitcast(F32), YAN.tk)

            def load_w(dst, src, c0, ncols, dcol=0):
                for k in range(8):
                    st_ = wst[k % 2]
                    s.dma("sp" if k % 2 == 0 else "act", st_[:, 0:ncols], src[k * 128:(k + 1) * 128, c0:c0 + ncols])
                    s.cp(dst[:, k, dcol:dcol + ncols], st_[:, 0:ncols], e="dve" if k % 2 == 0 else "pool")

            load_w(woutb, wout[l], 0, 1024)
            kT = s.sb([128, 8448], BF16, name="kT")
            Vg = s.sb([128, NKT, 2, 65], BF16, name="Vg")
            qT = [s.sb([128, 2304], BF16, name="qT%d" % i) for i in range(3)]
            nqT = [s.sb([128, 2304], BF16, name="nqT%d" % i) for i in range(2)]
            nkT = [s.sb([128, 2816], BF16, name="nkT%d" % i) for i in range(2)]
            Vn = s.sb([128, NNT, 4, 65], BF16, name="Vn")
            s.memset(Vg[:, :, :, 64:65], 1.0)
            s.memset(Vn[:, :, :, 64:65], 1.0)
            hT = [s.sb([128, 8, 128], BF16, name="hT%d" % i) for i in range(2)]
            tcs = [s.sb([128, 384], name="tcos%d" % i) for i in range(2)]
            tsns = [s.sb([128, 384], name="tsin%d" % i) for i in range(2)]
            sms = [s.sb([128, 64], name="sm%d" % i) for i in range(2)]
            tc_, tsn, sm = tcs[0], tsns[0], sms[0]
            cnt = [0]

            def front(srcT, row, j, w=None):
                if w is None:
                    i = cnt[0]; cnt[0] += 1
                    w = i % 2
                    banks = (pb[0], pb[1])
                else:
                    banks = (pb[w], pb[w])
                q = "sp" if w == 0 else "act"
                x_ = xt[w]
                ht_ = hts[w]
                s.dma(q, x_[:], V(srcT.t[row:row + 128, :], srcT.tk))
                hb_ = hbs[w]
                s.tt(ht_[:], x_[:], mod[j][:, 1024:2048], ALU.mult, e="pool")
                s.tt(hb_[:], ht_[:], mod[j][:, 0:1024], ALU.add, e="dve")
                h_ = hT[w]
                for half in range(2):
                    p = banks[half]
                    with s.atomic():
                        pbf = p.t[:, 0:256].bitcast(BF16)
                        for k in range(4):
                            kk_ = half * 4 + k
                            s.tr(V(pbf[:, k * 128:(k + 1) * 128], p.tk), hb_[:, kk_ * 128:(kk_ + 1) * 128], identb[:])
                        s.cp(h_[:, half * 4:half * 4 + 4, :], V(pbf.rearrange("p (k t) -> p k t", k=4), p.tk),
                             e="act" if half == 0 else "dve")
                return x_, h_

            def proj(h_, c0, ncols, dst, wsrc=None, w=None):
                wsrc = wsrc or wbuf
                o = 0
                bi = 2
                while o < ncols:
                    n = min(512, ncols - o)
                    p = pb[bi] if w is None else pb[2 + w]
                    with s.atomic():
                        for k in range(8):
                            s.mm(p[:, 0:n], h_[:, k, :], wsrc[:, k, c0 + o:c0 + o + n], start=(k == 0), stop=(k == 7))
                        s.cp(dst[:, o:o + n], p[:, 0:n], e="act" if bi == 2 else "dve")
                    o += n
                    bi = 5 - bi

            def rms_rope(src, H, gtab, scale_mode, rope, dst, w=0):
                sq = sqs[w]; sm = sms[w]; tc_ = tcs[w]; tsn = tsns[w]
                s.act(sq[:, 0:H * 64], src, AF.Square)
                s.red(sm[:, 0:H], V(sq.t[:, 0:H * 64].rearrange("p (h d) -> p h d", d=64), sq.tk), ALU.add)
                if scale_mode == "k":
                    s.ts(sm[:, 0:H], sm[:, 0:H], 1.0 / 64, ALU.mult, 1e-6, ALU.add)
                else:
                    s.ts(sm[:, 0:H], sm[:, 0:H], 64e-6, ALU.add)
                s.act(sm[:, 0:H], sm[:, 0:H], AF.Sqrt)
                s.recip(sm[:, 0:H], sm[:, 0:H])
                for h in range(H):
                    s.stt(V(dst.ap[:, h * 64:(h + 1) * 64], dst.tk), V(src.ap[:, h * 64:(h + 1) * 64], src.tk), sm[:, h:h + 1],
                          gtab[:, h * 64:(h + 1) * 64], ALU.mult, ALU.mult)
                if rope is not None:
                    cos_d, sin_d, rowfn = rope
                    s.dma("sp", tc_[:, 0:H * 64], cos_d[rowfn:rowfn + 128, :])
                    s.dma("act", tsn[:, 0:H * 64], sin_d[rowfn:rowfn + 128, :])
                    t1 = sq
                    v4 = lambda ap: ap.rearrange("p (g a d) -> p g a d", a=2, d=16)
                    d4 = v4(dst.ap); s4 = v4(tsn.t[:, 0:H * 64]); t4 = v4(t1.t[:, 0:H * 64])
                    s.tt(V(t4[:, :, 0, :], t1.tk), V(d4[:, :, 1, :], dst.tk), V(s4[:, :, 0, :], tsn.tk), ALU.mult, e="pool")
                    s.tt(V(t4[:, :, 1, :], t1.tk), V(d4[:, :, 0, :], dst.tk), V(s4[:, :, 1, :], tsn.tk), ALU.mult, e="pool")
                    s.tt(dst, dst, tc_[:, 0:H * 64], ALU.mult)
                    s.tt(dst, dst, t1[:, 0:H * 64], ALU.add)

            pt = [s.sb([128, 512], BF16, name="pt%d" % i) for i in range(3)]
            rcp = s.sb([128, 8], name="rcp")
            bias_t = [s.sb([128, 768], name="bias%d" % i) for i in range(2)]
            hbs.extend([_A(bias_t[i].t[:, 0:512].bitcast(BF16), bias_t[i].tk) for i in range(2)])
            ptc = [0]

            load_w(wbuf, wb[l], 768, 256)
            load_w(wbuf, wb[l], 1664, 512, dcol=256)
            kst = [_A(YAN.t[:, 5 + i, 0:128], YAN.tk) for i in range(2)]
            vst = [_A(YAN.t[:, 7 + i, 0:130].rearrange("p (g d) -> p g d", d=65), YAN.tk) for i in range(2)]
            for v_ in vst:
                s.memset(v_[:, :, 64:65], 1.0)

            def k_body(t):
                w = t % 2
                ctx_t = t < 2
                j = 1 if ctx_t else 0
                if ctx_t:
                    x_, h_ = front(Xc, 128 * t, j, w)
                else:
                    x_, h_ = front(XSc, 256 + 128 * (t - 2), j, w)
                pj = pe[0] if w == 0 else pjB
                proj(h_, 0, 768, pj, w=w)
                tn = 20 + t if ctx_t else t
                for pr_ in range(2):
                    p2 = pb[6 + w]
                    with s.atomic():
                        s.tr(p2[:, 0:128], pj[:, 256 + pr_ * 128:256 + (pr_ + 1) * 128], ident[:])
                        s.cp(nkT[pr_][:, tn * 128:(tn + 1) * 128], p2[:, 0:128], e="act" if pr_ == 0 else "dve")
                s.cp(Vn[:, tn, :, 0:64], V(pj.t[:, 512:768].rearrange("p (g d) -> p g d", d=64), pj.tk), e="pool")
                rope = None if ctx_t else (cosQ[:, 0:128], sinQ[:, 0:128], (t - 2) * 128)
                kr = hts[w]
                rms_rope(pj[:, 0:128], 2, GK, "k", rope, kr[:, 0:128], w)
                p = pb[6 + w]
                vsrc = V(pj.t[:, 128:256].rearrange("p (g d) -> p g d", d=64), pj.tk)
                if ctx_t:
                    with s.atomic():
                        s.tr(p[:, 0:128], kr[:, 0:128], ident[:])
                        s.cp(kT[:, t * 128:(t + 1) * 128], p[:, 0:128], e="act")
                    s.cp(Vg[:, t, :, 0:64], vsrc, e="pool")
                else:
                    i_ = t - 2
                    with s.atomic():
                        s.tr(p[:, 0:128], kr[:, 0:128], ident[:])
                        s.cp(kst[w][:], p[:, 0:128], e="act")
                    s.dma("sp" if w == 0 else "act", KTO[:, 128 * i_:128 * (i_ + 1)], kst[w][:])
                    s.cp(vst[w][:, :, 0:64], vsrc, e="pool")
                    s.dma("sp" if w == 0 else "act", VOd[128 * i_:128 * (i_ + 1), :], vst[w][:, :, :])

            ilB.run([(lambda t=t: k_body(t)) for t in range(18)], 2)
            s.allgather(KTO[:, :], KTG[:, :], GROUPS)
            s.allgather(VOd[:, :], VGd[:, :], GROUPS)
            for rho in range(4):
                s.dma("sp" if rho % 2 == 0 else "act", kT[:, 256 + 2048 * rho:256 + 2048 * (rho + 1)], KTG[128 * rho:128 * (rho + 1), :])
                s.dma("act" if rho % 2 == 0 else "sp",
                      V(Vg.t[:, 2 + 16 * rho:2 + 16 * (rho + 1), :, :].rearrange("p i g d -> p i (g d)"), Vg.tk),
                      V(VGd.t[2048 * rho:2048 * (rho + 1), :].rearrange("(i p) c -> p i c", p=128), VGd.tk))

            def n_body(t):
                w = t % 2
                j = 0
                x_, h_ = front(XSc, 128 * t, j, w)
                pj = pe[0] if w == 0 else pjB
                proj(h_, 256, 512, pj, w=w)
                for pr_ in range(2):
                    p = pb[6 + w]
                    with s.atomic():
                        s.tr(p[:, 0:128], pj[:, pr_ * 128:(pr_ + 1) * 128], ident[:])
                        s.cp(nkT[pr_][:, t * 128:(t + 1) * 128], p[:, 0:128], e="act" if pr_ == 0 else "dve")
                s.cp(Vn[:, t, :, 0:64], V(pj.t[:, 256:512].rearrange("p (g d) -> p g d", d=64), pj.tk), e="pool")

            ilB.run([(lambda t=t: n_body(t)) for t in (0, 1, 18, 19)], 2)
            for sl in range(6):
                hh_ = (sl // 2) + 3 * (sl % 2)
                load_w(wbuf, wb[l], 384 + hh_ * 64, 64, dcol=sl * 64)
            load_w(wbuf, wb[l], 1408, 256, dcol=384)
            own_src = lambda t: ((Xc, 128 * (t - 16)) if t >= 16 else (XSc, 256 + 128 * t))

            def q_body(t):
                w = t % 2
                j = 1 if t >= 16 else 0
                x_, h_ = front(*own_src(t), j, w)
                pj = pe[0] if w == 0 else pjB
                proj(h_, 0, 640, pj, w=w)
                rope = None if t >= 16 else (cosQ, sinQ, 128 * t)
                qr = hts[w]
                rms_rope(pj[:, 0:384], 6, GQ, "q", rope, qr[:, 0:384], w)
                for pr_ in range(3):
                    p = pb[6 + w]
                    with s.atomic():
                        s.tr(p[:, 0:128], qr[:, pr_ * 128:(pr_ + 1) * 128], ident[:])
                        s.cp(qT[pr_][:, t * 128:(t + 1) * 128], p[:, 0:128], e="act" if pr_ % 2 == 0 else "dve")
                s.ts(qr[:, 384:640], pj[:, 384:640], 0.125, ALU.mult, e="pool")
                for pr_ in range(2):
                    p = pb[6 + w]
                    with s.atomic():
                        s.tr(p[:, 0:128], qr[:, 384 + pr_ * 128:384 + (pr_ + 1) * 128], ident[:])
                        s.cp(nqT[pr_][:, t * 128:(t + 1) * 128], p[:, 0:128], e="act" if pr_ == 0 else "dve")

            ilB.run([(lambda t=t: q_body(t)) for t in range(NOWN)], 2)

            def attend(qsrc, g, qc0, nq, chunks, ksrc, vsrc_fn, bias_fn, dst_fn):
                nqs = nq // 128
                lo, hi = 64 * g, 64 * g + 64
                n_ = len(chunks)
                for ci in range(n_ + 1):
                    if ci < n_:
                        kc0, nk, cid = chunks[ci]
                        ps = pb[ci % 3]
                        bsrc = bias_fn(cid, nk) if bias_fn else None
                        s.mm(ps[0:nk, 0:nq], ksrc[lo:hi, kc0:kc0 + nk], qsrc[lo:hi, qc0:qc0 + nq], start=True, stop=(bsrc is None))
                        if bsrc is not None:
                            s.mm(ps[0:nk, 0:nq], bsrc, ident[:, 0:nq], start=False, stop=True)
                    if ci >= 1:
                        kc0p, nkp, cidp = chunks[ci - 1]
                        p_ = pt[ptc[0] % 3]; ptc[0] += 1
                        s.act(p_[0:nkp, 0:nq], pb[(ci - 1) % 3][0:nkp, 0:nq], AF.Exp)
                        for qs in range(nqs):
                            s.mm(pb[4 + qs][:, 0:65], p_[0:nkp, qs * 128:(qs + 1) * 128], vsrc_fn(cidp, nkp),
                                 start=(ci == 1), stop=(ci == n_))
                for qs in range(nqs):
                    s.recip(rcp[:, qs:qs + 1], pb[4 + qs][:, 64:65])
                    s.ts(dst_fn(qs), pb[4 + qs][:, 0:64], rcp[:, qs:qs + 1], ALU.mult)

            oT = _A(pe[0].t[0:65, 0:512], pe[0].tk)
            fin = [0]

            def attend_T(qsrc, g, qc0, nq, chunks, ksrc, vsrc_fn, dst_fn):
                nqs = nq // 128
                lo, hi = 64 * g, 64 * g + 64
                po = pb[4]
                n_ = len(chunks)
                for ci in range(n_ + 1):
                    if ci < n_:
                        kc0, nk, cid = chunks[ci]
                        s.mm(pb[ci % 3][0:nk, 0:nq], ksrc[lo:hi, kc0:kc0 + nk], qsrc[lo:hi, qc0:qc0 + nq])
                    if ci >= 1:
                        kc0p, nkp, cidp = chunks[ci - 1]
                        p_ = pt[ptc[0] % 3]; ptc[0] += 1
                        s.act(p_[0:nkp, 0:nq], pb[(ci - 1) % 3][0:nkp, 0:nq], AF.Exp)
                        s.mm(po[0:65, 0:nq], vsrc_fn(cidp, nkp), p_[0:nkp, 0:nq], start=(ci == 1), stop=(ci == n_))
                s.cp(oT[:, 0:nq], po[0:65, 0:nq], e="dve")
                for qs in range(nqs):
                    pf = pb[5 + fin[0] % 3]; fin[0] += 1
                    s.tr(pf[:, 0:65], oT[:, qs * 128:(qs + 1) * 128], ident[0:65, 0:65])
                    s.recip(rcp[:, qs:qs + 1], pf[:, 64:65])
                    s.ts(dst_fn(qs), pf[:, 0:64], rcp[:, qs:qs + 1], ALU.mult)

            def attend_T2(qsrc, qc0, nq, chunks, ksrc, vsrc_fn, dst_fn):
                nqs = nq // 128
                n_ = len(chunks)
                po = [pb[4], pb[5]]
                for ci in range(n_ + 1):
                    if ci < n_:
                        kc0, nk, cid = chunks[ci]
                        for g in range(2):
                            lo, hi = 64 * g, 64 * g + 64
                            s.mm(pb[(2 * ci + g) % 4][0:nk, 0:nq], ksrc[lo:hi, kc0:kc0 + nk], qsrc[lo:hi, qc0:qc0 + nq])
                    if ci >= 1:
                        kc0p, nkp, cidp = chunks[ci - 1]
                        for g in range(2):
                            p_ = pt[ptc[0] % 3]; ptc[0] += 1
                            s.act(p_[0:nkp, 0:nq], pb[(2 * (ci - 1) + g) % 4][0:nkp, 0:nq], AF.Exp)
                            s.mm(po[g][0:65, 0:nq], vsrc_fn(cidp, nkp, g), p_[0:nkp, 0:nq], start=(ci == 1), stop=(ci == n_))
                for g in range(2):
                    s.cp(oT[:, 0:nq], po[g][0:65, 0:nq], e="dve")
                    for qs in range(nqs):
                        pf = pb[6 + fin[0] % 2]; fin[0] += 1
                        s.tr(pf[:, 0:65], oT[:, qs * 128:(qs + 1) * 128], ident[0:65, 0:65])
                        s.recip(rcp[:, qs:qs + 1], pf[:, 64:65])
                        s.ts(dst_fn(qs, g), pf[:, 0:64], rcp[:, qs:qs + 1], ALU.mult)

            for qg in range(4):
                for pr_ in range(3):
                    attend_T2(qT[pr_], qg * 512, 512, [(c * 128, 128, c) for c in range(NKT)], kT,
                              lambda cid, nk, g: Vg[0:nk, cid, g, :],
                              lambda qs, g, qg=qg, pr_=pr_: YAN[:, qg * 4 + qs, (pr_ + 3 * g) * 64:(pr_ + 3 * g + 1) * 64])
            for h in range(6):
                pr_, g = h % 3, h // 3
                attend_T(qT[pr_], g, 2048, 256, [(c * 128, 128, c) for c in range(2)], kT,
                         lambda cid, nk, g=g: Vg[0:nk, cid, g, :],
                         lambda qs, h=h: YAN[:, 16 + qs, h * 64:(h + 1) * 64])
            def attend_na2(pr_, i, chunks, bts, base):
                qsrc, ksrc = nqT[pr_], nkT[pr_]
                qc0 = i * 128
                n_ = len(chunks)
                po = [pb[4], pb[5]]
                for ci in range(n_ + 1):
                    if ci < n_:
                        kc0, nk, cid = chunks[ci]
                        for g in range(2):
                            lo, hi = 64 * g, 64 * g + 64
                            ps = pb[(2 * ci + g) % 4]
                            has_b = cid < 20
                            s.mm(ps[0:nk, 0:128], ksrc[lo:hi, kc0:kc0 + nk], qsrc[lo:hi, qc0:qc0 + 128], start=True, stop=not has_b)
                            if has_b:
                                s.mm(ps[0:nk, 0:128], bts[g][:, cid * 128 - base:cid * 128 - base + nk], ident[:, 0:128],
                                     start=False, stop=True)
                    if ci >= 1:
                        kc0p, nkp, cidp = chunks[ci - 1]
                        for g in range(2):
                            hn = 2 * pr_ + g
                            p_ = pt[ptc[0] % 3]; ptc[0] += 1
                            s.act(p_[0:nkp, 0:128], pb[(2 * (ci - 1) + g) % 4][0:nkp, 0:128], AF.Exp)
                            s.mm(po[g][:, 0:65], p_[0:nkp, 0:128], Vn[0:nkp, cidp, hn, :], start=(ci == 1), stop=(ci == n_))
                for g in range(2):
                    hn = 2 * pr_ + g
                    s.recip(rcp[:, g:g + 1], po[g][:, 64:65])
                    s.ts(YAN[:, i, 384 + hn * 64:384 + (hn + 1) * 64], po[g][:, 0:64], rcp[:, g:g + 1], ALU.mult)

            nab_t = [s.sb([128, 768], name="nabx%d" % i) for i in range(2)] if False else None
            for i in range(16):
                chs = na_chunks(i)
                base = chs[0][0] * 128
                cls = na_class(i)
                nkeys = sum(nk for _, nk in chs)
                chunks = [(c * 128, nk, c) for c, nk in chs] + [(20 * 128, 128, 20), (21 * 128, 128, 21)]
                for pr_ in range(2):
                    bts = []
                    for g in range(2):
                        hn = 2 * pr_ + g
                        bt_ = bias_t[g]
                        s.dma("sp" if g == 0 else "act", bt_[:, 0:nkeys], nab[l, cls, hn, :, 0:nkeys])
                        bts.append(bt_)
                    attend_na2(pr_, i, chunks, bts, base)
            for hn in range(4):
                pr_, g = hn // 2, hn % 2
                attend(nqT[pr_], g, 2048, 256, [(20 * 128, 128, 20), (21 * 128, 128, 21)], nkT[pr_],
                       lambda cid, nk, hn=hn: Vn[0:nk, cid, hn, :], None,
                       lambda qs, hn=hn: YAN[:, 16 + qs, 384 + hn * 64:384 + (hn + 1) * 64])
            load_w(wbuf, wb[l], 0, 384)
            load_w(wbuf, wb[l], 1024, 384, dcol=384)
            load_w(wbuf, wb[l], 2176, 256, dcol=768)
            LNG = _A(kT.t[:, 0:2048].bitcast(F32), kT.tk); s.dma("sp", LNG[:], lng[l])
            LNB = _A(kT.t[:, 2048:4096].bitcast(F32), kT.tk); s.dma("act", LNB[:], lnb[l])
            Ff = _A(kT.t[:, 4096:5632].bitcast(F32).rearrange("p (s c) -> p s c", s=2), kT.tk)
            Bk = _A(kT.t[:, 5632:7168].bitcast(F32), kT.tk)
            cen = _A(kT.t[:, 7168:7936].bitcast(F32), kT.tk)
            ysb = _A(qT[1].t[:, 0:768].bitcast(F32), qT[1].tk)
            bsb = _A(qT[1].t[:, 768:1536].bitcast(F32), qT[1].tk)
            sqb = _A(qT[2].t[:, 0:768].bitcast(F32), qT[2].tk)
            YgT = _A(qT[0].t[:, 0:1024].rearrange("p (k t) -> p k t", k=8), qT[0].tk)
            for t in range(NOWN):
                j = 1 if t >= 16 else 0
                x_, h_ = front(*own_src(t), j)
                G = pe[0]
                proj(h_, 0, 1024, G)
                s.act(G[:], G[:], AF.Silu)
                for sr in range(2):
                    q = "sp" if sr == 0 else "act"
                    if t >= 16:
                        s.dma(q, Ff[:, sr, :], YGc[sr * 256 + 128 * (t - 16):sr * 256 + 128 * (t - 16) + 128, :])
                        rb = (2 + sr) * 256 + 128 - 128 * (t - 16)
                        s.dma(q, Bk[:, sr * 384:(sr + 1) * 384], YGc[rb:rb + 128, :])
                    else:
                        s.dma(q, Ff[:, sr, :], YF[sr][128 * t:128 * t + 128, :])
                        s.dma(q, Bk[:, sr * 384:(sr + 1) * 384], YBk[sr][1920 - 128 * t:1920 - 128 * t + 128, :])
                for sr in range(2):
                    s.mm(pb[6 + sr][:, 0:384], jmat[:], Bk[:, sr * 384:(sr + 1) * 384])
                v2 = lambda A_: V(A_.t[:, 0:384].rearrange("p (s c) -> p s c", s=2), A_.tk)
                for sr in range(2):
                    s.tt(ysb[:, sr * 192:(sr + 1) * 192], Ff[:, sr, 0:192], pb[6 + sr][:, 0:192], ALU.add)
                    s.tt(bsb[:, sr * 192:(sr + 1) * 192], Ff[:, sr, 192:384], pb[6 + sr][:, 192:384], ALU.add)
                v3 = lambda A_: V(A_.t[:, 0:384].rearrange("p (h d) -> p h d", d=64), A_.tk)
                s.red(sm[:, 0:6], v3(ysb), ALU.add)
                s.ts(sm[:, 0:6], sm[:, 0:6], 1.0 / 64, ALU.mult)
                s.tt(v3(cen), v3(ysb), V(sm.t[:, 0:6].unsqueeze(2).to_broadcast([128, 6, 64]), sm.tk), ALU.subtract)
                s.tt(sqb[:], cen[:], cen[:], ALU.mult, e="pool")
                s.red(sm[:, 8:14], v3(sqb), ALU.add)
                s.ts(sm[:, 8:14], sm[:, 8:14], 1.0 / 64, ALU.mult, 64e-5, ALU.add)
                s.act(sm[:, 8:14], sm[:, 8:14], AF.Sqrt)
                s.recip(sm[:, 8:14], sm[:, 8:14])
                s.tt(v3(cen), v3(cen), V(sm.t[:, 8:14].unsqueeze(2).to_broadcast([128, 6, 64]), sm.tk), ALU.mult)
                s.tt(cen[:], cen[:], GNG[:], ALU.mult)
                s.tt(cen[:], cen[:], GNB[:], ALU.add)
                s.tt(cen[:], cen[:], bsb[:], ALU.add)
                Yg = pe[1]
                s.tt(Yg[:, 0:384], cen[:], G[:, 0:384], ALU.mult)
                s.tt(Yg[:, 384:1024], YAN[:, t, :], G[:, 384:1024], ALU.mult)
                for half in range(2):
                    p = pb[half]
                    for k in range(4):
                        kk_ = half * 4 + k
                        s.tr(p[:, k * 128:(k + 1) * 128], Yg[:, kk_ * 128:(kk_ + 1) * 128], ident[:])
                    s.cp(YgT[:, half * 4:half * 4 + 4, :], V(p.t[:, :].rearrange("p (k t) -> p k t", k=4), p.tk),
                         e="act" if half == 0 else "dve")
                yo = pe[0]
                proj(YgT, 0, 1024, yo, wsrc=woutb)
                s.tt(yo[:], yo[:], mod[j][:, 2048:3072], ALU.mult)
                z = pe[1]
                s.stt(z[:], x_[:], ALPHA, yo[:], ALU.mult, ALU.add)
                s.red(sm[:, 16:17], z[:], ALU.add)
                s.ts(sm[:, 16:17], sm[:, 16:17], 1.0 / 1024, ALU.mult)
                s.ts(z[:], z[:], sm[:, 16:17], ALU.subtract)
                s.tt(yo[:], z[:], z[:], ALU.mult, e="pool")
                s.red(sm[:, 17:18], yo[:], ALU.add)
                s.ts(sm[:, 17:18], sm[:, 17:18], 1.0 / 1024, ALU.mult, 1e-5, ALU.add)
                s.act(sm[:, 17:18], sm[:, 17:18], AF.Sqrt)
                s.recip(sm[:, 17:18], sm[:, 17:18])
                s.stt(z[:], z[:], sm[:, 17:18], LNG[:], ALU.mult, ALU.mult)
                s.tt(z[:], z[:], LNB[:], ALU.add)
                q = "sp" if t % 2 == 0 else "act"
                if t < 16:
                    if last:
                        tickets.append(s.dma(q, out[128 * t:128 * t + 128, :], z[:]))
                    else:
                        s.dma(q, XN[128 * t:128 * t + 128, :], z[:])
                        if t % 2 == 1:
                            k = t // 2
                            s.allgather(XN[256 * k:256 * (k + 1), :], Xn[512 + 1024 * k:512 + 1024 * (k + 1), :], GROUPS)
                elif not last:
                    s.dma(q, Xn[128 * (t - 16):128 * (t - 16) + 128, :], z[:])
    s.finish(tickets)
    s.close()
    return nc


def rope_tables():
    t = np.arange(8192)
    row = (t // 64).astype(np.float32); col = (t % 64).astype(np.float32)
    inv = (10000.0 ** (-np.arange(16, dtype=np.float32) / 16)).astype(np.float32)
    ar = row[:, None] * inv; ac = col[:, None] * inv
    ang = np.concatenate([ar, ar, ac, ac], axis=-1).astype(np.float32)
    cos = np.cos(ang).astype(np.float32); sin = np.sin(ang).astype(np.float32)
    sgn = np.concatenate([-np.ones(16), np.ones(16), -np.ones(16), np.ones(16)]).astype(np.float32)
    return cos, sin * sgn


def na_bias(rpb, j):
    NEG = -30000.0
    out = np.full((5, 4, 128, 768), NEG, np.float32)
    tiles = {0: 0, 1: 1, 2: 7, 3: 14, 4: 15}
    for cls, i in tiles.items():
        chs = na_chunks(i)
        srow0 = chs[0][0] * 2
        nkeys = sum(nk for _, nk in chs)
        qrow_l = np.repeat(np.array([2 * i, 2 * i + 1]), 64)
        qcol = np.tile(np.arange(64), 2)
        r = 32 * j + qrow_l
        r_start = np.clip(r - 4, 0, 120)
        c_start = np.clip(qcol - 8, 0, 48)
        key = np.arange(nkeys)
        krow = (32 * j - 4) + srow0 + key // 64
        kcol = key % 64
        dr = krow[None, :] - r[:, None] + 7
        dc = kcol[None, :] - qcol[:, None] + 15
        inwin = ((krow[None, :] >= r_start[:, None]) & (krow[None, :] < r_start[:, None] + 8) &
                 (kcol[None, :] >= c_start[:, None]) & (kcol[None, :] < c_start[:, None] + 16))
        drc = np.clip(dr, 0, 14); dcc = np.clip(dc, 0, 30)
        for h in range(4):
            vals = rpb[h][drc, dcc]
            out[cls, h, :, 0:nkeys] = np.where(inwin, vals, NEG)
    return out


def consts_A():
    idx = np.arange(64)
    inclT = (idx[:, None] <= idx[None, :]).astype(np.float32)
    strictT = (idx[:, None] < idx[None, :]).astype(np.float32)
    strict = strictT.T.copy()
    mask = np.concatenate([inclT, strictT, inclT, -strictT, -strict], axis=1)
    rmask = np.ones((64, 256), np.float32); rmask[:, 0::64] = 0.0
    ident = np.eye(64, dtype=np.float32)
    return np.concatenate([ident, mask, mask, mask, rmask] + [ident] * 6, axis=1).astype(np.float32)


def host_F(inp, depth=4):
    cos, sinS = rope_tables()
    bc = lambda v: np.ascontiguousarray(np.broadcast_to(v[None, :], (128, v.shape[0]))).astype(np.float32)
    L = range(depth)
    shared = dict(
        wmod=np.ascontiguousarray(inp['w_mod'][:depth]),
        bmod=np.stack([bc(inp['b_mod'][l]) for l in L]),
        wb=np.ascontiguousarray(inp['w_in'][:depth, :, 1280:]),
        wout=np.ascontiguousarray(inp['w_out'][:depth]),
        cstA=consts_A(),
        cosK=np.tile(cos, (1, 2)), sinK=np.tile(sinS, (1, 2)),
        gk=np.stack([bc(np.tile(inp['gqa_k_norm'][l], 2)) for l in L]),
        gq=np.stack([bc(np.tile(inp['gqa_q_norm'][l], 6)) for l in L]),
        gng=np.stack([bc(inp['rwkv_gn_g'][l]) for l in L]), gnb=np.stack([bc(inp['rwkv_gn_b'][l]) for l in L]),
        lng=np.stack([bc(inp['ln_g'][l]) for l in L]), lnb=np.stack([bc(inp['ln_b'][l]) for l in L]),
        ident=np.eye(128, dtype=np.float32), jmat=np.ascontiguousarray(np.eye(128, dtype=np.float32)[::-1]),
    )
    per_batch = []
    for b in range(2):
        xa0 = np.zeros((XROWS, 1024), np.float32)
        xa0[0:256] = inp['ctx'][b]
        xa0[512:8704] = inp['x'][b].reshape(4, 8, 256, 1024).transpose(1, 0, 2, 3).reshape(8192, 1024)
        cvec = np.stack([inp['c'][b], inp['c_ctx']], axis=1)
        cv = np.ascontiguousarray(cvec.reshape(8, 128, 2).transpose(1, 0, 2).reshape(128, 16))
        per_batch.append(dict(xa0=xa0, cv=cv))
    nabs = [np.stack([na_bias(inp['na_rpb'][l], j) for l in L]) for j in range(4)]
    maps = []
    for c in range(8):
        b, r = c // 4, c % 4
        d, hh = r // 2, r % 2
        heads = [3 * hh + i for i in range(3)]
        cols = []
        for comp in (0, 384, 768):
            for h in heads:
                cols += list(range(comp + h * 64, comp + h * 64 + 64))
        cols += list(range(1152 + 32 * d, 1152 + 32 * d + 32))
        cols += list(range(1216 + 32 * d, 1216 + 32 * d + 32))
        cols = np.array(cols)
        hcols = np.concatenate([np.arange(h * 64, (h + 1) * 64) for h in heads])
        wa = np.ascontiguousarray(inp['w_in'][:depth][:, :, cols])
        cw = np.zeros((depth, 64, 33), np.float32); pv = np.zeros((depth, 64, 15), np.float32)
        for l in L:
            conv = inp['rwkv_conv'][l][:, cols]
            if d == 1:
                conv = conv[::-1]
            for ci in range(9):
                cw[l, :, ci * 3:ci * 3 + 3] = conv[:, ci * 64:(ci + 1) * 64].T
            cw[l, 0:32, 27:30] = conv[:, 576:608].T
            cw[l, 0:32, 30:33] = conv[:, 608:640].T
            for i, h in enumerate(heads):
                hs = slice(h * 64, (h + 1) * 64)
                pv[l, :, i * 5 + 0] = inp['decay_w0'][l][d, hs]
                pv[l, :, i * 5 + 1] = inp['iclr_a0'][l][d, hs]
                pv[l, :, i * 5 + 2] = inp['rwkv_k_k'][l][hs]
                pv[l, :, i * 5 + 3] = inp['rwkv_k_a'][l][hs]
                pv[l, :, i * 5 + 4] = inp['rwkv_r_k'][l][h]
        w2 = np.ascontiguousarray(inp['decay_w2'][:depth, d][:, :, hcols])
        a2 = np.ascontiguousarray(inp['iclr_a2'][:depth, d][:, :, hcols])
        own = slice(2048 * r, 2048 * r + 2048)
        jd = np.eye(128, dtype=np.float32)
        if d == 1:
            jd = np.ascontiguousarray(jd[::-1])
        dsel = np.zeros((128, 2), np.float32); dsel[:, d] = 1.0
        xs0 = np.zeros((2560, 1024), np.float32)
        lo = 2048 * r - 256
        for sr_ in range(2560):
            pass
        a0, a1 = max(lo, 0), min(lo + 2560, 8192)
        xs0[a0 - lo:a1 - lo] = inp['x'][b][a0:a1]
        m = dict(shared)
        m['xs0'] = xs0
        m.update(per_batch[b])
        m.update(wa=wa, cw=cw, pv=pv, w2=w2, a2=a2, nab=nabs[r],
                 cosQ=np.tile(cos[own], (1, 6)), sinQ=np.tile(sinS[own], (1, 6)), jd=jd, dsel=dsel)
        maps.append(m)
    return maps


from concourse.bass_utils import run_bass_kernel_spmd

_NC = {}


def kernel(**inputs):
    inp = {k: np.asarray(v, dtype=np.float32) for k, v in inputs.items()}
    if 'F' not in _NC:
        _NC['F'] = build_F(4)
    maps = host_F(inp, 4)
    res = run_bass_kernel_spmd(_NC['F'], maps, core_ids=list(range(8))).results
    x = np.stack([np.concatenate([res[b * 4 + r]["out"] for r in range(4)], axis=0) for b in range(2)])
    return np.ascontiguousarray(x.astype(np.float32))
```

```python
import contextlib
import numpy as np
import concourse.bass as bass
import concourse.mybir as mybir

F32 = mybir.dt.float32
BF16 = mybir.dt.bfloat16
AF = mybir.ActivationFunctionType
ALU = mybir.AluOpType
AX = mybir.AxisListType


class Tk:
    __slots__ = ("w", "r", "name", "excl", "acc")

    def __init__(self, name=""):
        self.w = None
        self.r = {}
        self.name = name
        self.excl = False
        self.acc = {}


class T:
    def __init__(self, S, t, name):
        self.t = t
        self.tk = Tk(name)
        self.name = name

    def __getitem__(self, idx):
        return V(self.t[idx], self.tk)


class V:
    __slots__ = ("ap", "tk")

    def __init__(self, ap, tk):
        self.ap = ap
        self.tk = tk


import threading


class _Worker(threading.Thread):
    def __init__(self, il, fn):
        super().__init__(daemon=True)
        self.il = il
        self.fn = fn
        self.go = threading.Event()
        self.done = False
        self.exc = None

    def run(self):
        self.go.wait(); self.go.clear()
        try:
            self.fn()
        except BaseException as e:
            self.exc = e
        self.done = True
        self.il.main_ev.set()

    def pause(self):
        self.il.main_ev.set()
        self.go.wait(); self.go.clear()


class Interleaver:
    def __init__(self, s):
        self.s = s
        self.main_ev = threading.Event()
        self.cur = None

    def run(self, fns, width):
        pending = list(fns)
        active = []
        self.s.yield_hook = self._hook
        try:
            while pending or active:
                while pending and len(active) < width:
                    w = _Worker(self, pending.pop(0)); w.start(); active.append(w)
                for w in list(active):
                    self.cur = w
                    self.main_ev.clear()
                    w.go.set()
                    self.main_ev.wait()
                    if w.exc is not None:
                        raise w.exc
                    if w.done:
                        active.remove(w)
        finally:
            self.s.yield_hook = None
            self.cur = None

    def _hook(self):
        w = self.cur
        if w is not None and threading.current_thread() is w and self.s.atomic_depth == 0:
            w.pause()


class S:
    ENG = ("pe", "act", "dve", "pool", "sp")
    yield_hook = None
    atomic_depth = 0

    @contextlib.contextmanager
    def atomic(self):
        self.atomic_depth += 1
        try:
            yield
        finally:
            self.atomic_depth -= 1
            if self.atomic_depth == 0 and self.yield_hook is not None:
                self.yield_hook()

    def __init__(self, nc):
        self.nc = nc
        self.es = contextlib.ExitStack()
        self.eng = {"pe": nc.tensor, "act": nc.scalar, "dve": nc.vector, "pool": nc.gpsimd, "sp": nc.sync}
        self.sem = {e: self.es.enter_context(nc.semaphore("s_" + e)) for e in self.ENG}
        self.cnt = {e: 0 for e in self.ENG}
        self.dq = {}
        for q in ("sp", "act", "pool"):
            sems = [self.es.enter_context(nc.semaphore("d_%s%d" % (q, i))) for i in range(8)]
            self.dq[q] = dict(sems=sems, cnt=[0] * 8, nxt=0)
            for i, s_ in enumerate(sems):
                self.sem[(q, i)] = s_
        self.waited = {}
        self.cur = self.es
        self.cc_keys = []
        self.n_tiles = 0
        self.n_instr = 0
        self.n_wait = 0

    def sb(self, shape, dt=F32, name=None):
        self.n_tiles += 1
        name = "%s_%d" % (name or "t", self.n_tiles)
        t = self.cur.enter_context(self.nc.sbuf_tensor("sb_" + name, list(shape), dt))
        return T(self, t, name)

    def dram(self, shape, name, dt=F32):
        self.n_tiles += 1
        t = self.nc.dram_tensor("%s_%d" % (name, self.n_tiles), list(shape), dt)
        return T(self, t.ap(), name)

    @contextlib.contextmanager
    def phase(self):
        prev = self.cur
        self.cur = contextlib.ExitStack()
        try:
            yield
        finally:
            self.barrier()
            self.cur.close()
            self.cur = prev

    def barrier(self):
        for e in self.ENG:
            for e2 in self.ENG:
                if e2 != e and self.cnt[e2] > 0:
                    self._wait(e, e2, self.cnt[e2])
            for q, d in self.dq.items():
                for i, c in enumerate(d["cnt"]):
                    if c > 0:
                        self._wait(e, (q, i), c)

    def allgather(self, src, dst, groups):
        if "cc" not in self.dq:
            sems = [self.es.enter_context(self.nc.semaphore("cc%d" % i)) for i in range(8)]
            self.dq["cc"] = dict(sems=sems, cnt=[0] * 8, nxt=0)
            for i, s_ in enumerate(sems):
                self.sem[("cc", i)] = s_
        d = self.dq["cc"]
        i = d["nxt"]; d["nxt"] = (i + 1) % 8
        key = ("cc", i)
        self._wait("pool", key, d["cnt"][i])
        self._deps("pool", [src], [dst])
        ins = self.nc.gpsimd.collective_compute("AllGather", mybir.AluOpType.bypass, replica_groups=groups,
                                                ins=[src.ap.opt()], outs=[dst.ap.opt()])
        d["cnt"][i] += 1
        ins.then_inc(self.sem[key])
        self._mark((key, d["cnt"][i]), [src], [dst])
        self.n_instr += 1
        return (key, d["cnt"][i])

    def ps(self, shape, dt=F32, name=None):
        self.n_tiles += 1
        name = name or "p%d" % self.n_tiles
        t = self.es.enter_context(self.nc.psum_tensor("ps_" + name, list(shape), dt))
        tt_ = T(self, t, name)
        tt_.tk.excl = True
        return tt_

    def close(self):
        self.es.close()

    def _wait(self, e, key, val):
        if val is None:
            return
        k = (e, key)
        if self.waited.get(k, 0) >= val:
            return
        self.waited[k] = val
        self.eng[e].wait_ge(self.sem[key], val)
        self.n_wait += 1

    def _deps(self, e, reads, writes, pe_acc=False):
        for v in list(reads) + list(writes):
            if v.tk.excl:
                for e2, n2 in v.tk.acc.items():
                    if e2 == e and e == "pe":
                        continue
                    self._wait(e, e2, n2)
        reads = [v for v in reads if not v.tk.excl]
        writes = [v for v in writes if not v.tk.excl]
        for v in reads:
            w = v.tk.w
            if w is not None:
                self._wait(e, w[0], w[1])
        for v in writes:
            tk = v.tk
            if tk.w is not None:
                if not (pe_acc and tk.w[0] == "pe" and e == "pe"):
                    self._wait(e, tk.w[0], tk.w[1])
            for re_, rn in tk.r.items():
                if re_ == e and e == "pe":
                    continue
                self._wait(e, re_, rn)

    def _mark(self, ticket, reads, writes):
        for v in list(reads) + list(writes):
            if v.tk.excl:
                v.tk.acc[ticket[0]] = ticket[1]
        reads = [v for v in reads if not v.tk.excl]
        writes = [v for v in writes if not v.tk.excl]
        for v in reads:
            v.tk.r[ticket[0]] = ticket[1]
        for v in writes:
            v.tk.w = ticket
            v.tk.r = {}

    def op(self, e, fn, reads, writes, pe_acc=False):
        reads = [v for v in reads if isinstance(v, V)]
        self._deps(e, reads, writes, pe_acc)
        ins = fn()
        self.cnt[e] += 1
        ins.then_inc(self.sem[e], 1)
        self._mark((e, self.cnt[e]), reads, writes)
        self.n_instr += 1
        if self.yield_hook is not None:
            self.yield_hook()
        return ins

    def dma(self, q, out, in_, **kw):
        d = self.dq[q]
        i = d["nxt"]
        d["nxt"] = (i + 1) % len(d["sems"])
        key = (q, i)
        self._wait(q, key, d["cnt"][i])
        reads = [in_] if isinstance(in_, V) else []
        writes = [out] if isinstance(out, V) else []
        self._deps(q, reads, writes)
        oa = out.ap if isinstance(out, V) else out
        ia = in_.ap if isinstance(in_, V) else in_
        ins = self.eng[q].dma_start(out=oa, in_=ia, **kw)
        d["cnt"][i] += 16
        ins.then_inc(self.sem[key], 16)
        self._mark((key, d["cnt"][i]), reads, writes)
        self.n_instr += 1
        if self.yield_hook is not None:
            self.yield_hook()
        return (key, d["cnt"][i])

    def wait_ticket(self, e, ticket):
        self._wait(e, ticket[0], ticket[1])

    def mm(self, out, lhsT, rhs, start=True, stop=True, **kw):
        return self.op("pe", lambda: self.nc.tensor.matmul(out.ap, lhsT.ap, rhs.ap, start=start, stop=stop, **kw),
                       [lhsT, rhs], [out], pe_acc=not start)

    def tr(self, out, in_, ident):
        return self.op("pe", lambda: self.nc.tensor.transpose(out.ap, in_.ap, ident.ap), [in_, ident], [out])

    def act(self, out, in_, func, bias=None, scale=None, accum_out=None, e="act"):
        kw = {}
        rd = [in_]
        if bias is not None:
            kw["bias"] = bias.ap if isinstance(bias, V) else bias
            rd.append(bias)
        if scale is not None:
            kw["scale"] = scale.ap if isinstance(scale, V) else scale
            rd.append(scale)
        wr = [out]
        if accum_out is not None:
            kw["accum_out"] = accum_out.ap
            wr.append(accum_out)
        return self.op("act", lambda: self.nc.scalar.activation(out.ap, in_.ap, func, **kw), rd, wr)

    def _ve(self, e):
        return {"dve": self.nc.vector, "pool": self.nc.gpsimd, "act": self.nc.scalar}[e]

    def tt(self, out, a, b, op, e="dve"):
        return self.op(e, lambda: self._ve(e).tensor_tensor(out.ap, a.ap, b.ap, op), [a, b], [out])

    def ts(self, out, a, s1, op0, s2=None, op1=None, e="dve", accum_out=None):
        rd = [a, s1, s2]
        a1 = s1.ap if isinstance(s1, V) else s1
        a2 = s2.ap if isinstance(s2, V) else s2
        kw = {}
        wr = [out]
        if op1 is not None:
            kw["op1"] = op1
        if accum_out is not None:
            kw["accum_out"] = accum_out.ap
            wr.append(accum_out)
        return self.op(e, lambda: self._ve(e).tensor_scalar(out.ap, a.ap, a1, a2, op0, **kw), rd, wr)

    def stt(self, out, a, s, b, op0, op1, e="dve"):
        sa = s.ap if isinstance(s, V) else s
        return self.op(e, lambda: self._ve(e).scalar_tensor_tensor(out.ap, a.ap, sa, b.ap, op0, op1), [a, s, b], [out])

    def cp(self, out, in_, e="dve"):
        if e == "act":
            return self.op("act", lambda: self.nc.scalar.copy(out.ap, in_.ap), [in_], [out])
        return self.op(e, lambda: self._ve(e).tensor_copy(out.ap, in_.ap), [in_], [out])

    def memset(self, out, val, e="pool"):
        return self.op(e, lambda: self._ve(e).memset(out.ap, val), [], [out])

    def red(self, out, in_, op, axis=AX.X, e="dve"):
        return self.op(e, lambda: self._ve(e).tensor_reduce(out.ap, in_.ap, axis, op), [in_], [out])

    def recip(self, out, in_):
        return self.op("dve", lambda: self.nc.vector.reciprocal(out.ap, in_.ap), [in_], [out])

    def finish(self, tickets):
        for t in tickets:
            self._wait("sp", t[0], t[1])


A_DEC = 0.6065306597126334
ALPHA = (2 * 4) ** 0.25
NOWN = 18
NKT = 66
NNT = 22
GROUPS = [[0, 1, 2, 3], [4, 5, 6, 7]]
XROWS = 8960


def na_chunks(i):
    if i == 0:
        return [(c, 128) for c in range(0, 6)]
    if i == 1:
        return [(c, 128) for c in range(1, 6)]
    if i == 15:
        return [(c, 128) for c in range(14, 19)] + [(19, 64)]
    return [(c, 128) for c in range(i, i + 4)] + [(i + 4, 64)]


def na_class(i):
    return {0: 0, 1: 1, 14: 3, 15: 4}.get(i, 2)


def lat_row(tau):
    rho, rem = divmod(tau, 2048)
    k, i = divmod(rem, 256)
    return 512 + 1024 * k + 256 * rho + i


class _A:
    def __init__(self, ap, tk):
        self.t = ap; self.tk = tk

    def __getitem__(self, idx):
        return V(self.t[idx], self.tk)


def build_F(depth=4):
    nc = bass.Bass("TRN2", target_bir_lowering=False)
    dt = nc.dram_tensor
    I = lambda n, sh: dt(n, sh, F32, kind="ExternalInput").ap()
    xa0 = I("xa0", [XROWS, 1024]); xs0 = I("xs0", [2560, 1024])
    cv = I("cv", [128, 16]); wmod = I("wmod", [depth, 1024, 3072]); bmod = I("bmod", [depth, 128, 3072])
    wb = I("wb", [depth, 1024, 2432]); wout = I("wout", [depth, 1024, 1024])
    wa = I("wa", [depth, 1024, 640]); cw = I("cw", [depth, 64, 33]); pvi = I("pv", [depth, 64, 15])
    w2 = I("w2", [depth, 32, 192]); a2 = I("a2", [depth, 32, 192])
    cstA = I("cstA", [64, 1664])
    cosK = I("cosK", [8192, 128]); sinK = I("sinK", [8192, 128])
    cosQ = I("cosQ", [2048, 384]); sinQ = I("sinQ", [2048, 384])
    dsel_in = I("dsel", [128, 2])
    gk = I("gk", [depth, 128, 128]); gq = I("gq", [depth, 128, 384])
    nab = I("nab", [depth, 5, 4, 128, 768])
    gng = I("gng", [depth, 128, 384]); gnb = I("gnb", [depth, 128, 384])
    lng = I("lng", [depth, 128, 1024]); lnb = I("lnb", [depth, 128, 1024])
    ident_in = I("ident", [128, 128]); jmat_in = I("jmat", [128, 128]); jd_in = I("jd", [128, 128])
    out = dt("out", [2048, 1024], F32, kind="ExternalOutput").ap()

    s = S(nc)
    tickets = []
    pb = [s.ps([128, 512], name="pb%d" % i) for i in range(8)]
    XA = [s.dram([XROWS, 1024], "XA%d" % i) for i in range(2)]
    YB = s.dram([8448, 384], "YB")
    YGc = s.dram([4 * 256, 384], "YGc")
    YGl = s.dram([16 * 4 * 512, 384], "YGl")
    XN = s.dram([2048, 1024], "XN")
    XS = s.dram([2560, 1024], "XS")
    KTO = s.dram([128, 2048], "KTO", BF16); KTG = s.dram([512, 2048], "KTG", BF16)
    VOd = s.dram([2048, 130], "VOd", BF16); VGd = s.dram([8192, 130], "VGd", BF16)
    YF = [s.dram([2048, 384], "YF%d" % i) for i in range(2)]
    YBk = [s.dram([2048, 384], "YBk%d" % i) for i in range(2)]
    xa0_T = _A(xa0, Tk("xa0")); xs0_T = _A(xs0, Tk("xs0"))

    _rd = {}

    def RR(q):
        if q not in _rd:
            pid = s.eng[q].partition_id()
            _rd[q] = pid % 4
        return _rd[q]

    dynq = ["sp", "act", "pool"]
    dync = [0]

    def dyndma(dst_v, src_fn):
        q = dynq[dync[0] % 3]; dync[0] += 1
        return s.dma(q, dst_v, src_fn(RR(q)))

    ident = s.sb([128, 128], name="ident"); s.dma("sp", ident[:], ident_in)
    jmat = s.sb([128, 128], name="jmat"); s.dma("act", jmat[:], jmat_in)
    jd = s.sb([128, 128], name="jd"); s.dma("sp", jd[:], jd_in)
    dsel = s.sb([128, 2], name="dsel"); s.dma("act", dsel[:], dsel_in)
    identb = s.sb([128, 128], BF16, name="identb"); s.cp(identb[:], ident[:])
    jdb = s.sb([128, 128], BF16, name="jdb"); s.cp(jdb[:], jd[:])
    ones = s.sb([128, 128], name="ones"); s.memset(ones[:], 1.0)
    cv_t = s.sb([128, 16], name="cv"); s.dma("sp", cv_t[:], cv)
    scv = s.sb([128, 16], name="scv"); s.act(scv[:], cv_t[:], AF.Silu)
    mod = [s.sb([128, 3072], name="mod%d" % j) for j in range(2)]
    for l in range(depth):
        Xc = xa0_T if l == 0 else XA[(l - 1) % 2]
        Xn = XA[l % 2]
        last = (l == depth - 1)

        if l == 0:
            XSc = xs0_T
        else:
            XSc = XS
            lat = Xc.t[512:8704, :]
            dyndma(V(XS.t[256:2304, :].rearrange("(o k i) c -> o k (i c)", o=1, k=8), XS.tk),
                   lambda r: V(lat.rearrange("(k rr i) c -> rr k (i c)", rr=4, i=256)[bass.ds(r, 1)], Xc.tk))
            units = lat.rearrange("(u i) c -> u (i c)", i=256)
            dyndma(V(XS.t[0:256, :].rearrange("(o i) c -> o (i c)", o=1), XS.tk),
                   lambda r: V(units[bass.ds(r + 27, 1), :], Xc.tk))
            dyndma(V(XS.t[2304:2560, :].rearrange("(o i) c -> o (i c)", o=1), XS.tk),
                   lambda r: V(units[bass.ds(r + 1, 1), :], Xc.tk))
        with s.phase():
            stage_ws = [s.sb([128, 8, 512], name="stage_w%d" % i) for i in range(2)]
            Rl = s.sb([128, 16, 128], name="Rl")
            bmod_ts = [s.sb([128, 512], name="bmodt%d" % i) for i in range(2)]
            for i in range(16):
                s.ts(Rl[:, i, :], ones[:], scv[:, i:i + 1], ALU.mult, e="dve" if i % 2 == 0 else "pool")
            for cb in range(6):
                stage_w = stage_ws[cb % 2]; bmod_t = bmod_ts[cb % 2]
                s.dma("sp", stage_w[:], wmod[l].rearrange("(k p) c -> p k c", p=128)[:, :, cb * 512:(cb + 1) * 512])
                s.dma("act", bmod_t[:], bmod[l][:, cb * 512:(cb + 1) * 512])
                for j in range(2):
                    ps = pb[(2 * cb + j) % 4]
                    for k in range(8):
                        s.mm(ps[:, :], Rl[:, 2 * k + j, :], stage_w[:, k, :], start=(k == 0), stop=(k == 7))
                    s.tt(mod[j][:, cb * 512:(cb + 1) * 512], ps[:, :], bmod_t[:], ALU.add, e="dve")
            for j in range(2):
                s.ts(mod[j][:, 1024:2048], mod[j][:, 1024:2048], 1.0, ALU.add, e="pool")

        with s.phase():
            cst_t = s.sb([64, 1664], name="cst"); s.dma("sp", cst_t[:], cstA)
            identA = cst_t[:, 0:64]
            mask3 = lambda h: cst_t[:, 64 + h * 320: 64 + (h + 1) * 320]
            rmask = cst_t[:, 1024:1280]
            idt3 = cst_t[:, 1280:1664]
            cw_t = s.sb([64, 33], name="cw"); s.dma("act", cw_t[:], cw[l])
            pv_t = s.sb([64, 16], name="pv"); s.dma("act", pv_t[:, 0:15], pvi[l])
            omk = s.sb([64, 3], name="omk")
            for h in range(3):
                s.ts(omk[:, h:h + 1], pv_t[:, h * 5 + 3:h * 5 + 4], -1.0, ALU.mult, 1.0, ALU.add)
            w2_t = s.sb([32, 192], name="w2"); s.dma("act", w2_t[:], w2[l])
            a2_t = s.sb([32, 192], name="a2"); s.dma("act", a2_t[:], a2[l])
            xt = [s.sb([128, 1024], name="xt%d" % i) for i in range(2)]
            ht = s.sb([128, 1024], name="ht")
            xr = s.sb([128, 1024], name="xr")
            hbA = s.sb([128, 1024], BF16, name="hbA")
            wab = s.sb([128, 8, 640], BF16, name="wab")
            for k in range(8):
                st_ = xt[k % 2]
                s.dma("sp" if k % 2 == 0 else "act", st_[:, 0:640], wa[l][k * 128:(k + 1) * 128, :])
                s.cp(wab[:, k, :], st_[:, 0:640], e="dve" if k % 2 == 0 else "pool")
            cts = [(i * 64, 64) for i in range(9)] + [(576, 32), (608, 32)]
            hg = [s.sb([128, 8, 258], BF16, name="hg%d" % i) for i in range(2)]
            for h_ in hg:
                s.memset(h_[:], 0.0)
            raw = [s.sb([64, 258], name="raw%d" % i) for i in range(2)]
            ctmp = [s.sb([64, 256], name="ctmp%d" % i) for i in range(2)]
            mk = lambda nm, shape=(64, 256), dt_=F32: [s.sb(list(shape), dt_, name="%s%d" % (nm, h)) for h in range(3)]
            uR, uK, uV = mk("uR"), mk("uK"), mk("uV", dt_=BF16)
            uD = s.sb([32, 256], name="uD"); uA = s.sb([32, 256], name="uA"); ddt = s.sb([32, 256], name="ddt")
            sg, ic, kk, tmp, kd, bd = mk("sg"), mk("ic"), mk("kk"), mk("tmp"), mk("kd"), mk("bd")
            cs, csx, csr = mk("cs"), mk("csx"), mk("csr")
            E1, E3 = mk("E1"), mk("E3")
            E4 = E1
            RH, KKH, kt, bt, kc, bc, rk = (mk("RH", dt_=BF16), mk("KKH", dt_=BF16), mk("kt", dt_=BF16), mk("bt", dt_=BF16),
                                           mk("kc", dt_=BF16), mk("bc", dt_=BF16), mk("rk", dt_=BF16))
            identAb = s.sb([64, 64], BF16, name="identAb"); s.cp(identAb[:], identA)
            onesb = s.sb([64, 1], BF16, name="onesb"); s.memset(onesb[:], 1.0)
            wc = mk("wc", (64, 4)); rn = tmp
            trT = [s.sb([64, 3, 256], BF16, name="trT%d" % i) for i in range(4)]
            scS = [s.sb([64, 3, 320], BF16, name="scS%d" % i) for i in range(4)]
            XYs = [[s.sb([64, 3, 128], BF16, name="XY%d_%d" % (c, i)) for i in range(2)] for c in range(4)]
            PQs = [[s.sb([64, 3, 128], BF16, name="PQ%d_%d" % (c, i)) for i in range(2)] for c in range(4)]
            KKpTs = [s.sb([64, 192], BF16, name="KKpT%d" % c) for c in range(4)]
            AVs = [s.sb([64, 192], BF16, name="AV%d" % c) for c in range(4)]
            Ulocs = [s.sb([64, 192], name="Uloc%d" % c) for c in range(4)]
            Us = [s.sb([64, 192], BF16, name="U%d" % c) for c in range(4)]
            STb = [s.sb([64, 192], BF16, name="STb%d" % i) for i in range(2)]
            bss = [s.sb([64, 4], name="bs%d" % c) for c in range(4)]
            il = Interleaver(s)
            ST = [s.sb([64, 192], name="ST%d" % i) for i in range(2)]
            YBuf = [s.sb([64, 4, 384], name="YBuf0")] * 2
            bs_ = s.sb([64, 4], name="bs")
            s.memset(ST[0][:], 0.0)
            s.memset(STb[0][:], 0.0)
            sti = 0
            acnt = [0]

            def frontA(g):
                hgt = hg[g % 2]
                for a in range(2):
                    u = 2 * g + a
                    i = acnt[0]; acnt[0] += 1
                    if u < 2:
                        bf_, br_ = 128 * u, 128 * (1 - u)
                        j = 1
                    else:
                        v = u - 2
                        bf_, br_ = lat_row(128 * v), lat_row(128 * (63 - v))
                        j = 0
                    x_ = xt[i % 2]
                    s.dma("sp", x_[:], V(Xc.t[bf_:bf_ + 128, :], Xc.tk))
                    s.dma("act", xr[:], V(Xc.t[br_:br_ + 128, :], Xc.tk))
                    s.act(x_[:], x_[:], AF.Identity, scale=dsel[:, 0:1])
                    s.stt(x_[:], xr[:], dsel[:, 1:2], x_[:], ALU.mult, ALU.add)
                    s.tt(ht[:], x_[:], mod[j][:, 1024:2048], ALU.mult, e="pool")
                    s.tt(hbA[:], ht[:], mod[j][:, 0:1024], ALU.add, e="dve")
                    for half in range(2):
                        p = pb[half]
                        with s.atomic():
                            pbf = p.t[:, 0:256].bitcast(BF16)
                            for k in range(4):
                                kk_ = half * 4 + k
                                s.tr(V(pbf[:, k * 128:(k + 1) * 128], p.tk), hbA[:, kk_ * 128:(kk_ + 1) * 128], jdb[:])
                            s.cp(hgt[:, half * 4:half * 4 + 4, 1 + 128 * a:1 + 128 * (a + 1)],
                                 V(pbf.rearrange("p (k t) -> p k t", k=4), p.tk), e="act" if half == 0 else "dve")

            NGRP = 33
            OPN = ('RH', 'KKH', 'kt', 'bt', 'kc', 'bc', 'rk', 'uV')
            opsets = [dict(RH=RH, KKH=KKH, kt=kt, bt=bt, kc=kc, bc=bc, rk=rk, uV=uV, wc=wc),
                      dict(RH=mk('RHb', dt_=BF16), KKH=mk('KKHb', dt_=BF16), kt=mk('ktb', dt_=BF16), bt=mk('btb', dt_=BF16),
                           kc=mk('kcb', dt_=BF16), bc=mk('bcb', dt_=BF16), rk=mk('rkb', dt_=BF16), uV=mk('uVb', dt_=BF16),
                           wc=mk('wcb', (64, 4)))]

            def pro1(g):
                O = opsets[g % 2]; uV = O['uV']
                first = g in (0, 1)
                lastg = g in (0, NGRP - 1)
                hgt = hg[g % 2]
                if g + 1 < NGRP:
                    frontA(g + 1)
                    hn_ = hg[(g + 1) % 2]
                    s.cp(hgt[:, :, 257:258], hn_[:, :, 1:2], e="pool")
                    s.cp(hn_[:, :, 0:1], hgt[:, :, 256:257], e="pool")
                for ci, (c0, M) in enumerate(cts):
                    pr = pb[ci % 2]
                    rw = raw[ci % 2]
                    with s.atomic():
                        for k in range(8):
                            s.mm(pr[0:M, 0:258], wab[:, k, c0:c0 + M], hgt[:, k, :], start=(k == 0), stop=(k == 7))
                        s.cp(rw[0:M, :], pr[0:M, 0:258], e="act" if ci % 2 == 0 else "dve")
                    if first:
                        s.memset(rw[0:M, 0:1], 0.0, e="pool")
                    if lastg:
                        s.memset(rw[0:M, 257:258], 0.0, e="pool")
                    dst = (uR, uK, uV)[ci // 3][ci % 3] if ci < 9 else (uD, uA)[ci - 9]
                    tm = ctmp[ci % 2]
                    s.act(tm[0:M, :], rw[0:M, 0:256], AF.Identity, scale=cw_t[0:M, ci * 3:ci * 3 + 1])
                    s.stt(tm[0:M, :], rw[0:M, 1:257], cw_t[0:M, ci * 3 + 1:ci * 3 + 2], tm[0:M, :], ALU.mult, ALU.add)
                    s.stt(dst[0:M, :], rw[0:M, 2:258], cw_t[0:M, ci * 3 + 2:ci * 3 + 3], tm[0:M, :], ALU.mult, ALU.add)
                s.act(ddt[:], uD[:], AF.Tanh)

            def prep_head(g, h):
                O = opsets[g % 2]
                RH, KKH, kt, bt, kc, bc, rk, wc = O['RH'], O['KKH'], O['kt'], O['bt'], O['kc'], O['bc'], O['rk'], O['wc']
                P = lambda i: pv_t[:, h * 5 + i:h * 5 + i + 1]
                pz = pb[2 + h]
                with s.atomic():
                    s.mm(pz[0:64, 0:256], w2_t[:, h * 64:(h + 1) * 64], ddt[:])
                    s.act(sg[h][:], pz[0:64, 0:256], AF.Sigmoid, bias=P(0))
                with s.atomic():
                    s.mm(pz[0:64, 256:512], a2_t[:, h * 64:(h + 1) * 64], uA[:])
                    s.act(ic[h][:], pz[0:64, 256:512], AF.Sigmoid, bias=P(1))
                s.act(kk[h][:], uK[h][:], AF.Identity, scale=P(2))
                s.act(tmp[h][:], kk[h][:], AF.Square)
                pss = pb[2 + h]
                with s.atomic():
                    s.mm(pss[0:64, 0:256], ones[0:64, 0:64], tmp[h][:])
                    s.ts(rn[h][:], pss[0:64, 0:256], 1e-12, ALU.max)
                s.act(rn[h][:], rn[h][:], AF.Ln)
                s.act(rn[h][:], rn[h][:], AF.Exp, scale=-0.5)
                s.tt(kk[h][:], kk[h][:], rn[h][:], ALU.mult)
                s.act(tmp[h][:], ic[h][:], AF.Identity, scale=P(3), bias=omk[:, h:h + 1])
                s.tt(kd[h][:], uK[h][:], tmp[h][:], ALU.mult, e="dve")
                s.tt(bd[h][:], kk[h][:], ic[h][:], ALU.mult, e="pool")
                s.op("dve", lambda h=h: nc.vector.tensor_tensor_scan(cs[h][:].ap, rmask.ap, sg[h][:].ap, 0.0, ALU.mult, ALU.add),
                     [rmask, sg[h][:]], [cs[h][:]])
                s.tt(csx[h][:], cs[h][:], sg[h][:], ALU.subtract)
                for c in range(4):
                    s.ts(csr[h][:, c * 64:(c + 1) * 64], cs[h][:, c * 64:(c + 1) * 64],
                         cs[h][:, c * 64 + 63:c * 64 + 64], ALU.subtract)
                s.act(wc[h][:], cs[h][:, 63::64], AF.Exp, scale=-A_DEC)
                s.act(E1[h][:], cs[h][:], AF.Exp, scale=-A_DEC)
                s.tt(RH[h][:], uR[h][:], E1[h][:], ALU.mult, e="dve")
                s.act(E1[h][:], csx[h][:], AF.Exp, scale=-A_DEC)
                s.tt(KKH[h][:], kk[h][:], E1[h][:], ALU.mult, e="pool")
                s.act(E3[h][:], cs[h][:], AF.Exp, scale=A_DEC)
                s.tt(kt[h][:], kd[h][:], E3[h][:], ALU.mult)
                s.tt(bt[h][:], bd[h][:], E3[h][:], ALU.mult, e="pool")
                s.act(E4[h][:], csr[h][:], AF.Exp, scale=A_DEC)
                s.tt(kc[h][:], kd[h][:], E4[h][:], ALU.mult)
                s.tt(bc[h][:], bd[h][:], E4[h][:], ALU.mult, e="pool")
                s.stt(rk[h][:], uR[h][:], P(4), kd[h][:], ALU.mult, ALU.mult)

            def chunk_body(g, c):
                n = 4 * g + c
                O = opsets[g % 2]
                RH, KKH, kt, bt, kc, bc, rk, uV = O['RH'], O['KKH'], O['kt'], O['bt'], O['kc'], O['bc'], O['rk'], O['uV']
                cc = slice(c * 64, (c + 1) * 64)
                tT = trT[c]; sS = scS[c]
                XY = XYs[c]; PQ = PQs[c]; KKpT = KKpTs[c]; AV = AVs[c]; Uloc = Ulocs[c]; U = Us[c]; bs_ = bss[c]
                p_ = c % 2
                for h in range(3):
                    ptr = pb[2 + p_]
                    with s.atomic():
                        ptrb = ptr.t[0:64, 0:128].bitcast(BF16)
                        for i, src_ in enumerate((KKH, bc, kc, uV)):
                            s.tr(V(ptrb[:, i * 64:(i + 1) * 64], ptr.tk), src_[h][:, cc], identAb[:])
                        s.cp(tT[:, h, :], V(ptrb[:, 0:256], ptr.tk), e="act")
                    psc = pb[4 + p_]
                    with s.atomic():
                        s.mm(psc[0:64, 0:64], kt[h][:, cc], RH[h][:, cc])
                        s.mm(psc[0:64, 64:128], kt[h][:, cc], KKH[h][:, cc])
                        s.mm(psc[0:64, 128:192], bt[h][:, cc], RH[h][:, cc])
                        s.mm(psc[0:64, 192:256], bt[h][:, cc], KKH[h][:, cc])
                        s.mm(psc[0:64, 256:320], KKH[h][:, cc], bt[h][:, cc])
                        s.tt(sS[:, h, :], psc[0:64, 0:320], mask3(h), ALU.mult)
                s.tt(PQ[0][:, :, :], sS[:, :, 192:320], V(idt3.ap.rearrange("p (h c) -> p h c", c=128), idt3.tk), ALU.add, e="pool")
                Xc_ = lambda lvl, h: (sS[:, h, 192:256] if lvl == 0 else XY[lvl % 2][:, h, 0:64])
                Yc_ = lambda lvl, h: (sS[:, h, 256:320] if lvl == 0 else XY[lvl % 2][:, h, 64:128])
                for lvl in range(5):
                    pn, pq = pb[4 + p_], pb[6 + p_]
                    nxt = XY[(lvl + 1) % 2]
                    with s.atomic():
                        for h in range(3):
                            s.mm(pn[0:64, h * 128:h * 128 + 64], Yc_(lvl, h), Xc_(lvl, h))
                            if lvl < 4:
                                s.mm(pn[0:64, h * 128 + 64:h * 128 + 128], Xc_(lvl, h), Yc_(lvl, h))
                        pn3 = pn.t[0:64, 0:384].rearrange("p (h c) -> p h c", c=128)
                        if lvl < 4:
                            s.cp(nxt[:, :, :], V(pn3, pn.tk), e="act")
                        else:
                            s.cp(nxt[:, :, 0:64], V(pn3[:, :, 0:64], pn.tk), e="act")
                    Pc, Pn = PQ[lvl % 2], PQ[(lvl + 1) % 2]
                    with s.atomic():
                        for h in range(3):
                            s.mm(pq[0:64, h * 128:h * 128 + 64], Pc[:, h, 64:128], nxt[:, h, 0:64])
                            if lvl < 4:
                                s.mm(pq[0:64, h * 128 + 64:h * 128 + 128], Pc[:, h, 0:64], nxt[:, h, 64:128])
                        pq3 = pq.t[0:64, 0:384].rearrange("p (h c) -> p h c", c=128)
                        if lvl < 4:
                            s.tt(Pn[:, :, :], V(pq3, pq.tk), Pc[:, :, :], ALU.add)
                        else:
                            s.tt(Pn[:, :, 0:64], V(pq3[:, :, 0:64], pq.tk), Pc[:, :, 0:64], ALU.add)
                TT = PQ[1]
                pk = pb[2 + p_]
                with s.atomic():
                    for h in range(3):
                        s.mm(pk[0:64, h * 64:(h + 1) * 64], tT[:, h, 0:64], TT[:, h, 0:64])
                        s.mm(pk[0:64, 192 + h * 64:192 + (h + 1) * 64], sS[:, h, 64:128], tT[:, h, 192:256])
                    s.cp(KKpT[:], pk[0:64, 0:192], e="act")
                    s.cp(AV[:], pk[0:64, 192:384], e="dve")
                pk3 = pb[6 + p_]
                with s.atomic():
                    for h in range(3):
                        s.mm(pk3[0:64, h * 64:(h + 1) * 64], TT[:, h, 0:64], AV[:, h * 64:(h + 1) * 64])
                    s.cp(Uloc[:], pk3[0:64, 0:192], e="act")

            def chunk_seq(g, c):
                n = 4 * g + c
                yb = YBuf[0]
                O = opsets[g % 2]
                RH, rk, wc = O['RH'], O['rk'], O['wc']
                cc = slice(c * 64, (c + 1) * 64)
                tT = trT[c]; sS = scS[c]
                KKpT = KKpTs[c]; Uloc = Ulocs[c]; U = Us[c]; bs_ = bss[c]
                Sc, Sn = ST[n % 2], ST[(n + 1) % 2]
                Scb, Snb = STb[n % 2], STb[(n + 1) % 2]
                pu = pb[0]
                with s.atomic():
                    for h in range(3):
                        s.mm(pu[0:64, h * 64:(h + 1) * 64], KKpT[:, h * 64:(h + 1) * 64], Scb[:, h * 64:(h + 1) * 64])
                    s.stt(U[:], pu[0:64, 0:192], -1.0, Uloc[:], ALU.mult, ALU.subtract)
                pS = pb[1]
                with s.atomic():
                    for h in range(3):
                        hs = slice(h * 64, (h + 1) * 64)
                        s.mm(pS[0:64, hs], tT[:, h, 128:192], tT[:, h, 192:256], start=True, stop=False)
                        s.mm(pS[0:64, hs], tT[:, h, 64:128], U[:, hs], start=False, stop=True)
                    for h in range(3):
                        hs = slice(h * 64, (h + 1) * 64)
                        s.stt(Sn[:, hs], Sc[:, hs], wc[h][:, c:c + 1], pS[0:64, hs], ALU.mult, ALU.add)
                    s.cp(Snb[:], Sn[:], e="act")
                py = pb[0]
                with s.atomic():
                    for h in range(3):
                        hs = slice(256 + h * 64, 256 + (h + 1) * 64)
                        hh = slice(h * 64, (h + 1) * 64)
                        s.mm(py[0:64, hs], RH[h][:, cc], Scb[:, hh], start=True, stop=False)
                        s.mm(py[0:64, hs], sS[:, h, 128:192], U[:, hh], start=False, stop=False)
                        s.mm(py[0:64, hs], sS[:, h, 0:64], tT[:, h, 192:256], start=False, stop=True)
                    s.cp(yb[:, c, 0:192], py[0:64, 256:448], e="act")
                pbn = pb[1]
                with s.atomic():
                    for h in range(3):
                        s.mm(pbn[0:64, 256 + h:256 + h + 1], rk[h][:, cc], onesb[:, 0:1])
                    s.cp(bs_[:, 0:3], pbn[0:64, 256:259], e="dve")
                for h in range(3):
                    s.ts(yb[:, c, 192 + h * 64:192 + (h + 1) * 64], tT[:, h, 192:256], bs_[:, h:h + 1], ALU.mult, e="pool")
            def seq_group(g):
                tbase = 256 * g
                for c in range(4):
                    chunk_seq(g, c)
                s.dma("sp" if g % 2 == 0 else "act",
                      V(YB.t[tbase:tbase + 256, :].rearrange("(c t) f -> t c f", t=64), YB.tk), YBuf[0][:, :, :])
                if g == 0:
                    s.allgather(YB[0:256, :], YGc[:, :], GROUPS)
                elif g % 2 == 0:
                    m = g // 2 - 1
                    s.allgather(YB[256 + 512 * m:256 + 512 * (m + 1), :], YGl[2048 * m:2048 * (m + 1), :], GROUPS)

            frontA(0)
            pro1(0)
            il.run([(lambda h=h: prep_head(0, h)) for h in range(3)], 3)
            for g in range(NGRP):
                wk1 = [(lambda c=c: chunk_body(g, c)) for c in range(4)]
                if g + 1 < NGRP:
                    wk1.append(lambda: pro1(g + 1))
                il.run(wk1, 5)
                wk2 = [lambda: seq_group(g)]
                if g + 1 < NGRP:
                    wk2 += [(lambda h=h: prep_head(g + 1, h)) for h in range(3)]
                il.run(wk2, 4)
        ygv = YGl.t.rearrange("(m sr i) c -> sr m (i c)", sr=4, i=512)
        for sr in range(2):
            dyndma(V(YF[sr].t.rearrange("(m i) c -> m (i c)", i=512), YF[sr].tk),
                   lambda r, sr=sr: V(ygv[sr][bass.ds(r * 4, 4), :], YGl.tk))
            dyndma(V(YBk[sr].t.rearrange("(m i) c -> m (i c)", i=512), YBk[sr].tk),
                   lambda r, sr=sr: V(ygv[2 + sr][bass.ds((3 - r) * 4, 4), :], YGl.tk))
        with s.phase():
            GK = s.sb([128, 128], name="GK"); s.dma("act", GK[:], gk[l])
            GQ = s.sb([128, 384], name="GQ"); s.dma("act", GQ[:], gq[l])
            GNG = s.sb([128, 384], name="GNG"); s.dma("act", GNG[:], gng[l])
            GNB = s.sb([128, 384], name="GNB"); s.dma("act", GNB[:], gnb[l])
            YAN = s.sb([128, 18, 640], BF16, name="YAN")
            wbuf = s.sb([128, 8, 1024], BF16, name="wbuf")
            woutb = s.sb([128, 8, 1024], BF16, name="woutb")
            xt = [s.sb([128, 1024], name="xt%d" % i) for i in range(2)]
            hts = [s.sb([128, 1024], name="ht%d" % i) for i in range(2)]
            ht = hts[0]
            pe = [s.sb([128, 1024], name="pe%d" % i) for i in range(2)]
            sqs = [pe[1], None]
            wst = xt
            hbs = []
            ilB = Interleaver(s)
            pjB = _A(YAN.t[:, 0:3, :].rearrange("p a c -> p (a c)").bitcast(F32), YAN.tk)
            sqs[1] = _A(YAN.t[:, 3:5, :].rearrange("p a c -> p (a c)").bitcast(F32), YAN.tk)

            def load_w(dst, src, c0, ncols, dcol=0):
                for k in range(8):
                    st_ = wst[k % 2]
                    s.dma("sp" if k % 2 == 0 else "act", st_[:, 0:ncols], src[k * 128:(k + 1) * 128, c0:c0 + ncols])
                    s.cp(dst[:, k, dcol:dcol + ncols], st_[:, 0:ncols], e="dve" if k % 2 == 0 else "pool")

            load_w(woutb, wout[l], 0, 1024)
            kT = s.sb([128, 8448], BF16, name="kT")
            Vg = s.sb([128, NKT, 2, 65], BF16, name="Vg")
            qT = [s.sb([128, 2304], BF16, name="qT%d" % i) for i in range(3)]
            nqT = [s.sb([128, 2304], BF16, name="nqT%d" % i) for i in range(2)]
            nkT = [s.sb([128, 2816], BF16, name="nkT%d" % i) for i in range(2)]
            Vn = s.sb([128, NNT, 4, 65], BF16, name="Vn")
            s.memset(Vg[:, :, :, 64:65], 1.0)
            s.memset(Vn[:, :, :, 64:65], 1.0)
            hT = [s.sb([128, 8, 128], BF16, name="hT%d" % i) for i in range(2)]
            tcs = [s.sb([128, 384], name="tcos%d" % i) for i in range(2)]
            tsns = [s.sb([128, 384], name="tsin%d" % i) for i in range(2)]
            sms = [s.sb([128, 64], name="sm%d" % i) for i in range(2)]
            tc_, tsn, sm = tcs[0], tsns[0], sms[0]
            cnt = [0]

            def front(srcT, row, j, w=None):
                if w is None:
                    i = cnt[0]; cnt[0] += 1
                    w = i % 2
                    banks = (pb[0], pb[1])
                else:
                    banks = (pb[w], pb[w])
                q = "sp" if w == 0 else "act"
                x_ = xt[w]
                ht_ = hts[w]
                s.dma(q, x_[:], V(srcT.t[row:row + 128, :], srcT.tk))
                hb_ = hbs[w]
                s.tt(ht_[:], x_[:], mod[j][:, 1024:2048], ALU.mult, e="pool")
                s.tt(hb_[:], ht_[:], mod[j][:, 0:1024], ALU.add, e="dve")
                h_ = hT[w]
                for half in range(2):
                    p = banks[half]
                    with s.atomic():
                        pbf = p.t[:, 0:256].bitcast(BF16)
                        for k in range(4):
                            kk_ = half * 4 + k
                            s.tr(V(pbf[:, k * 128:(k + 1) * 128], p.tk), hb_[:, kk_ * 128:(kk_ + 1) * 128], identb[:])
                        s.cp(h_[:, half * 4:half * 4 + 4, :], V(pbf.rearrange("p (k t) -> p k t", k=4), p.tk),
                             e="act" if half == 0 else "dve")
                return x_, h_

            def proj(h_, c0, ncols, dst, wsrc=None, w=None):
                wsrc = wsrc or wbuf
                o = 0
                bi = 2
                while o < ncols:
                    n = min(512, ncols - o)
                    p = pb[bi] if w is None else pb[2 + w]
                    with s.atomic():
                        for k in range(8):
                            s.mm(p[:, 0:n], h_[:, k, :], wsrc[:, k, c0 + o:c0 + o + n], start=(k == 0), stop=(k == 7))
                        s.cp(dst[:, o:o + n], p[:, 0:n], e="act" if bi == 2 else "dve")
                    o += n
                    bi = 5 - bi

            def rms_rope(src, H, gtab, scale_mode, rope, dst, w=0):
                sq = sqs[w]; sm = sms[w]; tc_ = tcs[w]; tsn = tsns[w]
                s.act(sq[:, 0:H * 64], src, AF.Square)
                s.red(sm[:, 0:H], V(sq.t[:, 0:H * 64].rearrange("p (h d) -> p h d", d=64), sq.tk), ALU.add)
                if scale_mode == "k":
                    s.ts(sm[:, 0:H], sm[:, 0:H], 1.0 / 64, ALU.mult, 1e-6, ALU.add)
                else:
                    s.ts(sm[:, 0:H], sm[:, 0:H], 64e-6, ALU.add)
                s.act(sm[:, 0:H], sm[:, 0:H], AF.Sqrt)
                s.recip(sm[:, 0:H], sm[:, 0:H])
                for h in range(H):
                    s.stt(V(dst.ap[:, h * 64:(h + 1) * 64], dst.tk), V(src.ap[:, h * 64:(h + 1) * 64], src.tk), sm[:, h:h + 1],
                          gtab[:, h * 64:(h + 1) * 64], ALU.mult, ALU.mult)
                if rope is not None:
                    cos_d, sin_d, rowfn = rope
                    s.dma("sp", tc_[:, 0:H * 64], cos_d[rowfn:rowfn + 128, :])
                    s.dma("act", tsn[:, 0:H * 64], sin_d[rowfn:rowfn + 128, :])
                    t1 = sq
                    v4 = lambda ap: ap.rearrange("p (g a d) -> p g a d", a=2, d=16)
                    d4 = v4(dst.ap); s4 = v4(tsn.t[:, 0:H * 64]); t4 = v4(t1.t[:, 0:H * 64])
                    s.tt(V(t4[:, :, 0, :], t1.tk), V(d4[:, :, 1, :], dst.tk), V(s4[:, :, 0, :], tsn.tk), ALU.mult, e="pool")
                    s.tt(V(t4[:, :, 1, :], t1.tk), V(d4[:, :, 0, :], dst.tk), V(s4[:, :, 1, :], tsn.tk), ALU.mult, e="pool")
                    s.tt(dst, dst, tc_[:, 0:H * 64], ALU.mult)
                    s.tt(dst, dst, t1[:, 0:H * 64], ALU.add)

            pt = [s.sb([128, 512], BF16, name="pt%d" % i) for i in range(3)]
            rcp = s.sb([128, 8], name="rcp")
            bias_t = [s.sb([128, 768], name="bias%d" % i) for i in range(2)]
            hbs.extend([_A(bias_t[i].t[:, 0:512].bitcast(BF16), bias_t[i].tk) for i in range(2)])
            ptc = [0]

            load_w(wbuf, wb[l], 768, 256)
            load_w(wbuf, wb[l], 1664, 512, dcol=256)
            kst = [_A(YAN.t[:, 5 + i, 0:128], YAN.tk) for i in range(2)]
            vst = [_A(YAN.t[:, 7 + i, 0:130].rearrange("p (g d) -> p g d", d=65), YAN.tk) for i in range(2)]
            for v_ in vst:
                s.memset(v_[:, :, 64:65], 1.0)

            def k_body(t):
                w = t % 2
                ctx_t = t < 2
                j = 1 if ctx_t else 0
                if ctx_t:
                    x_, h_ = front(Xc, 128 * t, j, w)
                else:
                    x_, h_ = front(XSc, 256 + 128 * (t - 2), j, w)
                pj = pe[0] if w == 0 else pjB
                proj(h_, 0, 768, pj, w=w)
                tn = 20 + t if ctx_t else t
                for pr_ in range(2):
                    p2 = pb[6 + w]
                    with s.atomic():
                        s.tr(p2[:, 0:128], pj[:, 256 + pr_ * 128:256 + (pr_ + 1) * 128], ident[:])
                        s.cp(nkT[pr_][:, tn * 128:(tn + 1) * 128], p2[:, 0:128], e="act" if pr_ == 0 else "dve")
                s.cp(Vn[:, tn, :, 0:64], V(pj.t[:, 512:768].rearrange("p (g d) -> p g d", d=64), pj.tk), e="pool")
                rope = None if ctx_t else (cosQ[:, 0:128], sinQ[:, 0:128], (t - 2) * 128)
                kr = hts[w]
                rms_rope(pj[:, 0:128], 2, GK, "k", rope, kr[:, 0:128], w)
                p = pb[6 + w]
                vsrc = V(pj.t[:, 128:256].rearrange("p (g d) -> p g d", d=64), pj.tk)
                if ctx_t:
                    with s.atomic():
                        s.tr(p[:, 0:128], kr[:, 0:128], ident[:])
                        s.cp(kT[:, t * 128:(t + 1) * 128], p[:, 0:128], e="act")
                    s.cp(Vg[:, t, :, 0:64], vsrc, e="pool")
                else:
                    i_ = t - 2
                    with s.atomic():
                        s.tr(p[:, 0:128], kr[:, 0:128], ident[:])
                        s.cp(kst[w][:], p[:, 0:128], e="act")
                    s.dma("sp" if w == 0 else "act", KTO[:, 128 * i_:128 * (i_ + 1)], kst[w][:])
                    s.cp(vst[w][:, :, 0:64], vsrc, e="pool")
                    s.dma("sp" if w == 0 else "act", VOd[128 * i_:128 * (i_ + 1), :], vst[w][:, :, :])

            ilB.run([(lambda t=t: k_body(t)) for t in range(18)], 2)
            s.allgather(KTO[:, :], KTG[:, :], GROUPS)
            s.allgather(VOd[:, :], VGd[:, :], GROUPS)
            for rho in range(4):
                s.dma("sp" if rho % 2 == 0 else "act", kT[:, 256 + 2048 * rho:256 + 2048 * (rho + 1)], KTG[128 * rho:128 * (rho + 1), :])
                s.dma("act" if rho % 2 == 0 else "sp",
                      V(Vg.t[:, 2 + 16 * rho:2 + 16 * (rho + 1), :, :].rearrange("p i g d -> p i (g d)"), Vg.tk),
                      V(VGd.t[2048 * rho:2048 * (rho + 1), :].rearrange("(i p) c -> p i c", p=128), VGd.tk))

            def n_body(t):
                w = t % 2
                j = 0
                x_, h_ = front(XSc, 128 * t, j, w)
                pj = pe[0] if w == 0 else pjB
                proj(h_, 256, 512, pj, w=w)
                for pr_ in range(2):
                    p = pb[6 + w]
                    with s.atomic():
                        s.tr(p[:, 0:128], pj[:, pr_ * 128:(pr_ + 1) * 128], ident[:])
                        s.cp(nkT[pr_][:, t * 128:(t + 1) * 128], p[:, 0:128], e="act" if pr_ == 0 else "dve")
                s.cp(Vn[:, t, :, 0:64], V(pj.t[:, 256:512].rearrange("p (g d) -> p g d", d=64), pj.tk), e="pool")

            ilB.run([(lambda t=t: n_body(t)) for t in (0, 1, 18, 19)], 2)
            for sl in range(6):
                hh_ = (sl // 2) + 3 * (sl % 2)
                load_w(wbuf, wb[l], 384 + hh_ * 64, 64, dcol=sl * 64)
            load_w(wbuf, wb[l], 1408, 256, dcol=384)
            own_src = lambda t: ((Xc, 128 * (t - 16)) if t >= 16 else (XSc, 256 + 128 * t))

            def q_body(t):
                w = t % 2
                j = 1 if t >= 16 else 0
                x_, h_ = front(*own_src(t), j, w)
                pj = pe[0] if w == 0 else pjB
                proj(h_, 0, 640, pj, w=w)
                rope = None if t >= 16 else (cosQ, sinQ, 128 * t)
                qr = hts[w]
                rms_rope(pj[:, 0:384], 6, GQ, "q", rope, qr[:, 0:384], w)
                for pr_ in range(3):
                    p = pb[6 + w]
                    with s.atomic():
                        s.tr(p[:, 0:128], qr[:, pr_ * 128:(pr_ + 1) * 128], ident[:])
                        s.cp(qT[pr_][:, t * 128:(t + 1) * 128], p[:, 0:128], e="act" if pr_ % 2 == 0 else "dve")
                s.ts(qr[:, 384:640], pj[:, 384:640], 0.125, ALU.mult, e="pool")
                for pr_ in range(2):
                    p = pb[6 + w]
                    with s.atomic():
                        s.tr(p[:, 0:128], qr[:, 384 + pr_ * 128:384 + (pr_ + 1) * 128], ident[:])
                        s.cp(nqT[pr_][:, t * 128:(t + 1) * 128], p[:, 0:128], e="act" if pr_ == 0 else "dve")

            ilB.run([(lambda t=t: q_body(t)) for t in range(NOWN)], 2)

            def attend(qsrc, g, qc0, nq, chunks, ksrc, vsrc_fn, bias_fn, dst_fn):
                nqs = nq // 128
                lo, hi = 64 * g, 64 * g + 64
                n_ = len(chunks)
                for ci in range(n_ + 1):
                    if ci < n_:
                        kc0, nk, cid = chunks[ci]
                        ps = pb[ci % 3]
                        bsrc = bias_fn(cid, nk) if bias_fn else None
                        s.mm(ps[0:nk, 0:nq], ksrc[lo:hi, kc0:kc0 + nk], qsrc[lo:hi, qc0:qc0 + nq], start=True, stop=(bsrc is None))
                        if bsrc is not None:
                            s.mm(ps[0:nk, 0:nq], bsrc, ident[:, 0:nq], start=False, stop=True)
                    if ci >= 1:
                        kc0p, nkp, cidp = chunks[ci - 1]
                        p_ = pt[ptc[0] % 3]; ptc[0] += 1
                        s.act(p_[0:nkp, 0:nq], pb[(ci - 1) % 3][0:nkp, 0:nq], AF.Exp)
                        for qs in range(nqs):
                            s.mm(pb[4 + qs][:, 0:65], p_[0:nkp, qs * 128:(qs + 1) * 128], vsrc_fn(cidp, nkp),
                                 start=(ci == 1), stop=(ci == n_))
                for qs in range(nqs):
                    s.recip(rcp[:, qs:qs + 1], pb[4 + qs][:, 64:65])
                    s.ts(dst_fn(qs), pb[4 + qs][:, 0:64], rcp[:, qs:qs + 1], ALU.mult)

            oT = _A(pe[0].t[0:65, 0:512], pe[0].tk)
            fin = [0]

            def attend_T(qsrc, g, qc0, nq, chunks, ksrc, vsrc_fn, dst_fn):
                nqs = nq // 128
                lo, hi = 64 * g, 64 * g + 64
                po = pb[4]
                n_ = len(chunks)
                for ci in range(n_ + 1):
                    if ci < n_:
                        kc0, nk, cid = chunks[ci]
                        s.mm(pb[ci % 3][0:nk, 0:nq], ksrc[lo:hi, kc0:kc0 + nk], qsrc[lo:hi, qc0:qc0 + nq])
                    if ci >= 1:
                        kc0p, nkp, cidp = chunks[ci - 1]
                        p_ = pt[ptc[0] % 3]; ptc[0] += 1
                        s.act(p_[0:nkp, 0:nq], pb[(ci - 1) % 3][0:nkp, 0:nq], AF.Exp)
                        s.mm(po[0:65, 0:nq], vsrc_fn(cidp, nkp), p_[0:nkp, 0:nq], start=(ci == 1), stop=(ci == n_))
                s.cp(oT[:, 0:nq], po[0:65, 0:nq], e="dve")
                for qs in range(nqs):
                    pf = pb[5 + fin[0] % 3]; fin[0] += 1
                    s.tr(pf[:, 0:65], oT[:, qs * 128:(qs + 1) * 128], ident[0:65, 0:65])
                    s.recip(rcp[:, qs:qs + 1], pf[:, 64:65])
                    s.ts(dst_fn(qs), pf[:, 0:64], rcp[:, qs:qs + 1], ALU.mult)

            def attend_T2(qsrc, qc0, nq, chunks, ksrc, vsrc_fn, dst_fn):
                nqs = nq // 128
                n_ = len(chunks)
                po = [pb[4], pb[5]]
                for ci in range(n_ + 1):
                    if ci < n_:
                        kc0, nk, cid = chunks[ci]
                        for g in range(2):
                            lo, hi = 64 * g, 64 * g + 64
                            s.mm(pb[(2 * ci + g) % 4][0:nk, 0:nq], ksrc[lo:hi, kc0:kc0 + nk], qsrc[lo:hi, qc0:qc0 + nq])
                    if ci >= 1:
                        kc0p, nkp, cidp = chunks[ci - 1]
                        for g in range(2):
                            p_ = pt[ptc[0] % 3]; ptc[0] += 1
                            s.act(p_[0:nkp, 0:nq], pb[(2 * (ci - 1) + g) % 4][0:nkp, 0:nq], AF.Exp)
                            s.mm(po[g][0:65, 0:nq], vsrc_fn(cidp, nkp, g), p_[0:nkp, 0:nq], start=(ci == 1), stop=(ci == n_))
                for g in range(2):
                    s.cp(oT[:, 0:nq], po[g][0:65, 0:nq], e="dve")
                    for qs in range(nqs):
                        pf = pb[6 + fin[0] % 2]; fin[0] += 1
                        s.tr(pf[:, 0:65], oT[:, qs * 128:(qs + 1) * 128], ident[0:65, 0:65])
                        s.recip(rcp[:, qs:qs + 1], pf[:, 64:65])
                        s.ts(dst_fn(qs, g), pf[:, 0:64], rcp[:, qs:qs + 1], ALU.mult)

            for qg in range(4):
                for pr_ in range(3):
                    attend_T2(qT[pr_], qg * 512, 512, [(c * 128, 128, c) for c in range(NKT)], kT,
                              lambda cid, nk, g: Vg[0:nk, cid, g, :],
                              lambda qs, g, qg=qg, pr_=pr_: YAN[:, qg * 4 + qs, (pr_ + 3 * g) * 64:(pr_ + 3 * g + 1) * 64])
            for h in range(6):
                pr_, g = h % 3, h // 3
                attend_T(qT[pr_], g, 2048, 256, [(c * 128, 128, c) for c in range(2)], kT,
                         lambda cid, nk, g=g: Vg[0:nk, cid, g, :],
                         lambda qs, h=h: YAN[:, 16 + qs, h * 64:(h + 1) * 64])
            def attend_na2(pr_, i, chunks, bts, base):
                qsrc, ksrc = nqT[pr_], nkT[pr_]
                qc0 = i * 128
                n_ = len(chunks)
                po = [pb[4], pb[5]]
                for ci in range(n_ + 1):
                    if ci < n_:
                        kc0, nk, cid = chunks[ci]
                        for g in range(2):
                            lo, hi = 64 * g, 64 * g + 64
                            ps = pb[(2 * ci + g) % 4]
                            has_b = cid < 20
                            s.mm(ps[0:nk, 0:128], ksrc[lo:hi, kc0:kc0 + nk], qsrc[lo:hi, qc0:qc0 + 128], start=True, stop=not has_b)
                            if has_b:
                                s.mm(ps[0:nk, 0:128], bts[g][:, cid * 128 - base:cid * 128 - base + nk], ident[:, 0:128],
                                     start=False, stop=True)
                    if ci >= 1:
                        kc0p, nkp, cidp = chunks[ci - 1]
                        for g in range(2):
                            hn = 2 * pr_ + g
                            p_ = pt[ptc[0] % 3]; ptc[0] += 1
                            s.act(p_[0:nkp, 0:128], pb[(2 * (ci - 1) + g) % 4][0:nkp, 0:128], AF.Exp)
                            s.mm(po[g][:, 0:65], p_[0:nkp, 0:128], Vn[0:nkp, cidp, hn, :], start=(ci == 1), stop=(ci == n_))
                for g in range(2):
                    hn = 2 * pr_ + g
                    s.recip(rcp[:, g:g + 1], po[g][:, 64:65])
                    s.ts(YAN[:, i, 384 + hn * 64:384 + (hn + 1) * 64], po[g][:, 0:64], rcp[:, g:g + 1], ALU.mult)

            nab_t = [s.sb([128, 768], name="nabx%d" % i) for i in range(2)] if False else None
            for i in range(16):
                chs = na_chunks(i)
                base = chs[0][0] * 128
                cls = na_class(i)
                nkeys = sum(nk for _, nk in chs)
                chunks = [(c * 128, nk, c) for c, nk in chs] + [(20 * 128, 128, 20), (21 * 128, 128, 21)]
                for pr_ in range(2):
                    bts = []
                    for g in range(2):
                        hn = 2 * pr_ + g
                        bt_ = bias_t[g]
                        s.dma("sp" if g == 0 else "act", bt_[:, 0:nkeys], nab[l, cls, hn, :, 0:nkeys])
                        bts.append(bt_)
                    attend_na2(pr_, i, chunks, bts, base)
            for hn in range(4):
                pr_, g = hn // 2, hn % 2
                attend(nqT[pr_], g, 2048, 256, [(20 * 128, 128, 20), (21 * 128, 128, 21)], nkT[pr_],
                       lambda cid, nk, hn=hn: Vn[0:nk, cid, hn, :], None,
                       lambda qs, hn=hn: YAN[:, 16 + qs, 384 + hn * 64:384 + (hn + 1) * 64])
            load_w(wbuf, wb[l], 0, 384)
            load_w(wbuf, wb[l], 1024, 384, dcol=384)
            load_w(wbuf, wb[l], 2176, 256, dcol=768)
            LNG = _A(kT.t[:, 0:2048].bitcast(F32), kT.tk); s.dma("sp", LNG[:], lng[l])
            LNB = _A(kT.t[:, 2048:4096].bitcast(F32), kT.tk); s.dma("act", LNB[:], lnb[l])
            Ff = _A(kT.t[:, 4096:5632].bitcast(F32).rearrange("p (s c) -> p s c", s=2), kT.tk)
            Bk = _A(kT.t[:, 5632:7168].bitcast(F32), kT.tk)
            cen = _A(kT.t[:, 7168:7936].bitcast(F32), kT.tk)
            ysb = _A(qT[1].t[:, 0:768].bitcast(F32), qT[1].tk)
            bsb = _A(qT[1].t[:, 768:1536].bitcast(F32), qT[1].tk)
            sqb = _A(qT[2].t[:, 0:768].bitcast(F32), qT[2].tk)
            YgT = _A(qT[0].t[:, 0:1024].rearrange("p (k t) -> p k t", k=8), qT[0].tk)
            for t in range(NOWN):
                j = 1 if t >= 16 else 0
                x_, h_ = front(*own_src(t), j)
                G = pe[0]
                proj(h_, 0, 1024, G)
                s.act(G[:], G[:], AF.Silu)
                for sr in range(2):
                    q = "sp" if sr == 0 else "act"
                    if t >= 16:
                        s.dma(q, Ff[:, sr, :], YGc[sr * 256 + 128 * (t - 16):sr * 256 + 128 * (t - 16) + 128, :])
                        rb = (2 + sr) * 256 + 128 - 128 * (t - 16)
                        s.dma(q, Bk[:, sr * 384:(sr + 1) * 384], YGc[rb:rb + 128, :])
                    else:
                        s.dma(q, Ff[:, sr, :], YF[sr][128 * t:128 * t + 128, :])
                        s.dma(q, Bk[:, sr * 384:(sr + 1) * 384], YBk[sr][1920 - 128 * t:1920 - 128 * t + 128, :])
                for sr in range(2):
                    s.mm(pb[6 + sr][:, 0:384], jmat[:], Bk[:, sr * 384:(sr + 1) * 384])
                v2 = lambda A_: V(A_.t[:, 0:384].rearrange("p (s c) -> p s c", s=2), A_.tk)
                for sr in range(2):
                    s.tt(ysb[:, sr * 192:(sr + 1) * 192], Ff[:, sr, 0:192], pb[6 + sr][:, 0:192], ALU.add)
                    s.tt(bsb[:, sr * 192:(sr + 1) * 192], Ff[:, sr, 192:384], pb[6 + sr][:, 192:384], ALU.add)
                v3 = lambda A_: V(A_.t[:, 0:384].rearrange("p (h d) -> p h d", d=64), A_.tk)
                s.red(sm[:, 0:6], v3(ysb), ALU.add)
                s.ts(sm[:, 0:6], sm[:, 0:6], 1.0 / 64, ALU.mult)
                s.tt(v3(cen), v3(ysb), V(sm.t[:, 0:6].unsqueeze(2).to_broadcast([128, 6, 64]), sm.tk), ALU.subtract)
                s.tt(sqb[:], cen[:], cen[:], ALU.mult, e="pool")
                s.red(sm[:, 8:14], v3(sqb), ALU.add)
                s.ts(sm[:, 8:14], sm[:, 8:14], 1.0 / 64, ALU.mult, 64e-5, ALU.add)
                s.act(sm[:, 8:14], sm[:, 8:14], AF.Sqrt)
                s.recip(sm[:, 8:14], sm[:, 8:14])
                s.tt(v3(cen), v3(cen), V(sm.t[:, 8:14].unsqueeze(2).to_broadcast([128, 6, 64]), sm.tk), ALU.mult)
                s.tt(cen[:], cen[:], GNG[:], ALU.mult)
                s.tt(cen[:], cen[:], GNB[:], ALU.add)
                s.tt(cen[:], cen[:], bsb[:], ALU.add)
                Yg = pe[1]
                s.tt(Yg[:, 0:384], cen[:], G[:, 0:384], ALU.mult)
                s.tt(Yg[:, 384:1024], YAN[:, t, :], G[:, 384:1024], ALU.mult)
                for half in range(2):
                    p = pb[half]
                    for k in range(4):
                        kk_ = half * 4 + k
                        s.tr(p[:, k * 128:(k + 1) * 128], Yg[:, kk_ * 128:(kk_ + 1) * 128], ident[:])
                    s.cp(YgT[:, half * 4:half * 4 + 4, :], V(p.t[:, :].rearrange("p (k t) -> p k t", k=4), p.tk),
                         e="act" if half == 0 else "dve")
                yo = pe[0]
                proj(YgT, 0, 1024, yo, wsrc=woutb)
                s.tt(yo[:], yo[:], mod[j][:, 2048:3072], ALU.mult)
                z = pe[1]
                s.stt(z[:], x_[:], ALPHA, yo[:], ALU.mult, ALU.add)
                s.red(sm[:, 16:17], z[:], ALU.add)
                s.ts(sm[:, 16:17], sm[:, 16:17], 1.0 / 1024, ALU.mult)
                s.ts(z[:], z[:], sm[:, 16:17], ALU.subtract)
                s.tt(yo[:], z[:], z[:], ALU.mult, e="pool")
                s.red(sm[:, 17:18], yo[:], ALU.add)
                s.ts(sm[:, 17:18], sm[:, 17:18], 1.0 / 1024, ALU.mult, 1e-5, ALU.add)
                s.act(sm[:, 17:18], sm[:, 17:18], AF.Sqrt)
                s.recip(sm[:, 17:18], sm[:, 17:18])
                s.stt(z[:], z[:], sm[:, 17:18], LNG[:], ALU.mult, ALU.mult)
                s.tt(z[:], z[:], LNB[:], ALU.add)
                q = "sp" if t % 2 == 0 else "act"
                if t < 16:
                    if last:
                        tickets.append(s.dma(q, out[128 * t:128 * t + 128, :], z[:]))
                    else:
                        s.dma(q, XN[128 * t:128 * t + 128, :], z[:])
                        if t % 2 == 1:
                            k = t // 2
                            s.allgather(XN[256 * k:256 * (k + 1), :], Xn[512 + 1024 * k:512 + 1024 * (k + 1), :], GROUPS)
                elif not last:
                    s.dma(q, Xn[128 * (t - 16):128 * (t - 16) + 128, :], z[:])
    s.finish(tickets)
    s.close()
    return nc


def rope_tables():
    t = np.arange(8192)
    row = (t // 64).astype(np.float32); col = (t % 64).astype(np.float32)
    inv = (10000.0 ** (-np.arange(16, dtype=np.float32) / 16)).astype(np.float32)
    ar = row[:, None] * inv; ac = col[:, None] * inv
    ang = np.concatenate([ar, ar, ac, ac], axis=-1).astype(np.float32)
    cos = np.cos(ang).astype(np.float32); sin = np.sin(ang).astype(np.float32)
    sgn = np.concatenate([-np.ones(16), np.ones(16), -np.ones(16), np.ones(16)]).astype(np.float32)
    return cos, sin * sgn


def na_bias(rpb, j):
    NEG = -30000.0
    out = np.full((5, 4, 128, 768), NEG, np.float32)
    tiles = {0: 0, 1: 1, 2: 7, 3: 14, 4: 15}
    for cls, i in tiles.items():
        chs = na_chunks(i)
        srow0 = chs[0][0] * 2
        nkeys = sum(nk for _, nk in chs)
        qrow_l = np.repeat(np.array([2 * i, 2 * i + 1]), 64)
        qcol = np.tile(np.arange(64), 2)
        r = 32 * j + qrow_l
        r_start = np.clip(r - 4, 0, 120)
        c_start = np.clip(qcol - 8, 0, 48)
        key = np.arange(nkeys)
        krow = (32 * j - 4) + srow0 + key // 64
        kcol = key % 64
        dr = krow[None, :] - r[:, None] + 7
        dc = kcol[None, :] - qcol[:, None] + 15
        inwin = ((krow[None, :] >= r_start[:, None]) & (krow[None, :] < r_start[:, None] + 8) &
                 (kcol[None, :] >= c_start[:, None]) & (kcol[None, :] < c_start[:, None] + 16))
        drc = np.clip(dr, 0, 14); dcc = np.clip(dc, 0, 30)
        for h in range(4):
            vals = rpb[h][drc, dcc]
            out[cls, h, :, 0:nkeys] = np.where(inwin, vals, NEG)
    return out


def consts_A():
    idx = np.arange(64)
    inclT = (idx[:, None] <= idx[None, :]).astype(np.float32)
    strictT = (idx[:, None] < idx[None, :]).astype(np.float32)
    strict = strictT.T.copy()
    mask = np.concatenate([inclT, strictT, inclT, -strictT, -strict], axis=1)
    rmask = np.ones((64, 256), np.float32); rmask[:, 0::64] = 0.0
    ident = np.eye(64, dtype=np.float32)
    return np.concatenate([ident, mask, mask, mask, rmask] + [ident] * 6, axis=1).astype(np.float32)


def host_F(inp, depth=4):
    cos, sinS = rope_tables()
    bc = lambda v: np.ascontiguousarray(np.broadcast_to(v[None, :], (128, v.shape[0]))).astype(np.float32)
    L = range(depth)
    shared = dict(
        wmod=np.ascontiguousarray(inp['w_mod'][:depth]),
        bmod=np.stack([bc(inp['b_mod'][l]) for l in L]),
        wb=np.ascontiguousarray(inp['w_in'][:depth, :, 1280:]),
        wout=np.ascontiguousarray(inp['w_out'][:depth]),
        cstA=consts_A(),
        cosK=np.tile(cos, (1, 2)), sinK=np.tile(sinS, (1, 2)),
        gk=np.stack([bc(np.tile(inp['gqa_k_norm'][l], 2)) for l in L]),
        gq=np.stack([bc(np.tile(inp['gqa_q_norm'][l], 6)) for l in L]),
        gng=np.stack([bc(inp['rwkv_gn_g'][l]) for l in L]), gnb=np.stack([bc(inp['rwkv_gn_b'][l]) for l in L]),
        lng=np.stack([bc(inp['ln_g'][l]) for l in L]), lnb=np.stack([bc(inp['ln_b'][l]) for l in L]),
        ident=np.eye(128, dtype=np.float32), jmat=np.ascontiguousarray(np.eye(128, dtype=np.float32)[::-1]),
    )
    per_batch = []
    for b in range(2):
        xa0 = np.zeros((XROWS, 1024), np.float32)
        xa0[0:256] = inp['ctx'][b]
        xa0[512:8704] = inp['x'][b].reshape(4, 8, 256, 1024).transpose(1, 0, 2, 3).reshape(8192, 1024)
        cvec = np.stack([inp['c'][b], inp['c_ctx']], axis=1)
        cv = np.ascontiguousarray(cvec.reshape(8, 128, 2).transpose(1, 0, 2).reshape(128, 16))
        per_batch.append(dict(xa0=xa0, cv=cv))
    nabs = [np.stack([na_bias(inp['na_rpb'][l], j) for l in L]) for j in range(4)]
    maps = []
    for c in range(8):
        b, r = c // 4, c % 4
        d, hh = r // 2, r % 2
        heads = [3 * hh + i for i in range(3)]
        cols = []
        for comp in (0, 384, 768):
            for h in heads:
                cols += list(range(comp + h * 64, comp + h * 64 + 64))
        cols += list(range(1152 + 32 * d, 1152 + 32 * d + 32))
        cols += list(range(1216 + 32 * d, 1216 + 32 * d + 32))
        cols = np.array(cols)
        hcols = np.concatenate([np.arange(h * 64, (h + 1) * 64) for h in heads])
        wa = np.ascontiguousarray(inp['w_in'][:depth][:, :, cols])
        cw = np.zeros((depth, 64, 33), np.float32); pv = np.zeros((depth, 64, 15), np.float32)
        for l in L:
            conv = inp['rwkv_conv'][l][:, cols]
            if d == 1:
                conv = conv[::-1]
            for ci in range(9):
                cw[l, :, ci * 3:ci * 3 + 3] = conv[:, ci * 64:(ci + 1) * 64].T
            cw[l, 0:32, 27:30] = conv[:, 576:608].T
            cw[l, 0:32, 30:33] = conv[:, 608:640].T
            for i, h in enumerate(heads):
                hs = slice(h * 64, (h + 1) * 64)
                pv[l, :, i * 5 + 0] = inp['decay_w0'][l][d, hs]
                pv[l, :, i * 5 + 1] = inp['iclr_a0'][l][d, hs]
                pv[l, :, i * 5 + 2] = inp['rwkv_k_k'][l][hs]
                pv[l, :, i * 5 + 3] = inp['rwkv_k_a'][l][hs]
                pv[l, :, i * 5 + 4] = inp['rwkv_r_k'][l][h]
        w2 = np.ascontiguousarray(inp['decay_w2'][:depth, d][:, :, hcols])
        a2 = np.ascontiguousarray(inp['iclr_a2'][:depth, d][:, :, hcols])
        own = slice(2048 * r, 2048 * r + 2048)
        jd = np.eye(128, dtype=np.float32)
        if d == 1:
            jd = np.ascontiguousarray(jd[::-1])
        dsel = np.zeros((128, 2), np.float32); dsel[:, d] = 1.0
        xs0 = np.zeros((2560, 1024), np.float32)
        lo = 2048 * r - 256
        for sr_ in range(2560):
            pass
        a0, a1 = max(lo, 0), min(lo + 2560, 8192)
        xs0[a0 - lo:a1 - lo] = inp['x'][b][a0:a1]
        m = dict(shared)
        m['xs0'] = xs0
        m.update(per_batch[b])
        m.update(wa=wa, cw=cw, pv=pv, w2=w2, a2=a2, nab=nabs[r],
                 cosQ=np.tile(cos[own], (1, 6)), sinQ=np.tile(sinS[own], (1, 6)), jd=jd, dsel=dsel)
        maps.append(m)
    return maps


from concourse.bass_utils import run_bass_kernel_spmd

_NC = {}


def kernel(**inputs):
    inp = {k: np.asarray(v, dtype=np.float32) for k, v in inputs.items()}
    if 'F' not in _NC:
        _NC['F'] = build_F(4)
    maps = host_F(inp, 4)
    res = run_bass_kernel_spmd(_NC['F'], maps, core_ids=list(range(8))).results
    x = np.stack([np.concatenate([res[b * 4 + r]["out"] for r in range(4)], axis=0) for b in range(2)])
    return np.ascontiguousarray(x.astype(np.float32))
```
